# Optimizing a Trainium2 kernel written in Bass

```python
import math
import jax, jax.numpy as jnp
from jax import lax
import numpy as np

D_MODEL = 1024
BATCH = 16
SEQ = 256
DEPTH = 4
DEC_BATCH = 8
DEC_SEQ = 4096
PAST_LEN = 256

F32 = jnp.float32
EPS = 1e-6
NEG_INF = -1e30
GRID_W = 64
N_DIR = 2
N_HEADS = 8
N_KV_HEADS = 2
GQA = N_HEADS // N_KV_HEADS
HEAD_DIM = 64
SCALE = HEAD_DIM ** -0.5
WINDOW = 128
ATT_BLOCK = 128
ROPE_BASE = 10000.0
SSM_GROUP_CH = 16
SSM_GROUPS = 16
SSM_WIDTH = SSM_GROUPS * SSM_GROUP_CH
SSM_STATE = 64
GDN_HEADS = 4
GDN_DK = 64
GDN_DV = 64
GDN_K_W = GDN_HEADS * GDN_DK
GDN_V_W = GDN_HEADS * GDN_DV
GDN_CONV = 3
GDN_CHUNK = 64
ATT_Q_W = N_HEADS * HEAD_DIM
ATT_KV_W = N_KV_HEADS * HEAD_DIM
MIX_W = ATT_Q_W + SSM_WIDTH + GDN_V_W
D_FF = 4 * D_MODEL
IN_SIZES = (ATT_Q_W, ATT_KV_W, ATT_KV_W, SSM_WIDTH, GDN_K_W, GDN_K_W, GDN_V_W, GDN_V_W, N_DIR * GDN_HEADS, N_DIR * GDN_HEADS)
IN_W = sum(IN_SIZES)

kernel_name = 'hybrid_dit_s5_swa_gdn_step'


def rms_norm(x, g):
    xf = x.astype(F32)
    y = xf * lax.rsqrt(jnp.mean(xf * xf, axis=-1, keepdims=True) + EPS)
    return (y * g.astype(F32)).astype(x.dtype)


def l2_normalize(x):
    return x * lax.rsqrt(jnp.sum(x * x, axis=-1, keepdims=True) + EPS)


def split_columns(z):
    outs, start = [], 0
    for size in IN_SIZES:
        outs.append(z[..., start:start + size])
        start += size
    return outs


def axial_rope(x):
    B, L, H, hd = x.shape
    n_rows = L // GRID_W
    rows = jnp.repeat(jnp.arange(n_rows, dtype=F32), GRID_W)
    cols = jnp.tile(jnp.arange(GRID_W, dtype=F32), n_rows)
    n_freq = hd // 4
    inv_freq = jnp.power(ROPE_BASE, -jnp.arange(n_freq, dtype=F32) / n_freq)
    ang = jnp.concatenate([rows[:, None] * inv_freq, cols[:, None] * inv_freq], axis=-1)
    cos = jnp.cos(ang)[None, :, None, :]
    sin = jnp.sin(ang)[None, :, None, :]
    xf = x.astype(F32)
    x1, x2 = xf[..., 0::2], xf[..., 1::2]
    out = jnp.stack([x1 * cos - x2 * sin, x1 * sin + x2 * cos], axis=-1).reshape(B, L, H, hd)
    return out.astype(x.dtype)


def context_attention(q, k, v, sink):
    B, S = q.shape[:2]
    nb = S // ATT_BLOCK
    kf, vf = k.astype(F32), v.astype(F32)
    qb = q.astype(F32).reshape(B, nb, ATT_BLOCK, N_KV_HEADS, GQA, HEAD_DIM).transpose(1, 0, 2, 3, 4, 5)
    sink_col = sink.astype(F32).reshape(1, N_KV_HEADS, GQA, 1, 1)

    def block(q_blk):
        s = jnp.einsum('bqkgd,bskd->bkgqs', q_blk, kf) * SCALE
        logits = jnp.concatenate([s, jnp.broadcast_to(sink_col, s.shape[:-1] + (1,))], axis=-1)
        probs = jax.nn.softmax(logits, axis=-1)[..., :-1]
        return jnp.einsum('bkgqs,bskd->bqkgd', probs, vf)

    o = lax.map(block, qb)
    return o.transpose(1, 0, 2, 3, 4, 5).reshape(B, S, ATT_Q_W)


def latent_attention(q, k, v, ck, cv, sink):
    B, L = q.shape[:2]
    nb = L // ATT_BLOCK
    n_loc = 3 * ATT_BLOCK
    pad = ((0, 0), (ATT_BLOCK, ATT_BLOCK), (0, 0), (0, 0))
    kp = jnp.pad(k.astype(F32), pad)
    vp = jnp.pad(v.astype(F32), pad)
    ckf, cvf = ck.astype(F32), cv.astype(F32)
    qb = q.astype(F32).reshape(B, nb, ATT_BLOCK, N_KV_HEADS, GQA, HEAD_DIM).transpose(1, 0, 2, 3, 4, 5)
    sink_col = sink.astype(F32).reshape(1, N_KV_HEADS, GQA, 1, 1)
    q_off = jnp.arange(ATT_BLOCK)
    k_off = jnp.arange(n_loc) - ATT_BLOCK

    def block(args):
        i, q_blk = args
        start = i * ATT_BLOCK
        k_blk = lax.dynamic_slice_in_dim(kp, start, n_loc, axis=1)
        v_blk = lax.dynamic_slice_in_dim(vp, start, n_loc, axis=1)
        q_pos = start + q_off
        k_pos = start + k_off
        valid = (jnp.abs(q_pos[:, None] - k_pos[None, :]) <= WINDOW) & (k_pos >= 0)[None, :] & (k_pos < L)[None, :]
        s_loc = jnp.where(valid, jnp.einsum('bqkgd,bskd->bkgqs', q_blk, k_blk) * SCALE, NEG_INF)
        s_ctx = jnp.einsum('bqkgd,bskd->bkgqs', q_blk, ckf) * SCALE
        logits = jnp.concatenate([s_loc, s_ctx, jnp.broadcast_to(sink_col, s_loc.shape[:-1] + (1,))], axis=-1)
        probs = jax.nn.softmax(logits, axis=-1)
        return (jnp.einsum('bkgqs,bskd->bqkgd', probs[..., :n_loc], v_blk)
                + jnp.einsum('bkgqs,bskd->bqkgd', probs[..., n_loc:-1], cvf))

    o = lax.map(block, (jnp.arange(nb), qb))
    return o.transpose(1, 0, 2, 3, 4, 5).reshape(B, L, ATT_Q_W)


def s5_direction(u, lam_re, lam_im, log_step, b_re, b_im, h0_re, h0_im, reverse):
    step = jnp.exp(log_step)[:, None]
    mag = jnp.exp(lam_re * step)
    ang = lam_im * step
    ab_re, ab_im = mag * jnp.cos(ang), mag * jnp.sin(ang)
    den = lam_re * lam_re + lam_im * lam_im
    nr, ni = ab_re - 1.0, ab_im
    f_re = (nr * lam_re + ni * lam_im) / den
    f_im = (ni * lam_re - nr * lam_im) / den
    bb_re = f_re[..., None] * b_re - f_im[..., None] * b_im
    bb_im = f_re[..., None] * b_im + f_im[..., None] * b_re
    bu_re = jnp.einsum('blgc,gpc->blgp', u, bb_re)
    bu_im = jnp.einsum('blgc,gpc->blgp', u, bb_im)
    first = -1 if reverse else 0
    bu_re = bu_re.at[:, first].add(ab_re * h0_re - ab_im * h0_im)
    bu_im = bu_im.at[:, first].add(ab_re * h0_im + ab_im * h0_re)
    a_re = jnp.broadcast_to(ab_re, bu_re.shape)
    a_im = jnp.broadcast_to(ab_im, bu_im.shape)

    def combine(e1, e2):
        a1r, a1i, b1r, b1i = e1
        a2r, a2i, b2r, b2i = e2
        return (a2r * a1r - a2i * a1i, a2r * a1i + a2i * a1r,
                a2r * b1r - a2i * b1i + b2r, a2r * b1i + a2i * b1r + b2i)

    _, _, h_re, h_im = lax.associative_scan(combine, (a_re, a_im, bu_re, bu_im), reverse=reverse, axis=1)
    return h_re, h_im


def ssm_branch(u, p, h0, want_state):
    B, L, _ = u.shape
    uf = u.astype(F32).reshape(B, L, SSM_GROUPS, SSM_GROUP_CH)
    h0 = h0.astype(F32)
    y = uf * p['ssm_d'].astype(F32).reshape(SSM_GROUPS, SSM_GROUP_CH)
    finals = []
    for d in range(N_DIR):
        reverse = d == 1
        h_re, h_im = s5_direction(uf, p['ssm_lam_re'][d].astype(F32), p['ssm_lam_im'][d].astype(F32),
                                  p['ssm_log_step'][d].astype(F32), p['ssm_b_re'][d].astype(F32),
                                  p['ssm_b_im'][d].astype(F32), h0[:, d, 0], h0[:, d, 1], reverse)
        y = y + (jnp.einsum('blgp,gcp->blgc', h_re, p['ssm_c_re'][d].astype(F32))
                 - jnp.einsum('blgp,gcp->blgc', h_im, p['ssm_c_im'][d].astype(F32)))
        if want_state:
            last = 0 if reverse else -1
            finals.append(jnp.stack([h_re[:, last], h_im[:, last]], axis=1))
    z = jax.nn.gelu(y.reshape(B, L, SSM_WIDTH))
    out = z * jax.nn.sigmoid(z @ p['ssm_w_glu'].astype(F32) + p['ssm_b_glu'].astype(F32))
    return out, (jnp.stack(finals, axis=1) if want_state else None)


def centred_conv(x, w):
    K, L = w.shape[0], x.shape[1]
    half = K // 2
    xp = jnp.pad(x, ((0, 0), (half, K - 1 - half), (0, 0)))
    return sum(xp[:, j:j + L] * w[j] for j in range(K))


def gated_delta_chunked(q, k, v, g, beta, s0):
    B, L, H, _ = q.shape
    dv = v.shape[-1]
    C = GDN_CHUNK
    n = L // C

    def chunks(t):
        return t.reshape(B, n, C, H, -1).transpose(1, 0, 3, 2, 4)

    qc, kc, vc = chunks(q), chunks(k), chunks(v)
    gc = jnp.cumsum(g.reshape(B, n, C, H).transpose(1, 0, 3, 2), axis=-1)
    bc = beta.reshape(B, n, C, H).transpose(1, 0, 3, 2)[..., None]
    incl = jnp.tril(jnp.ones((C, C), bool))
    strict = jnp.tril(jnp.ones((C, C), bool), -1)
    decay = jnp.exp(jnp.where(incl, gc[..., :, None] - gc[..., None, :], -jnp.inf))
    kb = kc * bc
    lower = jnp.where(strict, jnp.einsum('nbhid,nbhjd->nbhij', kb, kc) * decay, 0.0)
    eye = jnp.broadcast_to(jnp.eye(C, dtype=F32), lower.shape)
    t_inv = lax.linalg.triangular_solve(eye + lower, eye, left_side=True, lower=True)
    u = t_inv @ (vc * bc)
    w = t_inv @ (kb * jnp.exp(gc)[..., None])

    def step(S, xs):
        q_i, k_i, u_i, w_i, g_i, d_i = xs
        v_new = u_i - jnp.einsum('bhck,bhkv->bhcv', w_i, S)
        attn = jnp.einsum('bhik,bhjk->bhij', q_i, k_i) * d_i
        o = jnp.einsum('bhck,bhkv->bhcv', q_i * jnp.exp(g_i)[..., None], S) + attn @ v_new
        g_last = g_i[..., -1]
        S = (S * jnp.exp(g_last)[..., None, None]
             + jnp.einsum('bhck,bhcv->bhkv', k_i * jnp.exp(g_last[..., None] - g_i)[..., None], v_new))
        return S, o

    s_final, o = lax.scan(step, s0, (qc, kc, u, w, gc, decay))
    return o.transpose(1, 0, 3, 2, 4).reshape(B, L, H, dv), s_final


def gdn_branch(gq, gk, gv, gz, ga, gb, p, s0):
    B, L, _ = gq.shape
    qkv = jax.nn.silu(centred_conv(jnp.concatenate([gq, gk, gv], axis=-1).astype(F32), p['gdn_conv_w'].astype(F32)))
    q = l2_normalize(qkv[..., :GDN_K_W].reshape(B, L, GDN_HEADS, GDN_DK)) * (GDN_DK ** -0.5)
    k = l2_normalize(qkv[..., GDN_K_W:2 * GDN_K_W].reshape(B, L, GDN_HEADS, GDN_DK))
    v = qkv[..., 2 * GDN_K_W:].reshape(B, L, GDN_HEADS, GDN_DV)
    a = ga.astype(F32).reshape(B, L, N_DIR, GDN_HEADS)
    b = gb.astype(F32).reshape(B, L, N_DIR, GDN_HEADS)
    a_log = p['gdn_a_log'].astype(F32)
    dt_bias = p['gdn_dt_bias'].astype(F32)
    s0 = s0.astype(F32)
    outs, finals = [], []
    for d in range(N_DIR):
        g = -jnp.exp(a_log[d]) * jax.nn.softplus(a[:, :, d] + dt_bias[d])
        beta = jax.nn.sigmoid(b[:, :, d])
        seqs = (q, k, v, g, beta)
        if d == 1:
            seqs = tuple(jnp.flip(t, axis=1) for t in seqs)
        o_d, s_d = gated_delta_chunked(*seqs, s0[:, d])
        if d == 1:
            o_d = jnp.flip(o_d, axis=1)
        outs.append(o_d)
        finals.append(s_d)
    o = rms_norm(outs[0] + outs[1], p['gdn_norm_g']) * jax.nn.silu(gz.astype(F32).reshape(B, L, GDN_HEADS, GDN_DV))
    return o.reshape(B, L, GDN_V_W), jnp.stack(finals, axis=1)


def trunk_layer(x, cond, p, ctx_kv, ssm_h0, gdn_s0):
    is_context = ctx_kv is None
    B, L, _ = x.shape
    mod = (jax.nn.silu(cond) @ p['w_mod'] + p['b_mod'])[:, None, :]
    sh_a, sc_a, g_a, sh_m, sc_m, g_m = jnp.split(mod, 6, axis=-1)
    h = rms_norm(x, p['norm1_g']) * (1.0 + sc_a) + sh_a
    q, k, v, u, gq, gk, gv, gz, ga, gb = split_columns(h @ p['w_in'])
    q = rms_norm(q.reshape(B, L, N_HEADS, HEAD_DIM), p['q_norm_g'])
    k = rms_norm(k.reshape(B, L, N_KV_HEADS, HEAD_DIM), p['k_norm_g'])
    v = v.reshape(B, L, N_KV_HEADS, HEAD_DIM)
    if is_context:
        attn = context_attention(q, k, v, p['attn_sink'])
    else:
        attn = latent_attention(axial_rope(q), axial_rope(k), v, ctx_kv[0], ctx_kv[1], p['attn_sink'])
    ssm_out, ssm_final = ssm_branch(u, p, ssm_h0, is_context)
    gdn_out, gdn_final = gdn_branch(gq, gk, gv, gz, ga, gb, p, gdn_s0)
    mixed = jnp.concatenate([attn.astype(x.dtype), ssm_out.astype(x.dtype), gdn_out.astype(x.dtype)], axis=-1) @ p['w_out']
    x = x + g_a * mixed
    h2 = rms_norm(x, p['norm2_g']) * (1.0 + sc_m) + sh_m
    x = x + g_m * (jnp.square(jax.nn.relu(h2 @ p['w_ff1'])) @ p['w_ff2'])
    if is_context:
        return x, (k, v, ssm_final, gdn_final)
    return x, None


def setup_inputs(seed: int = 0) -> dict:
    key = jax.random.key(seed)
    keys = iter(jax.random.split(key, 40))

    def normal(shape, scale):
        return scale * jax.random.normal(next(keys), shape, F32)

    def uniform(shape, lo, hi):
        return jax.random.uniform(next(keys), shape, F32, lo, hi)

    D = D_MODEL
    out = {}
    out['x_prompt'] = normal((BATCH, SEQ, D), 1.0)
    out['x_sample'] = normal((DEC_BATCH, DEC_SEQ, D), 1.0)
    out['c'] = normal((DEC_BATCH, D), 1.0)
    out['cache_k'] = normal((DEC_BATCH, DEPTH, PAST_LEN, N_KV_HEADS, HEAD_DIM), 1.0)
    out['cache_v'] = normal((DEC_BATCH, DEPTH, PAST_LEN, N_KV_HEADS, HEAD_DIM), 1.0)
    out['state_ssm'] = normal((DEC_BATCH, DEPTH, N_DIR, 2, SSM_GROUPS, SSM_STATE), 0.5)
    out['state_gdn'] = normal((DEC_BATCH, DEPTH, N_DIR, GDN_HEADS, GDN_DK, GDN_DV), 0.1)
    out['c_ctx'] = normal((D,), 1.0)
    out['norm1_g'] = 1.0 + normal((DEPTH, D), 0.01)
    out['norm2_g'] = 1.0 + normal((DEPTH, D), 0.01)
    out['w_mod'] = normal((DEPTH, D, 6 * D), 0.5 * D ** -0.5)
    out['b_mod'] = normal((DEPTH, 6 * D), 0.01)
    out['w_in'] = normal((DEPTH, D, IN_W), D ** -0.5)
    out['q_norm_g'] = 1.0 + normal((DEPTH, HEAD_DIM), 0.01)
    out['k_norm_g'] = 1.0 + normal((DEPTH, HEAD_DIM), 0.01)
    out['attn_sink'] = normal((DEPTH, N_HEADS), 0.5)
    out['ssm_lam_re'] = -0.5 + normal((DEPTH, N_DIR, SSM_GROUPS, SSM_STATE), 0.01)
    out['ssm_lam_im'] = jnp.pi * jnp.arange(SSM_STATE, dtype=F32) + normal((DEPTH, N_DIR, SSM_GROUPS, SSM_STATE), 0.01)
    out['ssm_log_step'] = uniform((DEPTH, N_DIR, SSM_GROUPS), math.log(1e-3), math.log(1e-1))
    out['ssm_b_re'] = normal((DEPTH, N_DIR, SSM_GROUPS, SSM_STATE, SSM_GROUP_CH), (0.5 / SSM_GROUP_CH) ** 0.5)
    out['ssm_b_im'] = normal((DEPTH, N_DIR, SSM_GROUPS, SSM_STATE, SSM_GROUP_CH), (0.5 / SSM_GROUP_CH) ** 0.5)
    out['ssm_c_re'] = normal((DEPTH, N_DIR, SSM_GROUPS, SSM_GROUP_CH, SSM_STATE), (0.5 / SSM_STATE) ** 0.5)
    out['ssm_c_im'] = normal((DEPTH, N_DIR, SSM_GROUPS, SSM_GROUP_CH, SSM_STATE), (0.5 / SSM_STATE) ** 0.5)
    out['ssm_d'] = normal((DEPTH, SSM_WIDTH), 1.0)
    out['ssm_w_glu'] = normal((DEPTH, SSM_WIDTH, SSM_WIDTH), SSM_WIDTH ** -0.5)
    out['ssm_b_glu'] = normal((DEPTH, SSM_WIDTH), 0.01)
    out['gdn_conv_w'] = normal((DEPTH, GDN_CONV, 2 * GDN_K_W + GDN_V_W), GDN_CONV ** -0.5)
    out['gdn_a_log'] = jnp.log(uniform((DEPTH, N_DIR, GDN_HEADS), 1.0, 16.0))
    dt = jnp.exp(uniform((DEPTH, N_DIR, GDN_HEADS), math.log(1e-3), math.log(1e-1)))
    out['gdn_dt_bias'] = dt + jnp.log(-jnp.expm1(-dt))
    out['gdn_norm_g'] = 1.0 + normal((DEPTH, GDN_DV), 0.01)
    out['w_out'] = normal((DEPTH, MIX_W, D), MIX_W ** -0.5)
    out['w_ff1'] = normal((DEPTH, D, D_FF), D ** -0.5)
    out['w_ff2'] = normal((DEPTH, D_FF, D), D_FF ** -0.5)
    return out


def reference(x_prompt, x_sample, c, cache_k, cache_v, state_ssm, state_gdn, c_ctx,
              norm1_g, norm2_g, w_mod, b_mod, w_in, q_norm_g, k_norm_g, attn_sink,
              ssm_lam_re, ssm_lam_im, ssm_log_step, ssm_b_re, ssm_b_im, ssm_c_re, ssm_c_im,
              ssm_d, ssm_w_glu, ssm_b_glu, gdn_conv_w, gdn_a_log, gdn_dt_bias, gdn_norm_g,
              w_out, w_ff1, w_ff2):
    def layer_params(l):
        return {'norm1_g': norm1_g[l], 'norm2_g': norm2_g[l], 'w_mod': w_mod[l], 'b_mod': b_mod[l],
                'w_in': w_in[l], 'q_norm_g': q_norm_g[l], 'k_norm_g': k_norm_g[l], 'attn_sink': attn_sink[l],
                'ssm_lam_re': ssm_lam_re[l], 'ssm_lam_im': ssm_lam_im[l], 'ssm_log_step': ssm_log_step[l],
                'ssm_b_re': ssm_b_re[l], 'ssm_b_im': ssm_b_im[l], 'ssm_c_re': ssm_c_re[l], 'ssm_c_im': ssm_c_im[l],
                'ssm_d': ssm_d[l], 'ssm_w_glu': ssm_w_glu[l], 'ssm_b_glu': ssm_b_glu[l],
                'gdn_conv_w': gdn_conv_w[l], 'gdn_a_log': gdn_a_log[l], 'gdn_dt_bias': gdn_dt_bias[l],
                'gdn_norm_g': gdn_norm_g[l], 'w_out': w_out[l], 'w_ff1': w_ff1[l], 'w_ff2': w_ff2[l]}

    n_ctx_req = x_prompt.shape[0]
    ssm_zero = jnp.zeros((n_ctx_req, N_DIR, 2, SSM_GROUPS, SSM_STATE), F32)
    gdn_zero = jnp.zeros((n_ctx_req, N_DIR, GDN_HEADS, GDN_DK, GDN_DV), F32)
    cond_ctx = c_ctx[None, :]
    xp = x_prompt
    ks, vs, ss, gs = [], [], [], []
    for l in range(DEPTH):
        xp, (k_l, v_l, s_l, g_l) = trunk_layer(xp, cond_ctx, layer_params(l), None, ssm_zero, gdn_zero)
        ks.append(k_l)
        vs.append(v_l)
        ss.append(s_l)
        gs.append(g_l)
    y_prompt = xp
    new_cache_k = jnp.stack(ks, axis=1)
    new_cache_v = jnp.stack(vs, axis=1)
    new_state_ssm = jnp.stack(ss, axis=1)
    new_state_gdn = jnp.stack(gs, axis=1)

    xs = x_sample
    for l in range(DEPTH):
        xs, _ = trunk_layer(xs, c, layer_params(l), (cache_k[:, l], cache_v[:, l]), state_ssm[:, l], state_gdn[:, l])
    y_sample = xs
    return (y_prompt, y_sample, new_cache_k, new_cache_v, new_state_ssm, new_state_gdn)
```

```python
import contextlib
import math
import numpy as np
import concourse.bass as bass
import concourse.mybir as mybir
from concourse.bass_utils import run_bass_kernel_spmd

F32 = mybir.dt.float32
BF16 = mybir.dt.bfloat16
AF = mybir.ActivationFunctionType
ALU = mybir.AluOpType
AX = mybir.AxisListType

D = 1024
KC = 8
EPS = 1e-6
IN_W = 2064
NCORES = 8


class Buf:
    __slots__ = ("name", "ap", "last_w", "readers")

    def __init__(self, ap=None, name=""):
        self.ap = ap
        self.name = name
        self.last_w = None
        self.readers = []

    def __getitem__(self, k):
        return self.ap[k]


class Ring:
    def __init__(self, bufs):
        self.bufs = bufs
        self.i = 0

    def next(self):
        b = self.bufs[self.i]
        self.i = (self.i + 1) % len(self.bufs)
        return b


class Prog:
    ENG = ("pe", "act", "dve", "pool", "sp")

    def __init__(self, nc, stack, n_dma_sems=40):
        self.nc = nc
        self.lists = {e: [] for e in self.ENG}
        self.count = {e: 0 for e in self.ENG}
        self.known = {e: {} for e in self.ENG}
        self.n_dma_sems = n_dma_sems
        self.dma_val = [0] * n_dma_sems
        self.dma_rr = 0
        self.dma_rr_pool = 0
        self.esem = {e: stack.enter_context(nc.semaphore("sem_" + e)) for e in self.ENG}
        self.dsem = [stack.enter_context(nc.semaphore("dsem%d" % i)) for i in range(n_dma_sems)]
        self.ninst = 0

    def _need(self, eng, tok, waits):
        if tok is None:
            return
        key = (tok[0], tok[1])
        if self.known[eng].get(key, 0) >= tok[2]:
            return
        if tok[2] > waits.get(key, 0):
            waits[key] = tok[2]

    def _collect(self, eng, reads, writes, pe_ok=False):
        waits = {}
        for b in reads:
            self._need(eng, b.last_w, waits)
        for b in writes:
            self._need(eng, b.last_w, waits)
            for r in b.readers:
                self._need(eng, r, waits)
        out = []
        for key, val in waits.items():
            if key == ('e', 'pe') and eng == 'pe':
                continue
            out.append((key, val))
            self.known[eng][key] = val
        return out

    def _mark(self, tok, reads, writes):
        for b in reads:
            b.readers.append(tok)
            if len(b.readers) > 16:
                best = {}
                for t in b.readers:
                    k = (t[0], t[1])
                    if k not in best or best[k][2] < t[2]:
                        best[k] = t
                b.readers = list(best.values())
        for b in writes:
            b.last_w = tok
            b.readers = []

    def op(self, eng, fn, reads=(), writes=()):
        for key, val in self._collect(eng, reads, writes):
            self.lists[eng].append(('w', key, val))
        self.count[eng] += 1
        self.lists[eng].append(('i', fn))
        tok = ('e', eng, self.count[eng])
        self._mark(tok, reads, writes)
        self.ninst += 1
        return tok

    def do(self, eng, meth, *args, reads=(), writes=(), **kw):
        return self.op(eng, lambda h: getattr(h, meth)(*args, **kw), reads=reads, writes=writes)

    def dma(self, eng, out_ap, in_ap, reads=(), writes=(), **kw):
        half = self.n_dma_sems // 2
        if eng == 'pool':
            idx = half + self.dma_rr_pool
            self.dma_rr_pool = (self.dma_rr_pool + 1) % (self.n_dma_sems - half)
        else:
            idx = self.dma_rr
            self.dma_rr = (self.dma_rr + 1) % half
        wl = self._collect(eng, reads, writes)
        prev = self.dma_val[idx]
        if prev > 0 and self.known[eng].get(('d', idx), 0) < prev:
            wl.append((('d', idx), prev))
            self.known[eng][('d', idx)] = prev
        for key, val in wl:
            self.lists[eng].append(('w', key, val))
        self.dma_val[idx] += 16
        self.lists[eng].append(('dma', idx, out_ap, in_ap, kw))
        tok = ('d', idx, self.dma_val[idx])
        self._mark(tok, reads, writes)
        self.ninst += 1
        return tok

    def barrier(self):
        for eng in self.ENG:
            for other in self.ENG:
                if other == eng:
                    continue
                v = self.count[other]
                if v > 0 and self.known[eng].get(('e', other), 0) < v:
                    self.lists[eng].append(('w', ('e', other), v))
                    self.known[eng][('e', other)] = v
            for idx in range(self.n_dma_sems):
                v = self.dma_val[idx]
                if v > 0 and self.known[eng].get(('d', idx), 0) < v:
                    self.lists[eng].append(('w', ('d', idx), v))
                    self.known[eng][('d', idx)] = v

    def flush(self):
        nc = self.nc
        lists = self.lists
        self.lists = {e: [] for e in self.ENG}
        if not any(lists.values()):
            return
        esem, dsem = self.esem, self.dsem
        with nc.Block() as block:
            def replay(eng, h):
                my = esem[eng]
                for item in lists[eng]:
                    k = item[0]
                    if k == 'w':
                        key, val = item[1], item[2]
                        h.wait_ge(esem[key[1]] if key[0] == 'e' else dsem[key[1]], val)
                    elif k == 'i':
                        item[1](h).then_inc(my, 1)
                    else:
                        _, idx, o, i, kw = item
                        h.dma_start(out=o, in_=i, **kw).then_inc(dsem[idx], 16)

            @block.tensor
            def _(h):
                replay('pe', h)

            @block.scalar
            def _(h):
                replay('act', h)

            @block.vector
            def _(h):
                replay('dve', h)

            @block.gpsimd
            def _(h):
                replay('pool', h)

            @block.sync
            def _(h):
                replay('sp', h)


class K:
    pass


def bcast(ap, shape):
    return ap.to_broadcast(shape)


def build(cfg, dbg=False):
    LS, LP, NP, DEPTH, PAST = cfg["LS"], cfg["LP"], cfg["NP"], cfg["DEPTH"], cfg["PAST"]
    nc = bass.Bass("TRN2", target_bir_lowering=False)
    k = K()
    k.nc, k.cfg = nc, cfg

    def din(name, shape, dt=F32):
        return nc.dram_tensor(name, list(shape), dt, kind="ExternalInput").ap()

    def dout(name, shape, dt=F32):
        return nc.dram_tensor(name, list(shape), dt, kind="ExternalOutput").ap()

    def dscr(name, shape, dt=F32):
        return nc.dram_tensor(name, list(shape), dt, kind="ExternalOutput" if dbg else "Internal").ap()

    I = {}
    I["x_s"] = din("x_s", [LS, D])
    I["x_p"] = din("x_p", [NP * LP, D])
    I["cond2"] = din("cond2", [128, 2, KC])
    I["cache_k"] = din("cache_k", [DEPTH, PAST, 128])
    I["cache_v"] = din("cache_v", [DEPTH, PAST, 128])
    I["ssm_h0"] = din("ssm_h0", [DEPTH, 128, 16, 2])
    I["gdn_s0"] = din("gdn_s0", [DEPTH, 64, 8, 64])
    I["w_mod"] = din("w_mod", [DEPTH, D, 6 * D])
    I["b_modT"] = din("b_modT", [DEPTH, 128, 48])
    I["n1g"] = din("n1g", [DEPTH, 128, KC])
    I["n2g"] = din("n2g", [DEPTH, 128, KC])
    I["w_in"] = din("w_in", [DEPTH, D, IN_W])
    I["qk_g"] = din("qk_g", [DEPTH, 128, 2, 64])
    I["sink_b"] = din("sink_b", [DEPTH, 128, 8])
    I["lam"] = din("lam", [DEPTH, 128, 3, 16])
    I["ssm_bt"] = din("ssm_bt", [DEPTH, 128, 16, 2, 128])
    I["ssm_ct"] = din("ssm_ct", [DEPTH, 128, 16, 2, 128])
    I["ssm_dT"] = din("ssm_dT", [DEPTH, 128, 2])
    I["glu_bT"] = din("glu_bT", [DEPTH, 128, 2])
    I["w_glu"] = din("w_glu", [DEPTH, 256, 256])
    I["conv_wT"] = din("conv_wT", [DEPTH, 128, 6, 3])
    I["gdn_ab"] = din("gdn_ab", [DEPTH, 64, 2, 8])
    I["gdn_ng"] = din("gdn_ng", [DEPTH, 128, 1])
    I["w_out"] = din("w_out", [DEPTH, D, D])
    I["w_ff1"] = din("w_ff1", [DEPTH, D, 4 * D])
    I["w_ff2"] = din("w_ff2", [DEPTH, 4 * D, D])
    I["ident"] = din("ident", [128, 128])
    I["rope"] = din("rope", [LS, 2, 32])
    I["amask"] = din("amask", [128, 2, 128])
    I["gmask"] = din("gmask", [64, 6, 8, 64])
    I["gtri"] = din("gtri", [64, 4, 64])
    k.I = I
    O = {}
    O["y_s"] = dout("y_s", [LS, D])
    O["y_p"] = dout("y_p", [NP * LP, D])
    O["nk"] = dout("nk", [NP, DEPTH, LP, 128])
    O["nv"] = dout("nv", [NP, DEPTH, LP, 128])
    O["nssm"] = dout("nssm", [NP, DEPTH, 2, 2, 16, 64])
    O["ngdn"] = dout("ngdn", [NP, DEPTH, 2, 4, 64, 64])
    k.O = O
    seqs = [dict(name="s", L=LS, ctx=False, xin=I["x_s"], y=O["y_s"], pi=-1, cond=0)]
    for i in range(NP):
        seqs.append(dict(name="p%d" % i, L=LP, ctx=True, xin=I["x_p"][i * LP:(i + 1) * LP, :],
                         y=O["y_p"][i * LP:(i + 1) * LP, :], pi=i, cond=1))
    for s in seqs:
        L = s["L"]
        n = s["name"]
        s["qT"] = dscr("qT_" + n, [4, 128, L], BF16)
        s["kT"] = dscr("kT_" + n, [2, 128, L], BF16)
        s["v"] = dscr("v_" + n, [L, 128], BF16)
        s["fT"] = dscr("fT_" + n, [10, 128, L], BF16)
        s["gates"] = dscr("gates_" + n, [64, L // 64, 16], F32)
        s["mixT"] = dscr("mixT_" + n, [8, 128, L], BF16)
        s["go"] = dscr("go_" + n, [2, L, 256], F32)
    k.seqs = seqs

    with contextlib.ExitStack() as top:
        p = Prog(nc, top)
        k.p = p

        uid = [0]

        def sbt(st, name, shape, dt):
            uid[0] += 1
            t = st.enter_context(nc.sbuf_tensor("sb%d_%s" % (uid[0], name), list(shape), dt))
            return Buf(t, name)

        def pst(st, name, shape, dt):
            uid[0] += 1
            t = st.enter_context(nc.psum_tensor("ps%d_%s" % (uid[0], name), list(shape), dt))
            return Buf(t, name)

        k.sbt, k.pst = sbt, pst
        k.ident = sbt(top, "ident", [128, 128], F32)
        k.identb = sbt(top, "identb", [128, 128], BF16)
        k.modT = sbt(top, "modT", [128, DEPTH, 48, 2], F32)
        k.ones_f = sbt(top, "ones_f", [128, 128], F32)
        k.ones_b = sbt(top, "ones_b", [128, 128], BF16)
        p.dma('sp', k.ident[:], I["ident"][:, :], writes=[k.ident])
        p.do('dve', 'tensor_copy', k.identb[:], k.ident[:], reads=[k.ident], writes=[k.identb])
        p.do('dve', 'memset', k.ones_f[:], 1.0, writes=[k.ones_f])
        p.do('dve', 'memset', k.ones_b[:], 1.0, writes=[k.ones_b])

        stages = cfg.get("stages", "MABC")
        if "M" in stages:
            prologue_mod(k)
        if dbg:
            dm = dout("dbg_modT", [128, DEPTH * 96])
            p.dma('sp', dm[:, :], k.modT[:].rearrange("p l c t -> p (l c t)"), reads=[k.modT])
        for l in range(DEPTH):
            if "A" in stages:
                phase_a(k, l)
            if "B" in stages:
                for s in seqs:
                    if cfg.get("attn", True):
                        phase_attn(k, l, s)
                    if cfg.get("ssm", True):
                        phase_ssm(k, l, s)
                    if cfg.get("gdn", True):
                        phase_gdn(k, l, s)
            if "C" in stages:
                phase_c(k, l)
        p.barrier()
        p.flush()
    return nc


def end_phase(k):
    k.p.barrier()
    k.p.flush()


def prologue_mod(k):
    nc, p, I = k.nc, k.p, k.I
    DEPTH = k.cfg["DEPTH"]
    with contextlib.ExitStack() as st:
        cond = k.sbt(st, "cond", [128, 2, KC], F32)
        sc = k.sbt(st, "sc", [128, KC, 2], F32)
        bm = k.sbt(st, "bm", [128, DEPTH, 48], F32)
        wring = Ring([k.sbt(st, "wm%d" % i, [128, 6 * D], F32) for i in range(2)])
        pring = Ring([k.pst(st, "pm%d" % i, [128, 256, 2], F32) for i in range(2)])
        p.dma('sp', cond[:], I["cond2"][:, :, :], writes=[cond])
        p.dma('sp', bm[:], I["b_modT"].rearrange("l p c -> p l c"), writes=[bm])
        p.do('act', 'activation', sc[:].rearrange("p k c -> p c k"), cond[:], AF.Silu, reads=[cond], writes=[sc])
        for l in range(DEPTH):
            for kc in range(KC):
                w = wring.next()
                p.dma('sp' if kc % 2 == 0 else 'pool', w[:], I["w_mod"][l, kc * 128:(kc + 1) * 128, :], writes=[w])
                ps = pring.next()
                for fc in range(48):
                    p.do('pe', 'matmul', ps[:, fc, :], w[:, fc * 128:(fc + 1) * 128], sc[:, kc, :], start=True, stop=True,
                         reads=[w, sc], writes=[ps])
                if kc == 0:
                    p.do('dve', 'tensor_tensor', k.modT[:, l, :, :], ps[:, 0:48, :], bcast(bm[:, l, :].unsqueeze(2), [128, 48, 2]), ALU.add,
                         reads=[ps, bm], writes=[k.modT])
                else:
                    p.do('dve', 'tensor_tensor', k.modT[:, l, :, :], ps[:, 0:48, :], k.modT[:, l, :, :], ALU.add,
                         reads=[ps, k.modT], writes=[k.modT])
        end_phase(k)


def phase_a(k, l):
    nc, p, I, O = k.nc, k.p, k.I, k.O
    with contextlib.ExitStack() as st:
        sbt, pst = k.sbt, k.pst
        win = sbt(st, "win", [128, KC, IN_W], BF16)
        wsrc = I["w_in"][l].rearrange("(c p) n -> p c n", p=128)
        for c in range(KC):
            for hh in range(2):
                p.dma('pool', win[:, c, hh * 1032:(hh + 1) * 1032], wsrc[:, c, hh * 1032:(hh + 1) * 1032], writes=[win])
        n1g = sbt(st, "n1g", [128, KC], F32)
        sc1 = sbt(st, "sc1", [128, KC, 2], F32)
        qkg = sbt(st, "qkg", [128, 2, 64], F32)
        p.dma('sp', n1g[:], I["n1g"][l], writes=[n1g])
        p.dma('sp', qkg[:], I["qk_g"][l], writes=[qkg])
        p.do('dve', 'tensor_scalar', sc1[:], k.modT[:, l, 8:16, :], 1.0, None, ALU.add, reads=[k.modT], writes=[sc1])
        p.do('dve', 'tensor_tensor', sc1[:], sc1[:], bcast(n1g[:].unsqueeze(2), [128, KC, 2]), ALU.mult, reads=[sc1, n1g], writes=[sc1])
        xring = Ring([sbt(st, "xa%d" % i, [128, D], F32) for i in range(2)])
        junk = sbt(st, "junka", [128, D], F32)
        ssr = Ring([sbt(st, "ssa%d" % i, [128, 1], F32) for i in range(2)])
        xnr = Ring([sbt(st, "xna%d" % i, [128, D], BF16) for i in range(2)])
        hTr = Ring([sbt(st, "hTa%d" % i, [128, KC, 512], BF16) for i in range(2)])
        sq = sbt(st, "sqa", [128, 10, 64], F32)
        ss10 = sbt(st, "ss10", [128, 10], F32)
        qn = sbt(st, "qna", [128, 10, 64], F32)
        tmp = [sbt(st, "ropet%d" % i, [128, 10, 32], F32) for i in range(4)]
        qr = sbt(st, "qra", [128, 10, 64], BF16)
        kd = sbt(st, "kda", [128, 4, 64], BF16)
        vb = sbt(st, "vba", [128, 128], BF16)
        vf = sbt(st, "vfa", [128, 128], F32)
        rp = sbt(st, "rpa", [128, 2, 32], F32)
        stager = Ring([sbt(st, "stga%d" % i, [128, 6, 512], BF16) for i in range(2)])
        fstr = Ring([sbt(st, "fsta%d" % i, [128, 512], BF16) for i in range(3)])
        gsb = sbt(st, "gsba", [64, 8, 16], F32)
        pTr = Ring([pst(st, "pTa%d" % i, [128, KC, 128], BF16) for i in range(2)])
        pq = pst(st, "pqa", [128, 512], F32)
        pkv = pst(st, "pkva", [128, 512], F32)
        ptr = pst(st, "ptra", [128, 8, 128], BF16)
        pfr = Ring([pst(st, "pfa%d" % i, [128, 512], F32) for i in range(2)])
        pg = pst(st, "pga", [128, 32, 16], F32)
        ev = [0]
        p.barrier()

        for s in (k.seqs[::-1] if k.cfg.get('rev') else k.seqs):
            L, ctx, cond = s["L"], s["ctx"], s["cond"]
            xsrc = s["xin"] if l == 0 else s["y"]
            TB = min(512, L)
            ntb = TB // 128
            for b in range(L // TB):
                hT = hTr.next()
                for ti in range(ntb):
                    r0 = b * TB + ti * 128
                    xt = xring.next()
                    ss = ssr.next()
                    xn = xnr.next()
                    pT = pTr.next()
                    p.dma('sp', xt[:], xsrc[r0:r0 + 128, :], writes=[xt])
                    p.do('act', 'activation', junk[:], xt[:], AF.Square, accum_out=ss[:], reads=[xt], writes=[junk, ss])
                    p.do('act', 'activation', ss[:], ss[:], AF.Sqrt, scale=1.0 / D, bias=EPS, reads=[ss], writes=[ss])
                    p.do('dve', 'reciprocal', ss[:], ss[:], reads=[ss], writes=[ss])
                    p.do('dve', 'tensor_scalar', xn[:], xt[:], ss[:, 0:1], None, ALU.mult, reads=[xt, ss], writes=[xn])
                    for c in range(KC):
                        p.do('pe', 'transpose', pT[:, c, :], xn[:, c * 128:(c + 1) * 128], k.identb[:], reads=[xn, k.identb], writes=[pT])
                    for c in range(KC):
                        if c % 2 == 0:
                            p.do('act', 'activation', hT[:, c, ti * 128:(ti + 1) * 128], pT[:, c, :], AF.Identity, scale=sc1[:, c, cond:cond + 1], bias=k.modT[:, l, c, cond:cond + 1],
                                 reads=[pT, sc1, k.modT], writes=[hT])
                        else:
                            p.do('dve', 'tensor_scalar', hT[:, c, ti * 128:(ti + 1) * 128], pT[:, c, :], sc1[:, c, cond:cond + 1], k.modT[:, l, c, cond:cond + 1], ALU.mult, ALU.add,
                                 reads=[pT, sc1, k.modT], writes=[hT])
                stage = stager.next()
                if k.cfg.get("dbg_hT") and not hasattr(k, "_dh"):
                    k._dh = nc.dram_tensor("dbg_hT", [128, KC, 512], BF16, kind="ExternalOutput").ap()
                    p.dma('sp', k._dh[:, :, 0:TB], hT[:, :, 0:TB], reads=[hT])
                    k._dw = nc.dram_tensor("dbg_win", [128, KC, IN_W], BF16, kind="ExternalOutput").ap()
                    p.dma('sp', k._dw[:, :, :], win[:, :, :], reads=[win])
                for ti in range(ntb):
                    r0 = b * TB + ti * 128
                    tsl = slice(ti * 128, (ti + 1) * 128)
                    for kc in range(KC):
                        p.do('pe', 'matmul', pq[:], hT[:, kc, tsl], win[:, kc, 0:512], start=(kc == 0), stop=(kc == KC - 1), reads=[hT, win], writes=[pq])
                    for kc in range(KC):
                        p.do('pe', 'matmul', pkv[:, 0:256], hT[:, kc, tsl], win[:, kc, 512:768], start=(kc == 0), stop=(kc == KC - 1), reads=[hT, win], writes=[pkv])
                    p.do('act', 'activation', sq[:, 0:8, :], pq[:].rearrange("p (h d) -> p h d", d=64), AF.Square, reads=[pq], writes=[sq])
                    p.do('act', 'activation', sq[:, 8:10, :], pkv[:, 0:128].rearrange("p (h d) -> p h d", d=64), AF.Square, reads=[pkv], writes=[sq])
                    p.do('dve', 'tensor_reduce', ss10[:], sq[:], AX.X, ALU.add, reads=[sq], writes=[ss10])
                    p.do('act', 'activation', ss10[:], ss10[:], AF.Sqrt, scale=1.0 / 64, bias=EPS, reads=[ss10], writes=[ss10])
                    p.do('dve', 'reciprocal', ss10[:], ss10[:], reads=[ss10], writes=[ss10])
                    p.do('dve', 'tensor_tensor', qn[:, 0:8, :], pq[:].rearrange("p (h d) -> p h d", d=64), bcast(ss10[:, 0:8].unsqueeze(2), [128, 8, 64]), ALU.mult, reads=[pq, ss10], writes=[qn])
                    p.do('dve', 'tensor_tensor', qn[:, 8:10, :], pkv[:, 0:128].rearrange("p (h d) -> p h d", d=64), bcast(ss10[:, 8:10].unsqueeze(2), [128, 2, 64]), ALU.mult, reads=[pkv, ss10], writes=[qn])
                    p.do('pool', 'tensor_tensor', qn[:, 0:8, :], qn[:, 0:8, :], bcast(qkg[:, 0:1, :], [128, 8, 64]), ALU.mult, reads=[qn, qkg], writes=[qn])
                    p.do('pool', 'tensor_tensor', qn[:, 8:10, :], qn[:, 8:10, :], bcast(qkg[:, 1:2, :], [128, 2, 64]), ALU.mult, reads=[qn, qkg], writes=[qn])
                    p.do('act', 'copy', vb[:], pkv[:, 128:256], reads=[pkv], writes=[vb])
                    p.dma('sp', s["v"][r0:r0 + 128, :], vb[:], reads=[vb])
                    if ctx:
                        pi = s["pi"]
                        p.do('act', 'copy', vf[:], pkv[:, 128:256], reads=[pkv], writes=[vf])
                        p.dma('sp', O["nv"][pi, l, r0:r0 + 128, :], vf[:], reads=[vf])
                        p.dma('sp', O["nk"][pi, l, r0:r0 + 128, :], qn[:, 8:10, :].rearrange("p h d -> p (h d)"), reads=[qn])
                        p.do('dve', 'tensor_copy', qr[:], qn[:], reads=[qn], writes=[qr])
                    else:
                        p.dma('sp', rp[:], I["rope"][r0:r0 + 128, :, :], writes=[rp])
                        q4 = qn[:].rearrange("p h (i two) -> p h i two", two=2)
                        r4 = qr[:].rearrange("p h (i two) -> p h i two", two=2)
                        cosb = bcast(rp[:, 0:1, :], [128, 10, 32])
                        sinb = bcast(rp[:, 1:2, :], [128, 10, 32])
                        p.do('dve', 'tensor_tensor', tmp[0][:], q4[:, :, :, 0], cosb, ALU.mult, reads=[qn, rp], writes=[tmp[0]])
                        p.do('pool', 'tensor_tensor', tmp[1][:], q4[:, :, :, 1], sinb, ALU.mult, reads=[qn, rp], writes=[tmp[1]])
                        p.do('dve', 'tensor_tensor', tmp[2][:], q4[:, :, :, 0], sinb, ALU.mult, reads=[qn, rp], writes=[tmp[2]])
                        p.do('pool', 'tensor_tensor', tmp[3][:], q4[:, :, :, 1], cosb, ALU.mult, reads=[qn, rp], writes=[tmp[3]])
                        p.do('dve', 'tensor_tensor', r4[:, :, :, 0], tmp[0][:], tmp[1][:], ALU.subtract, reads=[tmp[0], tmp[1]], writes=[qr])
                        p.do('pool', 'tensor_tensor', r4[:, :, :, 1], tmp[2][:], tmp[3][:], ALU.add, reads=[tmp[2], tmp[3]], writes=[qr])
                    p.do('pool', 'tensor_copy', kd[:, 0:2, :], bcast(qr[:, 8:9, :], [128, 2, 64]), reads=[qr], writes=[kd])
                    p.do('pool', 'tensor_copy', kd[:, 2:4, :], bcast(qr[:, 9:10, :], [128, 2, 64]), reads=[qr], writes=[kd])
                    for c in range(4):
                        p.do('pe', 'transpose', ptr[:, c, :], qr[:, 2 * c:2 * c + 2, :].rearrange("p h d -> p (h d)"), k.identb[:], reads=[qr, k.identb], writes=[ptr])
                    for g in range(2):
                        p.do('pe', 'transpose', ptr[:, 4 + g, :], kd[:, 2 * g:2 * g + 2, :].rearrange("p h d -> p (h d)"), k.identb[:], reads=[kd, k.identb], writes=[ptr])
                    p.do('act', 'copy', stage[:, :, tsl], ptr[:, 0:6, :], reads=[ptr], writes=[stage])
                cols = slice(b * TB, (b + 1) * TB)
                p.dma('sp', s["qT"][:, :, cols].rearrange("c p t -> p c t"), stage[:, 0:4, 0:TB], reads=[stage])
                p.dma('sp', s["kT"][:, :, cols].rearrange("c p t -> p c t"), stage[:, 4:6, 0:TB], reads=[stage])
                for fc in range(10):
                    pf = pfr.next()
                    fst = fstr.next()
                    for kc in range(KC):
                        p.do('pe', 'matmul', pf[:, 0:TB], win[:, kc, 768 + fc * 128:768 + (fc + 1) * 128], hT[:, kc, 0:TB], start=(kc == 0), stop=(kc == KC - 1), reads=[hT, win], writes=[pf])
                    ev[0] += 1
                    if ev[0] % 2 == 0:
                        p.do('act', 'copy', fst[:, 0:TB], pf[:, 0:TB], reads=[pf], writes=[fst])
                    else:
                        p.do('dve', 'tensor_copy', fst[:, 0:TB], pf[:, 0:TB], reads=[pf], writes=[fst])
                    p.dma('sp', s["fT"][fc, :, cols], fst[:, 0:TB], reads=[fst])
                nch = TB // 64
                for j in range(nch):
                    for kc in range(KC):
                        p.do('pe', 'matmul', pg[0:64, j, :], hT[:, kc, j * 64:(j + 1) * 64], win[:, kc, 2048:2064], start=(kc == 0), stop=(kc == KC - 1), reads=[hT, win], writes=[pg])
                p.do('dve', 'tensor_copy', gsb[:, 0:nch, :], pg[0:64, 0:nch, :], reads=[pg], writes=[gsb])
                p.dma('sp', s["gates"][:, b * nch:(b + 1) * nch, :], gsb[:, 0:nch, :], reads=[gsb])
        end_phase(k)


def _chunkT(v, n=128):
    sh = v.shape[:-1]
    return np.ascontiguousarray(np.swapaxes(v.reshape(sh + (-1, n)), -1, -2))


def const_tables(cfg):
    LS = cfg["LS"]
    t = {}
    t["ident"] = np.eye(128, dtype=np.float32)
    n_rows = max(LS // 64, 1)
    rows = np.repeat(np.arange(n_rows, dtype=np.float32), 64)[:LS]
    cols = np.tile(np.arange(64, dtype=np.float32), n_rows)[:LS]
    inv_freq = np.power(np.float32(10000.0), -np.arange(16, dtype=np.float32) / np.float32(16)).astype(np.float32)
    ang = np.concatenate([rows[:, None] * inv_freq, cols[:, None] * inv_freq], axis=-1).astype(np.float32)
    t["rope"] = np.ascontiguousarray(np.stack([np.cos(ang), np.sin(ang)], axis=1).astype(np.float32))
    kk = np.arange(128)[:, None]
    qq = np.arange(128)[None, :]
    t["amask"] = np.ascontiguousarray(np.stack([(kk >= qq), (kk <= qq)], axis=1).astype(np.float32))
    i = np.arange(64)[:, None]
    j = np.arange(64)[None, :]
    low_incl = (i >= j).astype(np.float32)
    low_strict = (i > j).astype(np.float32)
    gm = np.zeros((64, 6, 8, 64), np.float32)
    for e in range(8):
        fwd = e < 4
        gm[:, 0, e, :] = low_strict if fwd else low_strict.T
        gm[:, 1, e, :] = low_incl if fwd else low_incl.T
        gm[:, 2, e, :] = low_strict.T if fwd else low_strict
        gm[:, 3, e, :] = low_incl.T if fwd else low_incl
        gm[:, 4, e, :] = np.eye(64, dtype=np.float32)
    t["gmask"] = gm
    gt = np.zeros((64, 4, 64), np.float32)
    gt[:, 0, :] = low_incl.T
    gt[:, 1, :] = low_incl
    gt[63, 2, :] = 1.0
    gt[0, 3, :] = 1.0
    t["gtri"] = gt
    return t


def host_weights(inp, cfg):
    DEPTH = cfg["DEPTH"]
    f = lambda a: np.ascontiguousarray(np.asarray(a, dtype=np.float32))
    w = {}
    w["w_mod"] = f(inp["w_mod"][:DEPTH])
    w["b_modT"] = _chunkT(f(inp["b_mod"][:DEPTH]))
    w["n1g"] = _chunkT(f(inp["norm1_g"][:DEPTH]))
    w["n2g"] = _chunkT(f(inp["norm2_g"][:DEPTH]))
    w["w_in"] = f(inp["w_in"][:DEPTH])
    qk = np.stack([f(inp["q_norm_g"][:DEPTH]), f(inp["k_norm_g"][:DEPTH])], axis=1)
    w["qk_g"] = np.ascontiguousarray(np.broadcast_to(qk[:, None], (DEPTH, 128, 2, 64)))
    w["sink_b"] = np.ascontiguousarray(np.broadcast_to(f(inp["attn_sink"][:DEPTH])[:, None], (DEPTH, 128, 8)))
    lam = np.zeros((DEPTH, 128, 3, 16), np.float32)
    bt = np.zeros((DEPTH, 128, 16, 2, 128), np.float32)
    ct = np.zeros((DEPTH, 128, 16, 2, 128), np.float32)
    lre, lim, lst = f(inp["ssm_lam_re"]), f(inp["ssm_lam_im"]), f(inp["ssm_log_step"])
    bre, bim, cre, cim = f(inp["ssm_b_re"]), f(inp["ssm_b_im"]), f(inp["ssm_c_re"]), f(inp["ssm_c_im"])
    for gp in range(8):
        for d in range(2):
            inst = gp * 2 + d
            for g2 in range(2):
                g = 2 * gp + g2
                gl = g % 8
                ps = slice(g2 * 64, (g2 + 1) * 64)
                lam[:, ps, 0, inst] = lre[:DEPTH, d, g, :]
                lam[:, ps, 1, inst] = lim[:DEPTH, d, g, :]
                lam[:, ps, 2, inst] = lst[:DEPTH, d, g][:, None]
                bt[:, ps, inst, 0, gl * 16:(gl + 1) * 16] = bre[:DEPTH, d, g]
                bt[:, ps, inst, 1, gl * 16:(gl + 1) * 16] = bim[:DEPTH, d, g]
                co0 = 32 * (gp % 4) + g2 * 16
                ct[:, ps, inst, 0, co0:co0 + 16] = np.swapaxes(cre[:DEPTH, d, g], -1, -2)
                ct[:, ps, inst, 1, co0:co0 + 16] = np.swapaxes(cim[:DEPTH, d, g], -1, -2)
    w["lam"], w["ssm_bt"], w["ssm_ct"] = lam, bt, ct
    w["ssm_dT"] = _chunkT(f(inp["ssm_d"][:DEPTH]))
    w["glu_bT"] = _chunkT(f(inp["ssm_b_glu"][:DEPTH]))
    w["w_glu"] = f(inp["ssm_w_glu"][:DEPTH])
    cw = f(inp["gdn_conv_w"][:DEPTH])
    w["conv_wT"] = np.ascontiguousarray(np.transpose(cw.reshape(DEPTH, 3, 6, 128), (0, 3, 2, 1)))
    ab = np.stack([f(inp["gdn_a_log"][:DEPTH]).reshape(DEPTH, 8), f(inp["gdn_dt_bias"][:DEPTH]).reshape(DEPTH, 8)], axis=1)
    w["gdn_ab"] = np.ascontiguousarray(np.broadcast_to(ab[:, None], (DEPTH, 64, 2, 8)))
    ng = f(inp["gdn_norm_g"][:DEPTH])
    w["gdn_ng"] = np.ascontiguousarray(np.concatenate([ng, ng], axis=1)[:, :, None])
    w["w_out"] = f(inp["w_out"][:DEPTH])
    w["w_ff1"] = f(inp["w_ff1"][:DEPTH])
    w["w_ff2"] = f(inp["w_ff2"][:DEPTH])
    return w


def host_core_inputs(inp, core, cfg):
    LS, LP, NP, DEPTH, PAST = cfg["LS"], cfg["LP"], cfg["NP"], cfg["DEPTH"], cfg["PAST"]
    f = lambda a: np.ascontiguousarray(np.asarray(a, dtype=np.float32))
    m = {}
    m["x_s"] = f(inp["x_sample"][core, :LS])
    m["x_p"] = f(inp["x_prompt"][core * NP:(core + 1) * NP, :LP]).reshape(NP * LP, D)
    c2 = np.stack([f(inp["c"][core]), f(inp["c_ctx"])], axis=0)
    m["cond2"] = np.ascontiguousarray(np.transpose(c2.reshape(2, KC, 128), (2, 0, 1)))
    m["cache_k"] = f(inp["cache_k"][core, :DEPTH, :PAST]).reshape(DEPTH, PAST, 128)
    m["cache_v"] = f(inp["cache_v"][core, :DEPTH, :PAST]).reshape(DEPTH, PAST, 128)
    ss = f(inp["state_ssm"][core, :DEPTH])
    h0 = np.zeros((DEPTH, 128, 16, 2), np.float32)
    for gp in range(8):
        for d in range(2):
            for g2 in range(2):
                h0[:, g2 * 64:(g2 + 1) * 64, gp * 2 + d, :] = np.transpose(ss[:, d, :, 2 * gp + g2, :], (0, 2, 1))
    m["ssm_h0"] = h0
    sg = f(inp["state_gdn"][core, :DEPTH])
    m["gdn_s0"] = np.ascontiguousarray(np.transpose(sg, (0, 3, 1, 2, 4)).reshape(DEPTH, 64, 8, 64))
    return m


def phase_c(k, l):
    nc, p, I, O = k.nc, k.p, k.I, k.O
    use_mix = k.cfg.get("use_mix", True)
    with contextlib.ExitStack() as st:
        sbt, pst = k.sbt, k.pst
        wout = sbt(st, "wout", [128, KC, D], BF16)
        wff1 = sbt(st, "wff1", [128, KC, 4 * D], BF16)
        wff2 = sbt(st, "wff2", [128, 32, D], BF16)
        s_out = I["w_out"][l].rearrange("(c p) n -> p c n", p=128)
        s_ff1 = I["w_ff1"][l].rearrange("(c p) n -> p c n", p=128)
        s_ff2 = I["w_ff2"][l].rearrange("(c p) n -> p c n", p=128)
        for c in range(KC):
            p.dma('pool', wout[:, c, :], s_out[:, c, :], writes=[wout])
        for c in range(KC):
            for q4 in range(4):
                p.dma('pool', wff1[:, c, q4 * D:(q4 + 1) * D], s_ff1[:, c, q4 * D:(q4 + 1) * D], writes=[wff1])
        for c in range(32):
            p.dma('pool', wff2[:, c, :], s_ff2[:, c, :], writes=[wff2])
        n2g = sbt(st, "n2g", [128, KC], F32)
        sc2 = sbt(st, "sc2", [128, KC, 2], F32)
        p.dma('sp', n2g[:], I["n2g"][l], writes=[n2g])
        p.do('dve', 'tensor_scalar', sc2[:], k.modT[:, l, 32:40, :], 1.0, None, ALU.add, reads=[k.modT], writes=[sc2])
        p.do('dve', 'tensor_tensor', sc2[:], sc2[:], bcast(n2g[:].unsqueeze(2), [128, KC, 2]), ALU.mult, reads=[sc2, n2g], writes=[sc2])
        gA = [sbt(st, "gA%d" % i, [128, D], F32) for i in range(2)]
        gM = [sbt(st, "gM%d" % i, [128, D], F32) for i in range(2)]
        gcol = Ring([sbt(st, "gcol%d" % i, [128, 128], F32) for i in range(2)])
        pgb = Ring([pst(st, "pgb%d" % i, [128, 512], F32) for i in range(2)])
        for cond in range(2):
            for gi, (dst, base) in enumerate(((gA[cond], 16), (gM[cond], 40))):
                for c in range(KC):
                    gc_ = gcol.next()
                    pb = pgb.next()
                    p.do('dve', 'tensor_copy', gc_[:], bcast(k.modT[:, l, base + c, cond:cond + 1], [128, 128]), reads=[k.modT], writes=[gc_])
                    p.do('pe', 'matmul', pb[:, 0:128], gc_[:], k.ident[:], start=True, stop=True, reads=[gc_, k.ident], writes=[pb])
                    p.do('act', 'copy', dst[:, c * 128:(c + 1) * 128], pb[:, 0:128], reads=[pb], writes=[dst])
        xring = Ring([sbt(st, "xc%d" % i, [128, D], F32) for i in range(2)])
        x1r = Ring([sbt(st, "x1c%d" % i, [128, D], F32) for i in range(1)])
        tmpc = sbt(st, "tmpc", [128, D], F32)
        junk = tmpc
        ssr = Ring([sbt(st, "ssc%d" % i, [128, 1], F32) for i in range(2)])
        xnr = Ring([sbt(st, "xnc%d" % i, [128, D], BF16) for i in range(1)])
        mixr = Ring([sbt(st, "mixc%d" % i, [128, KC, 128], BF16) for i in range(2)])
        h2r = Ring([sbt(st, "h2c%d" % i, [128, KC, 128], BF16) for i in range(2)])
        aTr = Ring([sbt(st, "aTc%d" % i, [128, 32, 128], BF16) for i in range(1)])
        rr = Ring([sbt(st, "rc%d" % i, [128, 4, 128], BF16) for i in range(2)])
        pTr = Ring([pst(st, "pTc%d" % i, [128, KC, 128], BF16) for i in range(1)])
        por = Ring([pst(st, "poc%d" % i, [128, 512], F32) for i in range(3)])
        pfr = Ring([pst(st, "pfc%d" % i, [128, 4, 128], F32) for i in range(2)])
        p.barrier()
        for s in k.seqs:
            L, cond = s["L"], s["cond"]
            xsrc = s["xin"] if (l == 0) else s["y"]
            for t in range(L // 128):
                rows = slice(t * 128, (t + 1) * 128)
                xt = xring.next()
                x1 = x1r.next()
                p.dma('sp', xt[:], xsrc[rows, :], writes=[xt])
                if use_mix:
                    mx = mixr.next()
                    p.dma('sp', mx[:], s["mixT"][:, :, rows].rearrange("c p t -> p c t"), writes=[mx])
                    for hh in range(2):
                        po = por.next()
                        for kc in range(KC):
                            p.do('pe', 'matmul', po[:], mx[:, kc, :], wout[:, kc, hh * 512:(hh + 1) * 512], start=(kc == 0), stop=(kc == KC - 1), reads=[mx, wout], writes=[po])
                        hs = slice(hh * 512, (hh + 1) * 512)
                        p.do('dve', 'tensor_tensor', tmpc[:, hs], po[:], gA[cond][:, hs], ALU.mult, reads=[po, gA[cond]], writes=[tmpc])
                        p.do('pool', 'tensor_tensor', x1[:, hs], tmpc[:, hs], xt[:, hs], ALU.add, reads=[tmpc, xt], writes=[x1])
                else:
                    p.do('pool', 'tensor_copy', x1[:], xt[:], reads=[xt], writes=[x1])
                ss = ssr.next()
                xn = xnr.next()
                pT = pTr.next()
                h2 = h2r.next()
                aT = aTr.next()
                p.do('act', 'activation', junk[:], x1[:], AF.Square, accum_out=ss[:], reads=[x1], writes=[junk, ss])
                p.do('act', 'activation', ss[:], ss[:], AF.Sqrt, scale=1.0 / D, bias=EPS, reads=[ss], writes=[ss])
                p.do('dve', 'reciprocal', ss[:], ss[:], reads=[ss], writes=[ss])
                p.do('dve', 'tensor_scalar', xn[:], x1[:], ss[:, 0:1], None, ALU.mult, reads=[x1, ss], writes=[xn])
                for c in range(KC):
                    p.do('pe', 'transpose', pT[:, c, :], xn[:, c * 128:(c + 1) * 128], k.identb[:], reads=[xn, k.identb], writes=[pT])
                for c in range(KC):
                    if c % 2 == 0:
                        p.do('act', 'activation', h2[:, c, :], pT[:, c, :], AF.Identity, scale=sc2[:, c, cond:cond + 1], bias=k.modT[:, l, 24 + c, cond:cond + 1], reads=[pT, sc2, k.modT], writes=[h2])
                    else:
                        p.do('dve', 'tensor_scalar', h2[:, c, :], pT[:, c, :], sc2[:, c, cond:cond + 1], k.modT[:, l, 24 + c, cond:cond + 1], ALU.mult, ALU.add, reads=[pT, sc2, k.modT], writes=[h2])
                for f4 in range(8):
                    pf = pfr.next()
                    r_ = rr.next()
                    for j in range(4):
                        fc = f4 * 4 + j
                        for kc in range(KC):
                            p.do('pe', 'matmul', pf[:, j, :], wff1[:, kc, fc * 128:(fc + 1) * 128], h2[:, kc, :], start=(kc == 0), stop=(kc == KC - 1), reads=[wff1, h2], writes=[pf])
                    p.do('act', 'activation', r_[:], pf[:], AF.Relu, reads=[pf], writes=[r_])
                    p.do('pool', 'tensor_tensor', aT[:, f4 * 4:(f4 + 1) * 4, :], r_[:], r_[:], ALU.mult, reads=[r_], writes=[aT])
                for hh in range(2):
                    po = por.next()
                    for fc in range(32):
                        p.do('pe', 'matmul', po[:], aT[:, fc, :], wff2[:, fc, hh * 512:(hh + 1) * 512], start=(fc == 0), stop=(fc == 31), reads=[aT, wff2], writes=[po])
                    hs = slice(hh * 512, (hh + 1) * 512)
                    p.do('dve', 'tensor_tensor', tmpc[:, hs], po[:], gM[cond][:, hs], ALU.mult, reads=[po, gM[cond]], writes=[tmpc])
                    p.do('dve', 'tensor_tensor', xt[:, hs], tmpc[:, hs], x1[:, hs], ALU.add, reads=[tmpc, x1], writes=[xt])
                p.dma('sp', s["y"][rows, :], xt[:], reads=[xt])
        end_phase(k)


def phase_attn(k, l, s):
    nc, p, I, O = k.nc, k.p, k.I, k.O
    L, ctx = s["L"], s["ctx"]
    PAST = k.cfg["PAST"]
    nb = L // 128
    SC = 0.125
    with contextlib.ExitStack() as st:
        sbt, pst = k.sbt, k.pst
        kT = sbt(st, "akT", [128, 2, L], BF16)
        vsb = sbt(st, "avsb", [128, nb, 128], BF16)
        vdup = sbt(st, "avdup", [128, nb, 2, 128], BF16)
        esk = sbt(st, "aesk", [128, 8], F32)
        mkf = sbt(st, "amkf", [128, 2, 128], F32)
        mk = sbt(st, "amk", [128, 2, 128], BF16)
        p.dma('sp', kT[:], s["kT"].rearrange("g p t -> p g t"), writes=[kT])
        vsrc = s["v"].rearrange("(n p) f -> p n f", p=128)
        for n0 in range(0, nb, 4):
            n1 = min(nb, n0 + 4)
            p.dma('sp', vsb[:, n0:n1, :], vsrc[:, n0:n1, :], writes=[vsb])
        p.dma('sp', esk[:], I["sink_b"][l], writes=[esk])
        p.dma('sp', mkf[:], I["amask"][:, :, :], writes=[mkf])
        p.do('act', 'activation', esk[:], esk[:], AF.Exp, reads=[esk], writes=[esk])
        p.do('dve', 'tensor_copy', mk[:], mkf[:], reads=[mkf], writes=[mk])
        for g in range(2):
            for hf in range(2):
                p.do('dve' if hf == 0 else 'pool', 'tensor_copy', vdup[:, :, g, hf * 64:(hf + 1) * 64], vsb[:, :, g * 64:(g + 1) * 64], reads=[vsb], writes=[vdup])
        pst_ring = Ring([pst(st, "aST%d" % i, [128, 4, 128], F32) for i in range(2)])
        po_ring = Ring([pst(st, "aO%d" % i, [128, 4, 128], F32) for i in range(2)])
        ps_ring = Ring([pst(st, "aS%d" % i, [128, 4, 128], F32) for i in range(2)])
        nctx = 0
        if not ctx:
            nctx = PAST // 128
            ckf = sbt(st, "ackf", [128, nctx, 128], F32)
            cvf = sbt(st, "acvf", [128, nctx, 128], F32)
            ckd = sbt(st, "ackd", [128, nctx, 2, 128], BF16)
            cvd = sbt(st, "acvd", [128, nctx, 2, 128], BF16)
            ckT = sbt(st, "ackT", [128, 2, PAST], BF16)
            pck = pst(st, "apck", [128, 8, 128], BF16)
            p.dma('sp', ckf[:], I["cache_k"][l].rearrange("(n p) f -> p n f", p=128), writes=[ckf])
            p.dma('sp', cvf[:], I["cache_v"][l].rearrange("(n p) f -> p n f", p=128), writes=[cvf])
            for g in range(2):
                for hf in range(2):
                    p.do('dve', 'tensor_copy', ckd[:, :, g, hf * 64:(hf + 1) * 64], ckf[:, :, g * 64:(g + 1) * 64], reads=[ckf], writes=[ckd])
                    p.do('pool', 'tensor_copy', cvd[:, :, g, hf * 64:(hf + 1) * 64], cvf[:, :, g * 64:(g + 1) * 64], reads=[cvf], writes=[cvd])
            for j in range(nctx):
                for g in range(2):
                    p.do('pe', 'transpose', pck[:, j * 2 + g, :], ckd[:, j, g, :], k.identb[:], reads=[ckd, k.identb], writes=[pck])
            for j in range(nctx):
                for g in range(2):
                    p.do('act', 'copy', ckT[:, g, j * 128:(j + 1) * 128], pck[:, j * 2 + g, :], reads=[pck], writes=[ckT])
        qr = Ring([sbt(st, "aq%d" % i, [128, 4, 128], BF16) for i in range(2)])
        qzr = Ring([sbt(st, "aqz%d" % i, [128, 4, 2, 128], BF16) for i in range(2)])
        for qz_ in qzr.bufs:
            p.do('pool', 'memset', qz_[:], 0.0, writes=[qz_])
        er = Ring([sbt(st, "aE%d" % i, [128, 4, 128], BF16) for i in range(3)])
        den = sbt(st, "aden", [128, 4, 128], F32)
        outr = Ring([sbt(st, "aout%d" % i, [128, 4, 128], BF16) for i in range(2)])
        lvl = k.cfg.get('att_lvl', 9)
        for i in range(nb if lvl >= 2 else 0):
            cols = slice(i * 128, (i + 1) * 128)
            q = qr.next()
            ao = outr.next()
            p.dma('sp', q[:], s["qT"][:, :, cols].rearrange("c p t -> p c t"), writes=[q])
            qz = qzr.next()
            p.do('pool', 'tensor_copy', qz[0:64, :, 0, :], q[0:64, :, :], reads=[q], writes=[qz])
            p.do('pool', 'tensor_copy', qz[64:128, :, 1, :], q[64:128, :, :], reads=[q], writes=[qz])
            for g in range(2):
                kbs = []
                if ctx:
                    for j in range(nb):
                        kbs.append((kT, j, vdup[:, j, g, :], None, vdup))
                else:
                    for j, mi in ((i - 1, 0), (i, None), (i + 1, 1)):
                        if 0 <= j < nb:
                            kbs.append((kT, j, vdup[:, j, g, :], mi, vdup))
                    for j in range(nctx):
                        kbs.append((ckT, j, cvd[:, j, g, :], None, cvd))
                po = po_ring.next()
                psm = ps_ring.next()
                for bi, (ksrc, j, vap, mi, vbuf) in enumerate(kbs):
                    pS = pst_ring.next()
                    for hh in range(4):
                        h = 4 * g + hh
                        hf, c = h % 2, h // 2
                        p.do('pe', 'matmul', pS[:, hh, :], ksrc[:, g, j * 128:(j + 1) * 128], qz[:, c, hf, :], start=True, stop=True, reads=[ksrc, qz], writes=[pS])
                    if lvl < 3:
                        continue
                    E = er.next()
                    p.do('act', 'activation', E[:], pS[:], AF.Exp, scale=SC, reads=[pS], writes=[E])
                    if mi is not None:
                        p.do('pool', 'tensor_tensor', E[:], E[:], bcast(mk[:, mi:mi + 1, :], [128, 4, 128]), ALU.mult, reads=[E, mk], writes=[E])
                    if lvl < 4:
                        continue
                    first, last = bi == 0, bi == len(kbs) - 1
                    p.do('pe', 'matmul', po[:].rearrange("p a b -> p (a b)"), vap, E[:].rearrange("p a b -> p (a b)"), start=first, stop=last, reads=[vbuf, E], writes=[po])
                    p.do('pe', 'matmul', psm[:].rearrange("p a b -> p (a b)"), k.ones_b[:], E[:].rearrange("p a b -> p (a b)"), start=first, stop=last, reads=[k.ones_b, E], writes=[psm])
                if lvl < 5:
                    continue
                p.do('dve', 'tensor_tensor', den[:], psm[:], bcast(esk[:, 4 * g:4 * g + 4].unsqueeze(2), [128, 4, 128]), ALU.add, reads=[psm, esk], writes=[den])
                p.do('dve', 'reciprocal', den[:], den[:], reads=[den], writes=[den])
                for hf in range(2):
                    ps_ = slice(64 * hf, 64 * hf + 64)
                    p.do('dve', 'tensor_tensor', ao[ps_, 2 * g:2 * g + 2, :], po[ps_, hf::2, :], den[ps_, hf::2, :], ALU.mult, reads=[po, den], writes=[ao])
            if lvl >= 6:
                p.dma('sp', s["mixT"][0:4, :, cols].rearrange("c p t -> p c t"), ao[:], reads=[ao])
        end_phase(k)


def ssm_scan_levels(L):
    K_ = int(math.log2(L))
    ops = []
    for kk in range(K_):
        s_ = 2 ** (kk + 1)
        ops.append((kk, 2 ** kk - 1, s_ - 1, s_, L // s_))
    for kk in range(K_ - 2, -1, -1):
        s_ = 2 ** (kk + 1)
        n = L // s_ - 1
        if n > 0:
            ops.append((kk, s_ - 1, s_ + 2 ** kk - 1, s_, n))
    return ops


def phase_ssm(k, l, s):
    nc, p, I, O = k.nc, k.p, k.I, k.O
    L, ctx = s["L"], s["ctx"]
    TB = min(512, L)
    nblk = L // TB
    NLV = int(math.log2(L))
    with contextlib.ExitStack() as st:
        sbt, pst = k.sbt, k.pst
        lam = sbt(st, "slam", [128, 3, 16], F32)
        dtm = sbt(st, "sdt", [128, 16], F32)
        res_ = sbt(st, "sres", [128, 16], F32)
        ims = sbt(st, "sims", [128, 16], F32)
        t = [sbt(st, "st%d" % i, [128, 16], F32) for i in range(6)]
        A = sbt(st, "sA", [128, 16, 12, 2], F32)
        nAi = sbt(st, "snAi", [128, 16, 12], F32)
        fre = sbt(st, "sfre", [128, 16], F32)
        fim = sbt(st, "sfim", [128, 16], F32)
        nfim = sbt(st, "snfim", [128, 16], F32)
        p.dma('sp', lam[:], I["lam"][l], writes=[lam])
        p.do('act', 'activation', dtm[:], lam[:, 2, :], AF.Exp, reads=[lam], writes=[dtm])
        p.do('dve', 'tensor_tensor', res_[:], lam[:, 0, :], dtm[:], ALU.mult, reads=[lam, dtm], writes=[res_])
        p.do('dve', 'tensor_tensor', ims[:], lam[:, 1, :], dtm[:], ALU.mult, reads=[lam, dtm], writes=[ims])
        mag, s8, sh, c8, x_, y_ = t
        p.do('act', 'activation', mag[:], res_[:], AF.Exp, scale=0.125, reads=[res_], writes=[mag])
        p.do('act', 'activation', s8[:], ims[:], AF.Sin, scale=0.125, reads=[ims], writes=[s8])
        p.do('act', 'activation', sh[:], ims[:], AF.Sin, scale=0.0625, reads=[ims], writes=[sh])
        p.do('dve', 'tensor_tensor', c8[:], sh[:], sh[:], ALU.mult, reads=[sh], writes=[c8])
        p.do('dve', 'tensor_scalar', c8[:], c8[:], -2.0, 1.0, ALU.mult, ALU.add, reads=[c8], writes=[c8])
        p.do('dve', 'tensor_tensor', x_[:], mag[:], c8[:], ALU.mult, reads=[mag, c8], writes=[x_])
        p.do('dve', 'tensor_tensor', y_[:], mag[:], s8[:], ALU.mult, reads=[mag, s8], writes=[y_])
        xb, yb = Buf(x_.ap, "x"), Buf(y_.ap, "y")

        def csquare(dst_re, dst_im, src_re, src_im, bufs_r, bufs_w):
            p.do('dve', 'tensor_tensor', mag[:], src_re, src_re, ALU.mult, reads=bufs_r, writes=[mag])
            p.do('dve', 'tensor_tensor', s8[:], src_im, src_im, ALU.mult, reads=bufs_r, writes=[s8])
            p.do('dve', 'scalar_tensor_tensor', sh[:], src_re, 2.0, src_im, ALU.mult, ALU.mult, reads=bufs_r, writes=[sh])
            p.do('dve', 'tensor_tensor', dst_re, mag[:], s8[:], ALU.subtract, reads=[mag, s8], writes=bufs_w)
            p.do('dve', 'tensor_copy', dst_im, sh[:], reads=[sh], writes=bufs_w)

        for _ in range(2):
            csquare(x_[:], y_[:], x_[:], y_[:], [x_, y_], [x_, y_])
        csquare(A[:, :, 0, 0], A[:, :, 0, 1], x_[:], y_[:], [x_, y_], [A])
        for kk in range(1, 12):
            csquare(A[:, :, kk, 0], A[:, :, kk, 1], A[:, :, kk - 1, 0], A[:, :, kk - 1, 1], [A], [A])
        p.do('dve', 'tensor_scalar', nAi[:], A[:, :, :, 1], -1.0, None, ALU.mult, reads=[A], writes=[nAi])
        nr, den, rr_, q1 = t[0], t[1], t[2], t[3]
        p.do('dve', 'tensor_scalar', nr[:], A[:, :, 0, 0], -1.0, None, ALU.add, reads=[A], writes=[nr])
        p.do('dve', 'tensor_tensor', den[:], lam[:, 0, :], lam[:, 0, :], ALU.mult, reads=[lam], writes=[den])
        p.do('dve', 'tensor_tensor', q1[:], lam[:, 1, :], lam[:, 1, :], ALU.mult, reads=[lam], writes=[q1])
        p.do('dve', 'tensor_tensor', den[:], den[:], q1[:], ALU.add, reads=[den, q1], writes=[den])
        p.do('dve', 'reciprocal', den[:], den[:], reads=[den], writes=[den])
        p.do('dve', 'tensor_tensor', fre[:], nr[:], lam[:, 0, :], ALU.mult, reads=[nr, lam], writes=[fre])
        p.do('dve', 'tensor_tensor', q1[:], A[:, :, 0, 1], lam[:, 1, :], ALU.mult, reads=[A, lam], writes=[q1])
        p.do('dve', 'tensor_tensor', fre[:], fre[:], q1[:], ALU.add, reads=[fre, q1], writes=[fre])
        p.do('dve', 'tensor_tensor', fre[:], fre[:], den[:], ALU.mult, reads=[fre, den], writes=[fre])
        p.do('dve', 'tensor_tensor', fim[:], A[:, :, 0, 1], lam[:, 0, :], ALU.mult, reads=[A, lam], writes=[fim])
        p.do('dve', 'tensor_tensor', q1[:], nr[:], lam[:, 1, :], ALU.mult, reads=[nr, lam], writes=[q1])
        p.do('dve', 'tensor_tensor', fim[:], fim[:], q1[:], ALU.subtract, reads=[fim, q1], writes=[fim])
        p.do('dve', 'tensor_tensor', fim[:], fim[:], den[:], ALU.mult, reads=[fim, den], writes=[fim])
        p.do('dve', 'tensor_scalar', nfim[:], fim[:], -1.0, None, ALU.mult, reads=[fim], writes=[nfim])
        lvl = k.cfg.get('att_lvl', 9)
        if lvl < 2:
            end_phase(k)
            return
        BbT = sbt(st, "sBbT", [128, 16, 2, 128], BF16)
        Ct = sbt(st, "sCt", [128, 16, 2, 128], F32)
        p.dma('sp', Ct[:], I["ssm_ct"][l], writes=[Ct])
        p.do('dve', 'tensor_scalar', Ct[:, :, 1, :], Ct[:, :, 1, :], -1.0, None, ALU.mult, reads=[Ct], writes=[Ct])
        btr = Ring([sbt(st, "sbt%d" % i, [128, 2, 128], F32) for i in range(2)])
        bbr = Ring([sbt(st, "sbb%d" % i, [128, 2, 128], F32) for i in range(2)])
        ptr = Ring([pst(st, "sptr%d" % i, [128, 4, 128], F32) for i in range(2)])
        for inst in range(16):
            bt_, bb, pt = btr.next(), bbr.next(), ptr.next()
            p.dma('sp', bt_[:], I["ssm_bt"][l][:, inst, :, :], writes=[bt_])
            fr, fi, nfi = fre[:, inst:inst + 1], fim[:, inst:inst + 1], nfim[:, inst:inst + 1]
            p.do('dve', 'tensor_scalar', bb[:, 0, :], bt_[:, 0, :], fr, None, ALU.mult, reads=[bt_, fre], writes=[bb])
            p.do('dve', 'scalar_tensor_tensor', bb[:, 0, :], bt_[:, 1, :], nfi, bb[:, 0, :], ALU.mult, ALU.add, reads=[bt_, nfim, bb], writes=[bb])
            p.do('dve', 'tensor_scalar', bb[:, 1, :], bt_[:, 1, :], fr, None, ALU.mult, reads=[bt_, fre], writes=[bb])
            p.do('dve', 'scalar_tensor_tensor', bb[:, 1, :], bt_[:, 0, :], fi, bb[:, 1, :], ALU.mult, ALU.add, reads=[bt_, fim, bb], writes=[bb])
            for ri in range(2):
                p.do('pe', 'transpose', pt[:, ri, :], bb[:, ri, :], k.ident[:], reads=[bb, k.ident], writes=[pt])
            p.do('act', 'copy', BbT[:, inst, :, :], pt[:, 0:2, :], reads=[pt], writes=[BbT])
        if lvl < 3:
            end_phase(k)
            return
        uT = sbt(st, "suT", [128, 2, L], BF16)
        yT = sbt(st, "syT", [128, 2, L], F32)
        p.dma('sp', uT[:], s["fT"][0:2].rearrange("c p t -> p c t"), writes=[uT])
        Hr = Ring([sbt(st, "sH%d" % i, [128, 2, L], F32) for i in range(2)])
        pbr = Ring([pst(st, "spb%d" % i, [128, 512], F32) for i in range(3)])
        pyr = Ring([pst(st, "spy%d" % i, [128, 512], F32) for i in range(2)])
        if not ctx:
            h0 = sbt(st, "sh0", [128, 16, 2], F32)
            t1 = sbt(st, "sht1", [128, 2], F32)
            p.dma('sp', h0[:], I["ssm_h0"][l], writes=[h0])
        sched = ssm_scan_levels(L)
        for gp in range(8):
            chunk, q = gp // 4, gp % 4
            for d in range(2):
                inst = gp * 2 + d
                H = Hr.next()
                for b in range(nblk):
                    blk = slice(b * TB, (b + 1) * TB)
                    for ri in range(2):
                        pb = pbr.next()
                        p.do('pe', 'matmul', pb[:, 0:TB], BbT[:, inst, ri, :], uT[:, chunk, blk], start=True, stop=True, reads=[BbT, uT], writes=[pb])
                        if ri == 0:
                            p.do('act', 'copy', H[:, ri, blk], pb[:, 0:TB], reads=[pb], writes=[H])
                        else:
                            p.do('act', 'copy', H[:, ri, blk], pb[:, 0:TB], reads=[pb], writes=[H])
                if not ctx:
                    pos = 0 if d == 0 else L - 1
                    ar, ai, nai = A[:, inst, 0, 0:1], A[:, inst, 0, 1:2], nAi[:, inst, 0:1]
                    p.do('dve', 'scalar_tensor_tensor', t1[:, 0:1], h0[:, inst, 0:1], ar, H[:, 0, pos:pos + 1], ALU.mult, ALU.add, reads=[h0, A, H], writes=[t1])
                    p.do('dve', 'scalar_tensor_tensor', t1[:, 1:2], h0[:, inst, 1:2], ar, H[:, 1, pos:pos + 1], ALU.mult, ALU.add, reads=[h0, A, H], writes=[t1])
                    p.do('dve', 'scalar_tensor_tensor', H[:, 0, pos:pos + 1], h0[:, inst, 1:2], nai, t1[:, 0:1], ALU.mult, ALU.add, reads=[h0, nAi, t1], writes=[H])
                    p.do('dve', 'scalar_tensor_tensor', H[:, 1, pos:pos + 1], h0[:, inst, 0:1], ai, t1[:, 1:2], ALU.mult, ALU.add, reads=[h0, A, t1], writes=[H])
                for (kk, rr0, rw0, sd, cnt) in (sched if lvl >= 4 else []):
                    if d == 0:
                        rs = slice(rr0, rr0 + (cnt - 1) * sd + 1, sd)
                        ws = slice(rw0, rw0 + (cnt - 1) * sd + 1, sd)
                    else:
                        a_r = L - 1 - (rr0 + (cnt - 1) * sd)
                        a_w = L - 1 - (rw0 + (cnt - 1) * sd)
                        rs = slice(a_r, a_r + (cnt - 1) * sd + 1, sd)
                        ws = slice(a_w, a_w + (cnt - 1) * sd + 1, sd)
                    ar, ai, nai = A[:, inst, kk, 0:1], A[:, inst, kk, 1:2], nAi[:, inst, kk:kk + 1]
                    p.do('dve', 'scalar_tensor_tensor', H[:, 0, ws], H[:, 0, rs], ar, H[:, 0, ws], ALU.mult, ALU.add, reads=[H, A], writes=[H])
                    p.do('dve', 'scalar_tensor_tensor', H[:, 0, ws], H[:, 1, rs], nai, H[:, 0, ws], ALU.mult, ALU.add, reads=[H, nAi], writes=[H])
                    p.do('dve', 'scalar_tensor_tensor', H[:, 1, ws], H[:, 1, rs], ar, H[:, 1, ws], ALU.mult, ALU.add, reads=[H, A], writes=[H])
                    p.do('dve', 'scalar_tensor_tensor', H[:, 1, ws], H[:, 0, rs], ai, H[:, 1, ws], ALU.mult, ALU.add, reads=[H, A], writes=[H])
                if lvl < 5:
                    continue
                if ctx:
                    pos = L - 1 if d == 0 else 0
                    for ri in range(2):
                        dst = O["nssm"][s["pi"], l, d, ri, 2 * gp:2 * gp + 2, :].rearrange("g (p o) -> (g p) o", o=1)
                        p.dma('sp', dst, H[:, ri, pos:pos + 1], reads=[H])
                qs = slice(0, 128)
                for b in range(nblk):
                    blk = slice(b * TB, (b + 1) * TB)
                    py = pyr.next()
                    p.do('pe', 'matmul', py[qs, 0:TB], Ct[:, inst, 0, :], H[:, 0, blk], start=True, stop=False, reads=[Ct, H], writes=[py])
                    p.do('pe', 'matmul', py[qs, 0:TB], Ct[:, inst, 1, :], H[:, 1, blk], start=False, stop=True, reads=[Ct, H], writes=[py])
                    if d == 0 and q == 0:
                        p.do('act', 'copy', yT[qs, chunk, blk], py[qs, 0:TB], reads=[py], writes=[yT])
                    else:
                        p.do('dve', 'tensor_tensor', yT[qs, chunk, blk], py[qs, 0:TB], yT[qs, chunk, blk], ALU.add, reads=[py, yT], writes=[yT])
        if lvl < 6:
            end_phase(k)
            return
        dT = sbt(st, "sdT", [128, 2], F32)
        gb = sbt(st, "sgb", [128, 2], F32)
        wg = sbt(st, "swg", [128, 2, 256], BF16)
        p.dma('sp', dT[:], I["ssm_dT"][l], writes=[dT])
        p.dma('sp', gb[:], I["glu_bT"][l], writes=[gb])
        p.dma('pool', wg[:], I["w_glu"][l].rearrange("(c p) n -> p c n", p=128), writes=[wg])
        zT = sbt(st, "szT", [128, 2, L], BF16)
        ytr = Ring([sbt(st, "syt%d" % i, [128, 512], F32) for i in range(2)])
        u1r = Ring([sbt(st, "su1%d" % i, [128, 512], F32) for i in range(2)])
        sgr = Ring([sbt(st, "ssg%d" % i, [128, 512], F32) for i in range(2)])
        outr = Ring([sbt(st, "sout%d" % i, [128, 512], BF16) for i in range(2)])
        for b in range(nblk):
            blk = slice(b * TB, (b + 1) * TB)
            for c in range(2):
                yt, u1, sg = ytr.next(), u1r.next(), sgr.next()
                p.do('dve', 'scalar_tensor_tensor', yt[:, 0:TB], uT[:, c, blk], dT[:, c:c + 1], yT[:, c, blk], ALU.mult, ALU.add, reads=[uT, dT, yT], writes=[yt])
                p.do('pool', 'tensor_tensor', u1[:, 0:TB], yt[:, 0:TB], yt[:, 0:TB], ALU.mult, reads=[yt], writes=[u1])
                p.do('pool', 'tensor_scalar', u1[:, 0:TB], u1[:, 0:TB], 0.044715, 1.0, ALU.mult, ALU.add, reads=[u1], writes=[u1])
                p.do('pool', 'tensor_tensor', u1[:, 0:TB], u1[:, 0:TB], yt[:, 0:TB], ALU.mult, reads=[u1, yt], writes=[u1])
                p.do('act', 'activation', sg[:, 0:TB], u1[:, 0:TB], AF.Sigmoid, scale=1.5957691216057308, reads=[u1], writes=[sg])
                p.do('dve', 'tensor_tensor', zT[:, c, blk], sg[:, 0:TB], yt[:, 0:TB], ALU.mult, reads=[sg, yt], writes=[zT])
            for mo in range(2):
                pg = pbr.next()
                sg = sgr.next()
                ot = outr.next()
                for kc in range(2):
                    p.do('pe', 'matmul', pg[:, 0:TB], wg[:, kc, mo * 128:(mo + 1) * 128], zT[:, kc, blk], start=(kc == 0), stop=(kc == 1), reads=[wg, zT], writes=[pg])
                p.do('act', 'activation', sg[:, 0:TB], pg[:, 0:TB], AF.Sigmoid, bias=gb[:, mo:mo + 1], reads=[pg, gb], writes=[sg])
                p.do('dve', 'tensor_tensor', ot[:, 0:TB], sg[:, 0:TB], zT[:, mo, blk], ALU.mult, reads=[sg, zT], writes=[ot])
                p.dma('sp', s["mixT"][4 + mo, :, blk], ot[:, 0:TB], reads=[ot])
        end_phase(k)


def phase_gdn(k, l, s):
    nc, p, I, O = k.nc, k.p, k.I, k.O
    L, ctx = s["L"], s["ctx"]
    nC = L // 64
    TB = min(512, L)
    with contextlib.ExitStack() as st:
        sbt, pst = k.sbt, k.pst
        qkv = sbt(st, "gqkv", [128, 6, L], BF16)
        kz = sbt(st, "gkz", [128, 4, L], BF16)
        bank = Ring([pst(st, "gbank%d" % i, [128, 512], F32) for i in range(7)])
        pbf = pst(st, "gpbf", [128, 1024], BF16)
        st1 = contextlib.ExitStack()
        xp = sbt(st1, "gxp", [128, 6, L + 2], BF16)
        cw = sbt(st1, "gcw", [128, 6, 3], F32)
        bo = sbt(st1, "gbo", [128, 128], BF16)
        p.dma('sp', cw[:], I["conv_wT"][l], writes=[cw])
        p.do('pool', 'memset', xp[:, :, 0:1], 0.0, writes=[xp])
        p.do('pool', 'memset', xp[:, :, L + 1:L + 2], 0.0, writes=[xp])
        for c in range(6):
            p.dma('sp', xp[:, c, 1:L + 1], s["fT"][2 + c, :, :], writes=[xp])
        p.do('pool', 'memset', bo[:], 0.0, writes=[bo])
        p.do('pool', 'memset', bo[0:64, 0:64], 1.0, writes=[bo])
        p.do('pool', 'memset', bo[64:128, 64:128], 1.0, writes=[bo])
        accr = Ring([sbt(st1, "gacc%d" % i, [128, 512], F32) for i in range(2)])
        silr = Ring([sbt(st1, "gsil%d" % i, [128, 512], F32) for i in range(2)])
        sqr = Ring([sbt(st1, "gsq%d" % i, [128, 512], BF16) for i in range(2)])
        rnr = Ring([sbt(st1, "grn%d" % i, [128, 512], F32) for i in range(2)])
        for c in range(6):
            for b in range(L // TB):
                cs = b * TB
                acc, sil = accr.next(), silr.next()
                p.do('dve', 'tensor_scalar', acc[:, 0:TB], xp[:, c, cs:cs + TB], cw[:, c, 0:1], None, ALU.mult, reads=[xp, cw], writes=[acc])
                p.do('dve', 'scalar_tensor_tensor', acc[:, 0:TB], xp[:, c, cs + 1:cs + 1 + TB], cw[:, c, 1:2], acc[:, 0:TB], ALU.mult, ALU.add, reads=[xp, cw, acc], writes=[acc])
                p.do('dve', 'scalar_tensor_tensor', acc[:, 0:TB], xp[:, c, cs + 2:cs + 2 + TB], cw[:, c, 2:3], acc[:, 0:TB], ALU.mult, ALU.add, reads=[xp, cw, acc], writes=[acc])
                if c >= 4:
                    p.do('act', 'activation', qkv[:, c, cs:cs + TB], acc[:, 0:TB], AF.Silu, reads=[acc], writes=[qkv])
                    continue
                sq, rn, pb = sqr.next(), rnr.next(), bank.next()
                p.do('act', 'activation', sil[:, 0:TB], acc[:, 0:TB], AF.Silu, reads=[acc], writes=[sil])
                p.do('pool', 'tensor_tensor', sq[:, 0:TB], sil[:, 0:TB], sil[:, 0:TB], ALU.mult, reads=[sil], writes=[sq])
                p.do('pe', 'matmul', pb[:, 0:TB], bo[:], sq[:, 0:TB], start=True, stop=True, reads=[bo, sq], writes=[pb])
                p.do('act', 'activation', rn[:, 0:TB], pb[:, 0:TB], AF.Sqrt, bias=EPS, reads=[pb], writes=[rn])
                p.do('dve', 'reciprocal', rn[:, 0:TB], rn[:, 0:TB], reads=[rn], writes=[rn])
                if c < 2:
                    p.do('dve', 'scalar_tensor_tensor', qkv[:, c, cs:cs + TB], sil[:, 0:TB], 0.125, rn[:, 0:TB], ALU.mult, ALU.mult, reads=[sil, rn], writes=[qkv])
                else:
                    p.do('dve', 'tensor_tensor', qkv[:, c, cs:cs + TB], sil[:, 0:TB], rn[:, 0:TB], ALU.mult, reads=[sil, rn], writes=[qkv])
        p.do('pool', 'memset', kz[:], 0.0, writes=[kz])
        for h_ in range(4):
            hs_ = slice(64 * (h_ % 2), 64 * (h_ % 2) + 64)
            p.do('dve' if h_ % 2 == 0 else 'pool', 'tensor_copy', kz[hs_, h_, :], qkv[hs_, 2 + h_ // 2, :], reads=[qkv, kz], writes=[kz])
        p.barrier()
        p.flush()
        st1.close()
        st2 = contextlib.ExitStack()
        gt = sbt(st2, "ggt", [64, nC, 16], F32)
        ab = sbt(st2, "gab", [64, 2, 8], F32)
        gm = sbt(st2, "ggm", [64, 6, 8, 64], F32)
        gtri = sbt(st2, "ggtri", [64, 4, 64], F32)
        p.dma('sp', gt[:], s["gates"][:, :, :], writes=[gt])
        p.dma('sp', ab[:], I["gdn_ab"][l], writes=[ab])
        p.dma('sp', gm[:], I["gmask"][:, :, :, :], writes=[gm])
        p.dma('sp', gtri[:], I["gtri"][:, :, :], writes=[gtri])
        names = ["gG", "gBeta", "gGs", "gBetas", "gGc", "gEgc", "gBg", "gGl", "gEgl", "gKd", "gT0"]
        T_ = {n: sbt(st2, n, [64, nC, 8], F32) for n in names}
        g_, be_, gS, beS, gcS, egcS, bgS, glS, eglS, kdS, t0 = [T_[n] for n in names]
        Aexp = sbt(st2, "gAexp", [64, 8], F32)
        p.do('act', 'activation', Aexp[:], ab[:, 0, :], AF.Exp, reads=[ab], writes=[Aexp])
        p.do('dve', 'tensor_tensor', t0[:], gt[:, :, 0:8], bcast(ab[:, 1:2, :], [64, nC, 8]), ALU.add, reads=[gt, ab], writes=[t0])
        p.do('act', 'activation', t0[:], t0[:], AF.Exp, reads=[t0], writes=[t0])
        p.do('act', 'activation', t0[:], t0[:], AF.Ln, bias=1.0, reads=[t0], writes=[t0])
        p.do('dve', 'tensor_tensor', g_[:], t0[:], bcast(Aexp[:].unsqueeze(1), [64, nC, 8]), ALU.mult, reads=[t0, Aexp], writes=[g_])
        p.do('dve', 'tensor_scalar', g_[:], g_[:], -1.0, None, ALU.mult, reads=[g_], writes=[g_])
        p.do('act', 'activation', be_[:], gt[:, :, 8:16], AF.Sigmoid, reads=[gt], writes=[be_])
        p.do('pool', 'tensor_copy', gS[:, :, 0:4], g_[:, :, 0:4], reads=[g_], writes=[gS])
        p.do('pool', 'tensor_copy', beS[:, :, 0:4], be_[:, :, 0:4], reads=[be_], writes=[beS])
        for sidx in range(nC):
            cb = nC - 1 - sidx
            p.do('pool', 'tensor_copy', gS[:, sidx, 4:8], g_[:, cb, 4:8], reads=[g_], writes=[gS])
            p.do('pool', 'tensor_copy', beS[:, sidx, 4:8], be_[:, cb, 4:8], reads=[be_], writes=[beS])
        NG = nC * 8
        def dir_matmul(dst, src, i_f, i_b):
            pa, pb_ = bank.next(), bank.next()
            flat = src[:].rearrange("p c e -> p (c e)")
            p.do('pe', 'matmul', pa[0:64, 0:NG], gtri[:, i_f, :], flat, start=True, stop=True, reads=[gtri, src], writes=[pa])
            p.do('pe', 'matmul', pb_[0:64, 0:NG], gtri[:, i_b, :], flat, start=True, stop=True, reads=[gtri, src], writes=[pb_])
            p.do('dve', 'tensor_copy', dst[:, :, 0:4], pa[0:64, 0:NG].rearrange("p (c e) -> p c e", e=8)[:, :, 0:4], reads=[pa], writes=[dst])
            p.do('dve', 'tensor_copy', dst[:, :, 4:8], pb_[0:64, 0:NG].rearrange("p (c e) -> p c e", e=8)[:, :, 4:8], reads=[pb_], writes=[dst])

        dir_matmul(gcS, gS, 0, 1)
        p.do('act', 'activation', egcS[:], gcS[:], AF.Exp, reads=[gcS], writes=[egcS])
        p.do('dve', 'tensor_tensor', bgS[:], beS[:], egcS[:], ALU.mult, reads=[beS, egcS], writes=[bgS])
        dir_matmul(glS, gcS, 2, 3)
        p.do('act', 'activation', eglS[:], glS[:], AF.Exp, reads=[glS], writes=[eglS])
        p.do('dve', 'tensor_tensor', kdS[:], glS[:], gcS[:], ALU.subtract, reads=[glS, gcS], writes=[kdS])
        p.do('act', 'activation', kdS[:], kdS[:], AF.Exp, reads=[kdS], writes=[kdS])
        S = sbt(st2, "gS", [64, 8, 64], F32)
        Sb = sbt(st2, "gSb", [64, 8, 64], BF16)
        if ctx:
            p.do('pool', 'memset', S[:], 0.0, writes=[S])
        else:
            p.dma('sp', S[:], I["gdn_s0"][l], writes=[S])
        p.do('pool', 'tensor_copy', Sb[:], S[:], reads=[S], writes=[Sb])

        def T3(name, dt=F32, n=2):
            return Ring([sbt(st2, "%s%d" % (name, i), [64, 8, 64], dt) for i in range(n)])

        dgR, XR, E1R, E2R = T3("gdg"), T3("gX"), T3("gE1"), T3("gE2")
        DsR, DtsR, DtiR = T3("gDs"), T3("gDts"), T3("gDti")
        NR, NtR, AtR, AccR = T3("gN"), T3("gNt"), T3("gAt"), T3("gAcc")
        VbR, RR, KdR = T3("gVb"), T3("gR"), T3("gKd_")
        UR, WTR, VnR, OR, TmR = T3("gU"), T3("gWT"), T3("gVn"), T3("gO"), T3("gTm")
        I8 = gm[:, 4, :, :]
        qodR = Ring([sbt(st2, "gqod%d" % i, [64, 2, 2, 64], BF16) for i in range(2)])

        def bview(b):
            return b[0:64, :].rearrange("p (e j) -> p e j", e=8)

        def colb(tab, sidx):
            return bcast(tab[:, sidx, :].unsqueeze(2), [64, 8, 64])

        for sidx in range(nC):
            ce = [sidx if e < 4 else nC - 1 - sidx for e in range(8)]
            tk = [slice(ce[e] * 64, ce[e] * 64 + 64) for e in range(8)]
            qT = [qkv[64 * (e % 4 % 2):64 * (e % 4 % 2) + 64, 0 + (e % 4) // 2, tk[e]] for e in range(8)]
            A1b, A2b = bank.next(), bank.next()
            A1, A2 = bview(A1b), bview(A2b)
            for e in range(8):
                h_ = e % 4
                p.do('pe', 'matmul', A1[:, e, :], kz[:, h_, tk[e]], qkv[:, 2 + h_ // 2, tk[e]], start=True, stop=True, reads=[kz, qkv], writes=[A1b])
            for e in range(8):
                h_ = e % 4
                p.do('pe', 'matmul', A2[:, e, :], kz[:, h_, tk[e]], qkv[:, 0 + h_ // 2, tk[e]], start=True, stop=True, reads=[kz, qkv], writes=[A2b])
            pt4 = pbf[0:64, :].rearrange("p (a j) -> p a j", a=8)
            ptv = pbf[0:64, :].rearrange("p (a j) -> p a j", a=16)
            for kind in range(2):
                for d_ in range(2):
                    for c_ in range(2):
                        e0 = d_ * 4 + c_ * 2
                        p.do('pe', 'transpose', pt4[:, kind * 4 + d_ * 2 + c_, :], qkv[:, 2 + 2 * kind + c_, tk[e0]], k.identb[:], reads=[qkv, k.identb], writes=[pbf])
            Vb, R_, Kd = VbR.next(), RR.next(), KdR.next()
            p.do('dve', 'tensor_tensor', Vb[:], ptv[:, 8:16, :], colb(beS, sidx), ALU.mult, reads=[pbf, beS], writes=[Vb])
            p.do('dve', 'tensor_tensor', R_[:], ptv[:, 0:8, :], colb(bgS, sidx), ALU.mult, reads=[pbf, bgS], writes=[R_])
            p.do('dve', 'tensor_tensor', Kd[:], ptv[:, 0:8, :], colb(kdS, sidx), ALU.mult, reads=[pbf, kdS], writes=[Kd])
            dg, X, E1, E2 = dgR.next(), XR.next(), E1R.next(), E2R.next()
            PGb, PBb = bank.next(), bank.next()
            PG, PB = bview(PGb), bview(PBb)
            p.do('pool', 'tensor_tensor', dg[:], I8, colb(gcS, sidx), ALU.mult, reads=[gm, gcS], writes=[dg])
            for e in range(8):
                p.do('pe', 'matmul', PG[:, e, :], k.ones_f[0:64, 0:64], dg[:, e, :], start=True, stop=True, reads=[k.ones_f, dg], writes=[PGb])
            p.do('dve', 'tensor_tensor', X[:], PG, colb(gcS, sidx), ALU.subtract, reads=[PGb, gcS], writes=[X])
            p.do('pool', 'tensor_scalar', E1[:], X[:], 0.0, None, ALU.max, reads=[X], writes=[E1])
            p.do('pool', 'tensor_scalar', E2[:], X[:], 0.0, None, ALU.min, reads=[X], writes=[E2])
            p.do('act', 'activation', E1[:], E1[:], AF.Exp, scale=-1.0, reads=[E1], writes=[E1])
            p.do('act', 'activation', E2[:], E2[:], AF.Exp, reads=[E2], writes=[E2])
            Ds, Dts, Dti = DsR.next(), DtsR.next(), DtiR.next()
            p.do('dve', 'tensor_tensor', Ds[:], E1[:], gm[:, 0, :, :], ALU.mult, reads=[E1, gm], writes=[Ds])
            p.do('pool', 'tensor_tensor', Dts[:], E2[:], gm[:, 2, :, :], ALU.mult, reads=[E2, gm], writes=[Dts])
            p.do('dve', 'tensor_tensor', Dti[:], E2[:], gm[:, 3, :, :], ALU.mult, reads=[E2, gm], writes=[Dti])
            dg2 = dgR.next()
            p.do('pool', 'tensor_tensor', dg2[:], I8, colb(beS, sidx), ALU.mult, reads=[gm, beS], writes=[dg2])
            for e in range(8):
                p.do('pe', 'matmul', PB[:, e, :], k.ones_f[0:64, 0:64], dg2[:, e, :], start=True, stop=True, reads=[k.ones_f, dg2], writes=[PBb])
            N_, Nt, At, Acc = NR.next(), NtR.next(), AtR.next(), AccR.next()
            p.do('dve', 'tensor_tensor', N_[:], A1, Ds[:], ALU.mult, reads=[A1b, Ds], writes=[N_])
            p.do('dve', 'tensor_tensor', N_[:], N_[:], colb(beS, sidx), ALU.mult, reads=[N_, beS], writes=[N_])
            p.do('dve', 'tensor_tensor', Nt[:], A1, Dts[:], ALU.mult, reads=[A1b, Dts], writes=[Nt])
            p.do('dve', 'tensor_tensor', Nt[:], PB, Nt[:], ALU.mult, reads=[PBb, Nt], writes=[Nt])
            p.do('dve', 'tensor_tensor', At[:], A2, Dti[:], ALU.mult, reads=[A2b, Dti], writes=[At])
            p.do('dve', 'scalar_tensor_tensor', Acc[:], Nt[:], -1.0, I8, ALU.mult, ALU.add, reads=[Nt, gm], writes=[Acc])
            P_, Pt = N_, Nt
            for lv in range(5):
                PPa, PPb, PAb = bank.next(), bank.next(), bank.next()
                Pn, Ptn = NR.next(), NtR.next()
                for e in range(8):
                    p.do('pe', 'matmul', bview(PPa)[:, e, :], Pt[:, e, :], P_[:, e, :], start=True, stop=True, reads=[Pt, P_], writes=[PPa])
                for e in range(8):
                    p.do('pe', 'matmul', bview(PPb)[:, e, :], P_[:, e, :], Pt[:, e, :], start=True, stop=True, reads=[Pt, P_], writes=[PPb])
                p.do('act', 'copy', Pn[:], bview(PPa), reads=[PPa], writes=[Pn])
                p.do('dve', 'tensor_copy', Ptn[:], bview(PPb), reads=[PPb], writes=[Ptn])
                for e in range(8):
                    p.do('pe', 'matmul', bview(PAb)[:, e, :], Pn[:, e, :], Acc[:, e, :], start=True, stop=True, reads=[Pn, Acc], writes=[PAb])
                p.do('dve', 'tensor_tensor', Acc[:], Acc[:], bview(PAb), ALU.add, reads=[Acc, PAb], writes=[Acc])
                P_, Pt = Pn, Ptn
            PUb, PWb = bank.next(), bank.next()
            U_, WT = UR.next(), WTR.next()
            for e in range(8):
                p.do('pe', 'matmul', bview(PUb)[:, e, :], Acc[:, e, :], Vb[:, e, :], start=True, stop=True, reads=[Acc, Vb], writes=[PUb])
            for e in range(8):
                p.do('pe', 'matmul', bview(PWb)[:, e, :], R_[:, e, :], Acc[:, e, :], start=True, stop=True, reads=[Acc, R_], writes=[PWb])
            p.do('act', 'copy', U_[:], bview(PUb), reads=[PUb], writes=[U_])
            p.do('act', 'copy', WT[:], bview(PWb), reads=[PWb], writes=[WT])
            PWSb, POb, PO2b, PKVb = bank.next(), bank.next(), bank.next(), bank.next()
            Vn, Oo, Tm = VnR.next(), OR.next(), TmR.next()
            for e in range(8):
                p.do('pe', 'matmul', bview(PWSb)[:, e, :], WT[:, e, :], S[:, e, :], start=True, stop=True, reads=[WT, S], writes=[PWSb])
            p.do('dve', 'tensor_tensor', Vn[:], U_[:], bview(PWSb), ALU.subtract, reads=[U_, PWSb], writes=[Vn])
            qod = qodR.next()
            for d_ in range(2):
                cc = sidx if d_ == 0 else nC - 1 - sidx
                p.do('pool', 'tensor_copy', qod[:, d_, :, :], qkv[64:128, 0:2, cc * 64:(cc + 1) * 64], reads=[qkv], writes=[qod])
            for e in range(8):
                h_ = e % 4
                lq = qT[e] if h_ % 2 == 0 else qod[:, e // 4, h_ // 2, :]
                p.do('pe', 'matmul', bview(POb)[:, e, :], lq, Sb[:, e, :], start=True, stop=True, reads=[qkv, qod, Sb], writes=[POb])
            for e in range(8):
                p.do('pe', 'matmul', bview(PO2b)[:, e, :], At[:, e, :], Vn[:, e, :], start=True, stop=True, reads=[At, Vn], writes=[PO2b])
            for e in range(8):
                p.do('pe', 'matmul', bview(PKVb)[:, e, :], Kd[:, e, :], Vn[:, e, :], start=True, stop=True, reads=[Kd, Vn], writes=[PKVb])
            p.do('dve', 'tensor_tensor', Tm[:], bview(POb), colb(egcS, sidx), ALU.mult, reads=[POb, egcS], writes=[Tm])
            p.do('dve', 'tensor_tensor', Oo[:], Tm[:], bview(PO2b), ALU.add, reads=[Tm, PO2b], writes=[Oo])
            cf, cb = sidx, nC - 1 - sidx
            p.dma('sp', s["go"][0, cf * 64:(cf + 1) * 64, :].rearrange("t (h v) -> t h v", h=4), Oo[:, 0:4, :], reads=[Oo])
            p.dma('sp', s["go"][1, cb * 64:(cb + 1) * 64, :].rearrange("t (h v) -> t h v", h=4), Oo[:, 4:8, :], reads=[Oo])
            Tm2 = TmR.next()
            p.do('pool', 'tensor_tensor', Tm2[:], S[:], colb(eglS, sidx), ALU.mult, reads=[S, eglS], writes=[Tm2])
            p.do('dve', 'tensor_tensor', S[:], Tm2[:], bview(PKVb), ALU.add, reads=[Tm2, PKVb], writes=[S])
            p.do('pool', 'tensor_copy', Sb[:], S[:], reads=[S], writes=[Sb])
        if ctx:
            p.dma('sp', O["ngdn"][s["pi"], l].rearrange("d h k v -> k (d h) v"), S[:], reads=[S])
        p.barrier()
        p.flush()
        st2.close()
        gz = sbt(st, "ggz", [128, 2, L], BF16)
        ng = sbt(st, "gng", [128, 1], F32)
        p.dma('sp', gz[:], s["fT"][8:10].rearrange("c p t -> p c t"), writes=[gz])
        p.dma('sp', ng[:], I["gdn_ng"][l], writes=[ng])
        p.do('act', 'activation', gz[:], gz[:], AF.Silu, reads=[gz], writes=[gz])
        o0r = Ring([sbt(st, "go0%d" % i, [128, 4, 64], F32) for i in range(2)])
        o1r = Ring([sbt(st, "go1%d" % i, [128, 4, 64], F32) for i in range(2)])
        sqo = sbt(st, "gsqo", [128, 4, 64], F32)
        ss4 = sbt(st, "gss4", [128, 4], F32)
        onr = Ring([sbt(st, "gon%d" % i, [128, 4, 64], BF16) for i in range(2)])
        outr = Ring([sbt(st, "gout%d" % i, [128, 2, 128], BF16) for i in range(2)])
        for t in range(L // 128):
            rows = slice(t * 128, (t + 1) * 128)
            o0, o1, on, ot = o0r.next(), o1r.next(), onr.next(), outr.next()
            p.dma('sp', o0[:], s["go"][0, rows, :].rearrange("t (h v) -> t h v", h=4), writes=[o0])
            p.dma('sp', o1[:], s["go"][1, rows, :].rearrange("t (h v) -> t h v", h=4), writes=[o1])
            p.do('pool', 'tensor_tensor', o0[:], o0[:], o1[:], ALU.add, reads=[o0, o1], writes=[o0])
            p.do('act', 'activation', sqo[:], o0[:], AF.Square, reads=[o0], writes=[sqo])
            p.do('dve', 'tensor_reduce', ss4[:], sqo[:], AX.X, ALU.add, reads=[sqo], writes=[ss4])
            p.do('act', 'activation', ss4[:], ss4[:], AF.Sqrt, scale=1.0 / 64, bias=EPS, reads=[ss4], writes=[ss4])
            p.do('dve', 'reciprocal', ss4[:], ss4[:], reads=[ss4], writes=[ss4])
            p.do('dve', 'tensor_tensor', on[:], o0[:], bcast(ss4[:].unsqueeze(2), [128, 4, 64]), ALU.mult, reads=[o0, ss4], writes=[on])
            pv = pbf[:, 0:256].rearrange("p (c t) -> p c t", c=2)
            for c in range(2):
                p.do('pe', 'transpose', pv[:, c, :], on[:, 2 * c:2 * c + 2, :].rearrange("p h v -> p (h v)"), k.identb[:], reads=[on, k.identb], writes=[pbf])
            for c in range(2):
                p.do('dve', 'scalar_tensor_tensor', ot[:, c, :], pv[:, c, :], ng[:, 0:1], gz[:, c, rows], ALU.mult, ALU.mult, reads=[pbf, ng, gz], writes=[ot])
            p.dma('sp', s["mixT"][6:8, :, rows].rearrange("c p t -> p c t"), ot[:], reads=[ot])
        end_phase(k)


FULL_CFG = dict(LS=4096, LP=256, NP=2, DEPTH=4, PAST=256, stages="MABC")


def kernel(**inputs):
    cfg = dict(FULL_CFG)
    nc = build(cfg)
    consts = const_tables(cfg)
    w = host_weights(inputs, cfg)
    in_maps = []
    for c in range(NCORES):
        m = {}
        m.update(consts)
        m.update(w)
        m.update(host_core_inputs(inputs, c, cfg))
        in_maps.append(m)
    res = run_bass_kernel_spmd(nc, in_maps, core_ids=list(range(NCORES)))
    R = res.results
    NP, LP, DEPTH = cfg["NP"], cfg["LP"], cfg["DEPTH"]
    y_sample = np.stack([np.asarray(R[c]["y_s"], dtype=np.float32) for c in range(NCORES)], axis=0)
    y_prompt = np.concatenate([np.asarray(R[c]["y_p"], dtype=np.float32).reshape(NP, LP, D) for c in range(NCORES)], axis=0)
    nk = np.concatenate([np.asarray(R[c]["nk"], dtype=np.float32).reshape(NP, DEPTH, LP, 2, 64) for c in range(NCORES)], axis=0)
    nv = np.concatenate([np.asarray(R[c]["nv"], dtype=np.float32).reshape(NP, DEPTH, LP, 2, 64) for c in range(NCORES)], axis=0)
    nssm = np.concatenate([np.asarray(R[c]["nssm"], dtype=np.float32) for c in range(NCORES)], axis=0)
    ngdn = np.concatenate([np.asarray(R[c]["ngdn"], dtype=np.float32) for c in range(NCORES)], axis=0)
    return (y_prompt, y_sample, nk, nv, nssm, ngdn)
```

```python
import contextlib
import math
import numpy as np
import concourse.bass as bass
import concourse.mybir as mybir
from concourse.bass_utils import run_bass_kernel_spmd

F32 = mybir.dt.float32
BF16 = mybir.dt.bfloat16
AF = mybir.ActivationFunctionType
ALU = mybir.AluOpType
AX = mybir.AxisListType

D = 1024
KC = 8
EPS = 1e-6
IN_W = 2064
NCORES = 8


class Buf:
    __slots__ = ("name", "ap", "last_w", "readers")

    def __init__(self, ap=None, name=""):
        self.ap = ap
        self.name = name
        self.last_w = None
        self.readers = []

    def __getitem__(self, k):
        return self.ap[k]


class Ring:
    def __init__(self, bufs):
        self.bufs = bufs
        self.i = 0

    def next(self):
        b = self.bufs[self.i]
        self.i = (self.i + 1) % len(self.bufs)
        return b


class Prog:
    ENG = ("pe", "act", "dve", "pool", "sp")

    def __init__(self, nc, stack, n_dma_sems=40):
        self.nc = nc
        self.lists = {e: [] for e in self.ENG}
        self.count = {e: 0 for e in self.ENG}
        self.known = {e: {} for e in self.ENG}
        self.n_dma_sems = n_dma_sems
        self.dma_val = [0] * n_dma_sems
        self.dma_rr = 0
        self.dma_rr_pool = 0
        self.esem = {e: stack.enter_context(nc.semaphore("sem_" + e)) for e in self.ENG}
        self.dsem = [stack.enter_context(nc.semaphore("dsem%d" % i)) for i in range(n_dma_sems)]
        self.ninst = 0

    def _need(self, eng, tok, waits):
        if tok is None:
            return
        key = (tok[0], tok[1])
        if self.known[eng].get(key, 0) >= tok[2]:
            return
        if tok[2] > waits.get(key, 0):
            waits[key] = tok[2]

    def _collect(self, eng, reads, writes, pe_ok=False):
        waits = {}
        for b in reads:
            self._need(eng, b.last_w, waits)
        for b in writes:
            self._need(eng, b.last_w, waits)
            for r in b.readers:
                self._need(eng, r, waits)
        out = []
        for key, val in waits.items():
            if key == ('e', 'pe') and eng == 'pe':
                continue
            out.append((key, val))
            self.known[eng][key] = val
        return out

    def _mark(self, tok, reads, writes):
        for b in reads:
            b.readers.append(tok)
            if len(b.readers) > 16:
                best = {}
                for t in b.readers:
                    k = (t[0], t[1])
                    if k not in best or best[k][2] < t[2]:
                        best[k] = t
                b.readers = list(best.values())
        for b in writes:
            b.last_w = tok
            b.readers = []

    def op(self, eng, fn, reads=(), writes=()):
        for key, val in self._collect(eng, reads, writes):
            self.lists[eng].append(('w', key, val))
        self.count[eng] += 1
        self.lists[eng].append(('i', fn))
        tok = ('e', eng, self.count[eng])
        self._mark(tok, reads, writes)
        self.ninst += 1
        return tok

    def do(self, eng, meth, *args, reads=(), writes=(), **kw):
        return self.op(eng, lambda h: getattr(h, meth)(*args, **kw), reads=reads, writes=writes)

    def dma(self, eng, out_ap, in_ap, reads=(), writes=(), **kw):
        half = self.n_dma_sems // 2
        if eng == 'pool':
            idx = half + self.dma_rr_pool
            self.dma_rr_pool = (self.dma_rr_pool + 1) % (self.n_dma_sems - half)
        else:
            idx = self.dma_rr
            self.dma_rr = (self.dma_rr + 1) % half
        wl = self._collect(eng, reads, writes)
        prev = self.dma_val[idx]
        if prev > 0 and self.known[eng].get(('d', idx), 0) < prev:
            wl.append((('d', idx), prev))
            self.known[eng][('d', idx)] = prev
        for key, val in wl:
            self.lists[eng].append(('w', key, val))
        self.dma_val[idx] += 16
        self.lists[eng].append(('dma', idx, out_ap, in_ap, kw))
        tok = ('d', idx, self.dma_val[idx])
        self._mark(tok, reads, writes)
        self.ninst += 1
        return tok

    def barrier(self):
        for eng in self.ENG:
            for other in self.ENG:
                if other == eng:
                    continue
                v = self.count[other]
                if v > 0 and self.known[eng].get(('e', other), 0) < v:
                    self.lists[eng].append(('w', ('e', other), v))
                    self.known[eng][('e', other)] = v
            for idx in range(self.n_dma_sems):
                v = self.dma_val[idx]
                if v > 0 and self.known[eng].get(('d', idx), 0) < v:
                    self.lists[eng].append(('w', ('d', idx), v))
                    self.known[eng][('d', idx)] = v

    def flush(self):
        nc = self.nc
        lists = self.lists
        self.lists = {e: [] for e in self.ENG}
        if not any(lists.values()):
            return
        esem, dsem = self.esem, self.dsem
        with nc.Block() as block:
            def replay(eng, h):
                my = esem[eng]
                for item in lists[eng]:
                    k = item[0]
                    if k == 'w':
                        key, val = item[1], item[2]
                        h.wait_ge(esem[key[1]] if key[0] == 'e' else dsem[key[1]], val)
                    elif k == 'i':
                        item[1](h).then_inc(my, 1)
                    else:
                        _, idx, o, i, kw = item
                        h.dma_start(out=o, in_=i, **kw).then_inc(dsem[idx], 16)

            @block.tensor
            def _(h):
                replay('pe', h)

            @block.scalar
            def _(h):
                replay('act', h)

            @block.vector
            def _(h):
                replay('dve', h)

            @block.gpsimd
            def _(h):
                replay('pool', h)

            @block.sync
            def _(h):
                replay('sp', h)


class K:
    pass


def bcast(ap, shape):
    return ap.to_broadcast(shape)


def build(cfg, dbg=False):
    LS, LP, NP, DEPTH, PAST = cfg["LS"], cfg["LP"], cfg["NP"], cfg["DEPTH"], cfg["PAST"]
    nc = bass.Bass("TRN2", target_bir_lowering=False)
    k = K()
    k.nc, k.cfg = nc, cfg

    def din(name, shape, dt=F32):
        return nc.dram_tensor(name, list(shape), dt, kind="ExternalInput").ap()

    def dout(name, shape, dt=F32):
        return nc.dram_tensor(name, list(shape), dt, kind="ExternalOutput").ap()

    def dscr(name, shape, dt=F32):
        return nc.dram_tensor(name, list(shape), dt, kind="ExternalOutput" if dbg else "Internal").ap()

    I = {}
    I["x_s"] = din("x_s", [LS, D])
    I["x_p"] = din("x_p", [NP * LP, D])
    I["cond2"] = din("cond2", [128, 2, KC])
    I["cache_k"] = din("cache_k", [DEPTH, PAST, 128])
    I["cache_v"] = din("cache_v", [DEPTH, PAST, 128])
    I["ssm_h0"] = din("ssm_h0", [DEPTH, 128, 16, 2])
    I["gdn_s0"] = din("gdn_s0", [DEPTH, 64, 8, 64])
    I["w_mod"] = din("w_mod", [DEPTH, D, 6 * D])
    I["b_modT"] = din("b_modT", [DEPTH, 128, 48])
    I["n1g"] = din("n1g", [DEPTH, 128, KC])
    I["n2g"] = din("n2g", [DEPTH, 128, KC])
    I["w_in"] = din("w_in", [DEPTH, D, IN_W])
    I["qk_g"] = din("qk_g", [DEPTH, 128, 2, 64])
    I["sink_b"] = din("sink_b", [DEPTH, 128, 8])
    I["lam"] = din("lam", [DEPTH, 128, 3, 16])
    I["ssm_bt"] = din("ssm_bt", [DEPTH, 128, 16, 2, 128])
    I["ssm_ct"] = din("ssm_ct", [DEPTH, 128, 16, 2, 128])
    I["ssm_dT"] = din("ssm_dT", [DEPTH, 128, 2])
    I["glu_bT"] = din("glu_bT", [DEPTH, 128, 2])
    I["w_glu"] = din("w_glu", [DEPTH, 256, 256])
    I["conv_wT"] = din("conv_wT", [DEPTH, 128, 6, 3])
    I["gdn_ab"] = din("gdn_ab", [DEPTH, 64, 2, 8])
    I["gdn_ng"] = din("gdn_ng", [DEPTH, 128, 1])
    I["w_out"] = din("w_out", [DEPTH, D, D])
    I["w_ff1"] = din("w_ff1", [DEPTH, D, 4 * D])
    I["w_ff2"] = din("w_ff2", [DEPTH, 4 * D, D])
    I["ident"] = din("ident", [128, 128])
    I["rope"] = din("rope", [LS, 2, 32])
    I["amask"] = din("amask", [128, 2, 128])
    I["gmask"] = din("gmask", [64, 6, 8, 64])
    I["gtri"] = din("gtri", [64, 4, 64])
    k.I = I
    O = {}
    O["y_s"] = dout("y_s", [LS, D])
    O["y_p"] = dout("y_p", [NP * LP, D])
    O["nk"] = dout("nk", [NP, DEPTH, LP, 128])
    O["nv"] = dout("nv", [NP, DEPTH, LP, 128])
    O["nssm"] = dout("nssm", [NP, DEPTH, 2, 2, 16, 64])
    O["ngdn"] = dout("ngdn", [NP, DEPTH, 2, 4, 64, 64])
    k.O = O
    seqs = [dict(name="s", L=LS, ctx=False, xin=I["x_s"], y=O["y_s"], pi=-1, cond=0)]
    for i in range(NP):
        seqs.append(dict(name="p%d" % i, L=LP, ctx=True, xin=I["x_p"][i * LP:(i + 1) * LP, :],
                         y=O["y_p"][i * LP:(i + 1) * LP, :], pi=i, cond=1))
    for s in seqs:
        L = s["L"]
        n = s["name"]
        s["qT"] = dscr("qT_" + n, [4, 128, L], BF16)
        s["kT"] = dscr("kT_" + n, [2, 128, L], BF16)
        s["v"] = dscr("v_" + n, [L, 128], BF16)
        s["fT"] = dscr("fT_" + n, [10, 128, L], BF16)
        s["gates"] = dscr("gates_" + n, [64, L // 64, 16], F32)
        s["mixT"] = dscr("mixT_" + n, [8, 128, L], BF16)
        s["go"] = dscr("go_" + n, [2, L, 256], F32)
    k.seqs = seqs

    with contextlib.ExitStack() as top:
        p = Prog(nc, top)
        k.p = p

        uid = [0]

        def sbt(st, name, shape, dt):
            uid[0] += 1
            t = st.enter_context(nc.sbuf_tensor("sb%d_%s" % (uid[0], name), list(shape), dt))
            return Buf(t, name)

        def pst(st, name, shape, dt):
            uid[0] += 1
            t = st.enter_context(nc.psum_tensor("ps%d_%s" % (uid[0], name), list(shape), dt))
            return Buf(t, name)

        k.sbt, k.pst = sbt, pst
        k.ident = sbt(top, "ident", [128, 128], F32)
        k.identb = sbt(top, "identb", [128, 128], BF16)
        k.modT = sbt(top, "modT", [128, DEPTH, 48, 2], F32)
        k.ones_f = sbt(top, "ones_f", [128, 128], F32)
        k.ones_b = sbt(top, "ones_b", [128, 128], BF16)
        p.dma('sp', k.ident[:], I["ident"][:, :], writes=[k.ident])
        p.do('dve', 'tensor_copy', k.identb[:], k.ident[:], reads=[k.ident], writes=[k.identb])
        p.do('dve', 'memset', k.ones_f[:], 1.0, writes=[k.ones_f])
        p.do('dve', 'memset', k.ones_b[:], 1.0, writes=[k.ones_b])

        stages = cfg.get("stages", "MABC")
        if "M" in stages:
            prologue_mod(k)
        if dbg:
            dm = dout("dbg_modT", [128, DEPTH * 96])
            p.dma('sp', dm[:, :], k.modT[:].rearrange("p l c t -> p (l c t)"), reads=[k.modT])
        for l in range(DEPTH):
            if "A" in stages:
                phase_a(k, l)
            if "B" in stages:
                for s in seqs:
                    if cfg.get("attn", True):
                        phase_attn(k, l, s)
                    if cfg.get("ssm", True):
                        phase_ssm(k, l, s)
                    if cfg.get("gdn", True):
                        phase_gdn(k, l, s)
            if "C" in stages:
                phase_c(k, l)
        p.barrier()
        p.flush()
    return nc


def end_phase(k):
    k.p.barrier()
    k.p.flush()


def prologue_mod(k):
    nc, p, I = k.nc, k.p, k.I
    DEPTH = k.cfg["DEPTH"]
    with contextlib.ExitStack() as st:
        cond = k.sbt(st, "cond", [128, 2, KC], F32)
        sc = k.sbt(st, "sc", [128, KC, 2], F32)
        bm = k.sbt(st, "bm", [128, DEPTH, 48], F32)
        wring = Ring([k.sbt(st, "wm%d" % i, [128, 6 * D], F32) for i in range(2)])
        pring = Ring([k.pst(st, "pm%d" % i, [128, 256, 2], F32) for i in range(2)])
        p.dma('sp', cond[:], I["cond2"][:, :, :], writes=[cond])
        p.dma('sp', bm[:], I["b_modT"].rearrange("l p c -> p l c"), writes=[bm])
        p.do('act', 'activation', sc[:].rearrange("p k c -> p c k"), cond[:], AF.Silu, reads=[cond], writes=[sc])
        for l in range(DEPTH):
            for kc in range(KC):
                w = wring.next()
                p.dma('sp' if kc % 2 == 0 else 'pool', w[:], I["w_mod"][l, kc * 128:(kc + 1) * 128, :], writes=[w])
                ps = pring.next()
                for fc in range(48):
                    p.do('pe', 'matmul', ps[:, fc, :], w[:, fc * 128:(fc + 1) * 128], sc[:, kc, :], start=True, stop=True,
                         reads=[w, sc], writes=[ps])
                if kc == 0:
                    p.do('dve', 'tensor_tensor', k.modT[:, l, :, :], ps[:, 0:48, :], bcast(bm[:, l, :].unsqueeze(2), [128, 48, 2]), ALU.add,
                         reads=[ps, bm], writes=[k.modT])
                else:
                    p.do('dve', 'tensor_tensor', k.modT[:, l, :, :], ps[:, 0:48, :], k.modT[:, l, :, :], ALU.add,
                         reads=[ps, k.modT], writes=[k.modT])
        end_phase(k)


def phase_a(k, l):
    nc, p, I, O = k.nc, k.p, k.I, k.O
    with contextlib.ExitStack() as st:
        sbt, pst = k.sbt, k.pst
        win = sbt(st, "win", [128, KC, IN_W], BF16)
        wsrc = I["w_in"][l].rearrange("(c p) n -> p c n", p=128)
        for c in range(KC):
            for hh in range(2):
                p.dma('pool', win[:, c, hh * 1032:(hh + 1) * 1032], wsrc[:, c, hh * 1032:(hh + 1) * 1032], writes=[win])
        n1g = sbt(st, "n1g", [128, KC], F32)
        sc1 = sbt(st, "sc1", [128, KC, 2], F32)
        qkg = sbt(st, "qkg", [128, 2, 64], F32)
        p.dma('sp', n1g[:], I["n1g"][l], writes=[n1g])
        p.dma('sp', qkg[:], I["qk_g"][l], writes=[qkg])
        p.do('dve', 'tensor_scalar', sc1[:], k.modT[:, l, 8:16, :], 1.0, None, ALU.add, reads=[k.modT], writes=[sc1])
        p.do('dve', 'tensor_tensor', sc1[:], sc1[:], bcast(n1g[:].unsqueeze(2), [128, KC, 2]), ALU.mult, reads=[sc1, n1g], writes=[sc1])
        xring = Ring([sbt(st, "xa%d" % i, [128, D], F32) for i in range(2)])
        junk = sbt(st, "junka", [128, D], F32)
        ssr = Ring([sbt(st, "ssa%d" % i, [128, 1], F32) for i in range(2)])
        xnr = Ring([sbt(st, "xna%d" % i, [128, D], BF16) for i in range(2)])
        hTr = Ring([sbt(st, "hTa%d" % i, [128, KC, 512], BF16) for i in range(2)])
        sq = sbt(st, "sqa", [128, 10, 64], F32)
        ss10 = sbt(st, "ss10", [128, 10], F32)
        qn = sbt(st, "qna", [128, 10, 64], F32)
        tmp = [sbt(st, "ropet%d" % i, [128, 10, 32], F32) for i in range(4)]
        qr = sbt(st, "qra", [128, 10, 64], BF16)
        kd = sbt(st, "kda", [128, 4, 64], BF16)
        vb = sbt(st, "vba", [128, 128], BF16)
        vf = sbt(st, "vfa", [128, 128], F32)
        rp = sbt(st, "rpa", [128, 2, 32], F32)
        stager = Ring([sbt(st, "stga%d" % i, [128, 6, 512], BF16) for i in range(2)])
        fstr = Ring([sbt(st, "fsta%d" % i, [128, 512], BF16) for i in range(3)])
        gsb = sbt(st, "gsba", [64, 8, 16], F32)
        pTr = Ring([pst(st, "pTa%d" % i, [128, KC, 128], BF16) for i in range(2)])
        pq = pst(st, "pqa", [128, 512], F32)
        pkv = pst(st, "pkva", [128, 512], F32)
        ptr = pst(st, "ptra", [128, 8, 128], BF16)
        pfr = Ring([pst(st, "pfa%d" % i, [128, 512], F32) for i in range(2)])
        pg = pst(st, "pga", [128, 32, 16], F32)
        ev = [0]
        p.barrier()

        for s in (k.seqs[::-1] if k.cfg.get('rev') else k.seqs):
            L, ctx, cond = s["L"], s["ctx"], s["cond"]
            xsrc = s["xin"] if l == 0 else s["y"]
            TB = min(512, L)
            ntb = TB // 128
            for b in range(L // TB):
                hT = hTr.next()
                for ti in range(ntb):
                    r0 = b * TB + ti * 128
                    xt = xring.next()
                    ss = ssr.next()
                    xn = xnr.next()
                    pT = pTr.next()
                    p.dma('sp', xt[:], xsrc[r0:r0 + 128, :], writes=[xt])
                    p.do('act', 'activation', junk[:], xt[:], AF.Square, accum_out=ss[:], reads=[xt], writes=[junk, ss])
                    p.do('act', 'activation', ss[:], ss[:], AF.Sqrt, scale=1.0 / D, bias=EPS, reads=[ss], writes=[ss])
                    p.do('dve', 'reciprocal', ss[:], ss[:], reads=[ss], writes=[ss])
                    p.do('dve', 'tensor_scalar', xn[:], xt[:], ss[:, 0:1], None, ALU.mult, reads=[xt, ss], writes=[xn])
                    for c in range(KC):
                        p.do('pe', 'transpose', pT[:, c, :], xn[:, c * 128:(c + 1) * 128], k.identb[:], reads=[xn, k.identb], writes=[pT])
                    for c in range(KC):
                        if c % 2 == 0:
                            p.do('act', 'activation', hT[:, c, ti * 128:(ti + 1) * 128], pT[:, c, :], AF.Identity, scale=sc1[:, c, cond:cond + 1], bias=k.modT[:, l, c, cond:cond + 1],
                                 reads=[pT, sc1, k.modT], writes=[hT])
                        else:
                            p.do('dve', 'tensor_scalar', hT[:, c, ti * 128:(ti + 1) * 128], pT[:, c, :], sc1[:, c, cond:cond + 1], k.modT[:, l, c, cond:cond + 1], ALU.mult, ALU.add,
                                 reads=[pT, sc1, k.modT], writes=[hT])
                stage = stager.next()
                if k.cfg.get("dbg_hT") and not hasattr(k, "_dh"):
                    k._dh = nc.dram_tensor("dbg_hT", [128, KC, 512], BF16, kind="ExternalOutput").ap()
                    p.dma('sp', k._dh[:, :, 0:TB], hT[:, :, 0:TB], reads=[hT])
                    k._dw = nc.dram_tensor("dbg_win", [128, KC, IN_W], BF16, kind="ExternalOutput").ap()
                    p.dma('sp', k._dw[:, :, :], win[:, :, :], reads=[win])
                for ti in range(ntb):
                    r0 = b * TB + ti * 128
                    tsl = slice(ti * 128, (ti + 1) * 128)
                    for kc in range(KC):
                        p.do('pe', 'matmul', pq[:], hT[:, kc, tsl], win[:, kc, 0:512], start=(kc == 0), stop=(kc == KC - 1), reads=[hT, win], writes=[pq])
                    for kc in range(KC):
                        p.do('pe', 'matmul', pkv[:, 0:256], hT[:, kc, tsl], win[:, kc, 512:768], start=(kc == 0), stop=(kc == KC - 1), reads=[hT, win], writes=[pkv])
                    p.do('act', 'activation', sq[:, 0:8, :], pq[:].rearrange("p (h d) -> p h d", d=64), AF.Square, reads=[pq], writes=[sq])
                    p.do('act', 'activation', sq[:, 8:10, :], pkv[:, 0:128].rearrange("p (h d) -> p h d", d=64), AF.Square, reads=[pkv], writes=[sq])
                    p.do('dve', 'tensor_reduce', ss10[:], sq[:], AX.X, ALU.add, reads=[sq], writes=[ss10])
                    p.do('act', 'activation', ss10[:], ss10[:], AF.Sqrt, scale=1.0 / 64, bias=EPS, reads=[ss10], writes=[ss10])
                    p.do('dve', 'reciprocal', ss10[:], ss10[:], reads=[ss10], writes=[ss10])
                    p.do('dve', 'tensor_tensor', qn[:, 0:8, :], pq[:].rearrange("p (h d) -> p h d", d=64), bcast(ss10[:, 0:8].unsqueeze(2), [128, 8, 64]), ALU.mult, reads=[pq, ss10], writes=[qn])
                    p.do('dve', 'tensor_tensor', qn[:, 8:10, :], pkv[:, 0:128].rearrange("p (h d) -> p h d", d=64), bcast(ss10[:, 8:10].unsqueeze(2), [128, 2, 64]), ALU.mult, reads=[pkv, ss10], writes=[qn])
                    p.do('pool', 'tensor_tensor', qn[:, 0:8, :], qn[:, 0:8, :], bcast(qkg[:, 0:1, :], [128, 8, 64]), ALU.mult, reads=[qn, qkg], writes=[qn])
                    p.do('pool', 'tensor_tensor', qn[:, 8:10, :], qn[:, 8:10, :], bcast(qkg[:, 1:2, :], [128, 2, 64]), ALU.mult, reads=[qn, qkg], writes=[qn])
                    p.do('act', 'copy', vb[:], pkv[:, 128:256], reads=[pkv], writes=[vb])
                    p.dma('sp', s["v"][r0:r0 + 128, :], vb[:], reads=[vb])
                    if ctx:
                        pi = s["pi"]
                        p.do('act', 'copy', vf[:], pkv[:, 128:256], reads=[pkv], writes=[vf])
                        p.dma('sp', O["nv"][pi, l, r0:r0 + 128, :], vf[:], reads=[vf])
                        p.dma('sp', O["nk"][pi, l, r0:r0 + 128, :], qn[:, 8:10, :].rearrange("p h d -> p (h d)"), reads=[qn])
                        p.do('dve', 'tensor_copy', qr[:], qn[:], reads=[qn], writes=[qr])
                    else:
                        p.dma('sp', rp[:], I["rope"][r0:r0 + 128, :, :], writes=[rp])
                        q4 = qn[:].rearrange("p h (i two) -> p h i two", two=2)
                        r4 = qr[:].rearrange("p h (i two) -> p h i two", two=2)
                        cosb = bcast(rp[:, 0:1, :], [128, 10, 32])
                        sinb = bcast(rp[:, 1:2, :], [128, 10, 32])
                        p.do('dve', 'tensor_tensor', tmp[0][:], q4[:, :, :, 0], cosb, ALU.mult, reads=[qn, rp], writes=[tmp[0]])
                        p.do('pool', 'tensor_tensor', tmp[1][:], q4[:, :, :, 1], sinb, ALU.mult, reads=[qn, rp], writes=[tmp[1]])
                        p.do('dve', 'tensor_tensor', tmp[2][:], q4[:, :, :, 0], sinb, ALU.mult, reads=[qn, rp], writes=[tmp[2]])
                        p.do('pool', 'tensor_tensor', tmp[3][:], q4[:, :, :, 1], cosb, ALU.mult, reads=[qn, rp], writes=[tmp[3]])
                        p.do('dve', 'tensor_tensor', r4[:, :, :, 0], tmp[0][:], tmp[1][:], ALU.subtract, reads=[tmp[0], tmp[1]], writes=[qr])
                        p.do('pool', 'tensor_tensor', r4[:, :, :, 1], tmp[2][:], tmp[3][:], ALU.add, reads=[tmp[2], tmp[3]], writes=[qr])
                    p.do('pool', 'tensor_copy', kd[:, 0:2, :], bcast(qr[:, 8:9, :], [128, 2, 64]), reads=[qr], writes=[kd])
                    p.do('pool', 'tensor_copy', kd[:, 2:4, :], bcast(qr[:, 9:10, :], [128, 2, 64]), reads=[qr], writes=[kd])
                    for c in range(4):
                        p.do('pe', 'transpose', ptr[:, c, :], qr[:, 2 * c:2 * c + 2, :].rearrange("p h d -> p (h d)"), k.identb[:], reads=[qr, k.identb], writes=[ptr])
                    for g in range(2):
                        p.do('pe', 'transpose', ptr[:, 4 + g, :], kd[:, 2 * g:2 * g + 2, :].rearrange("p h d -> p (h d)"), k.identb[:], reads=[kd, k.identb], writes=[ptr])
                    p.do('act', 'copy', stage[:, :, tsl], ptr[:, 0:6, :], reads=[ptr], writes=[stage])
                cols = slice(b * TB, (b + 1) * TB)
                p.dma('sp', s["qT"][:, :, cols].rearrange("c p t -> p c t"), stage[:, 0:4, 0:TB], reads=[stage])
                p.dma('sp', s["kT"][:, :, cols].rearrange("c p t -> p c t"), stage[:, 4:6, 0:TB], reads=[stage])
                for fc in range(10):
                    pf = pfr.next()
                    fst = fstr.next()
                    for kc in range(KC):
                        p.do('pe', 'matmul', pf[:, 0:TB], win[:, kc, 768 + fc * 128:768 + (fc + 1) * 128], hT[:, kc, 0:TB], start=(kc == 0), stop=(kc == KC - 1), reads=[hT, win], writes=[pf])
                    ev[0] += 1
                    if ev[0] % 2 == 0:
                        p.do('act', 'copy', fst[:, 0:TB], pf[:, 0:TB], reads=[pf], writes=[fst])
                    else:
                        p.do('dve', 'tensor_copy', fst[:, 0:TB], pf[:, 0:TB], reads=[pf], writes=[fst])
                    p.dma('sp', s["fT"][fc, :, cols], fst[:, 0:TB], reads=[fst])
                nch = TB // 64
                for j in range(nch):
                    for kc in range(KC):
                        p.do('pe', 'matmul', pg[0:64, j, :], hT[:, kc, j * 64:(j + 1) * 64], win[:, kc, 2048:2064], start=(kc == 0), stop=(kc == KC - 1), reads=[hT, win], writes=[pg])
                p.do('dve', 'tensor_copy', gsb[:, 0:nch, :], pg[0:64, 0:nch, :], reads=[pg], writes=[gsb])
                p.dma('sp', s["gates"][:, b * nch:(b + 1) * nch, :], gsb[:, 0:nch, :], reads=[gsb])
        end_phase(k)


def _chunkT(v, n=128):
    sh = v.shape[:-1]
    return np.ascontiguousarray(np.swapaxes(v.reshape(sh + (-1, n)), -1, -2))


def const_tables(cfg):
    LS = cfg["LS"]
    t = {}
    t["ident"] = np.eye(128, dtype=np.float32)
    n_rows = max(LS // 64, 1)
    rows = np.repeat(np.arange(n_rows, dtype=np.float32), 64)[:LS]
    cols = np.tile(np.arange(64, dtype=np.float32), n_rows)[:LS]
    inv_freq = np.power(np.float32(10000.0), -np.arange(16, dtype=np.float32) / np.float32(16)).astype(np.float32)
    ang = np.concatenate([rows[:, None] * inv_freq, cols[:, None] * inv_freq], axis=-1).astype(np.float32)
    t["rope"] = np.ascontiguousarray(np.stack([np.cos(ang), np.sin(ang)], axis=1).astype(np.float32))
    kk = np.arange(128)[:, None]
    qq = np.arange(128)[None, :]
    t["amask"] = np.ascontiguousarray(np.stack([(kk >= qq), (kk <= qq)], axis=1).astype(np.float32))
    i = np.arange(64)[:, None]
    j = np.arange(64)[None, :]
    low_incl = (i >= j).astype(np.float32)
    low_strict = (i > j).astype(np.float32)
    gm = np.zeros((64, 6, 8, 64), np.float32)
    for e in range(8):
        fwd = e < 4
        gm[:, 0, e, :] = low_strict if fwd else low_strict.T
        gm[:, 1, e, :] = low_incl if fwd else low_incl.T
        gm[:, 2, e, :] = low_strict.T if fwd else low_strict
        gm[:, 3, e, :] = low_incl.T if fwd else low_incl
        gm[:, 4, e, :] = np.eye(64, dtype=np.float32)
    t["gmask"] = gm
    gt = np.zeros((64, 4, 64), np.float32)
    gt[:, 0, :] = low_incl.T
    gt[:, 1, :] = low_incl
    gt[63, 2, :] = 1.0
    gt[0, 3, :] = 1.0
    t["gtri"] = gt
    return t


def host_weights(inp, cfg):
    DEPTH = cfg["DEPTH"]
    f = lambda a: np.ascontiguousarray(np.asarray(a, dtype=np.float32))
    w = {}
    w["w_mod"] = f(inp["w_mod"][:DEPTH])
    w["b_modT"] = _chunkT(f(inp["b_mod"][:DEPTH]))
    w["n1g"] = _chunkT(f(inp["norm1_g"][:DEPTH]))
    w["n2g"] = _chunkT(f(inp["norm2_g"][:DEPTH]))
    w["w_in"] = f(inp["w_in"][:DEPTH])
    qk = np.stack([f(inp["q_norm_g"][:DEPTH]), f(inp["k_norm_g"][:DEPTH])], axis=1)
    w["qk_g"] = np.ascontiguousarray(np.broadcast_to(qk[:, None], (DEPTH, 128, 2, 64)))
    w["sink_b"] = np.ascontiguousarray(np.broadcast_to(f(inp["attn_sink"][:DEPTH])[:, None], (DEPTH, 128, 8)))
    lam = np.zeros((DEPTH, 128, 3, 16), np.float32)
    bt = np.zeros((DEPTH, 128, 16, 2, 128), np.float32)
    ct = np.zeros((DEPTH, 128, 16, 2, 128), np.float32)
    lre, lim, lst = f(inp["ssm_lam_re"]), f(inp["ssm_lam_im"]), f(inp["ssm_log_step"])
    bre, bim, cre, cim = f(inp["ssm_b_re"]), f(inp["ssm_b_im"]), f(inp["ssm_c_re"]), f(inp["ssm_c_im"])
    for gp in range(8):
        for d in range(2):
            inst = gp * 2 + d
            for g2 in range(2):
                g = 2 * gp + g2
                gl = g % 8
                ps = slice(g2 * 64, (g2 + 1) * 64)
                lam[:, ps, 0, inst] = lre[:DEPTH, d, g, :]
                lam[:, ps, 1, inst] = lim[:DEPTH, d, g, :]
                lam[:, ps, 2, inst] = lst[:DEPTH, d, g][:, None]
                bt[:, ps, inst, 0, gl * 16:(gl + 1) * 16] = bre[:DEPTH, d, g]
                bt[:, ps, inst, 1, gl * 16:(gl + 1) * 16] = bim[:DEPTH, d, g]
                co0 = 32 * (gp % 4) + g2 * 16
                ct[:, ps, inst, 0, co0:co0 + 16] = np.swapaxes(cre[:DEPTH, d, g], -1, -2)
                ct[:, ps, inst, 1, co0:co0 + 16] = np.swapaxes(cim[:DEPTH, d, g], -1, -2)
    w["lam"], w["ssm_bt"], w["ssm_ct"] = lam, bt, ct
    w["ssm_dT"] = _chunkT(f(inp["ssm_d"][:DEPTH]))
    w["glu_bT"] = _chunkT(f(inp["ssm_b_glu"][:DEPTH]))
    w["w_glu"] = f(inp["ssm_w_glu"][:DEPTH])
    cw = f(inp["gdn_conv_w"][:DEPTH])
    w["conv_wT"] = np.ascontiguousarray(np.transpose(cw.reshape(DEPTH, 3, 6, 128), (0, 3, 2, 1)))
    ab = np.stack([f(inp["gdn_a_log"][:DEPTH]).reshape(DEPTH, 8), f(inp["gdn_dt_bias"][:DEPTH]).reshape(DEPTH, 8)], axis=1)
    w["gdn_ab"] = np.ascontiguousarray(np.broadcast_to(ab[:, None], (DEPTH, 64, 2, 8)))
    ng = f(inp["gdn_norm_g"][:DEPTH])
    w["gdn_ng"] = np.ascontiguousarray(np.concatenate([ng, ng], axis=1)[:, :, None])
    w["w_out"] = f(inp["w_out"][:DEPTH])
    w["w_ff1"] = f(inp["w_ff1"][:DEPTH])
    w["w_ff2"] = f(inp["w_ff2"][:DEPTH])
    return w


def host_core_inputs(inp, core, cfg):
    LS, LP, NP, DEPTH, PAST = cfg["LS"], cfg["LP"], cfg["NP"], cfg["DEPTH"], cfg["PAST"]
    f = lambda a: np.ascontiguousarray(np.asarray(a, dtype=np.float32))
    m = {}
    m["x_s"] = f(inp["x_sample"][core, :LS])
    m["x_p"] = f(inp["x_prompt"][core * NP:(core + 1) * NP, :LP]).reshape(NP * LP, D)
    c2 = np.stack([f(inp["c"][core]), f(inp["c_ctx"])], axis=0)
    m["cond2"] = np.ascontiguousarray(np.transpose(c2.reshape(2, KC, 128), (2, 0, 1)))
    m["cache_k"] = f(inp["cache_k"][core, :DEPTH, :PAST]).reshape(DEPTH, PAST, 128)
    m["cache_v"] = f(inp["cache_v"][core, :DEPTH, :PAST]).reshape(DEPTH, PAST, 128)
    ss = f(inp["state_ssm"][core, :DEPTH])
    h0 = np.zeros((DEPTH, 128, 16, 2), np.float32)
    for gp in range(8):
        for d in range(2):
            for g2 in range(2):
                h0[:, g2 * 64:(g2 + 1) * 64, gp * 2 + d, :] = np.transpose(ss[:, d, :, 2 * gp + g2, :], (0, 2, 1))
    m["ssm_h0"] = h0
    sg = f(inp["state_gdn"][core, :DEPTH])
    m["gdn_s0"] = np.ascontiguousarray(np.transpose(sg, (0, 3, 1, 2, 4)).reshape(DEPTH, 64, 8, 64))
    return m


def phase_c(k, l):
    nc, p, I, O = k.nc, k.p, k.I, k.O
    use_mix = k.cfg.get("use_mix", True)
    with contextlib.ExitStack() as st:
        sbt, pst = k.sbt, k.pst
        wout = sbt(st, "wout", [128, KC, D], BF16)
        wff1 = sbt(st, "wff1", [128, KC, 4 * D], BF16)
        wff2 = sbt(st, "wff2", [128, 32, D], BF16)
        s_out = I["w_out"][l].rearrange("(c p) n -> p c n", p=128)
        s_ff1 = I["w_ff1"][l].rearrange("(c p) n -> p c n", p=128)
        s_ff2 = I["w_ff2"][l].rearrange("(c p) n -> p c n", p=128)
        for c in range(KC):
            p.dma('pool', wout[:, c, :], s_out[:, c, :], writes=[wout])
        for c in range(KC):
            for q4 in range(4):
                p.dma('pool', wff1[:, c, q4 * D:(q4 + 1) * D], s_ff1[:, c, q4 * D:(q4 + 1) * D], writes=[wff1])
        for c in range(32):
            p.dma('pool', wff2[:, c, :], s_ff2[:, c, :], writes=[wff2])
        n2g = sbt(st, "n2g", [128, KC], F32)
        sc2 = sbt(st, "sc2", [128, KC, 2], F32)
        p.dma('sp', n2g[:], I["n2g"][l], writes=[n2g])
        p.do('dve', 'tensor_scalar', sc2[:], k.modT[:, l, 32:40, :], 1.0, None, ALU.add, reads=[k.modT], writes=[sc2])
        p.do('dve', 'tensor_tensor', sc2[:], sc2[:], bcast(n2g[:].unsqueeze(2), [128, KC, 2]), ALU.mult, reads=[sc2, n2g], writes=[sc2])
        gA1 = sbt(st, "gA", [128, D], F32)
        gM1 = sbt(st, "gM", [128, D], F32)
        gA, gM = [gA1, gA1], [gM1, gM1]
        gcol = Ring([sbt(st, "gcol%d" % i, [128, 128], F32) for i in range(1)])
        pgb = Ring([pst(st, "pgb%d" % i, [128, 512], F32) for i in range(2)])
        cur_cond = [None]

        def build_gates(cond):
            if cur_cond[0] == cond:
                return
            cur_cond[0] = cond
            for gi, (dst, base) in enumerate(((gA[cond], 16), (gM[cond], 40))):
                for c in range(KC):
                    gc_ = gcol.next()
                    pb = pgb.next()
                    p.do('dve', 'tensor_copy', gc_[:], bcast(k.modT[:, l, base + c, cond:cond + 1], [128, 128]), reads=[k.modT], writes=[gc_])
                    p.do('pe', 'matmul', pb[:, 0:128], gc_[:], k.ident[:], start=True, stop=True, reads=[gc_, k.ident], writes=[pb])
                    p.do('act', 'copy', dst[:, c * 128:(c + 1) * 128], pb[:, 0:128], reads=[pb], writes=[dst])
        xring = Ring([sbt(st, "xc%d" % i, [128, D], F32) for i in range(3)])
        x1r = Ring([sbt(st, "x1c%d" % i, [128, D], F32) for i in range(2)])
        tmpc = sbt(st, "tmpc", [128, D], F32)
        junk = tmpc
        ssr = Ring([sbt(st, "ssc%d" % i, [128, 1], F32) for i in range(2)])
        xnr = Ring([sbt(st, "xnc%d" % i, [128, D], BF16) for i in range(1)])
        mixr = Ring([sbt(st, "mixc%d" % i, [128, KC, 128], BF16) for i in range(2)])
        h2r = Ring([sbt(st, "h2c%d" % i, [128, KC, 128], BF16) for i in range(2)])
        aTr = Ring([sbt(st, "aTc%d" % i, [128, 32, 128], BF16) for i in range(2)])
        rr = Ring([sbt(st, "rc%d" % i, [128, 4, 128], BF16) for i in range(2)])
        pTr = Ring([pst(st, "pTc%d" % i, [128, KC, 128], BF16) for i in range(1)])
        por = Ring([pst(st, "poc%d" % i, [128, 512], F32) for i in range(3)])
        pfr = Ring([pst(st, "pfc%d" % i, [128, 4, 128], F32) for i in range(2)])
        p.barrier()
        items = [(s, t) for s in k.seqs for t in range(s["L"] // 128)]

        def prefetch(it):
            s_, t_ = it
            rows_ = slice(t_ * 128, (t_ + 1) * 128)
            xsrc_ = s_["xin"] if (l == 0) else s_["y"]
            xt_ = xring.next()
            p.dma('sp', xt_[:], xsrc_[rows_, :], writes=[xt_])
            mx_ = None
            if use_mix:
                mx_ = mixr.next()
                p.dma('sp', mx_[:], s_["mixT"][:, :, rows_].rearrange("c p t -> p c t"), writes=[mx_])
            return xt_, mx_

        def front(it, ld):
            s, t = it
            cond = s["cond"]
            xt, mx = ld
            build_gates(cond)
            x1 = x1r.next()
            if use_mix:
                for hh in range(2):
                    po = por.next()
                    for kc in range(KC):
                        p.do('pe', 'matmul', po[:], mx[:, kc, :], wout[:, kc, hh * 512:(hh + 1) * 512], start=(kc == 0), stop=(kc == KC - 1), reads=[mx, wout], writes=[po])
                    hs = slice(hh * 512, (hh + 1) * 512)
                    p.do('dve', 'tensor_tensor', x1[:, hs], po[:], gA[cond][:, hs], ALU.mult, reads=[po, gA[cond]], writes=[x1])
                    p.do('pool', 'tensor_tensor', x1[:, hs], x1[:, hs], xt[:, hs], ALU.add, reads=[x1, xt], writes=[x1])
            else:
                p.do('pool', 'tensor_copy', x1[:], xt[:], reads=[xt], writes=[x1])
            ss = ssr.next()
            xn = xnr.next()
            pT = pTr.next()
            h2 = h2r.next()
            p.do('act', 'activation', junk[:], x1[:], AF.Square, accum_out=ss[:], reads=[x1], writes=[junk, ss])
            p.do('act', 'activation', ss[:], ss[:], AF.Sqrt, scale=1.0 / D, bias=EPS, reads=[ss], writes=[ss])
            p.do('dve', 'reciprocal', ss[:], ss[:], reads=[ss], writes=[ss])
            p.do('dve', 'tensor_scalar', xn[:], x1[:], ss[:, 0:1], None, ALU.mult, reads=[x1, ss], writes=[xn])
            for c in range(KC):
                p.do('pe', 'transpose', pT[:, c, :], xn[:, c * 128:(c + 1) * 128], k.identb[:], reads=[xn, k.identb], writes=[pT])
            for c in range(KC):
                if c % 2 == 0:
                    p.do('act', 'activation', h2[:, c, :], pT[:, c, :], AF.Identity, scale=sc2[:, c, cond:cond + 1], bias=k.modT[:, l, 24 + c, cond:cond + 1], reads=[pT, sc2, k.modT], writes=[h2])
                else:
                    p.do('dve', 'tensor_scalar', h2[:, c, :], pT[:, c, :], sc2[:, c, cond:cond + 1], k.modT[:, l, 24 + c, cond:cond + 1], ALU.mult, ALU.add, reads=[pT, sc2, k.modT], writes=[h2])
            return dict(xt=xt, x1=x1, h2=h2)

        def back(it, C):
            s, t = it
            cond = s["cond"]
            rows = slice(t * 128, (t + 1) * 128)
            xt, x1, h2 = C["xt"], C["x1"], C["h2"]
            aT = aTr.next()
            for f4 in range(8):
                pf = pfr.next()
                r_ = rr.next()
                for j in range(4):
                    fc = f4 * 4 + j
                    for kc in range(KC):
                        p.do('pe', 'matmul', pf[:, j, :], wff1[:, kc, fc * 128:(fc + 1) * 128], h2[:, kc, :], start=(kc == 0), stop=(kc == KC - 1), reads=[wff1, h2], writes=[pf])
                p.do('act', 'activation', r_[:], pf[:], AF.Relu, reads=[pf], writes=[r_])
                p.do('pool', 'tensor_tensor', aT[:, f4 * 4:(f4 + 1) * 4, :], r_[:], r_[:], ALU.mult, reads=[r_], writes=[aT])
            for hh in range(2):
                po = por.next()
                for fc in range(32):
                    p.do('pe', 'matmul', po[:], aT[:, fc, :], wff2[:, fc, hh * 512:(hh + 1) * 512], start=(fc == 0), stop=(fc == 31), reads=[aT, wff2], writes=[po])
                hs = slice(hh * 512, (hh + 1) * 512)
                p.do('dve', 'tensor_tensor', xt[:, hs], po[:], gM[cond][:, hs], ALU.mult, reads=[po, gM[cond]], writes=[xt])
                p.do('dve', 'tensor_tensor', xt[:, hs], xt[:, hs], x1[:, hs], ALU.add, reads=[xt, x1], writes=[xt])
            p.dma('sp', s["y"][rows, :], xt[:], reads=[xt])

        n_it = len(items)
        loads = {0: prefetch(items[0])}
        if n_it > 1:
            loads[1] = prefetch(items[1])
        ctxs = {0: front(items[0], loads[0])}
        for ii in range(n_it):
            if ii + 2 < n_it:
                loads[ii + 2] = prefetch(items[ii + 2])
            same = ii + 1 < n_it and items[ii + 1][0]["cond"] == items[ii][0]["cond"]
            if ii + 1 < n_it and same:
                ctxs[ii + 1] = front(items[ii + 1], loads[ii + 1])
            back(items[ii], ctxs[ii])
            if ii + 1 < n_it and not same:
                ctxs[ii + 1] = front(items[ii + 1], loads[ii + 1])
        end_phase(k)


def phase_attn(k, l, s):
    nc, p, I, O = k.nc, k.p, k.I, k.O
    L, ctx = s["L"], s["ctx"]
    PAST = k.cfg["PAST"]
    nb = L // 128
    SC = 0.125
    with contextlib.ExitStack() as st:
        sbt, pst = k.sbt, k.pst
        kT = sbt(st, "akT", [128, 2, L], BF16)
        vsb = sbt(st, "avsb", [128, nb, 128], BF16)
        vdup = sbt(st, "avdup", [128, nb, 2, 128], BF16)
        esk = sbt(st, "aesk", [128, 8], F32)
        mkf = sbt(st, "amkf", [128, 2, 128], F32)
        mk = sbt(st, "amk", [128, 2, 128], BF16)
        p.dma('sp', kT[:], s["kT"].rearrange("g p t -> p g t"), writes=[kT])
        vsrc = s["v"].rearrange("(n p) f -> p n f", p=128)
        for n0 in range(0, nb, 4):
            n1 = min(nb, n0 + 4)
            p.dma('sp', vsb[:, n0:n1, :], vsrc[:, n0:n1, :], writes=[vsb])
        p.dma('sp', esk[:], I["sink_b"][l], writes=[esk])
        p.dma('sp', mkf[:], I["amask"][:, :, :], writes=[mkf])
        p.do('act', 'activation', esk[:], esk[:], AF.Exp, reads=[esk], writes=[esk])
        p.do('dve', 'tensor_copy', mk[:], mkf[:], reads=[mkf], writes=[mk])
        for g in range(2):
            for hf in range(2):
                p.do('dve' if hf == 0 else 'pool', 'tensor_copy', vdup[:, :, g, hf * 64:(hf + 1) * 64], vsb[:, :, g * 64:(g + 1) * 64], reads=[vsb], writes=[vdup])
        pst_ring = Ring([pst(st, "aST%d" % i, [128, 4, 128], F32) for i in range(2)])
        po_ring = Ring([pst(st, "aO%d" % i, [128, 4, 128], F32) for i in range(2)])
        ps_ring = Ring([pst(st, "aS%d" % i, [128, 4, 128], F32) for i in range(2)])
        nctx = 0
        if not ctx:
            nctx = PAST // 128
            ckf = sbt(st, "ackf", [128, nctx, 128], F32)
            cvf = sbt(st, "acvf", [128, nctx, 128], F32)
            ckd = sbt(st, "ackd", [128, nctx, 2, 128], BF16)
            cvd = sbt(st, "acvd", [128, nctx, 2, 128], BF16)
            ckT = sbt(st, "ackT", [128, 2, PAST], BF16)
            pck = pst(st, "apck", [128, 8, 128], BF16)
            p.dma('sp', ckf[:], I["cache_k"][l].rearrange("(n p) f -> p n f", p=128), writes=[ckf])
            p.dma('sp', cvf[:], I["cache_v"][l].rearrange("(n p) f -> p n f", p=128), writes=[cvf])
            for g in range(2):
                for hf in range(2):
                    p.do('dve', 'tensor_copy', ckd[:, :, g, hf * 64:(hf + 1) * 64], ckf[:, :, g * 64:(g + 1) * 64], reads=[ckf], writes=[ckd])
                    p.do('pool', 'tensor_copy', cvd[:, :, g, hf * 64:(hf + 1) * 64], cvf[:, :, g * 64:(g + 1) * 64], reads=[cvf], writes=[cvd])
            for j in range(nctx):
                for g in range(2):
                    p.do('pe', 'transpose', pck[:, j * 2 + g, :], ckd[:, j, g, :], k.identb[:], reads=[ckd, k.identb], writes=[pck])
            for j in range(nctx):
                for g in range(2):
                    p.do('act', 'copy', ckT[:, g, j * 128:(j + 1) * 128], pck[:, j * 2 + g, :], reads=[pck], writes=[ckT])
        qr = Ring([sbt(st, "aq%d" % i, [128, 4, 128], BF16) for i in range(2)])
        qzr = Ring([sbt(st, "aqz%d" % i, [128, 4, 2, 128], BF16) for i in range(2)])
        for qz_ in qzr.bufs:
            p.do('pool', 'memset', qz_[:], 0.0, writes=[qz_])
        er = Ring([sbt(st, "aE%d" % i, [128, 4, 128], BF16) for i in range(3)])
        den = sbt(st, "aden", [128, 4, 128], F32)
        outr = Ring([sbt(st, "aout%d" % i, [128, 4, 128], BF16) for i in range(2)])
        lvl = k.cfg.get('att_lvl', 9)
        for i in range(nb if lvl >= 2 else 0):
            cols = slice(i * 128, (i + 1) * 128)
            q = qr.next()
            ao = outr.next()
            p.dma('sp', q[:], s["qT"][:, :, cols].rearrange("c p t -> p c t"), writes=[q])
            qz = qzr.next()
            p.do('act', 'copy', qz[0:64, :, 0, :], q[0:64, :, :], reads=[q], writes=[qz])
            p.do('dve', 'tensor_copy', qz[64:128, :, 1, :], q[64:128, :, :], reads=[q], writes=[qz])
            for g in range(2):
                kbs = []
                if ctx:
                    for j in range(nb):
                        kbs.append((kT, j, vdup[:, j, g, :], None, vdup))
                else:
                    for j, mi in ((i - 1, 0), (i, None), (i + 1, 1)):
                        if 0 <= j < nb:
                            kbs.append((kT, j, vdup[:, j, g, :], mi, vdup))
                    for j in range(nctx):
                        kbs.append((ckT, j, cvd[:, j, g, :], None, cvd))
                po = po_ring.next()
                psm = ps_ring.next()
                for bi, (ksrc, j, vap, mi, vbuf) in enumerate(kbs):
                    pS = pst_ring.next()
                    for hh in range(4):
                        h = 4 * g + hh
                        hf, c = h % 2, h // 2
                        p.do('pe', 'matmul', pS[:, hh, :], ksrc[:, g, j * 128:(j + 1) * 128], qz[:, c, hf, :], start=True, stop=True, reads=[ksrc, qz], writes=[pS])
                    if lvl < 3:
                        continue
                    E = er.next()
                    p.do('act', 'activation', E[:], pS[:], AF.Exp, scale=SC, reads=[pS], writes=[E])
                    if mi is not None:
                        p.do('dve', 'tensor_tensor', E[:], E[:], bcast(mk[:, mi:mi + 1, :], [128, 4, 128]), ALU.mult, reads=[E, mk], writes=[E])
                    if lvl < 4:
                        continue
                    first, last = bi == 0, bi == len(kbs) - 1
                    p.do('pe', 'matmul', po[:].rearrange("p a b -> p (a b)"), vap, E[:].rearrange("p a b -> p (a b)"), start=first, stop=last, reads=[vbuf, E], writes=[po])
                    p.do('pe', 'matmul', psm[:].rearrange("p a b -> p (a b)"), k.ones_b[:], E[:].rearrange("p a b -> p (a b)"), start=first, stop=last, reads=[k.ones_b, E], writes=[psm])
                if lvl < 5:
                    continue
                p.do('dve', 'tensor_tensor', den[:], psm[:], bcast(esk[:, 4 * g:4 * g + 4].unsqueeze(2), [128, 4, 128]), ALU.add, reads=[psm, esk], writes=[den])
                p.do('dve', 'reciprocal', den[:], den[:], reads=[den], writes=[den])
                for hf in range(2):
                    ps_ = slice(64 * hf, 64 * hf + 64)
                    p.do('dve', 'tensor_tensor', ao[ps_, 2 * g:2 * g + 2, :], po[ps_, hf::2, :], den[ps_, hf::2, :], ALU.mult, reads=[po, den], writes=[ao])
            if lvl >= 6:
                p.dma('sp', s["mixT"][0:4, :, cols].rearrange("c p t -> p c t"), ao[:], reads=[ao])
        end_phase(k)


def ssm_scan_levels(L):
    K_ = int(math.log2(L))
    ops = []
    for kk in range(K_):
        s_ = 2 ** (kk + 1)
        ops.append((kk, 2 ** kk - 1, s_ - 1, s_, L // s_))
    for kk in range(K_ - 2, -1, -1):
        s_ = 2 ** (kk + 1)
        n = L // s_ - 1
        if n > 0:
            ops.append((kk, s_ - 1, s_ + 2 ** kk - 1, s_, n))
    return ops


def phase_ssm(k, l, s):
    nc, p, I, O = k.nc, k.p, k.I, k.O
    L, ctx = s["L"], s["ctx"]
    TB = min(512, L)
    nblk = L // TB
    NLV = int(math.log2(L))
    with contextlib.ExitStack() as st:
        sbt, pst = k.sbt, k.pst
        lam = sbt(st, "slam", [128, 3, 16], F32)
        dtm = sbt(st, "sdt", [128, 16], F32)
        res_ = sbt(st, "sres", [128, 16], F32)
        ims = sbt(st, "sims", [128, 16], F32)
        t = [sbt(st, "st%d" % i, [128, 16], F32) for i in range(6)]
        A = sbt(st, "sA", [128, 16, 12, 2], F32)
        nAi = sbt(st, "snAi", [128, 16, 12], F32)
        fre = sbt(st, "sfre", [128, 16], F32)
        fim = sbt(st, "sfim", [128, 16], F32)
        nfim = sbt(st, "snfim", [128, 16], F32)
        p.dma('sp', lam[:], I["lam"][l], writes=[lam])
        p.do('act', 'activation', dtm[:], lam[:, 2, :], AF.Exp, reads=[lam], writes=[dtm])
        p.do('dve', 'tensor_tensor', res_[:], lam[:, 0, :], dtm[:], ALU.mult, reads=[lam, dtm], writes=[res_])
        p.do('dve', 'tensor_tensor', ims[:], lam[:, 1, :], dtm[:], ALU.mult, reads=[lam, dtm], writes=[ims])
        mag, s8, sh, c8, x_, y_ = t
        p.do('act', 'activation', mag[:], res_[:], AF.Exp, scale=0.125, reads=[res_], writes=[mag])
        p.do('act', 'activation', s8[:], ims[:], AF.Sin, scale=0.125, reads=[ims], writes=[s8])
        p.do('act', 'activation', sh[:], ims[:], AF.Sin, scale=0.0625, reads=[ims], writes=[sh])
        p.do('dve', 'tensor_tensor', c8[:], sh[:], sh[:], ALU.mult, reads=[sh], writes=[c8])
        p.do('dve', 'tensor_scalar', c8[:], c8[:], -2.0, 1.0, ALU.mult, ALU.add, reads=[c8], writes=[c8])
        p.do('dve', 'tensor_tensor', x_[:], mag[:], c8[:], ALU.mult, reads=[mag, c8], writes=[x_])
        p.do('dve', 'tensor_tensor', y_[:], mag[:], s8[:], ALU.mult, reads=[mag, s8], writes=[y_])
        xb, yb = Buf(x_.ap, "x"), Buf(y_.ap, "y")

        def csquare(dst_re, dst_im, src_re, src_im, bufs_r, bufs_w):
            p.do('dve', 'tensor_tensor', mag[:], src_re, src_re, ALU.mult, reads=bufs_r, writes=[mag])
            p.do('dve', 'tensor_tensor', s8[:], src_im, src_im, ALU.mult, reads=bufs_r, writes=[s8])
            p.do('dve', 'scalar_tensor_tensor', sh[:], src_re, 2.0, src_im, ALU.mult, ALU.mult, reads=bufs_r, writes=[sh])
            p.do('dve', 'tensor_tensor', dst_re, mag[:], s8[:], ALU.subtract, reads=[mag, s8], writes=bufs_w)
            p.do('dve', 'tensor_copy', dst_im, sh[:], reads=[sh], writes=bufs_w)

        for _ in range(2):
            csquare(x_[:], y_[:], x_[:], y_[:], [x_, y_], [x_, y_])
        csquare(A[:, :, 0, 0], A[:, :, 0, 1], x_[:], y_[:], [x_, y_], [A])
        for kk in range(1, 12):
            csquare(A[:, :, kk, 0], A[:, :, kk, 1], A[:, :, kk - 1, 0], A[:, :, kk - 1, 1], [A], [A])
        p.do('dve', 'tensor_scalar', nAi[:], A[:, :, :, 1], -1.0, None, ALU.mult, reads=[A], writes=[nAi])
        nr, den, rr_, q1 = t[0], t[1], t[2], t[3]
        p.do('dve', 'tensor_scalar', nr[:], A[:, :, 0, 0], -1.0, None, ALU.add, reads=[A], writes=[nr])
        p.do('dve', 'tensor_tensor', den[:], lam[:, 0, :], lam[:, 0, :], ALU.mult, reads=[lam], writes=[den])
        p.do('dve', 'tensor_tensor', q1[:], lam[:, 1, :], lam[:, 1, :], ALU.mult, reads=[lam], writes=[q1])
        p.do('dve', 'tensor_tensor', den[:], den[:], q1[:], ALU.add, reads=[den, q1], writes=[den])
        p.do('dve', 'reciprocal', den[:], den[:], reads=[den], writes=[den])
        p.do('dve', 'tensor_tensor', fre[:], nr[:], lam[:, 0, :], ALU.mult, reads=[nr, lam], writes=[fre])
        p.do('dve', 'tensor_tensor', q1[:], A[:, :, 0, 1], lam[:, 1, :], ALU.mult, reads=[A, lam], writes=[q1])
        p.do('dve', 'tensor_tensor', fre[:], fre[:], q1[:], ALU.add, reads=[fre, q1], writes=[fre])
        p.do('dve', 'tensor_tensor', fre[:], fre[:], den[:], ALU.mult, reads=[fre, den], writes=[fre])
        p.do('dve', 'tensor_tensor', fim[:], A[:, :, 0, 1], lam[:, 0, :], ALU.mult, reads=[A, lam], writes=[fim])
        p.do('dve', 'tensor_tensor', q1[:], nr[:], lam[:, 1, :], ALU.mult, reads=[nr, lam], writes=[q1])
        p.do('dve', 'tensor_tensor', fim[:], fim[:], q1[:], ALU.subtract, reads=[fim, q1], writes=[fim])
        p.do('dve', 'tensor_tensor', fim[:], fim[:], den[:], ALU.mult, reads=[fim, den], writes=[fim])
        p.do('dve', 'tensor_scalar', nfim[:], fim[:], -1.0, None, ALU.mult, reads=[fim], writes=[nfim])
        lvl = k.cfg.get('att_lvl', 9)
        if lvl < 2:
            end_phase(k)
            return
        BbT = sbt(st, "sBbT", [128, 16, 2, 128], BF16)
        Ct = sbt(st, "sCt", [128, 16, 2, 128], F32)
        p.dma('sp', Ct[:], I["ssm_ct"][l], writes=[Ct])
        p.do('dve', 'tensor_scalar', Ct[:, :, 1, :], Ct[:, :, 1, :], -1.0, None, ALU.mult, reads=[Ct], writes=[Ct])
        btr = Ring([sbt(st, "sbt%d" % i, [128, 2, 128], F32) for i in range(2)])
        bbr = Ring([sbt(st, "sbb%d" % i, [128, 2, 128], F32) for i in range(2)])
        ptr = Ring([pst(st, "sptr%d" % i, [128, 4, 128], F32) for i in range(2)])
        for inst in range(16):
            bt_, bb, pt = btr.next(), bbr.next(), ptr.next()
            p.dma('sp', bt_[:], I["ssm_bt"][l][:, inst, :, :], writes=[bt_])
            fr, fi, nfi = fre[:, inst:inst + 1], fim[:, inst:inst + 1], nfim[:, inst:inst + 1]
            p.do('dve', 'tensor_scalar', bb[:, 0, :], bt_[:, 0, :], fr, None, ALU.mult, reads=[bt_, fre], writes=[bb])
            p.do('dve', 'scalar_tensor_tensor', bb[:, 0, :], bt_[:, 1, :], nfi, bb[:, 0, :], ALU.mult, ALU.add, reads=[bt_, nfim, bb], writes=[bb])
            p.do('dve', 'tensor_scalar', bb[:, 1, :], bt_[:, 1, :], fr, None, ALU.mult, reads=[bt_, fre], writes=[bb])
            p.do('dve', 'scalar_tensor_tensor', bb[:, 1, :], bt_[:, 0, :], fi, bb[:, 1, :], ALU.mult, ALU.add, reads=[bt_, fim, bb], writes=[bb])
            for ri in range(2):
                p.do('pe', 'transpose', pt[:, ri, :], bb[:, ri, :], k.ident[:], reads=[bb, k.ident], writes=[pt])
            p.do('act', 'copy', BbT[:, inst, :, :], pt[:, 0:2, :], reads=[pt], writes=[BbT])
        if lvl < 3:
            end_phase(k)
            return
        uT = sbt(st, "suT", [128, 2, L], BF16)
        yT = sbt(st, "syT", [128, 2, L], F32)
        p.dma('sp', uT[:], s["fT"][0:2].rearrange("c p t -> p c t"), writes=[uT])
        Hr = Ring([sbt(st, "sH%d" % i, [128, 2, L], F32) for i in range(2)])
        pbr = Ring([pst(st, "spb%d" % i, [128, 512], F32) for i in range(3)])
        pyr = Ring([pst(st, "spy%d" % i, [128, 512], F32) for i in range(2)])
        if not ctx:
            h0 = sbt(st, "sh0", [128, 16, 2], F32)
            t1s = [sbt(st, "sht1%d" % i, [128, 2], F32) for i in range(2)]
            p.dma('sp', h0[:], I["ssm_h0"][l], writes=[h0])
        sched = ssm_scan_levels(L)
        Hparts = {id(H_): (Buf(None, 're'), Buf(None, 'im')) for H_ in Hr.bufs}
        for gp in range(8):
            chunk, q = gp // 4, gp % 4
            Hs = [Hr.next(), Hr.next()]
            for d in range(2):
                inst = gp * 2 + d
                H = Hs[d]
                for b in range(nblk):
                    blk = slice(b * TB, (b + 1) * TB)
                    for ri in range(2):
                        pb = pbr.next()
                        p.do('pe', 'matmul', pb[:, 0:TB], BbT[:, inst, ri, :], uT[:, chunk, blk], start=True, stop=True, reads=[BbT, uT], writes=[pb])
                        p.do('act', 'copy', H[:, ri, blk], pb[:, 0:TB], reads=[pb], writes=[H])
                if not ctx:
                    pos = 0 if d == 0 else L - 1
                    ar, ai, nai = A[:, inst, 0, 0:1], A[:, inst, 0, 1:2], nAi[:, inst, 0:1]
                    t1 = t1s[d]
                    p.do('dve', 'scalar_tensor_tensor', t1[:, 0:1], h0[:, inst, 0:1], ar, H[:, 0, pos:pos + 1], ALU.mult, ALU.add, reads=[h0, A, H], writes=[t1])
                    p.do('dve', 'scalar_tensor_tensor', t1[:, 1:2], h0[:, inst, 1:2], ar, H[:, 1, pos:pos + 1], ALU.mult, ALU.add, reads=[h0, A, H], writes=[t1])
                    p.do('dve', 'scalar_tensor_tensor', H[:, 0, pos:pos + 1], h0[:, inst, 1:2], nai, t1[:, 0:1], ALU.mult, ALU.add, reads=[h0, nAi, t1], writes=[H])
                    p.do('dve', 'scalar_tensor_tensor', H[:, 1, pos:pos + 1], h0[:, inst, 0:1], ai, t1[:, 1:2], ALU.mult, ALU.add, reads=[h0, A, t1], writes=[H])
            for (kk, rr0, rw0, sd, cnt) in (sched if lvl >= 4 else []):
                for d in range(2):
                    inst = gp * 2 + d
                    H = Hs[d]
                    if d == 0:
                        rs = slice(rr0, rr0 + (cnt - 1) * sd + 1, sd)
                        ws = slice(rw0, rw0 + (cnt - 1) * sd + 1, sd)
                    else:
                        a_r = L - 1 - (rr0 + (cnt - 1) * sd)
                        a_w = L - 1 - (rw0 + (cnt - 1) * sd)
                        rs = slice(a_r, a_r + (cnt - 1) * sd + 1, sd)
                        ws = slice(a_w, a_w + (cnt - 1) * sd + 1, sd)
                    ar, ai, nai = A[:, inst, kk, 0:1], A[:, inst, kk, 1:2], nAi[:, inst, kk:kk + 1]
                    Hre, Him = Hparts[id(H)]
                    p.do('dve', 'scalar_tensor_tensor', H[:, :, ws], H[:, :, rs], ar, H[:, :, ws], ALU.mult, ALU.add, reads=[H, Hre, Him, A], writes=[Hre, Him])
                    p.do('dve', 'scalar_tensor_tensor', H[:, 0, ws], H[:, 1, rs], nai, H[:, 0, ws], ALU.mult, ALU.add, reads=[H, Hre, Him, nAi], writes=[Hre])
                    p.do('dve', 'scalar_tensor_tensor', H[:, 1, ws], H[:, 0, rs], ai, H[:, 1, ws], ALU.mult, ALU.add, reads=[H, Him, Hre, A], writes=[Him])
            if lvl < 5:
                continue
            if ctx:
                for d in range(2):
                    pos = L - 1 if d == 0 else 0
                    for ri in range(2):
                        dst = O["nssm"][s["pi"], l, d, ri, 2 * gp:2 * gp + 2, :].rearrange("g (p o) -> (g p) o", o=1)
                        p.dma('sp', dst, Hs[d][:, ri, pos:pos + 1], reads=[Hs[d]] + list(Hparts[id(Hs[d])]))
            for b in range(nblk):
                blk = slice(b * TB, (b + 1) * TB)
                py = pyr.next()
                n_ = 0
                for d in range(2):
                    inst = gp * 2 + d
                    for ri in range(2):
                        p.do('pe', 'matmul', py[:, 0:TB], Ct[:, inst, ri, :], Hs[d][:, ri, blk], start=(n_ == 0), stop=(n_ == 3), reads=[Ct, Hs[d]] + list(Hparts[id(Hs[d])]), writes=[py])
                        n_ += 1
                if q == 0:
                    p.do('act', 'copy', yT[:, chunk, blk], py[:, 0:TB], reads=[py], writes=[yT])
                else:
                    p.do('dve', 'tensor_tensor', yT[:, chunk, blk], py[:, 0:TB], yT[:, chunk, blk], ALU.add, reads=[py, yT], writes=[yT])
        if lvl < 6:
            end_phase(k)
            return
        dT = sbt(st, "sdT", [128, 2], F32)
        gb = sbt(st, "sgb", [128, 2], F32)
        wg = sbt(st, "swg", [128, 2, 256], BF16)
        p.dma('sp', dT[:], I["ssm_dT"][l], writes=[dT])
        p.dma('sp', gb[:], I["glu_bT"][l], writes=[gb])
        p.dma('pool', wg[:], I["w_glu"][l].rearrange("(c p) n -> p c n", p=128), writes=[wg])
        zT = sbt(st, "szT", [128, 2, L], BF16)
        ytr = Ring([sbt(st, "syt%d" % i, [128, 512], F32) for i in range(2)])
        u1r = Ring([sbt(st, "su1%d" % i, [128, 512], F32) for i in range(2)])
        sgr = Ring([sbt(st, "ssg%d" % i, [128, 512], F32) for i in range(2)])
        outr = Ring([sbt(st, "sout%d" % i, [128, 512], BF16) for i in range(2)])
        for b in range(nblk):
            blk = slice(b * TB, (b + 1) * TB)
            for c in range(2):
                yt, u1, sg = ytr.next(), u1r.next(), sgr.next()
                p.do('dve', 'scalar_tensor_tensor', yt[:, 0:TB], uT[:, c, blk], dT[:, c:c + 1], yT[:, c, blk], ALU.mult, ALU.add, reads=[uT, dT, yT], writes=[yt])
                p.do('pool', 'tensor_tensor', u1[:, 0:TB], yt[:, 0:TB], yt[:, 0:TB], ALU.mult, reads=[yt], writes=[u1])
                p.do('pool', 'tensor_scalar', u1[:, 0:TB], u1[:, 0:TB], 0.044715, 1.0, ALU.mult, ALU.add, reads=[u1], writes=[u1])
                p.do('pool', 'tensor_tensor', u1[:, 0:TB], u1[:, 0:TB], yt[:, 0:TB], ALU.mult, reads=[u1, yt], writes=[u1])
                p.do('act', 'activation', sg[:, 0:TB], u1[:, 0:TB], AF.Sigmoid, scale=1.5957691216057308, reads=[u1], writes=[sg])
                p.do('dve', 'tensor_tensor', zT[:, c, blk], sg[:, 0:TB], yt[:, 0:TB], ALU.mult, reads=[sg, yt], writes=[zT])
            for mo in range(2):
                pg = pbr.next()
                sg = sgr.next()
                ot = outr.next()
                for kc in range(2):
                    p.do('pe', 'matmul', pg[:, 0:TB], wg[:, kc, mo * 128:(mo + 1) * 128], zT[:, kc, blk], start=(kc == 0), stop=(kc == 1), reads=[wg, zT], writes=[pg])
                p.do('act', 'activation', sg[:, 0:TB], pg[:, 0:TB], AF.Sigmoid, bias=gb[:, mo:mo + 1], reads=[pg, gb], writes=[sg])
                p.do('dve', 'tensor_tensor', ot[:, 0:TB], sg[:, 0:TB], zT[:, mo, blk], ALU.mult, reads=[sg, zT], writes=[ot])
                p.dma('sp', s["mixT"][4 + mo, :, blk], ot[:, 0:TB], reads=[ot])
        end_phase(k)


def phase_gdn(k, l, s):
    nc, p, I, O = k.nc, k.p, k.I, k.O
    L, ctx = s["L"], s["ctx"]
    nC = L // 64
    TB = min(512, L)
    with contextlib.ExitStack() as st:
        sbt, pst = k.sbt, k.pst
        qkv = sbt(st, "gqkv", [128, 6, L], BF16)
        kz = sbt(st, "gkz", [128, 4, L], BF16)
        bank = Ring([pst(st, "gbank%d" % i, [128, 512], F32) for i in range(7)])
        pbf = pst(st, "gpbf", [128, 1024], BF16)
        st1 = contextlib.ExitStack()
        xp = sbt(st1, "gxp", [128, 6, L + 2], BF16)
        cw = sbt(st1, "gcw", [128, 6, 3], F32)
        bo = sbt(st1, "gbo", [128, 128], BF16)
        p.dma('sp', cw[:], I["conv_wT"][l], writes=[cw])
        p.do('pool', 'memset', xp[:, :, 0:1], 0.0, writes=[xp])
        p.do('pool', 'memset', xp[:, :, L + 1:L + 2], 0.0, writes=[xp])
        for c in range(6):
            p.dma('sp', xp[:, c, 1:L + 1], s["fT"][2 + c, :, :], writes=[xp])
        p.do('pool', 'memset', bo[:], 0.0, writes=[bo])
        p.do('pool', 'memset', bo[0:64, 0:64], 1.0, writes=[bo])
        p.do('pool', 'memset', bo[64:128, 64:128], 1.0, writes=[bo])
        accr = Ring([sbt(st1, "gacc%d" % i, [128, 512], F32) for i in range(2)])
        silr = Ring([sbt(st1, "gsil%d" % i, [128, 512], F32) for i in range(2)])
        sqr = Ring([sbt(st1, "gsq%d" % i, [128, 512], BF16) for i in range(2)])
        rnr = Ring([sbt(st1, "grn%d" % i, [128, 512], F32) for i in range(2)])
        for c in range(6):
            for b in range(L // TB):
                cs = b * TB
                acc, sil = accr.next(), silr.next()
                p.do('dve', 'tensor_scalar', acc[:, 0:TB], xp[:, c, cs:cs + TB], cw[:, c, 0:1], None, ALU.mult, reads=[xp, cw], writes=[acc])
                p.do('dve', 'scalar_tensor_tensor', acc[:, 0:TB], xp[:, c, cs + 1:cs + 1 + TB], cw[:, c, 1:2], acc[:, 0:TB], ALU.mult, ALU.add, reads=[xp, cw, acc], writes=[acc])
                p.do('dve', 'scalar_tensor_tensor', acc[:, 0:TB], xp[:, c, cs + 2:cs + 2 + TB], cw[:, c, 2:3], acc[:, 0:TB], ALU.mult, ALU.add, reads=[xp, cw, acc], writes=[acc])
                if c >= 4:
                    p.do('act', 'activation', qkv[:, c, cs:cs + TB], acc[:, 0:TB], AF.Silu, reads=[acc], writes=[qkv])
                    continue
                sq, rn, pb = sqr.next(), rnr.next(), bank.next()
                p.do('act', 'activation', sil[:, 0:TB], acc[:, 0:TB], AF.Silu, reads=[acc], writes=[sil])
                p.do('pool', 'tensor_tensor', sq[:, 0:TB], sil[:, 0:TB], sil[:, 0:TB], ALU.mult, reads=[sil], writes=[sq])
                p.do('pe', 'matmul', pb[:, 0:TB], bo[:], sq[:, 0:TB], start=True, stop=True, reads=[bo, sq], writes=[pb])
                p.do('act', 'activation', rn[:, 0:TB], pb[:, 0:TB], AF.Sqrt, bias=EPS, reads=[pb], writes=[rn])
                p.do('dve', 'reciprocal', rn[:, 0:TB], rn[:, 0:TB], reads=[rn], writes=[rn])
                if c < 2:
                    p.do('dve', 'scalar_tensor_tensor', qkv[:, c, cs:cs + TB], sil[:, 0:TB], 0.125, rn[:, 0:TB], ALU.mult, ALU.mult, reads=[sil, rn], writes=[qkv])
                else:
                    p.do('dve', 'tensor_tensor', qkv[:, c, cs:cs + TB], sil[:, 0:TB], rn[:, 0:TB], ALU.mult, reads=[sil, rn], writes=[qkv])
        p.do('pool', 'memset', kz[:], 0.0, writes=[kz])
        for h_ in range(4):
            hs_ = slice(64 * (h_ % 2), 64 * (h_ % 2) + 64)
            p.do('dve' if h_ % 2 == 0 else 'pool', 'tensor_copy', kz[hs_, h_, :], qkv[hs_, 2 + h_ // 2, :], reads=[qkv, kz], writes=[kz])
        p.barrier()
        p.flush()
        st1.close()
        st2 = contextlib.ExitStack()
        gt = sbt(st2, "ggt", [64, nC, 16], F32)
        ab = sbt(st2, "gab", [64, 2, 8], F32)
        gm = sbt(st2, "ggm", [64, 6, 8, 64], F32)
        gtri = sbt(st2, "ggtri", [64, 4, 64], F32)
        p.dma('sp', gt[:], s["gates"][:, :, :], writes=[gt])
        p.dma('sp', ab[:], I["gdn_ab"][l], writes=[ab])
        p.dma('sp', gm[:], I["gmask"][:, :, :, :], writes=[gm])
        p.dma('sp', gtri[:], I["gtri"][:, :, :], writes=[gtri])
        names = ["gG", "gBeta", "gGs", "gBetas", "gGc", "gEgc", "gBg", "gGl", "gEgl", "gKd", "gT0"]
        T_ = {n: sbt(st2, n, [64, nC, 8], F32) for n in names}
        g_, be_, gS, beS, gcS, egcS, bgS, glS, eglS, kdS, t0 = [T_[n] for n in names]
        Aexp = sbt(st2, "gAexp", [64, 8], F32)
        p.do('act', 'activation', Aexp[:], ab[:, 0, :], AF.Exp, reads=[ab], writes=[Aexp])
        p.do('dve', 'tensor_tensor', t0[:], gt[:, :, 0:8], bcast(ab[:, 1:2, :], [64, nC, 8]), ALU.add, reads=[gt, ab], writes=[t0])
        p.do('act', 'activation', t0[:], t0[:], AF.Exp, reads=[t0], writes=[t0])
        p.do('act', 'activation', t0[:], t0[:], AF.Ln, bias=1.0, reads=[t0], writes=[t0])
        p.do('dve', 'tensor_tensor', g_[:], t0[:], bcast(Aexp[:].unsqueeze(1), [64, nC, 8]), ALU.mult, reads=[t0, Aexp], writes=[g_])
        p.do('dve', 'tensor_scalar', g_[:], g_[:], -1.0, None, ALU.mult, reads=[g_], writes=[g_])
        p.do('act', 'activation', be_[:], gt[:, :, 8:16], AF.Sigmoid, reads=[gt], writes=[be_])
        p.do('pool', 'tensor_copy', gS[:, :, 0:4], g_[:, :, 0:4], reads=[g_], writes=[gS])
        p.do('pool', 'tensor_copy', beS[:, :, 0:4], be_[:, :, 0:4], reads=[be_], writes=[beS])
        for sidx in range(nC):
            cb = nC - 1 - sidx
            p.do('pool', 'tensor_copy', gS[:, sidx, 4:8], g_[:, cb, 4:8], reads=[g_], writes=[gS])
            p.do('pool', 'tensor_copy', beS[:, sidx, 4:8], be_[:, cb, 4:8], reads=[be_], writes=[beS])
        NG = nC * 8
        def dir_matmul(dst, src, i_f, i_b):
            pa, pb_ = bank.next(), bank.next()
            flat = src[:].rearrange("p c e -> p (c e)")
            p.do('pe', 'matmul', pa[0:64, 0:NG], gtri[:, i_f, :], flat, start=True, stop=True, reads=[gtri, src], writes=[pa])
            p.do('pe', 'matmul', pb_[0:64, 0:NG], gtri[:, i_b, :], flat, start=True, stop=True, reads=[gtri, src], writes=[pb_])
            p.do('dve', 'tensor_copy', dst[:, :, 0:4], pa[0:64, 0:NG].rearrange("p (c e) -> p c e", e=8)[:, :, 0:4], reads=[pa], writes=[dst])
            p.do('dve', 'tensor_copy', dst[:, :, 4:8], pb_[0:64, 0:NG].rearrange("p (c e) -> p c e", e=8)[:, :, 4:8], reads=[pb_], writes=[dst])

        dir_matmul(gcS, gS, 0, 1)
        p.do('act', 'activation', egcS[:], gcS[:], AF.Exp, reads=[gcS], writes=[egcS])
        p.do('dve', 'tensor_tensor', bgS[:], beS[:], egcS[:], ALU.mult, reads=[beS, egcS], writes=[bgS])
        dir_matmul(glS, gcS, 2, 3)
        p.do('act', 'activation', eglS[:], glS[:], AF.Exp, reads=[glS], writes=[eglS])
        p.do('dve', 'tensor_tensor', kdS[:], glS[:], gcS[:], ALU.subtract, reads=[glS, gcS], writes=[kdS])
        p.do('act', 'activation', kdS[:], kdS[:], AF.Exp, reads=[kdS], writes=[kdS])
        use_r = k.cfg.get('fp32r', False)

        off = k.cfg.get('r32_off', '')

        def r32(ap):
            return ap.bitcast(mybir.dt.float32r) if (use_r and 'u' not in off) else ap

        def r32l(ap):
            return ap.bitcast(mybir.dt.float32r) if (use_r and 'l' not in off) else ap

        def r32s(ap):
            return ap.bitcast(mybir.dt.float32r) if (use_r and 's' not in off) else ap

        S = sbt(st2, "gS", [64, 8, 64], F32)
        Sb = sbt(st2, "gSb", [64, 8, 64], BF16)
        S0t = sbt(st2, "gS0t", [64, 8, 64], F32)
        if ctx:
            p.do('pool', 'memset', S0t[:], 0.0, writes=[S0t])
        else:
            p.dma('sp', S0t[:], I["gdn_s0"][l], writes=[S0t])
        p.do('dve', 'tensor_copy', r32s(S[:]), S0t[:], reads=[S0t], writes=[S])
        p.do('pool', 'tensor_copy', Sb[:], S[:], reads=[S], writes=[Sb])

        def T3(name, dt=F32, n=2):
            return Ring([sbt(st2, "%s%d" % (name, i), [64, 8, 64], dt) for i in range(n)])

        dgR, XR, E1R, E2R = T3("gdg", n=4), T3("gX"), T3("gE1"), T3("gE2")
        NR, NtR, AtR, AccR = T3("gN", n=4), T3("gNt", n=4), T3("gAt"), T3("gAcc")
        VbR, RR, KdR = T3("gVb"), T3("gR"), T3("gKd_")
        UR, WTR, VnR, OR, TmR = T3("gU"), T3("gWT"), T3("gVn"), T3("gO"), T3("gTm")
        I8 = gm[:, 4, :, :]
        qodR = Ring([sbt(st2, "gqod%d" % i, [64, 2, 2, 64], BF16) for i in range(2)])

        def bview(b):
            return b[0:64, :].rearrange("p (e j) -> p e j", e=8)

        def colb(tab, sidx):
            return bcast(tab[:, sidx, :].unsqueeze(2), [64, 8, 64])

        def stage1(sidx, C):
            ce = [sidx if e < 4 else nC - 1 - sidx for e in range(8)]
            tk = [slice(ce[e] * 64, ce[e] * 64 + 64) for e in range(8)]
            qT = [qkv[64 * (e % 4 % 2):64 * (e % 4 % 2) + 64, 0 + (e % 4) // 2, tk[e]] for e in range(8)]
            pt4 = pbf[0:64, :].rearrange("p (a j) -> p a j", a=8)
            ptv = pbf[0:64, :].rearrange("p (a j) -> p a j", a=16)
            for kind in range(2):
                for d_ in range(2):
                    for c_ in range(2):
                        e0 = d_ * 4 + c_ * 2
                        p.do('pe', 'transpose', pt4[:, kind * 4 + d_ * 2 + c_, :], qkv[:, 2 + 2 * kind + c_, tk[e0]], k.identb[:], reads=[qkv, k.identb], writes=[pbf])
            Vb, R_, Kd = VbR.next(), RR.next(), KdR.next()
            p.do('dve', 'tensor_tensor', r32(Vb[:]), ptv[:, 8:16, :], colb(beS, sidx), ALU.mult, reads=[pbf, beS], writes=[Vb])
            p.do('dve', 'tensor_tensor', r32(R_[:]), ptv[:, 0:8, :], colb(bgS, sidx), ALU.mult, reads=[pbf, bgS], writes=[R_])
            p.do('dve', 'tensor_tensor', r32s(Kd[:]), ptv[:, 0:8, :], colb(kdS, sidx), ALU.mult, reads=[pbf, kdS], writes=[Kd])
            yield
            dg, X, E1, E2 = dgR.next(), XR.next(), E1R.next(), E2R.next()
            PGb, PBb = bank.next(), bank.next()
            PG, PB = bview(PGb), bview(PBb)
            p.do('pool', 'tensor_tensor', dg[:], I8, colb(gcS, sidx), ALU.mult, reads=[gm, gcS], writes=[dg])
            for e in range(8):
                p.do('pe', 'matmul', PG[:, e, :], k.ones_f[0:64, 0:64], dg[:, e, :], start=True, stop=True, reads=[k.ones_f, dg], writes=[PGb])
            p.do('dve', 'tensor_tensor', X[:], PG, colb(gcS, sidx), ALU.subtract, reads=[PGb, gcS], writes=[X])
            yield
            p.do('act', 'activation', E1[:], X[:], AF.Relu, reads=[X], writes=[E1])
            p.do('act', 'activation', E2[:], X[:], AF.Relu, scale=-1.0, reads=[X], writes=[E2])
            p.do('act', 'activation', E1[:], E1[:], AF.Exp, scale=-1.0, reads=[E1], writes=[E1])
            p.do('act', 'activation', E2[:], E2[:], AF.Exp, scale=-1.0, reads=[E2], writes=[E2])
            yield
            A1b, A2b = bank.next(), bank.next()
            A1, A2 = bview(A1b), bview(A2b)
            for e in range(8):
                h_ = e % 4
                p.do('pe', 'matmul', A1[:, e, :], kz[:, h_, tk[e]], qkv[:, 2 + h_ // 2, tk[e]], start=True, stop=True, reads=[kz, qkv], writes=[A1b])
            for e in range(8):
                h_ = e % 4
                p.do('pe', 'matmul', A2[:, e, :], kz[:, h_, tk[e]], qkv[:, 0 + h_ // 2, tk[e]], start=True, stop=True, reads=[kz, qkv], writes=[A2b])
            Ds, Dts, Dti = E1, X, E2
            p.do('dve', 'tensor_tensor', Ds[:], E1[:], gm[:, 0, :, :], ALU.mult, reads=[E1, gm], writes=[Ds])
            p.do('dve', 'tensor_tensor', Dts[:], E2[:], gm[:, 2, :, :], ALU.mult, reads=[E2, gm], writes=[Dts])
            p.do('dve', 'tensor_tensor', Dti[:], E2[:], gm[:, 3, :, :], ALU.mult, reads=[E2, gm], writes=[Dti])
            dg2 = dgR.next()
            p.do('pool', 'tensor_tensor', dg2[:], I8, colb(beS, sidx), ALU.mult, reads=[gm, beS], writes=[dg2])
            for e in range(8):
                p.do('pe', 'matmul', PB[:, e, :], k.ones_f[0:64, 0:64], dg2[:, e, :], start=True, stop=True, reads=[k.ones_f, dg2], writes=[PBb])
            N_, Nt, At, Acc = NR.next(), NtR.next(), AtR.next(), AccR.next()
            p.do('dve', 'tensor_tensor', r32l(N_[:]), A1, Ds[:], ALU.mult, reads=[A1b, Ds], writes=[N_])
            p.do('dve', 'tensor_tensor', r32l(N_[:]), N_[:], colb(beS, sidx), ALU.mult, reads=[N_, beS], writes=[N_])
            p.do('dve', 'tensor_tensor', r32l(Nt[:]), A1, Dts[:], ALU.mult, reads=[A1b, Dts], writes=[Nt])
            p.do('dve', 'tensor_tensor', r32l(Nt[:]), PB, Nt[:], ALU.mult, reads=[PBb, Nt], writes=[Nt])
            p.do('dve', 'tensor_tensor', r32s(At[:]), A2, Dti[:], ALU.mult, reads=[A2b, Dti], writes=[At])
            p.do('dve', 'scalar_tensor_tensor', r32(Acc[:]), Nt[:], -1.0, I8, ALU.mult, ALU.add, reads=[Nt, gm], writes=[Acc])
            yield
            P_, Pt = N_, Nt
            for lv in range(5):
                PPa, PPb, PAb = bank.next(), bank.next(), bank.next()
                Pn, Ptn = NR.next(), NtR.next()
                for e in range(8):
                    p.do('pe', 'matmul', bview(PPa)[:, e, :], r32l(Pt[:, e, :]), r32l(P_[:, e, :]), start=True, stop=True, reads=[Pt, P_], writes=[PPa])
                for e in range(8):
                    p.do('pe', 'matmul', bview(PPb)[:, e, :], r32l(P_[:, e, :]), r32l(Pt[:, e, :]), start=True, stop=True, reads=[Pt, P_], writes=[PPb])
                p.do('act', 'copy', r32l(Pn[:]), bview(PPa), reads=[PPa], writes=[Pn])
                p.do('act', 'copy', r32l(Ptn[:]), bview(PPb), reads=[PPb], writes=[Ptn])
                yield
                for e in range(8):
                    p.do('pe', 'matmul', bview(PAb)[:, e, :], r32(Pn[:, e, :]), r32(Acc[:, e, :]), start=True, stop=True, reads=[Pn, Acc], writes=[PAb])
                p.do('dve', 'tensor_tensor', r32(Acc[:]), Acc[:], bview(PAb), ALU.add, reads=[Acc, PAb], writes=[Acc])
                P_, Pt = Pn, Ptn
                yield
            PUb, PWb = bank.next(), bank.next()
            U_, WT = UR.next(), WTR.next()
            for e in range(8):
                p.do('pe', 'matmul', bview(PUb)[:, e, :], r32(Acc[:, e, :]), r32(Vb[:, e, :]), start=True, stop=True, reads=[Acc, Vb], writes=[PUb])
            for e in range(8):
                p.do('pe', 'matmul', bview(PWb)[:, e, :], r32(R_[:, e, :]), r32(Acc[:, e, :]), start=True, stop=True, reads=[Acc, R_], writes=[PWb])
            p.do('act', 'copy', U_[:], bview(PUb), reads=[PUb], writes=[U_])
            p.do('act', 'copy', r32s(WT[:]), bview(PWb), reads=[PWb], writes=[WT])
            C.update(dict(U_=U_, WT=WT, At=At, Kd=Kd, qT=qT))
            return

        def stage2(sidx, C):
            U_, WT, At, Kd, qT = C['U_'], C['WT'], C['At'], C['Kd'], C['qT']
            PWSb, POb, PO2b, PKVb = bank.next(), bank.next(), bank.next(), bank.next()
            Vn, Oo, Tm = VnR.next(), OR.next(), TmR.next()
            for e in range(8):
                p.do('pe', 'matmul', bview(PWSb)[:, e, :], r32s(WT[:, e, :]), r32s(S[:, e, :]), start=True, stop=True, reads=[WT, S], writes=[PWSb])
            p.do('dve', 'tensor_tensor', r32s(Vn[:]), U_[:], bview(PWSb), ALU.subtract, reads=[U_, PWSb], writes=[Vn])
            qod = qodR.next()
            for d_ in range(2):
                cc = sidx if d_ == 0 else nC - 1 - sidx
                p.do('act', 'copy', qod[:, d_, :, :], qkv[64:128, 0:2, cc * 64:(cc + 1) * 64], reads=[qkv], writes=[qod])
            for e in range(8):
                h_ = e % 4
                lq = qT[e] if h_ % 2 == 0 else qod[:, e // 4, h_ // 2, :]
                p.do('pe', 'matmul', bview(POb)[:, e, :], lq, Sb[:, e, :], start=True, stop=True, reads=[qkv, qod, Sb], writes=[POb])
            for e in range(8):
                p.do('pe', 'matmul', bview(PO2b)[:, e, :], r32s(At[:, e, :]), r32s(Vn[:, e, :]), start=True, stop=True, reads=[At, Vn], writes=[PO2b])
            for e in range(8):
                p.do('pe', 'matmul', bview(PKVb)[:, e, :], r32s(Kd[:, e, :]), r32s(Vn[:, e, :]), start=True, stop=True, reads=[Kd, Vn], writes=[PKVb])
            p.do('dve', 'tensor_tensor', Tm[:], bview(POb), colb(egcS, sidx), ALU.mult, reads=[POb, egcS], writes=[Tm])
            p.do('dve', 'tensor_tensor', Oo[:], Tm[:], bview(PO2b), ALU.add, reads=[Tm, PO2b], writes=[Oo])
            cf, cb = sidx, nC - 1 - sidx
            p.dma('sp', s["go"][0, cf * 64:(cf + 1) * 64, :].rearrange("t (h v) -> t h v", h=4), Oo[:, 0:4, :], reads=[Oo])
            p.dma('sp', s["go"][1, cb * 64:(cb + 1) * 64, :].rearrange("t (h v) -> t h v", h=4), Oo[:, 4:8, :], reads=[Oo])
            Tm2 = TmR.next()
            p.do('pool', 'tensor_tensor', Tm2[:], S[:], colb(eglS, sidx), ALU.mult, reads=[S, eglS], writes=[Tm2])
            p.do('dve', 'tensor_tensor', r32s(S[:]), Tm2[:], bview(PKVb), ALU.add, reads=[Tm2, PKVb], writes=[S])
            p.do('act', 'copy', Sb[:], S[:], reads=[S], writes=[Sb])

        GG = k.cfg.get("gdn_group", 2)
        for s0_ in range(0, nC, GG):
            grp = list(range(s0_, min(nC, s0_ + GG)))
            ctxs = {si: {} for si in grp}
            gens = [stage1(si, ctxs[si]) for si in grp]
            alive = list(gens)
            while alive:
                nxt = []
                for g_it in alive:
                    try:
                        next(g_it)
                        nxt.append(g_it)
                    except StopIteration:
                        pass
                alive = nxt
            for si in grp:
                stage2(si, ctxs[si])
        if ctx:
            p.dma('sp', O["ngdn"][s["pi"], l].rearrange("d h k v -> k (d h) v"), S[:], reads=[S])
        p.barrier()
        p.flush()
        st2.close()
        gz = sbt(st, "ggz", [128, 2, L], BF16)
        ng = sbt(st, "gng", [128, 1], F32)
        p.dma('sp', gz[:], s["fT"][8:10].rearrange("c p t -> p c t"), writes=[gz])
        p.dma('sp', ng[:], I["gdn_ng"][l], writes=[ng])
        p.do('act', 'activation', gz[:], gz[:], AF.Silu, reads=[gz], writes=[gz])
        o0r = Ring([sbt(st, "go0%d" % i, [128, 4, 64], F32) for i in range(2)])
        o1r = Ring([sbt(st, "go1%d" % i, [128, 4, 64], F32) for i in range(2)])
        sqo = sbt(st, "gsqo", [128, 4, 64], F32)
        ss4 = sbt(st, "gss4", [128, 4], F32)
        onr = Ring([sbt(st, "gon%d" % i, [128, 4, 64], BF16) for i in range(2)])
        outr = Ring([sbt(st, "gout%d" % i, [128, 2, 128], BF16) for i in range(2)])
        for t in range(L // 128):
            rows = slice(t * 128, (t + 1) * 128)
            o0, o1, on, ot = o0r.next(), o1r.next(), onr.next(), outr.next()
            p.dma('sp', o0[:], s["go"][0, rows, :].rearrange("t (h v) -> t h v", h=4), writes=[o0])
            p.dma('sp', o1[:], s["go"][1, rows, :].rearrange("t (h v) -> t h v", h=4), writes=[o1])
            p.do('pool', 'tensor_tensor', o0[:], o0[:], o1[:], ALU.add, reads=[o0, o1], writes=[o0])
            p.do('act', 'activation', sqo[:], o0[:], AF.Square, reads=[o0], writes=[sqo])
            p.do('dve', 'tensor_reduce', ss4[:], sqo[:], AX.X, ALU.add, reads=[sqo], writes=[ss4])
            p.do('act', 'activation', ss4[:], ss4[:], AF.Sqrt, scale=1.0 / 64, bias=EPS, reads=[ss4], writes=[ss4])
            p.do('dve', 'reciprocal', ss4[:], ss4[:], reads=[ss4], writes=[ss4])
            p.do('dve', 'tensor_tensor', on[:], o0[:], bcast(ss4[:].unsqueeze(2), [128, 4, 64]), ALU.mult, reads=[o0, ss4], writes=[on])
            pv = pbf[:, 0:256].rearrange("p (c t) -> p c t", c=2)
            for c in range(2):
                p.do('pe', 'transpose', pv[:, c, :], on[:, 2 * c:2 * c + 2, :].rearrange("p h v -> p (h v)"), k.identb[:], reads=[on, k.identb], writes=[pbf])
            for c in range(2):
                p.do('dve', 'scalar_tensor_tensor', ot[:, c, :], pv[:, c, :], ng[:, 0:1], gz[:, c, rows], ALU.mult, ALU.mult, reads=[pbf, ng, gz], writes=[ot])
            p.dma('sp', s["mixT"][6:8, :, rows].rearrange("c p t -> p c t"), ot[:], reads=[ot])
        end_phase(k)


FULL_CFG = dict(LS=4096, LP=256, NP=2, DEPTH=4, PAST=256, stages="MABC")


def kernel(**inputs):
    cfg = dict(FULL_CFG)
    nc = build(cfg)
    consts = const_tables(cfg)
    w = host_weights(inputs, cfg)
    in_maps = []
    for c in range(NCORES):
        m = {}
        m.update(consts)
        m.update(w)
        m.update(host_core_inputs(inputs, c, cfg))
        in_maps.append(m)
    res = run_bass_kernel_spmd(nc, in_maps, core_ids=list(range(NCORES)))
    R = res.results
    NP, LP, DEPTH = cfg["NP"], cfg["LP"], cfg["DEPTH"]
    y_sample = np.stack([np.asarray(R[c]["y_s"], dtype=np.float32) for c in range(NCORES)], axis=0)
    y_prompt = np.concatenate([np.asarray(R[c]["y_p"], dtype=np.float32).reshape(NP, LP, D) for c in range(NCORES)], axis=0)
    nk = np.concatenate([np.asarray(R[c]["nk"], dtype=np.float32).reshape(NP, DEPTH, LP, 2, 64) for c in range(NCORES)], axis=0)
    nv = np.concatenate([np.asarray(R[c]["nv"], dtype=np.float32).reshape(NP, DEPTH, LP, 2, 64) for c in range(NCORES)], axis=0)
    nssm = np.concatenate([np.asarray(R[c]["nssm"], dtype=np.float32) for c in range(NCORES)], axis=0)
    ngdn = np.concatenate([np.asarray(R[c]["ngdn"], dtype=np.float32) for c in range(NCORES)], axis=0)
    return (y_prompt, y_sample, nk, nv, nssm, ngdn)
```

```python
import contextlib
import math
import numpy as np
import concourse.bass as bass
import concourse.mybir as mybir
from concourse.bass_utils import run_bass_kernel_spmd

F32 = mybir.dt.float32
BF16 = mybir.dt.bfloat16
AF = mybir.ActivationFunctionType
ALU = mybir.AluOpType
AX = mybir.AxisListType

D = 1024
KC = 8
EPS = 1e-6
IN_W = 2064
NCORES = 8


class Buf:
    __slots__ = ("name", "ap", "last_w", "readers")

    def __init__(self, ap=None, name=""):
        self.ap = ap
        self.name = name
        self.last_w = None
        self.readers = []

    def __getitem__(self, k):
        return self.ap[k]


class Ring:
    def __init__(self, bufs):
        self.bufs = bufs
        self.i = 0

    def next(self):
        b = self.bufs[self.i]
        self.i = (self.i + 1) % len(self.bufs)
        return b


class Prog:
    ENG = ("pe", "act", "dve", "pool", "sp")

    def __init__(self, nc, stack, n_dma_sems=40):
        self.nc = nc
        self.lists = {e: [] for e in self.ENG}
        self.count = {e: 0 for e in self.ENG}
        self.known = {e: {} for e in self.ENG}
        self.n_dma_sems = n_dma_sems
        self.dma_val = [0] * n_dma_sems
        self.dma_rr = 0
        self.dma_rr_pool = 0
        self.esem = {e: stack.enter_context(nc.semaphore("sem_" + e)) for e in self.ENG}
        self.dsem = [stack.enter_context(nc.semaphore("dsem%d" % i)) for i in range(n_dma_sems)]
        self.ninst = 0

    def _need(self, eng, tok, waits):
        if tok is None:
            return
        key = (tok[0], tok[1])
        if self.known[eng].get(key, 0) >= tok[2]:
            return
        if tok[2] > waits.get(key, 0):
            waits[key] = tok[2]

    def _collect(self, eng, reads, writes, pe_ok=False):
        waits = {}
        for b in reads:
            self._need(eng, b.last_w, waits)
        for b in writes:
            self._need(eng, b.last_w, waits)
            for r in b.readers:
                self._need(eng, r, waits)
        out = []
        for key, val in waits.items():
            if key == ('e', 'pe') and eng == 'pe':
                continue
            out.append((key, val))
            self.known[eng][key] = val
        return out

    def _mark(self, tok, reads, writes):
        for b in reads:
            b.readers.append(tok)
            if len(b.readers) > 16:
                best = {}
                for t in b.readers:
                    k = (t[0], t[1])
                    if k not in best or best[k][2] < t[2]:
                        best[k] = t
                b.readers = list(best.values())
        for b in writes:
            b.last_w = tok
            b.readers = []

    def op(self, eng, fn, reads=(), writes=()):
        for key, val in self._collect(eng, reads, writes):
            self.lists[eng].append(('w', key, val))
        self.count[eng] += 1
        self.lists[eng].append(('i', fn))
        tok = ('e', eng, self.count[eng])
        self._mark(tok, reads, writes)
        self.ninst += 1
        return tok

    def do(self, eng, meth, *args, reads=(), writes=(), **kw):
        return self.op(eng, lambda h: getattr(h, meth)(*args, **kw), reads=reads, writes=writes)

    def dma(self, eng, out_ap, in_ap, reads=(), writes=(), **kw):
        half = self.n_dma_sems // 2
        if eng == 'pool':
            idx = half + self.dma_rr_pool
            self.dma_rr_pool = (self.dma_rr_pool + 1) % (self.n_dma_sems - half)
        else:
            idx = self.dma_rr
            self.dma_rr = (self.dma_rr + 1) % half
        wl = self._collect(eng, reads, writes)
        prev = self.dma_val[idx]
        if prev > 0 and self.known[eng].get(('d', idx), 0) < prev:
            wl.append((('d', idx), prev))
            self.known[eng][('d', idx)] = prev
        for key, val in wl:
            self.lists[eng].append(('w', key, val))
        self.dma_val[idx] += 16
        self.lists[eng].append(('dma', idx, out_ap, in_ap, kw))
        tok = ('d', idx, self.dma_val[idx])
        self._mark(tok, reads, writes)
        self.ninst += 1
        return tok

    def barrier(self):
        for eng in self.ENG:
            for other in self.ENG:
                if other == eng:
                    continue
                v = self.count[other]
                if v > 0 and self.known[eng].get(('e', other), 0) < v:
                    self.lists[eng].append(('w', ('e', other), v))
                    self.known[eng][('e', other)] = v
            for idx in range(self.n_dma_sems):
                v = self.dma_val[idx]
                if v > 0 and self.known[eng].get(('d', idx), 0) < v:
                    self.lists[eng].append(('w', ('d', idx), v))
                    self.known[eng][('d', idx)] = v

    def flush(self):
        nc = self.nc
        lists = self.lists
        self.lists = {e: [] for e in self.ENG}
        if not any(lists.values()):
            return
        esem, dsem = self.esem, self.dsem
        with nc.Block() as block:
            def replay(eng, h):
                my = esem[eng]
                for item in lists[eng]:
                    k = item[0]
                    if k == 'w':
                        key, val = item[1], item[2]
                        h.wait_ge(esem[key[1]] if key[0] == 'e' else dsem[key[1]], val)
                    elif k == 'i':
                        item[1](h).then_inc(my, 1)
                    else:
                        _, idx, o, i, kw = item
                        h.dma_start(out=o, in_=i, **kw).then_inc(dsem[idx], 16)

            @block.tensor
            def _(h):
                replay('pe', h)

            @block.scalar
            def _(h):
                replay('act', h)

            @block.vector
            def _(h):
                replay('dve', h)

            @block.gpsimd
            def _(h):
                replay('pool', h)

            @block.sync
            def _(h):
                replay('sp', h)


class K:
    pass


def bcast(ap, shape):
    return ap.to_broadcast(shape)


def build(cfg, dbg=False):
    LS, LP, NP, DEPTH, PAST = cfg["LS"], cfg["LP"], cfg["NP"], cfg["DEPTH"], cfg["PAST"]
    nc = bass.Bass("TRN2", target_bir_lowering=False)
    k = K()
    k.nc, k.cfg = nc, cfg

    def din(name, shape, dt=F32):
        return nc.dram_tensor(name, list(shape), dt, kind="ExternalInput").ap()

    def dout(name, shape, dt=F32):
        return nc.dram_tensor(name, list(shape), dt, kind="ExternalOutput").ap()

    def dscr(name, shape, dt=F32):
        return nc.dram_tensor(name, list(shape), dt, kind="ExternalOutput" if dbg else "Internal").ap()

    I = {}
    I["x_s"] = din("x_s", [LS, D])
    I["x_p"] = din("x_p", [NP * LP, D])
    I["cond2"] = din("cond2", [128, 2, KC])
    I["cache_k"] = din("cache_k", [DEPTH, PAST, 128])
    I["cache_v"] = din("cache_v", [DEPTH, PAST, 128])
    I["ssm_h0"] = din("ssm_h0", [DEPTH, 128, 16, 2])
    I["gdn_s0"] = din("gdn_s0", [DEPTH, 64, 8, 64])
    I["w_mod"] = din("w_mod", [DEPTH, D, 6 * D])
    I["b_modT"] = din("b_modT", [DEPTH, 128, 48])
    I["n1g"] = din("n1g", [DEPTH, 128, KC])
    I["n2g"] = din("n2g", [DEPTH, 128, KC])
    I["w_in"] = din("w_in", [DEPTH, D, IN_W])
    I["qk_g"] = din("qk_g", [DEPTH, 128, 2, 64])
    I["sink_b"] = din("sink_b", [DEPTH, 128, 8])
    I["lam"] = din("lam", [DEPTH, 128, 3, 16])
    I["ssm_bt"] = din("ssm_bt", [DEPTH, 128, 16, 2, 128])
    I["ssm_ct"] = din("ssm_ct", [DEPTH, 128, 16, 2, 128])
    I["ssm_dT"] = din("ssm_dT", [DEPTH, 128, 2])
    I["glu_bT"] = din("glu_bT", [DEPTH, 128, 2])
    I["w_glu"] = din("w_glu", [DEPTH, 256, 256])
    I["conv_wT"] = din("conv_wT", [DEPTH, 128, 6, 3])
    I["gdn_ab"] = din("gdn_ab", [DEPTH, 64, 2, 8])
    I["gdn_ng"] = din("gdn_ng", [DEPTH, 128, 1])
    I["w_out"] = din("w_out", [DEPTH, D, D])
    I["w_ff1"] = din("w_ff1", [DEPTH, D, 4 * D])
    I["w_ff2"] = din("w_ff2", [DEPTH, 4 * D, D])
    I["ident"] = din("ident", [128, 128])
    I["rope"] = din("rope", [LS, 2, 32])
    I["amask"] = din("amask", [128, 2, 128])
    I["gmask"] = din("gmask", [64, 6, 8, 64])
    I["gtri"] = din("gtri", [64, 4, 64])
    k.I = I
    O = {}
    O["y_s"] = dout("y_s", [LS, D])
    O["y_p"] = dout("y_p", [NP * LP, D])
    O["nk"] = dout("nk", [NP, DEPTH, LP, 128])
    O["nv"] = dout("nv", [NP, DEPTH, LP, 128])
    O["nssm"] = dout("nssm", [NP, DEPTH, 2, 2, 16, 64])
    O["ngdn"] = dout("ngdn", [NP, DEPTH, 2, 4, 64, 64])
    k.O = O
    seqs = [dict(name="s", L=LS, ctx=False, xin=I["x_s"], y=O["y_s"], pi=-1, cond=0)]
    for i in range(NP):
        seqs.append(dict(name="p%d" % i, L=LP, ctx=True, xin=I["x_p"][i * LP:(i + 1) * LP, :],
                         y=O["y_p"][i * LP:(i + 1) * LP, :], pi=i, cond=1))
    for s in seqs:
        L = s["L"]
        n = s["name"]
        s["qT"] = dscr("qT_" + n, [4, 128, L], BF16)
        s["kT"] = dscr("kT_" + n, [2, 128, L], BF16)
        s["v"] = dscr("v_" + n, [L, 128], BF16)
        s["fT"] = dscr("fT_" + n, [10, 128, L], BF16)
        s["gates"] = dscr("gates_" + n, [64, L // 64, 16], F32)
        s["mixT"] = dscr("mixT_" + n, [8, 128, L], BF16)
        s["go"] = dscr("go_" + n, [2, L, 256], F32)
    k.seqs = seqs

    with contextlib.ExitStack() as top:
        p = Prog(nc, top)
        k.p = p

        uid = [0]

        def sbt(st, name, shape, dt):
            uid[0] += 1
            t = st.enter_context(nc.sbuf_tensor("sb%d_%s" % (uid[0], name), list(shape), dt))
            return Buf(t, name)

        def pst(st, name, shape, dt):
            uid[0] += 1
            t = st.enter_context(nc.psum_tensor("ps%d_%s" % (uid[0], name), list(shape), dt))
            return Buf(t, name)

        k.sbt, k.pst = sbt, pst
        k.ident = sbt(top, "ident", [128, 128], F32)
        k.identb = sbt(top, "identb", [128, 128], BF16)
        k.modT = sbt(top, "modT", [128, DEPTH, 48, 2], F32)
        k.ones_f = sbt(top, "ones_f", [128, 128], F32)
        k.ones_b = sbt(top, "ones_b", [128, 128], BF16)
        p.dma('sp', k.ident[:], I["ident"][:, :], writes=[k.ident])
        p.do('dve', 'tensor_copy', k.identb[:], k.ident[:], reads=[k.ident], writes=[k.identb])
        p.do('dve', 'memset', k.ones_f[:], 1.0, writes=[k.ones_f])
        p.do('dve', 'memset', k.ones_b[:], 1.0, writes=[k.ones_b])

        stages = cfg.get("stages", "MABC")
        if "M" in stages:
            prologue_mod(k)
        if dbg:
            dm = dout("dbg_modT", [128, DEPTH * 96])
            p.dma('sp', dm[:, :], k.modT[:].rearrange("p l c t -> p (l c t)"), reads=[k.modT])
        for l in range(DEPTH):
            if "A" in stages:
                phase_a(k, l)
            cst, cpre = None, None
            if "B" in stages:
                for si, s in enumerate(seqs):
                    if si == 1 and "C" in stages and cfg.get("c_prefetch", True):
                        cst = contextlib.ExitStack()
                        cpre = phase_c_weights(k, l, cst)
                    if cfg.get("attn", True):
                        phase_attn(k, l, s)
                    if cfg.get("ssm", True):
                        phase_ssm(k, l, s)
                    if cfg.get("gdn", True):
                        phase_gdn(k, l, s)
            if "C" in stages:
                phase_c(k, l, cpre)
            if cst is not None:
                cst.close()
        p.barrier()
        p.flush()
    return nc


def end_phase(k):
    k.p.barrier()
    k.p.flush()


def prologue_mod(k):
    nc, p, I = k.nc, k.p, k.I
    DEPTH = k.cfg["DEPTH"]
    with contextlib.ExitStack() as st:
        cond = k.sbt(st, "cond", [128, 2, KC], F32)
        sc = k.sbt(st, "sc", [128, KC, 2], F32)
        bm = k.sbt(st, "bm", [128, DEPTH, 48], F32)
        wring = Ring([k.sbt(st, "wm%d" % i, [128, 6 * D], F32) for i in range(2)])
        pring = Ring([k.pst(st, "pm%d" % i, [128, 256, 2], F32) for i in range(2)])
        p.dma('sp', cond[:], I["cond2"][:, :, :], writes=[cond])
        p.dma('sp', bm[:], I["b_modT"].rearrange("l p c -> p l c"), writes=[bm])
        p.do('act', 'activation', sc[:].rearrange("p k c -> p c k"), cond[:], AF.Silu, reads=[cond], writes=[sc])
        for l in range(DEPTH):
            for kc in range(KC):
                w = wring.next()
                p.dma('sp' if kc % 2 == 0 else 'pool', w[:], I["w_mod"][l, kc * 128:(kc + 1) * 128, :], writes=[w])
                ps = pring.next()
                for fc in range(48):
                    p.do('pe', 'matmul', ps[:, fc, :], w[:, fc * 128:(fc + 1) * 128], sc[:, kc, :], start=True, stop=True,
                         reads=[w, sc], writes=[ps])
                if kc == 0:
                    p.do('dve', 'tensor_tensor', k.modT[:, l, :, :], ps[:, 0:48, :], bcast(bm[:, l, :].unsqueeze(2), [128, 48, 2]), ALU.add,
                         reads=[ps, bm], writes=[k.modT])
                else:
                    p.do('dve', 'tensor_tensor', k.modT[:, l, :, :], ps[:, 0:48, :], k.modT[:, l, :, :], ALU.add,
                         reads=[ps, k.modT], writes=[k.modT])
        end_phase(k)


def phase_a(k, l):
    nc, p, I, O = k.nc, k.p, k.I, k.O
    with contextlib.ExitStack() as st:
        sbt, pst = k.sbt, k.pst
        win = sbt(st, "win", [128, KC, IN_W], BF16)
        wsrc = I["w_in"][l].rearrange("(c p) n -> p c n", p=128)
        for c in range(KC):
            for hh in range(2):
                p.dma('pool', win[:, c, hh * 1032:(hh + 1) * 1032], wsrc[:, c, hh * 1032:(hh + 1) * 1032], writes=[win])
        n1g = sbt(st, "n1g", [128, KC], F32)
        sc1 = sbt(st, "sc1", [128, KC, 2], F32)
        qkg = sbt(st, "qkg", [128, 2, 64], F32)
        p.dma('sp', n1g[:], I["n1g"][l], writes=[n1g])
        p.dma('sp', qkg[:], I["qk_g"][l], writes=[qkg])
        p.do('dve', 'tensor_scalar', sc1[:], k.modT[:, l, 8:16, :], 1.0, None, ALU.add, reads=[k.modT], writes=[sc1])
        p.do('dve', 'tensor_tensor', sc1[:], sc1[:], bcast(n1g[:].unsqueeze(2), [128, KC, 2]), ALU.mult, reads=[sc1, n1g], writes=[sc1])
        xring = Ring([sbt(st, "xa%d" % i, [128, D], F32) for i in range(2)])
        junk = sbt(st, "junka", [128, D], F32)
        ssr = Ring([sbt(st, "ssa%d" % i, [128, 1], F32) for i in range(2)])
        xnr = Ring([sbt(st, "xna%d" % i, [128, D], BF16) for i in range(2)])
        hTr = Ring([sbt(st, "hTa%d" % i, [128, KC, 512], BF16) for i in range(2)])
        sq = sbt(st, "sqa", [128, 10, 64], F32)
        ss10 = sbt(st, "ss10", [128, 10], F32)
        qn = sbt(st, "qna", [128, 10, 64], F32)
        tmp = [sbt(st, "ropet%d" % i, [128, 10, 32], F32) for i in range(4)]
        qr = sbt(st, "qra", [128, 10, 64], BF16)
        kd = sbt(st, "kda", [128, 4, 64], BF16)
        vb = sbt(st, "vba", [128, 128], BF16)
        vf = sbt(st, "vfa", [128, 128], F32)
        rp = sbt(st, "rpa", [128, 2, 32], F32)
        stager = Ring([sbt(st, "stga%d" % i, [128, 6, 512], BF16) for i in range(2)])
        fstr = Ring([sbt(st, "fsta%d" % i, [128, 512], BF16) for i in range(3)])
        gsb = sbt(st, "gsba", [64, 8, 16], F32)
        pTr = Ring([pst(st, "pTa%d" % i, [128, KC, 128], BF16) for i in range(2)])
        pq = pst(st, "pqa", [128, 512], F32)
        pkv = pst(st, "pkva", [128, 512], F32)
        ptr = pst(st, "ptra", [128, 8, 128], BF16)
        pfr = Ring([pst(st, "pfa%d" % i, [128, 512], F32) for i in range(2)])
        pg = pst(st, "pga", [128, 32, 16], F32)
        ev = [0]
        p.barrier()

        for s in (k.seqs[::-1] if k.cfg.get('rev') else k.seqs):
            L, ctx, cond = s["L"], s["ctx"], s["cond"]
            xsrc = s["xin"] if l == 0 else s["y"]
            TB = min(512, L)
            ntb = TB // 128
            for b in range(L // TB):
                hT = hTr.next()
                for ti in range(ntb):
                    r0 = b * TB + ti * 128
                    xt = xring.next()
                    ss = ssr.next()
                    xn = xnr.next()
                    pT = pTr.next()
                    p.dma('sp', xt[:], xsrc[r0:r0 + 128, :], writes=[xt])
                    p.do('act', 'activation', junk[:], xt[:], AF.Square, accum_out=ss[:], reads=[xt], writes=[junk, ss])
                    p.do('act', 'activation', ss[:], ss[:], AF.Sqrt, scale=1.0 / D, bias=EPS, reads=[ss], writes=[ss])
                    p.do('dve', 'reciprocal', ss[:], ss[:], reads=[ss], writes=[ss])
                    p.do('dve', 'tensor_scalar', xn[:], xt[:], ss[:, 0:1], None, ALU.mult, reads=[xt, ss], writes=[xn])
                    for c in range(KC):
                        p.do('pe', 'transpose', pT[:, c, :], xn[:, c * 128:(c + 1) * 128], k.identb[:], reads=[xn, k.identb], writes=[pT])
                    for c in range(KC):
                        if c % 2 == 0:
                            p.do('act', 'activation', hT[:, c, ti * 128:(ti + 1) * 128], pT[:, c, :], AF.Identity, scale=sc1[:, c, cond:cond + 1], bias=k.modT[:, l, c, cond:cond + 1],
                                 reads=[pT, sc1, k.modT], writes=[hT])
                        else:
                            p.do('dve', 'tensor_scalar', hT[:, c, ti * 128:(ti + 1) * 128], pT[:, c, :], sc1[:, c, cond:cond + 1], k.modT[:, l, c, cond:cond + 1], ALU.mult, ALU.add,
                                 reads=[pT, sc1, k.modT], writes=[hT])
                stage = stager.next()
                if k.cfg.get("dbg_hT") and not hasattr(k, "_dh"):
                    k._dh = nc.dram_tensor("dbg_hT", [128, KC, 512], BF16, kind="ExternalOutput").ap()
                    p.dma('sp', k._dh[:, :, 0:TB], hT[:, :, 0:TB], reads=[hT])
                    k._dw = nc.dram_tensor("dbg_win", [128, KC, IN_W], BF16, kind="ExternalOutput").ap()
                    p.dma('sp', k._dw[:, :, :], win[:, :, :], reads=[win])
                for ti in range(ntb):
                    r0 = b * TB + ti * 128
                    tsl = slice(ti * 128, (ti + 1) * 128)
                    for kc in range(KC):
                        p.do('pe', 'matmul', pq[:], hT[:, kc, tsl], win[:, kc, 0:512], start=(kc == 0), stop=(kc == KC - 1), reads=[hT, win], writes=[pq])
                    for kc in range(KC):
                        p.do('pe', 'matmul', pkv[:, 0:256], hT[:, kc, tsl], win[:, kc, 512:768], start=(kc == 0), stop=(kc == KC - 1), reads=[hT, win], writes=[pkv])
                    p.do('act', 'activation', sq[:, 0:8, :], pq[:].rearrange("p (h d) -> p h d", d=64), AF.Square, reads=[pq], writes=[sq])
                    p.do('act', 'activation', sq[:, 8:10, :], pkv[:, 0:128].rearrange("p (h d) -> p h d", d=64), AF.Square, reads=[pkv], writes=[sq])
                    p.do('dve', 'tensor_reduce', ss10[:], sq[:], AX.X, ALU.add, reads=[sq], writes=[ss10])
                    p.do('act', 'activation', ss10[:], ss10[:], AF.Sqrt, scale=1.0 / 64, bias=EPS, reads=[ss10], writes=[ss10])
                    p.do('dve', 'reciprocal', ss10[:], ss10[:], reads=[ss10], writes=[ss10])
                    p.do('dve', 'tensor_tensor', qn[:, 0:8, :], pq[:].rearrange("p (h d) -> p h d", d=64), bcast(ss10[:, 0:8].unsqueeze(2), [128, 8, 64]), ALU.mult, reads=[pq, ss10], writes=[qn])
                    p.do('dve', 'tensor_tensor', qn[:, 8:10, :], pkv[:, 0:128].rearrange("p (h d) -> p h d", d=64), bcast(ss10[:, 8:10].unsqueeze(2), [128, 2, 64]), ALU.mult, reads=[pkv, ss10], writes=[qn])
                    p.do('pool', 'tensor_tensor', qn[:, 0:8, :], qn[:, 0:8, :], bcast(qkg[:, 0:1, :], [128, 8, 64]), ALU.mult, reads=[qn, qkg], writes=[qn])
                    p.do('pool', 'tensor_tensor', qn[:, 8:10, :], qn[:, 8:10, :], bcast(qkg[:, 1:2, :], [128, 2, 64]), ALU.mult, reads=[qn, qkg], writes=[qn])
                    p.do('act', 'copy', vb[:], pkv[:, 128:256], reads=[pkv], writes=[vb])
                    p.dma('sp', s["v"][r0:r0 + 128, :], vb[:], reads=[vb])
                    if ctx:
                        pi = s["pi"]
                        p.do('act', 'copy', vf[:], pkv[:, 128:256], reads=[pkv], writes=[vf])
                        p.dma('sp', O["nv"][pi, l, r0:r0 + 128, :], vf[:], reads=[vf])
                        p.dma('sp', O["nk"][pi, l, r0:r0 + 128, :], qn[:, 8:10, :].rearrange("p h d -> p (h d)"), reads=[qn])
                        p.do('dve', 'tensor_copy', qr[:], qn[:], reads=[qn], writes=[qr])
                    else:
                        p.dma('sp', rp[:], I["rope"][r0:r0 + 128, :, :], writes=[rp])
                        q4 = qn[:].rearrange("p h (i two) -> p h i two", two=2)
                        r4 = qr[:].rearrange("p h (i two) -> p h i two", two=2)
                        cosb = bcast(rp[:, 0:1, :], [128, 10, 32])
                        sinb = bcast(rp[:, 1:2, :], [128, 10, 32])
                        p.do('dve', 'tensor_tensor', tmp[0][:], q4[:, :, :, 0], cosb, ALU.mult, reads=[qn, rp], writes=[tmp[0]])
                        p.do('pool', 'tensor_tensor', tmp[1][:], q4[:, :, :, 1], sinb, ALU.mult, reads=[qn, rp], writes=[tmp[1]])
                        p.do('dve', 'tensor_tensor', tmp[2][:], q4[:, :, :, 0], sinb, ALU.mult, reads=[qn, rp], writes=[tmp[2]])
                        p.do('pool', 'tensor_tensor', tmp[3][:], q4[:, :, :, 1], cosb, ALU.mult, reads=[qn, rp], writes=[tmp[3]])
                        p.do('dve', 'tensor_tensor', r4[:, :, :, 0], tmp[0][:], tmp[1][:], ALU.subtract, reads=[tmp[0], tmp[1]], writes=[qr])
                        p.do('pool', 'tensor_tensor', r4[:, :, :, 1], tmp[2][:], tmp[3][:], ALU.add, reads=[tmp[2], tmp[3]], writes=[qr])
                    p.do('pool', 'tensor_copy', kd[:, 0:2, :], bcast(qr[:, 8:9, :], [128, 2, 64]), reads=[qr], writes=[kd])
                    p.do('pool', 'tensor_copy', kd[:, 2:4, :], bcast(qr[:, 9:10, :], [128, 2, 64]), reads=[qr], writes=[kd])
                    for c in range(4):
                        p.do('pe', 'transpose', ptr[:, c, :], qr[:, 2 * c:2 * c + 2, :].rearrange("p h d -> p (h d)"), k.identb[:], reads=[qr, k.identb], writes=[ptr])
                    for g in range(2):
                        p.do('pe', 'transpose', ptr[:, 4 + g, :], kd[:, 2 * g:2 * g + 2, :].rearrange("p h d -> p (h d)"), k.identb[:], reads=[kd, k.identb], writes=[ptr])
                    p.do('act', 'copy', stage[:, :, tsl], ptr[:, 0:6, :], reads=[ptr], writes=[stage])
                cols = slice(b * TB, (b + 1) * TB)
                p.dma('sp', s["qT"][:, :, cols].rearrange("c p t -> p c t"), stage[:, 0:4, 0:TB], reads=[stage])
                p.dma('sp', s["kT"][:, :, cols].rearrange("c p t -> p c t"), stage[:, 4:6, 0:TB], reads=[stage])
                for fc in range(10):
                    pf = pfr.next()
                    fst = fstr.next()
                    for kc in range(KC):
                        p.do('pe', 'matmul', pf[:, 0:TB], win[:, kc, 768 + fc * 128:768 + (fc + 1) * 128], hT[:, kc, 0:TB], start=(kc == 0), stop=(kc == KC - 1), reads=[hT, win], writes=[pf])
                    ev[0] += 1
                    if ev[0] % 2 == 0:
                        p.do('act', 'copy', fst[:, 0:TB], pf[:, 0:TB], reads=[pf], writes=[fst])
                    else:
                        p.do('dve', 'tensor_copy', fst[:, 0:TB], pf[:, 0:TB], reads=[pf], writes=[fst])
                    p.dma('sp', s["fT"][fc, :, cols], fst[:, 0:TB], reads=[fst])
                nch = TB // 64
                for j in range(nch):
                    for kc in range(KC):
                        p.do('pe', 'matmul', pg[0:64, j, :], hT[:, kc, j * 64:(j + 1) * 64], win[:, kc, 2048:2064], start=(kc == 0), stop=(kc == KC - 1), reads=[hT, win], writes=[pg])
                p.do('dve', 'tensor_copy', gsb[:, 0:nch, :], pg[0:64, 0:nch, :], reads=[pg], writes=[gsb])
                p.dma('sp', s["gates"][:, b * nch:(b + 1) * nch, :], gsb[:, 0:nch, :], reads=[gsb])
        end_phase(k)


def _chunkT(v, n=128):
    sh = v.shape[:-1]
    return np.ascontiguousarray(np.swapaxes(v.reshape(sh + (-1, n)), -1, -2))


def const_tables(cfg):
    LS = cfg["LS"]
    t = {}
    t["ident"] = np.eye(128, dtype=np.float32)
    n_rows = max(LS // 64, 1)
    rows = np.repeat(np.arange(n_rows, dtype=np.float32), 64)[:LS]
    cols = np.tile(np.arange(64, dtype=np.float32), n_rows)[:LS]
    inv_freq = np.power(np.float32(10000.0), -np.arange(16, dtype=np.float32) / np.float32(16)).astype(np.float32)
    ang = np.concatenate([rows[:, None] * inv_freq, cols[:, None] * inv_freq], axis=-1).astype(np.float32)
    t["rope"] = np.ascontiguousarray(np.stack([np.cos(ang), np.sin(ang)], axis=1).astype(np.float32))
    kk = np.arange(128)[:, None]
    qq = np.arange(128)[None, :]
    t["amask"] = np.ascontiguousarray(np.stack([(kk >= qq), (kk <= qq)], axis=1).astype(np.float32))
    i = np.arange(64)[:, None]
    j = np.arange(64)[None, :]
    low_incl = (i >= j).astype(np.float32)
    low_strict = (i > j).astype(np.float32)
    gm = np.zeros((64, 6, 8, 64), np.float32)
    for e in range(8):
        fwd = e < 4
        gm[:, 0, e, :] = low_strict if fwd else low_strict.T
        gm[:, 1, e, :] = low_incl if fwd else low_incl.T
        gm[:, 2, e, :] = low_strict.T if fwd else low_strict
        gm[:, 3, e, :] = low_incl.T if fwd else low_incl
        gm[:, 4, e, :] = np.eye(64, dtype=np.float32)
    t["gmask"] = gm
    gt = np.zeros((64, 4, 64), np.float32)
    gt[:, 0, :] = low_incl.T
    gt[:, 1, :] = low_incl
    gt[63, 2, :] = 1.0
    gt[0, 3, :] = 1.0
    t["gtri"] = gt
    return t


def host_weights(inp, cfg):
    DEPTH = cfg["DEPTH"]
    f = lambda a: np.ascontiguousarray(np.asarray(a, dtype=np.float32))
    w = {}
    w["w_mod"] = f(inp["w_mod"][:DEPTH])
    w["b_modT"] = _chunkT(f(inp["b_mod"][:DEPTH]))
    w["n1g"] = _chunkT(f(inp["norm1_g"][:DEPTH]))
    w["n2g"] = _chunkT(f(inp["norm2_g"][:DEPTH]))
    w["w_in"] = f(inp["w_in"][:DEPTH])
    qk = np.stack([f(inp["q_norm_g"][:DEPTH]), f(inp["k_norm_g"][:DEPTH])], axis=1)
    w["qk_g"] = np.ascontiguousarray(np.broadcast_to(qk[:, None], (DEPTH, 128, 2, 64)))
    w["sink_b"] = np.ascontiguousarray(np.broadcast_to(f(inp["attn_sink"][:DEPTH])[:, None], (DEPTH, 128, 8)))
    lam = np.zeros((DEPTH, 128, 3, 16), np.float32)
    bt = np.zeros((DEPTH, 128, 16, 2, 128), np.float32)
    ct = np.zeros((DEPTH, 128, 16, 2, 128), np.float32)
    lre, lim, lst = f(inp["ssm_lam_re"]), f(inp["ssm_lam_im"]), f(inp["ssm_log_step"])
    bre, bim, cre, cim = f(inp["ssm_b_re"]), f(inp["ssm_b_im"]), f(inp["ssm_c_re"]), f(inp["ssm_c_im"])
    for gp in range(8):
        for d in range(2):
            inst = gp * 2 + d
            for g2 in range(2):
                g = 2 * gp + g2
                gl = g % 8
                ps = slice(g2 * 64, (g2 + 1) * 64)
                lam[:, ps, 0, inst] = lre[:DEPTH, d, g, :]
                lam[:, ps, 1, inst] = lim[:DEPTH, d, g, :]
                lam[:, ps, 2, inst] = lst[:DEPTH, d, g][:, None]
                bt[:, ps, inst, 0, gl * 16:(gl + 1) * 16] = bre[:DEPTH, d, g]
                bt[:, ps, inst, 1, gl * 16:(gl + 1) * 16] = bim[:DEPTH, d, g]
                co0 = 32 * (gp % 4) + g2 * 16
                ct[:, ps, inst, 0, co0:co0 + 16] = np.swapaxes(cre[:DEPTH, d, g], -1, -2)
                ct[:, ps, inst, 1, co0:co0 + 16] = np.swapaxes(cim[:DEPTH, d, g], -1, -2)
    w["lam"], w["ssm_bt"], w["ssm_ct"] = lam, bt, ct
    w["ssm_dT"] = _chunkT(f(inp["ssm_d"][:DEPTH]))
    w["glu_bT"] = _chunkT(f(inp["ssm_b_glu"][:DEPTH]))
    w["w_glu"] = f(inp["ssm_w_glu"][:DEPTH])
    cw = f(inp["gdn_conv_w"][:DEPTH])
    w["conv_wT"] = np.ascontiguousarray(np.transpose(cw.reshape(DEPTH, 3, 6, 128), (0, 3, 2, 1)))
    ab = np.stack([f(inp["gdn_a_log"][:DEPTH]).reshape(DEPTH, 8), f(inp["gdn_dt_bias"][:DEPTH]).reshape(DEPTH, 8)], axis=1)
    w["gdn_ab"] = np.ascontiguousarray(np.broadcast_to(ab[:, None], (DEPTH, 64, 2, 8)))
    ng = f(inp["gdn_norm_g"][:DEPTH])
    w["gdn_ng"] = np.ascontiguousarray(np.concatenate([ng, ng], axis=1)[:, :, None])
    w["w_out"] = f(inp["w_out"][:DEPTH])
    w["w_ff1"] = f(inp["w_ff1"][:DEPTH])
    w["w_ff2"] = f(inp["w_ff2"][:DEPTH])
    return w


def host_core_inputs(inp, core, cfg):
    LS, LP, NP, DEPTH, PAST = cfg["LS"], cfg["LP"], cfg["NP"], cfg["DEPTH"], cfg["PAST"]
    f = lambda a: np.ascontiguousarray(np.asarray(a, dtype=np.float32))
    m = {}
    m["x_s"] = f(inp["x_sample"][core, :LS])
    m["x_p"] = f(inp["x_prompt"][core * NP:(core + 1) * NP, :LP]).reshape(NP * LP, D)
    c2 = np.stack([f(inp["c"][core]), f(inp["c_ctx"])], axis=0)
    m["cond2"] = np.ascontiguousarray(np.transpose(c2.reshape(2, KC, 128), (2, 0, 1)))
    m["cache_k"] = f(inp["cache_k"][core, :DEPTH, :PAST]).reshape(DEPTH, PAST, 128)
    m["cache_v"] = f(inp["cache_v"][core, :DEPTH, :PAST]).reshape(DEPTH, PAST, 128)
    ss = f(inp["state_ssm"][core, :DEPTH])
    h0 = np.zeros((DEPTH, 128, 16, 2), np.float32)
    for gp in range(8):
        for d in range(2):
            for g2 in range(2):
                h0[:, g2 * 64:(g2 + 1) * 64, gp * 2 + d, :] = np.transpose(ss[:, d, :, 2 * gp + g2, :], (0, 2, 1))
    m["ssm_h0"] = h0
    sg = f(inp["state_gdn"][core, :DEPTH])
    m["gdn_s0"] = np.ascontiguousarray(np.transpose(sg, (0, 3, 1, 2, 4)).reshape(DEPTH, 64, 8, 64))
    return m


def phase_c_weights(k, l, st):
    p, I = k.p, k.I
    wff1 = k.sbt(st, "wff1", [128, KC, 4 * D], BF16)
    wff2 = k.sbt(st, "wff2", [128, 32, D], BF16)
    s_ff1 = I["w_ff1"][l].rearrange("(c p) n -> p c n", p=128)
    s_ff2 = I["w_ff2"][l].rearrange("(c p) n -> p c n", p=128)
    for c in range(KC):
        for q4 in range(4):
            p.dma('pool', wff1[:, c, q4 * D:(q4 + 1) * D], s_ff1[:, c, q4 * D:(q4 + 1) * D], writes=[wff1])
    for c in range(32):
        p.dma('pool', wff2[:, c, :], s_ff2[:, c, :], writes=[wff2])
    return wff1, wff2


def phase_c(k, l, pre=None):
    nc, p, I, O = k.nc, k.p, k.I, k.O
    use_mix = k.cfg.get("use_mix", True)
    with contextlib.ExitStack() as st:
        sbt, pst = k.sbt, k.pst
        wff1, wff2 = pre if pre is not None else phase_c_weights(k, l, st)
        wout = sbt(st, "wout", [128, KC, D], BF16)
        s_out = I["w_out"][l].rearrange("(c p) n -> p c n", p=128)
        for c in range(KC):
            p.dma('pool', wout[:, c, :], s_out[:, c, :], writes=[wout])
        n2g = sbt(st, "n2g", [128, KC], F32)
        sc2 = sbt(st, "sc2", [128, KC, 2], F32)
        p.dma('sp', n2g[:], I["n2g"][l], writes=[n2g])
        p.do('dve', 'tensor_scalar', sc2[:], k.modT[:, l, 32:40, :], 1.0, None, ALU.add, reads=[k.modT], writes=[sc2])
        p.do('dve', 'tensor_tensor', sc2[:], sc2[:], bcast(n2g[:].unsqueeze(2), [128, KC, 2]), ALU.mult, reads=[sc2, n2g], writes=[sc2])
        gA1 = sbt(st, "gA", [128, D], F32)
        gM1 = sbt(st, "gM", [128, D], F32)
        gA, gM = [gA1, gA1], [gM1, gM1]
        gcol = Ring([sbt(st, "gcol%d" % i, [128, 128], F32) for i in range(1)])
        pgb = Ring([pst(st, "pgb%d" % i, [128, 512], F32) for i in range(2)])
        cur_cond = [None]

        def build_gates(cond):
            if cur_cond[0] == cond:
                return
            cur_cond[0] = cond
            for gi, (dst, base) in enumerate(((gA[cond], 16), (gM[cond], 40))):
                for c in range(KC):
                    gc_ = gcol.next()
                    pb = pgb.next()
                    p.do('dve', 'tensor_copy', gc_[:], bcast(k.modT[:, l, base + c, cond:cond + 1], [128, 128]), reads=[k.modT], writes=[gc_])
                    p.do('pe', 'matmul', pb[:, 0:128], gc_[:], k.ident[:], start=True, stop=True, reads=[gc_, k.ident], writes=[pb])
                    p.do('act', 'copy', dst[:, c * 128:(c + 1) * 128], pb[:, 0:128], reads=[pb], writes=[dst])
        xring = Ring([sbt(st, "xc%d" % i, [128, D], F32) for i in range(3)])
        x1r = Ring([sbt(st, "x1c%d" % i, [128, D], F32) for i in range(2)])
        tmpc = sbt(st, "tmpc", [128, D], F32)
        junk = tmpc
        ssr = Ring([sbt(st, "ssc%d" % i, [128, 1], F32) for i in range(2)])
        xnr = Ring([sbt(st, "xnc%d" % i, [128, D], BF16) for i in range(1)])
        mixr = Ring([sbt(st, "mixc%d" % i, [128, KC, 128], BF16) for i in range(2)])
        h2r = Ring([sbt(st, "h2c%d" % i, [128, KC, 128], BF16) for i in range(2)])
        aTr = Ring([sbt(st, "aTc%d" % i, [128, 32, 128], BF16) for i in range(2)])
        rr = Ring([sbt(st, "rc%d" % i, [128, 4, 128], BF16) for i in range(2)])
        pTr = Ring([pst(st, "pTc%d" % i, [128, KC, 128], BF16) for i in range(1)])
        por = Ring([pst(st, "poc%d" % i, [128, 512], F32) for i in range(3)])
        pfr = Ring([pst(st, "pfc%d" % i, [128, 4, 128], F32) for i in range(2)])
        p.barrier()
        items = [(s, t) for s in k.seqs for t in range(s["L"] // 128)]

        def prefetch(it):
            s_, t_ = it
            rows_ = slice(t_ * 128, (t_ + 1) * 128)
            xsrc_ = s_["xin"] if (l == 0) else s_["y"]
            xt_ = xring.next()
            p.dma('sp', xt_[:], xsrc_[rows_, :], writes=[xt_])
            mx_ = None
            if use_mix:
                mx_ = mixr.next()
                p.dma('sp', mx_[:], s_["mixT"][:, :, rows_].rearrange("c p t -> p c t"), writes=[mx_])
            return xt_, mx_

        def front(it, ld):
            s, t = it
            cond = s["cond"]
            xt, mx = ld
            build_gates(cond)
            x1 = x1r.next()
            if use_mix:
                for hh in range(2):
                    po = por.next()
                    for kc in range(KC):
                        p.do('pe', 'matmul', po[:], mx[:, kc, :], wout[:, kc, hh * 512:(hh + 1) * 512], start=(kc == 0), stop=(kc == KC - 1), reads=[mx, wout], writes=[po])
                    hs = slice(hh * 512, (hh + 1) * 512)
                    p.do('dve', 'tensor_tensor', x1[:, hs], po[:], gA[cond][:, hs], ALU.mult, reads=[po, gA[cond]], writes=[x1])
                    p.do('pool', 'tensor_tensor', x1[:, hs], x1[:, hs], xt[:, hs], ALU.add, reads=[x1, xt], writes=[x1])
            else:
                p.do('pool', 'tensor_copy', x1[:], xt[:], reads=[xt], writes=[x1])
            ss = ssr.next()
            xn = xnr.next()
            pT = pTr.next()
            h2 = h2r.next()
            p.do('act', 'activation', junk[:], x1[:], AF.Square, accum_out=ss[:], reads=[x1], writes=[junk, ss])
            p.do('act', 'activation', ss[:], ss[:], AF.Sqrt, scale=1.0 / D, bias=EPS, reads=[ss], writes=[ss])
            p.do('dve', 'reciprocal', ss[:], ss[:], reads=[ss], writes=[ss])
            p.do('dve', 'tensor_scalar', xn[:], x1[:], ss[:, 0:1], None, ALU.mult, reads=[x1, ss], writes=[xn])
            for c in range(KC):
                p.do('pe', 'transpose', pT[:, c, :], xn[:, c * 128:(c + 1) * 128], k.identb[:], reads=[xn, k.identb], writes=[pT])
            for c in range(KC):
                if c % 2 == 0:
                    p.do('act', 'activation', h2[:, c, :], pT[:, c, :], AF.Identity, scale=sc2[:, c, cond:cond + 1], bias=k.modT[:, l, 24 + c, cond:cond + 1], reads=[pT, sc2, k.modT], writes=[h2])
                else:
                    p.do('dve', 'tensor_scalar', h2[:, c, :], pT[:, c, :], sc2[:, c, cond:cond + 1], k.modT[:, l, 24 + c, cond:cond + 1], ALU.mult, ALU.add, reads=[pT, sc2, k.modT], writes=[h2])
            return dict(xt=xt, x1=x1, h2=h2)

        def back(it, C):
            s, t = it
            cond = s["cond"]
            rows = slice(t * 128, (t + 1) * 128)
            xt, x1, h2 = C["xt"], C["x1"], C["h2"]
            aT = aTr.next()
            for f4 in range(8):
                pf = pfr.next()
                r_ = rr.next()
                for j in range(4):
                    fc = f4 * 4 + j
                    for kc in range(KC):
                        p.do('pe', 'matmul', pf[:, j, :], wff1[:, kc, fc * 128:(fc + 1) * 128], h2[:, kc, :], start=(kc == 0), stop=(kc == KC - 1), reads=[wff1, h2], writes=[pf])
                p.do('act', 'activation', r_[:], pf[:], AF.Relu, reads=[pf], writes=[r_])
                p.do('pool', 'tensor_tensor', aT[:, f4 * 4:(f4 + 1) * 4, :], r_[:], r_[:], ALU.mult, reads=[r_], writes=[aT])
            for hh in range(2):
                po = por.next()
                for fc in range(32):
                    p.do('pe', 'matmul', po[:], aT[:, fc, :], wff2[:, fc, hh * 512:(hh + 1) * 512], start=(fc == 0), stop=(fc == 31), reads=[aT, wff2], writes=[po])
                hs = slice(hh * 512, (hh + 1) * 512)
                p.do('dve', 'tensor_tensor', xt[:, hs], po[:], gM[cond][:, hs], ALU.mult, reads=[po, gM[cond]], writes=[xt])
                p.do('dve', 'tensor_tensor', xt[:, hs], xt[:, hs], x1[:, hs], ALU.add, reads=[xt, x1], writes=[xt])
            p.dma('sp', s["y"][rows, :], xt[:], reads=[xt])

        n_it = len(items)
        loads = {0: prefetch(items[0])}
        if n_it > 1:
            loads[1] = prefetch(items[1])
        ctxs = {0: front(items[0], loads[0])}
        for ii in range(n_it):
            if ii + 2 < n_it:
                loads[ii + 2] = prefetch(items[ii + 2])
            same = ii + 1 < n_it and items[ii + 1][0]["cond"] == items[ii][0]["cond"]
            if ii + 1 < n_it and same:
                ctxs[ii + 1] = front(items[ii + 1], loads[ii + 1])
            back(items[ii], ctxs[ii])
            if ii + 1 < n_it and not same:
                ctxs[ii + 1] = front(items[ii + 1], loads[ii + 1])
        end_phase(k)


def phase_attn(k, l, s):
    nc, p, I, O = k.nc, k.p, k.I, k.O
    L, ctx = s["L"], s["ctx"]
    PAST = k.cfg["PAST"]
    nb = L // 128
    SC = 0.125
    with contextlib.ExitStack() as st:
        sbt, pst = k.sbt, k.pst
        kT = sbt(st, "akT", [128, 2, L], BF16)
        vsb = sbt(st, "avsb", [128, nb, 128], BF16)
        vdup = sbt(st, "avdup", [128, nb, 2, 128], BF16)
        esk = sbt(st, "aesk", [128, 8], F32)
        mkf = sbt(st, "amkf", [128, 2, 128], F32)
        mk = sbt(st, "amk", [128, 2, 128], BF16)
        p.dma('sp', kT[:], s["kT"].rearrange("g p t -> p g t"), writes=[kT])
        vsrc = s["v"].rearrange("(n p) f -> p n f", p=128)
        for n0 in range(0, nb, 4):
            n1 = min(nb, n0 + 4)
            p.dma('sp', vsb[:, n0:n1, :], vsrc[:, n0:n1, :], writes=[vsb])
        p.dma('sp', esk[:], I["sink_b"][l], writes=[esk])
        p.dma('sp', mkf[:], I["amask"][:, :, :], writes=[mkf])
        p.do('act', 'activation', esk[:], esk[:], AF.Exp, reads=[esk], writes=[esk])
        p.do('dve', 'tensor_copy', mk[:], mkf[:], reads=[mkf], writes=[mk])
        for g in range(2):
            for hf in range(2):
                p.do('dve' if hf == 0 else 'pool', 'tensor_copy', vdup[:, :, g, hf * 64:(hf + 1) * 64], vsb[:, :, g * 64:(g + 1) * 64], reads=[vsb], writes=[vdup])
        pst_ring = Ring([pst(st, "aST%d" % i, [128, 4, 128], F32) for i in range(2)])
        po_ring = Ring([pst(st, "aO%d" % i, [128, 4, 128], F32) for i in range(2)])
        ps_ring = Ring([pst(st, "aS%d" % i, [128, 4, 128], F32) for i in range(2)])
        nctx = 0
        if not ctx:
            nctx = PAST // 128
            ckf = sbt(st, "ackf", [128, nctx, 128], F32)
            cvf = sbt(st, "acvf", [128, nctx, 128], F32)
            ckd = sbt(st, "ackd", [128, nctx, 2, 128], BF16)
            cvd = sbt(st, "acvd", [128, nctx, 2, 128], BF16)
            ckT = sbt(st, "ackT", [128, 2, PAST], BF16)
            pck = pst(st, "apck", [128, 8, 128], BF16)
            p.dma('sp', ckf[:], I["cache_k"][l].rearrange("(n p) f -> p n f", p=128), writes=[ckf])
            p.dma('sp', cvf[:], I["cache_v"][l].rearrange("(n p) f -> p n f", p=128), writes=[cvf])
            for g in range(2):
                for hf in range(2):
                    p.do('dve', 'tensor_copy', ckd[:, :, g, hf * 64:(hf + 1) * 64], ckf[:, :, g * 64:(g + 1) * 64], reads=[ckf], writes=[ckd])
                    p.do('pool', 'tensor_copy', cvd[:, :, g, hf * 64:(hf + 1) * 64], cvf[:, :, g * 64:(g + 1) * 64], reads=[cvf], writes=[cvd])
            for j in range(nctx):
                for g in range(2):
                    p.do('pe', 'transpose', pck[:, j * 2 + g, :], ckd[:, j, g, :], k.identb[:], reads=[ckd, k.identb], writes=[pck])
            for j in range(nctx):
                for g in range(2):
                    p.do('act', 'copy', ckT[:, g, j * 128:(j + 1) * 128], pck[:, j * 2 + g, :], reads=[pck], writes=[ckT])
        qr = Ring([sbt(st, "aq%d" % i, [128, 4, 128], BF16) for i in range(2)])
        qzr = Ring([sbt(st, "aqz%d" % i, [128, 4, 2, 128], BF16) for i in range(2)])
        for qz_ in qzr.bufs:
            p.do('pool', 'memset', qz_[:], 0.0, writes=[qz_])
        er = Ring([sbt(st, "aE%d" % i, [128, 4, 128], BF16) for i in range(3)])
        den = sbt(st, "aden", [128, 4, 128], F32)
        outr = Ring([sbt(st, "aout%d" % i, [128, 4, 128], BF16) for i in range(2)])
        lvl = k.cfg.get('att_lvl', 9)
        for i in range(nb if lvl >= 2 else 0):
            cols = slice(i * 128, (i + 1) * 128)
            q = qr.next()
            ao = outr.next()
            p.dma('sp', q[:], s["qT"][:, :, cols].rearrange("c p t -> p c t"), writes=[q])
            qz = qzr.next()
            p.do('act', 'copy', qz[0:64, :, 0, :], q[0:64, :, :], reads=[q], writes=[qz])
            p.do('dve', 'tensor_copy', qz[64:128, :, 1, :], q[64:128, :, :], reads=[q], writes=[qz])
            for g in range(2):
                kbs = []
                if ctx:
                    for j in range(nb):
                        kbs.append((kT, j, vdup[:, j, g, :], None, vdup))
                else:
                    for j, mi in ((i - 1, 0), (i, None), (i + 1, 1)):
                        if 0 <= j < nb:
                            kbs.append((kT, j, vdup[:, j, g, :], mi, vdup))
                    for j in range(nctx):
                        kbs.append((ckT, j, cvd[:, j, g, :], None, cvd))
                po = po_ring.next()
                psm = ps_ring.next()
                pend = None
                for bi in range(len(kbs) + 1):
                    cur = None
                    if bi < len(kbs):
                        ksrc, j, vap, mi, vbuf = kbs[bi]
                        pS = pst_ring.next()
                        for hh in range(4):
                            h = 4 * g + hh
                            hf, c = h % 2, h // 2
                            p.do('pe', 'matmul', pS[:, hh, :], ksrc[:, g, j * 128:(j + 1) * 128], qz[:, c, hf, :], start=True, stop=True, reads=[ksrc, qz], writes=[pS])
                        E = er.next()
                        p.do('act', 'activation', E[:], pS[:], AF.Exp, scale=SC, reads=[pS], writes=[E])
                        if mi is not None:
                            p.do('dve', 'tensor_tensor', E[:], E[:], bcast(mk[:, mi:mi + 1, :], [128, 4, 128]), ALU.mult, reads=[E, mk], writes=[E])
                        cur = (E, vap, vbuf, bi)
                    if pend is not None:
                        E_, vap_, vbuf_, b_ = pend
                        first, last = b_ == 0, b_ == len(kbs) - 1
                        p.do('pe', 'matmul', po[:].rearrange("p a b -> p (a b)"), vap_, E_[:].rearrange("p a b -> p (a b)"), start=first, stop=last, reads=[vbuf_, E_], writes=[po])
                        p.do('pe', 'matmul', psm[:].rearrange("p a b -> p (a b)"), k.ones_b[:], E_[:].rearrange("p a b -> p (a b)"), start=first, stop=last, reads=[k.ones_b, E_], writes=[psm])
                    pend = cur
                if lvl < 5:
                    continue
                p.do('dve', 'tensor_tensor', den[:], psm[:], bcast(esk[:, 4 * g:4 * g + 4].unsqueeze(2), [128, 4, 128]), ALU.add, reads=[psm, esk], writes=[den])
                p.do('dve', 'reciprocal', den[:], den[:], reads=[den], writes=[den])
                for hf in range(2):
                    ps_ = slice(64 * hf, 64 * hf + 64)
                    p.do('dve', 'tensor_tensor', ao[ps_, 2 * g:2 * g + 2, :], po[ps_, hf::2, :], den[ps_, hf::2, :], ALU.mult, reads=[po, den], writes=[ao])
            if lvl >= 6:
                p.dma('sp', s["mixT"][0:4, :, cols].rearrange("c p t -> p c t"), ao[:], reads=[ao])
        end_phase(k)


def ssm_scan_levels(L):
    K_ = int(math.log2(L))
    ops = []
    for kk in range(K_):
        s_ = 2 ** (kk + 1)
        ops.append((kk, 2 ** kk - 1, s_ - 1, s_, L // s_))
    for kk in range(K_ - 2, -1, -1):
        s_ = 2 ** (kk + 1)
        n = L // s_ - 1
        if n > 0:
            ops.append((kk, s_ - 1, s_ + 2 ** kk - 1, s_, n))
    return ops


def phase_ssm(k, l, s):
    nc, p, I, O = k.nc, k.p, k.I, k.O
    L, ctx = s["L"], s["ctx"]
    TB = min(512, L)
    nblk = L // TB
    NLV = int(math.log2(L))
    with contextlib.ExitStack() as st:
        sbt, pst = k.sbt, k.pst
        lam = sbt(st, "slam", [128, 3, 16], F32)
        dtm = sbt(st, "sdt", [128, 16], F32)
        res_ = sbt(st, "sres", [128, 16], F32)
        ims = sbt(st, "sims", [128, 16], F32)
        t = [sbt(st, "st%d" % i, [128, 16], F32) for i in range(6)]
        A = sbt(st, "sA", [128, 16, 12, 2], F32)
        nAi = sbt(st, "snAi", [128, 16, 12], F32)
        fre = sbt(st, "sfre", [128, 16], F32)
        fim = sbt(st, "sfim", [128, 16], F32)
        nfim = sbt(st, "snfim", [128, 16], F32)
        p.dma('sp', lam[:], I["lam"][l], writes=[lam])
        p.do('act', 'activation', dtm[:], lam[:, 2, :], AF.Exp, reads=[lam], writes=[dtm])
        p.do('dve', 'tensor_tensor', res_[:], lam[:, 0, :], dtm[:], ALU.mult, reads=[lam, dtm], writes=[res_])
        p.do('dve', 'tensor_tensor', ims[:], lam[:, 1, :], dtm[:], ALU.mult, reads=[lam, dtm], writes=[ims])
        mag, s8, sh, c8, x_, y_ = t
        p.do('act', 'activation', mag[:], res_[:], AF.Exp, scale=0.125, reads=[res_], writes=[mag])
        p.do('act', 'activation', s8[:], ims[:], AF.Sin, scale=0.125, reads=[ims], writes=[s8])
        p.do('act', 'activation', sh[:], ims[:], AF.Sin, scale=0.0625, reads=[ims], writes=[sh])
        p.do('dve', 'tensor_tensor', c8[:], sh[:], sh[:], ALU.mult, reads=[sh], writes=[c8])
        p.do('dve', 'tensor_scalar', c8[:], c8[:], -2.0, 1.0, ALU.mult, ALU.add, reads=[c8], writes=[c8])
        p.do('dve', 'tensor_tensor', x_[:], mag[:], c8[:], ALU.mult, reads=[mag, c8], writes=[x_])
        p.do('dve', 'tensor_tensor', y_[:], mag[:], s8[:], ALU.mult, reads=[mag, s8], writes=[y_])
        xb, yb = Buf(x_.ap, "x"), Buf(y_.ap, "y")

        def csquare(dst_re, dst_im, src_re, src_im, bufs_r, bufs_w):
            p.do('dve', 'tensor_tensor', mag[:], src_re, src_re, ALU.mult, reads=bufs_r, writes=[mag])
            p.do('dve', 'tensor_tensor', s8[:], src_im, src_im, ALU.mult, reads=bufs_r, writes=[s8])
            p.do('dve', 'scalar_tensor_tensor', sh[:], src_re, 2.0, src_im, ALU.mult, ALU.mult, reads=bufs_r, writes=[sh])
            p.do('dve', 'tensor_tensor', dst_re, mag[:], s8[:], ALU.subtract, reads=[mag, s8], writes=bufs_w)
            p.do('dve', 'tensor_copy', dst_im, sh[:], reads=[sh], writes=bufs_w)

        for _ in range(2):
            csquare(x_[:], y_[:], x_[:], y_[:], [x_, y_], [x_, y_])
        csquare(A[:, :, 0, 0], A[:, :, 0, 1], x_[:], y_[:], [x_, y_], [A])
        for kk in range(1, 12):
            csquare(A[:, :, kk, 0], A[:, :, kk, 1], A[:, :, kk - 1, 0], A[:, :, kk - 1, 1], [A], [A])
        p.do('dve', 'tensor_scalar', nAi[:], A[:, :, :, 1], -1.0, None, ALU.mult, reads=[A], writes=[nAi])
        nr, den, rr_, q1 = t[0], t[1], t[2], t[3]
        p.do('dve', 'tensor_scalar', nr[:], A[:, :, 0, 0], -1.0, None, ALU.add, reads=[A], writes=[nr])
        p.do('dve', 'tensor_tensor', den[:], lam[:, 0, :], lam[:, 0, :], ALU.mult, reads=[lam], writes=[den])
        p.do('dve', 'tensor_tensor', q1[:], lam[:, 1, :], lam[:, 1, :], ALU.mult, reads=[lam], writes=[q1])
        p.do('dve', 'tensor_tensor', den[:], den[:], q1[:], ALU.add, reads=[den, q1], writes=[den])
        p.do('dve', 'reciprocal', den[:], den[:], reads=[den], writes=[den])
        p.do('dve', 'tensor_tensor', fre[:], nr[:], lam[:, 0, :], ALU.mult, reads=[nr, lam], writes=[fre])
        p.do('dve', 'tensor_tensor', q1[:], A[:, :, 0, 1], lam[:, 1, :], ALU.mult, reads=[A, lam], writes=[q1])
        p.do('dve', 'tensor_tensor', fre[:], fre[:], q1[:], ALU.add, reads=[fre, q1], writes=[fre])
        p.do('dve', 'tensor_tensor', fre[:], fre[:], den[:], ALU.mult, reads=[fre, den], writes=[fre])
        p.do('dve', 'tensor_tensor', fim[:], A[:, :, 0, 1], lam[:, 0, :], ALU.mult, reads=[A, lam], writes=[fim])
        p.do('dve', 'tensor_tensor', q1[:], nr[:], lam[:, 1, :], ALU.mult, reads=[nr, lam], writes=[q1])
        p.do('dve', 'tensor_tensor', fim[:], fim[:], q1[:], ALU.subtract, reads=[fim, q1], writes=[fim])
        p.do('dve', 'tensor_tensor', fim[:], fim[:], den[:], ALU.mult, reads=[fim, den], writes=[fim])
        p.do('dve', 'tensor_scalar', nfim[:], fim[:], -1.0, None, ALU.mult, reads=[fim], writes=[nfim])
        lvl = k.cfg.get('att_lvl', 9)
        if lvl < 2:
            end_phase(k)
            return
        BbT = sbt(st, "sBbT", [128, 16, 2, 128], BF16)
        Ct = sbt(st, "sCt", [128, 16, 2, 128], F32)
        p.dma('sp', Ct[:], I["ssm_ct"][l], writes=[Ct])
        p.do('dve', 'tensor_scalar', Ct[:, :, 1, :], Ct[:, :, 1, :], -1.0, None, ALU.mult, reads=[Ct], writes=[Ct])
        btr = Ring([sbt(st, "sbt%d" % i, [128, 2, 128], F32) for i in range(2)])
        bbr = Ring([sbt(st, "sbb%d" % i, [128, 2, 128], F32) for i in range(2)])
        ptr = Ring([pst(st, "sptr%d" % i, [128, 4, 128], F32) for i in range(2)])
        for inst in range(16):
            bt_, bb, pt = btr.next(), bbr.next(), ptr.next()
            p.dma('sp', bt_[:], I["ssm_bt"][l][:, inst, :, :], writes=[bt_])
            fr, fi, nfi = fre[:, inst:inst + 1], fim[:, inst:inst + 1], nfim[:, inst:inst + 1]
            p.do('dve', 'tensor_scalar', bb[:, 0, :], bt_[:, 0, :], fr, None, ALU.mult, reads=[bt_, fre], writes=[bb])
            p.do('dve', 'scalar_tensor_tensor', bb[:, 0, :], bt_[:, 1, :], nfi, bb[:, 0, :], ALU.mult, ALU.add, reads=[bt_, nfim, bb], writes=[bb])
            p.do('dve', 'tensor_scalar', bb[:, 1, :], bt_[:, 1, :], fr, None, ALU.mult, reads=[bt_, fre], writes=[bb])
            p.do('dve', 'scalar_tensor_tensor', bb[:, 1, :], bt_[:, 0, :], fi, bb[:, 1, :], ALU.mult, ALU.add, reads=[bt_, fim, bb], writes=[bb])
            for ri in range(2):
                p.do('pe', 'transpose', pt[:, ri, :], bb[:, ri, :], k.ident[:], reads=[bb, k.ident], writes=[pt])
            p.do('act', 'copy', BbT[:, inst, :, :], pt[:, 0:2, :], reads=[pt], writes=[BbT])
        if lvl < 3:
            end_phase(k)
            return
        uT = sbt(st, "suT", [128, 2, L], BF16)
        yT = sbt(st, "syT", [128, 2, L], F32)
        p.dma('sp', uT[:], s["fT"][0:2].rearrange("c p t -> p c t"), writes=[uT])
        Hr = Ring([sbt(st, "sH%d" % i, [128, 2, L], F32) for i in range(2)])
        pbr = Ring([pst(st, "spb%d" % i, [128, 512], F32) for i in range(3)])
        pyr = Ring([pst(st, "spy%d" % i, [128, 512], F32) for i in range(2)])
        if not ctx:
            h0 = sbt(st, "sh0", [128, 16, 2], F32)
            t1s = [sbt(st, "sht1%d" % i, [128, 2], F32) for i in range(2)]
            p.dma('sp', h0[:], I["ssm_h0"][l], writes=[h0])
        sched = ssm_scan_levels(L)
        Hparts = {id(H_): (Buf(None, 're'), Buf(None, 'im')) for H_ in Hr.bufs}
        for gp in range(8):
            chunk, q = gp // 4, gp % 4
            Hs = [Hr.next(), Hr.next()]
            for d in range(2):
                inst = gp * 2 + d
                H = Hs[d]
                for b in range(nblk):
                    blk = slice(b * TB, (b + 1) * TB)
                    for ri in range(2):
                        pb = pbr.next()
                        p.do('pe', 'matmul', pb[:, 0:TB], BbT[:, inst, ri, :], uT[:, chunk, blk], start=True, stop=True, reads=[BbT, uT], writes=[pb])
                        p.do('act', 'copy', H[:, ri, blk], pb[:, 0:TB], reads=[pb], writes=[H])
                if not ctx:
                    pos = 0 if d == 0 else L - 1
                    ar, ai, nai = A[:, inst, 0, 0:1], A[:, inst, 0, 1:2], nAi[:, inst, 0:1]
                    t1 = t1s[d]
                    p.do('dve', 'scalar_tensor_tensor', t1[:, 0:1], h0[:, inst, 0:1], ar, H[:, 0, pos:pos + 1], ALU.mult, ALU.add, reads=[h0, A, H], writes=[t1])
                    p.do('dve', 'scalar_tensor_tensor', t1[:, 1:2], h0[:, inst, 1:2], ar, H[:, 1, pos:pos + 1], ALU.mult, ALU.add, reads=[h0, A, H], writes=[t1])
                    p.do('dve', 'scalar_tensor_tensor', H[:, 0, pos:pos + 1], h0[:, inst, 1:2], nai, t1[:, 0:1], ALU.mult, ALU.add, reads=[h0, nAi, t1], writes=[H])
                    p.do('dve', 'scalar_tensor_tensor', H[:, 1, pos:pos + 1], h0[:, inst, 0:1], ai, t1[:, 1:2], ALU.mult, ALU.add, reads=[h0, A, t1], writes=[H])
            for (kk, rr0, rw0, sd, cnt) in (sched if lvl >= 4 else []):
                for d in range(2):
                    inst = gp * 2 + d
                    H = Hs[d]
                    if d == 0:
                        rs = slice(rr0, rr0 + (cnt - 1) * sd + 1, sd)
                        ws = slice(rw0, rw0 + (cnt - 1) * sd + 1, sd)
                    else:
                        a_r = L - 1 - (rr0 + (cnt - 1) * sd)
                        a_w = L - 1 - (rw0 + (cnt - 1) * sd)
                        rs = slice(a_r, a_r + (cnt - 1) * sd + 1, sd)
                        ws = slice(a_w, a_w + (cnt - 1) * sd + 1, sd)
                    ar, ai, nai = A[:, inst, kk, 0:1], A[:, inst, kk, 1:2], nAi[:, inst, kk:kk + 1]
                    Hre, Him = Hparts[id(H)]
                    p.do('dve', 'scalar_tensor_tensor', H[:, :, ws], H[:, :, rs], ar, H[:, :, ws], ALU.mult, ALU.add, reads=[H, Hre, Him, A], writes=[Hre, Him])
                    p.do('dve', 'scalar_tensor_tensor', H[:, 0, ws], H[:, 1, rs], nai, H[:, 0, ws], ALU.mult, ALU.add, reads=[H, Hre, Him, nAi], writes=[Hre])
                    p.do('dve', 'scalar_tensor_tensor', H[:, 1, ws], H[:, 0, rs], ai, H[:, 1, ws], ALU.mult, ALU.add, reads=[H, Him, Hre, A], writes=[Him])
            if lvl < 5:
                continue
            if ctx:
                for d in range(2):
                    pos = L - 1 if d == 0 else 0
                    for ri in range(2):
                        dst = O["nssm"][s["pi"], l, d, ri, 2 * gp:2 * gp + 2, :].rearrange("g (p o) -> (g p) o", o=1)
                        p.dma('sp', dst, Hs[d][:, ri, pos:pos + 1], reads=[Hs[d]] + list(Hparts[id(Hs[d])]))
            for b in range(nblk):
                blk = slice(b * TB, (b + 1) * TB)
                py = pyr.next()
                n_ = 0
                for d in range(2):
                    inst = gp * 2 + d
                    for ri in range(2):
                        p.do('pe', 'matmul', py[:, 0:TB], Ct[:, inst, ri, :], Hs[d][:, ri, blk], start=(n_ == 0), stop=(n_ == 3), reads=[Ct, Hs[d]] + list(Hparts[id(Hs[d])]), writes=[py])
                        n_ += 1
                if q == 0:
                    p.do('act', 'copy', yT[:, chunk, blk], py[:, 0:TB], reads=[py], writes=[yT])
                else:
                    p.do('dve', 'tensor_tensor', yT[:, chunk, blk], py[:, 0:TB], yT[:, chunk, blk], ALU.add, reads=[py, yT], writes=[yT])
        if lvl < 6:
            end_phase(k)
            return
        dT = sbt(st, "sdT", [128, 2], F32)
        gb = sbt(st, "sgb", [128, 2], F32)
        wg = sbt(st, "swg", [128, 2, 256], BF16)
        p.dma('sp', dT[:], I["ssm_dT"][l], writes=[dT])
        p.dma('sp', gb[:], I["glu_bT"][l], writes=[gb])
        p.dma('pool', wg[:], I["w_glu"][l].rearrange("(c p) n -> p c n", p=128), writes=[wg])
        zT = sbt(st, "szT", [128, 2, L], BF16)
        ytr = Ring([sbt(st, "syt%d" % i, [128, 512], F32) for i in range(2)])
        u1r = Ring([sbt(st, "su1%d" % i, [128, 512], F32) for i in range(2)])
        sgr = Ring([sbt(st, "ssg%d" % i, [128, 512], F32) for i in range(2)])
        outr = Ring([sbt(st, "sout%d" % i, [128, 512], BF16) for i in range(2)])
        for b in range(nblk):
            blk = slice(b * TB, (b + 1) * TB)
            for c in range(2):
                yt, u1, sg = ytr.next(), u1r.next(), sgr.next()
                p.do('dve', 'scalar_tensor_tensor', yt[:, 0:TB], uT[:, c, blk], dT[:, c:c + 1], yT[:, c, blk], ALU.mult, ALU.add, reads=[uT, dT, yT], writes=[yt])
                p.do('pool', 'tensor_tensor', u1[:, 0:TB], yt[:, 0:TB], yt[:, 0:TB], ALU.mult, reads=[yt], writes=[u1])
                p.do('pool', 'tensor_scalar', u1[:, 0:TB], u1[:, 0:TB], 0.044715, 1.0, ALU.mult, ALU.add, reads=[u1], writes=[u1])
                p.do('pool', 'tensor_tensor', u1[:, 0:TB], u1[:, 0:TB], yt[:, 0:TB], ALU.mult, reads=[u1, yt], writes=[u1])
                p.do('act', 'activation', sg[:, 0:TB], u1[:, 0:TB], AF.Sigmoid, scale=1.5957691216057308, reads=[u1], writes=[sg])
                p.do('dve', 'tensor_tensor', zT[:, c, blk], sg[:, 0:TB], yt[:, 0:TB], ALU.mult, reads=[sg, yt], writes=[zT])
            for mo in range(2):
                pg = pbr.next()
                sg = sgr.next()
                ot = outr.next()
                for kc in range(2):
                    p.do('pe', 'matmul', pg[:, 0:TB], wg[:, kc, mo * 128:(mo + 1) * 128], zT[:, kc, blk], start=(kc == 0), stop=(kc == 1), reads=[wg, zT], writes=[pg])
                p.do('act', 'activation', sg[:, 0:TB], pg[:, 0:TB], AF.Sigmoid, bias=gb[:, mo:mo + 1], reads=[pg, gb], writes=[sg])
                p.do('dve', 'tensor_tensor', ot[:, 0:TB], sg[:, 0:TB], zT[:, mo, blk], ALU.mult, reads=[sg, zT], writes=[ot])
                p.dma('sp', s["mixT"][4 + mo, :, blk], ot[:, 0:TB], reads=[ot])
        end_phase(k)


def phase_gdn(k, l, s):
    nc, p, I, O = k.nc, k.p, k.I, k.O
    L, ctx = s["L"], s["ctx"]
    nC = L // 64
    TB = min(512, L)
    with contextlib.ExitStack() as st:
        sbt, pst = k.sbt, k.pst
        qkv = sbt(st, "gqkv", [128, 6, L], BF16)
        kz = sbt(st, "gkz", [128, 4, L], BF16)
        bank = Ring([pst(st, "gbank%d" % i, [128, 512], F32) for i in range(7)])
        pbf = pst(st, "gpbf", [128, 1024], BF16)
        st1 = contextlib.ExitStack()
        xp = sbt(st1, "gxp", [128, 6, L + 2], BF16)
        cw = sbt(st1, "gcw", [128, 6, 3], F32)
        bo = sbt(st1, "gbo", [128, 128], BF16)
        p.dma('sp', cw[:], I["conv_wT"][l], writes=[cw])
        p.do('pool', 'memset', xp[:, :, 0:1], 0.0, writes=[xp])
        p.do('pool', 'memset', xp[:, :, L + 1:L + 2], 0.0, writes=[xp])
        for c in range(6):
            p.dma('sp', xp[:, c, 1:L + 1], s["fT"][2 + c, :, :], writes=[xp])
        p.do('pool', 'memset', bo[:], 0.0, writes=[bo])
        p.do('pool', 'memset', bo[0:64, 0:64], 1.0, writes=[bo])
        p.do('pool', 'memset', bo[64:128, 64:128], 1.0, writes=[bo])
        accr = Ring([sbt(st1, "gacc%d" % i, [128, 512], F32) for i in range(2)])
        silr = Ring([sbt(st1, "gsil%d" % i, [128, 512], F32) for i in range(2)])
        sqr = Ring([sbt(st1, "gsq%d" % i, [128, 512], BF16) for i in range(2)])
        rnr = Ring([sbt(st1, "grn%d" % i, [128, 512], F32) for i in range(2)])
        for c in range(6):
            for b in range(L // TB):
                cs = b * TB
                acc, sil = accr.next(), silr.next()
                p.do('dve', 'tensor_scalar', acc[:, 0:TB], xp[:, c, cs:cs + TB], cw[:, c, 0:1], None, ALU.mult, reads=[xp, cw], writes=[acc])
                p.do('dve', 'scalar_tensor_tensor', acc[:, 0:TB], xp[:, c, cs + 1:cs + 1 + TB], cw[:, c, 1:2], acc[:, 0:TB], ALU.mult, ALU.add, reads=[xp, cw, acc], writes=[acc])
                p.do('dve', 'scalar_tensor_tensor', acc[:, 0:TB], xp[:, c, cs + 2:cs + 2 + TB], cw[:, c, 2:3], acc[:, 0:TB], ALU.mult, ALU.add, reads=[xp, cw, acc], writes=[acc])
                if c >= 4:
                    p.do('act', 'activation', qkv[:, c, cs:cs + TB], acc[:, 0:TB], AF.Silu, reads=[acc], writes=[qkv])
                    continue
                sq, rn, pb = sqr.next(), rnr.next(), bank.next()
                p.do('act', 'activation', sil[:, 0:TB], acc[:, 0:TB], AF.Silu, reads=[acc], writes=[sil])
                p.do('pool', 'tensor_tensor', sq[:, 0:TB], sil[:, 0:TB], sil[:, 0:TB], ALU.mult, reads=[sil], writes=[sq])
                p.do('pe', 'matmul', pb[:, 0:TB], bo[:], sq[:, 0:TB], start=True, stop=True, reads=[bo, sq], writes=[pb])
                p.do('act', 'activation', rn[:, 0:TB], pb[:, 0:TB], AF.Sqrt, bias=EPS, reads=[pb], writes=[rn])
                p.do('dve', 'reciprocal', rn[:, 0:TB], rn[:, 0:TB], reads=[rn], writes=[rn])
                if c < 2:
                    p.do('dve', 'scalar_tensor_tensor', qkv[:, c, cs:cs + TB], sil[:, 0:TB], 0.125, rn[:, 0:TB], ALU.mult, ALU.mult, reads=[sil, rn], writes=[qkv])
                else:
                    p.do('dve', 'tensor_tensor', qkv[:, c, cs:cs + TB], sil[:, 0:TB], rn[:, 0:TB], ALU.mult, reads=[sil, rn], writes=[qkv])
        p.do('pool', 'memset', kz[:], 0.0, writes=[kz])
        for h_ in range(4):
            hs_ = slice(64 * (h_ % 2), 64 * (h_ % 2) + 64)
            p.do('dve' if h_ % 2 == 0 else 'pool', 'tensor_copy', kz[hs_, h_, :], qkv[hs_, 2 + h_ // 2, :], reads=[qkv, kz], writes=[kz])
        p.barrier()
        p.flush()
        st1.close()
        st2 = contextlib.ExitStack()
        gt = sbt(st2, "ggt", [64, nC, 16], F32)
        ab = sbt(st2, "gab", [64, 2, 8], F32)
        gm = sbt(st2, "ggm", [64, 6, 8, 64], F32)
        gtri = sbt(st2, "ggtri", [64, 4, 64], F32)
        p.dma('sp', gt[:], s["gates"][:, :, :], writes=[gt])
        p.dma('sp', ab[:], I["gdn_ab"][l], writes=[ab])
        p.dma('sp', gm[:], I["gmask"][:, :, :, :], writes=[gm])
        p.dma('sp', gtri[:], I["gtri"][:, :, :], writes=[gtri])
        names = ["gG", "gBeta", "gGs", "gBetas", "gGc", "gEgc", "gBg", "gGl", "gEgl", "gKd", "gT0"]
        T_ = {n: sbt(st2, n, [64, nC, 8], F32) for n in names}
        g_, be_, gS, beS, gcS, egcS, bgS, glS, eglS, kdS, t0 = [T_[n] for n in names]
        Aexp = sbt(st2, "gAexp", [64, 8], F32)
        p.do('act', 'activation', Aexp[:], ab[:, 0, :], AF.Exp, reads=[ab], writes=[Aexp])
        p.do('dve', 'tensor_tensor', t0[:], gt[:, :, 0:8], bcast(ab[:, 1:2, :], [64, nC, 8]), ALU.add, reads=[gt, ab], writes=[t0])
        p.do('act', 'activation', t0[:], t0[:], AF.Exp, reads=[t0], writes=[t0])
        p.do('act', 'activation', t0[:], t0[:], AF.Ln, bias=1.0, reads=[t0], writes=[t0])
        p.do('dve', 'tensor_tensor', g_[:], t0[:], bcast(Aexp[:].unsqueeze(1), [64, nC, 8]), ALU.mult, reads=[t0, Aexp], writes=[g_])
        p.do('dve', 'tensor_scalar', g_[:], g_[:], -1.0, None, ALU.mult, reads=[g_], writes=[g_])
        p.do('act', 'activation', be_[:], gt[:, :, 8:16], AF.Sigmoid, reads=[gt], writes=[be_])
        p.do('pool', 'tensor_copy', gS[:, :, 0:4], g_[:, :, 0:4], reads=[g_], writes=[gS])
        p.do('pool', 'tensor_copy', beS[:, :, 0:4], be_[:, :, 0:4], reads=[be_], writes=[beS])
        for sidx in range(nC):
            cb = nC - 1 - sidx
            p.do('pool', 'tensor_copy', gS[:, sidx, 4:8], g_[:, cb, 4:8], reads=[g_], writes=[gS])
            p.do('pool', 'tensor_copy', beS[:, sidx, 4:8], be_[:, cb, 4:8], reads=[be_], writes=[beS])
        NG = nC * 8
        def dir_matmul(dst, src, i_f, i_b):
            pa, pb_ = bank.next(), bank.next()
            flat = src[:].rearrange("p c e -> p (c e)")
            p.do('pe', 'matmul', pa[0:64, 0:NG], gtri[:, i_f, :], flat, start=True, stop=True, reads=[gtri, src], writes=[pa])
            p.do('pe', 'matmul', pb_[0:64, 0:NG], gtri[:, i_b, :], flat, start=True, stop=True, reads=[gtri, src], writes=[pb_])
            p.do('dve', 'tensor_copy', dst[:, :, 0:4], pa[0:64, 0:NG].rearrange("p (c e) -> p c e", e=8)[:, :, 0:4], reads=[pa], writes=[dst])
            p.do('dve', 'tensor_copy', dst[:, :, 4:8], pb_[0:64, 0:NG].rearrange("p (c e) -> p c e", e=8)[:, :, 4:8], reads=[pb_], writes=[dst])

        dir_matmul(gcS, gS, 0, 1)
        p.do('act', 'activation', egcS[:], gcS[:], AF.Exp, reads=[gcS], writes=[egcS])
        p.do('dve', 'tensor_tensor', bgS[:], beS[:], egcS[:], ALU.mult, reads=[beS, egcS], writes=[bgS])
        dir_matmul(glS, gcS, 2, 3)
        p.do('act', 'activation', eglS[:], glS[:], AF.Exp, reads=[glS], writes=[eglS])
        p.do('dve', 'tensor_tensor', kdS[:], glS[:], gcS[:], ALU.subtract, reads=[glS, gcS], writes=[kdS])
        p.do('act', 'activation', kdS[:], kdS[:], AF.Exp, reads=[kdS], writes=[kdS])
        use_r = k.cfg.get('fp32r', False)

        off = k.cfg.get('r32_off', '')

        def r32(ap):
            return ap.bitcast(mybir.dt.float32r) if (use_r and 'u' not in off) else ap

        def r32l(ap):
            return ap.bitcast(mybir.dt.float32r) if (use_r and 'l' not in off) else ap

        def r32s(ap):
            return ap.bitcast(mybir.dt.float32r) if (use_r and 's' not in off) else ap

        S = sbt(st2, "gS", [64, 8, 64], F32)
        Sb = sbt(st2, "gSb", [64, 8, 64], BF16)
        S0t = sbt(st2, "gS0t", [64, 8, 64], F32)
        if ctx:
            p.do('pool', 'memset', S0t[:], 0.0, writes=[S0t])
        else:
            p.dma('sp', S0t[:], I["gdn_s0"][l], writes=[S0t])
        p.do('dve', 'tensor_copy', r32s(S[:]), S0t[:], reads=[S0t], writes=[S])
        p.do('pool', 'tensor_copy', Sb[:], S[:], reads=[S], writes=[Sb])

        small = L <= 256

        def T3(name, dt=F32, n=2):
            if small:
                n = 2 if n == 4 or name == "gTm" else 1
            return Ring([sbt(st2, "%s%d" % (name, i), [64, 8, 64], dt) for i in range(n)])

        dgR, XR, E1R, E2R = T3("gdg", n=4), T3("gX"), T3("gE1"), T3("gE2")
        NR, NtR, AtR, AccR = T3("gN", n=4), T3("gNt", n=4), T3("gAt"), T3("gAcc")
        VbR, RR, KdR = T3("gVb"), T3("gR"), T3("gKd_")
        UR, WTR, VnR, OR, TmR = T3("gU"), T3("gWT"), T3("gVn"), T3("gO"), T3("gTm")
        I8 = gm[:, 4, :, :]
        qodR = Ring([sbt(st2, "gqod%d" % i, [64, 2, 2, 64], BF16) for i in range(2)])

        def bview(b):
            return b[0:64, :].rearrange("p (e j) -> p e j", e=8)

        def colb(tab, sidx):
            return bcast(tab[:, sidx, :].unsqueeze(2), [64, 8, 64])

        def stage1(sidx, C):
            ce = [sidx if e < 4 else nC - 1 - sidx for e in range(8)]
            tk = [slice(ce[e] * 64, ce[e] * 64 + 64) for e in range(8)]
            qT = [qkv[64 * (e % 4 % 2):64 * (e % 4 % 2) + 64, 0 + (e % 4) // 2, tk[e]] for e in range(8)]
            pt4 = pbf[0:64, :].rearrange("p (a j) -> p a j", a=8)
            ptv = pbf[0:64, :].rearrange("p (a j) -> p a j", a=16)
            for kind in range(2):
                for d_ in range(2):
                    for c_ in range(2):
                        e0 = d_ * 4 + c_ * 2
                        p.do('pe', 'transpose', pt4[:, kind * 4 + d_ * 2 + c_, :], qkv[:, 2 + 2 * kind + c_, tk[e0]], k.identb[:], reads=[qkv, k.identb], writes=[pbf])
            Vb, R_, Kd = VbR.next(), RR.next(), KdR.next()
            p.do('dve', 'tensor_tensor', r32(Vb[:]), ptv[:, 8:16, :], colb(beS, sidx), ALU.mult, reads=[pbf, beS], writes=[Vb])
            p.do('dve', 'tensor_tensor', r32(R_[:]), ptv[:, 0:8, :], colb(bgS, sidx), ALU.mult, reads=[pbf, bgS], writes=[R_])
            p.do('dve', 'tensor_tensor', r32s(Kd[:]), ptv[:, 0:8, :], colb(kdS, sidx), ALU.mult, reads=[pbf, kdS], writes=[Kd])
            yield
            dg, X, E1, E2 = dgR.next(), XR.next(), E1R.next(), E2R.next()
            PGb, PBb = bank.next(), bank.next()
            PG, PB = bview(PGb), bview(PBb)
            p.do('pool', 'tensor_tensor', dg[:], I8, colb(gcS, sidx), ALU.mult, reads=[gm, gcS], writes=[dg])
            for e in range(8):
                p.do('pe', 'matmul', PG[:, e, :], k.ones_f[0:64, 0:64], dg[:, e, :], start=True, stop=True, reads=[k.ones_f, dg], writes=[PGb])
            p.do('dve', 'tensor_tensor', X[:], PG, colb(gcS, sidx), ALU.subtract, reads=[PGb, gcS], writes=[X])
            yield
            p.do('act', 'activation', E1[:], X[:], AF.Relu, reads=[X], writes=[E1])
            p.do('act', 'activation', E2[:], X[:], AF.Relu, scale=-1.0, reads=[X], writes=[E2])
            p.do('act', 'activation', E1[:], E1[:], AF.Exp, scale=-1.0, reads=[E1], writes=[E1])
            p.do('act', 'activation', E2[:], E2[:], AF.Exp, scale=-1.0, reads=[E2], writes=[E2])
            yield
            A1b, A2b = bank.next(), bank.next()
            A1, A2 = bview(A1b), bview(A2b)
            for e in range(8):
                h_ = e % 4
                p.do('pe', 'matmul', A1[:, e, :], kz[:, h_, tk[e]], qkv[:, 2 + h_ // 2, tk[e]], start=True, stop=True, reads=[kz, qkv], writes=[A1b])
            for e in range(8):
                h_ = e % 4
                p.do('pe', 'matmul', A2[:, e, :], kz[:, h_, tk[e]], qkv[:, 0 + h_ // 2, tk[e]], start=True, stop=True, reads=[kz, qkv], writes=[A2b])
            Ds, Dts, Dti = E1, X, E2
            p.do('dve', 'tensor_tensor', Ds[:], E1[:], gm[:, 0, :, :], ALU.mult, reads=[E1, gm], writes=[Ds])
            p.do('dve', 'tensor_tensor', Dts[:], E2[:], gm[:, 2, :, :], ALU.mult, reads=[E2, gm], writes=[Dts])
            p.do('dve', 'tensor_tensor', Dti[:], E2[:], gm[:, 3, :, :], ALU.mult, reads=[E2, gm], writes=[Dti])
            dg2 = dgR.next()
            p.do('pool', 'tensor_tensor', dg2[:], I8, colb(beS, sidx), ALU.mult, reads=[gm, beS], writes=[dg2])
            for e in range(8):
                p.do('pe', 'matmul', PB[:, e, :], k.ones_f[0:64, 0:64], dg2[:, e, :], start=True, stop=True, reads=[k.ones_f, dg2], writes=[PBb])
            N_, Nt, At, Acc = NR.next(), NtR.next(), AtR.next(), AccR.next()
            p.do('dve', 'tensor_tensor', r32l(N_[:]), A1, Ds[:], ALU.mult, reads=[A1b, Ds], writes=[N_])
            p.do('dve', 'tensor_tensor', r32l(N_[:]), N_[:], colb(beS, sidx), ALU.mult, reads=[N_, beS], writes=[N_])
            p.do('dve', 'tensor_tensor', r32l(Nt[:]), A1, Dts[:], ALU.mult, reads=[A1b, Dts], writes=[Nt])
            p.do('dve', 'tensor_tensor', r32l(Nt[:]), PB, Nt[:], ALU.mult, reads=[PBb, Nt], writes=[Nt])
            p.do('dve', 'tensor_tensor', r32s(At[:]), A2, Dti[:], ALU.mult, reads=[A2b, Dti], writes=[At])
            p.do('dve', 'scalar_tensor_tensor', r32(Acc[:]), Nt[:], -1.0, I8, ALU.mult, ALU.add, reads=[Nt, gm], writes=[Acc])
            yield
            P_, Pt = N_, Nt
            for lv in range(5):
                PPa, PPb, PAb = bank.next(), bank.next(), bank.next()
                Pn, Ptn = NR.next(), NtR.next()
                for e in range(8):
                    p.do('pe', 'matmul', bview(PPa)[:, e, :], r32l(Pt[:, e, :]), r32l(P_[:, e, :]), start=True, stop=True, reads=[Pt, P_], writes=[PPa])
                for e in range(8):
                    p.do('pe', 'matmul', bview(PPb)[:, e, :], r32l(P_[:, e, :]), r32l(Pt[:, e, :]), start=True, stop=True, reads=[Pt, P_], writes=[PPb])
                p.do('act', 'copy', r32l(Pn[:]), bview(PPa), reads=[PPa], writes=[Pn])
                p.do('act', 'copy', r32l(Ptn[:]), bview(PPb), reads=[PPb], writes=[Ptn])
                yield
                for e in range(8):
                    p.do('pe', 'matmul', bview(PAb)[:, e, :], r32(Pn[:, e, :]), r32(Acc[:, e, :]), start=True, stop=True, reads=[Pn, Acc], writes=[PAb])
                p.do('dve', 'tensor_tensor', r32(Acc[:]), Acc[:], bview(PAb), ALU.add, reads=[Acc, PAb], writes=[Acc])
                P_, Pt = Pn, Ptn
                yield
            PUb, PWb = bank.next(), bank.next()
            U_, WT = UR.next(), WTR.next()
            for e in range(8):
                p.do('pe', 'matmul', bview(PUb)[:, e, :], r32(Acc[:, e, :]), r32(Vb[:, e, :]), start=True, stop=True, reads=[Acc, Vb], writes=[PUb])
            for e in range(8):
                p.do('pe', 'matmul', bview(PWb)[:, e, :], r32(R_[:, e, :]), r32(Acc[:, e, :]), start=True, stop=True, reads=[Acc, R_], writes=[PWb])
            p.do('act', 'copy', U_[:], bview(PUb), reads=[PUb], writes=[U_])
            p.do('act', 'copy', r32s(WT[:]), bview(PWb), reads=[PWb], writes=[WT])
            C.update(dict(U_=U_, WT=WT, At=At, Kd=Kd, qT=qT))
            return

        def stage2(sidx, C):
            U_, WT, At, Kd, qT = C['U_'], C['WT'], C['At'], C['Kd'], C['qT']
            PWSb, POb, PO2b, PKVb = bank.next(), bank.next(), bank.next(), bank.next()
            Vn, Oo, Tm = VnR.next(), OR.next(), TmR.next()
            for e in range(8):
                p.do('pe', 'matmul', bview(PWSb)[:, e, :], r32s(WT[:, e, :]), r32s(S[:, e, :]), start=True, stop=True, reads=[WT, S], writes=[PWSb])
            p.do('dve', 'tensor_tensor', r32s(Vn[:]), U_[:], bview(PWSb), ALU.subtract, reads=[U_, PWSb], writes=[Vn])
            qod = qodR.next()
            for d_ in range(2):
                cc = sidx if d_ == 0 else nC - 1 - sidx
                p.do('act', 'copy', qod[:, d_, :, :], qkv[64:128, 0:2, cc * 64:(cc + 1) * 64], reads=[qkv], writes=[qod])
            for e in range(8):
                h_ = e % 4
                lq = qT[e] if h_ % 2 == 0 else qod[:, e // 4, h_ // 2, :]
                p.do('pe', 'matmul', bview(POb)[:, e, :], lq, Sb[:, e, :], start=True, stop=True, reads=[qkv, qod, Sb], writes=[POb])
            for e in range(8):
                p.do('pe', 'matmul', bview(PO2b)[:, e, :], r32s(At[:, e, :]), r32s(Vn[:, e, :]), start=True, stop=True, reads=[At, Vn], writes=[PO2b])
            for e in range(8):
                p.do('pe', 'matmul', bview(PKVb)[:, e, :], r32s(Kd[:, e, :]), r32s(Vn[:, e, :]), start=True, stop=True, reads=[Kd, Vn], writes=[PKVb])
            p.do('dve', 'tensor_tensor', Tm[:], bview(POb), colb(egcS, sidx), ALU.mult, reads=[POb, egcS], writes=[Tm])
            p.do('dve', 'tensor_tensor', Oo[:], Tm[:], bview(PO2b), ALU.add, reads=[Tm, PO2b], writes=[Oo])
            cf, cb = sidx, nC - 1 - sidx
            p.dma('sp', s["go"][0, cf * 64:(cf + 1) * 64, :].rearrange("t (h v) -> t h v", h=4), Oo[:, 0:4, :], reads=[Oo])
            p.dma('sp', s["go"][1, cb * 64:(cb + 1) * 64, :].rearrange("t (h v) -> t h v", h=4), Oo[:, 4:8, :], reads=[Oo])
            Tm2 = TmR.next()
            p.do('pool', 'tensor_tensor', Tm2[:], S[:], colb(eglS, sidx), ALU.mult, reads=[S, eglS], writes=[Tm2])
            p.do('dve', 'tensor_tensor', r32s(S[:]), Tm2[:], bview(PKVb), ALU.add, reads=[Tm2, PKVb], writes=[S])
            p.do('act', 'copy', Sb[:], S[:], reads=[S], writes=[Sb])

        GG = 1 if small else k.cfg.get("gdn_group", 2)
        for s0_ in range(0, nC, GG):
            grp = list(range(s0_, min(nC, s0_ + GG)))
            ctxs = {si: {} for si in grp}
            gens = [stage1(si, ctxs[si]) for si in grp]
            alive = list(gens)
            while alive:
                nxt = []
                for g_it in alive:
                    try:
                        next(g_it)
                        nxt.append(g_it)
                    except StopIteration:
                        pass
                alive = nxt
            for si in grp:
                stage2(si, ctxs[si])
        if ctx:
            p.dma('sp', O["ngdn"][s["pi"], l].rearrange("d h k v -> k (d h) v"), S[:], reads=[S])
        p.barrier()
        p.flush()
        st2.close()
        gz = sbt(st, "ggz", [128, 2, L], BF16)
        ng = sbt(st, "gng", [128, 1], F32)
        p.dma('sp', gz[:], s["fT"][8:10].rearrange("c p t -> p c t"), writes=[gz])
        p.dma('sp', ng[:], I["gdn_ng"][l], writes=[ng])
        p.do('act', 'activation', gz[:], gz[:], AF.Silu, reads=[gz], writes=[gz])
        o0r = Ring([sbt(st, "go0%d" % i, [128, 4, 64], F32) for i in range(2)])
        o1r = Ring([sbt(st, "go1%d" % i, [128, 4, 64], F32) for i in range(2)])
        sqo = sbt(st, "gsqo", [128, 4, 64], F32)
        ss4 = sbt(st, "gss4", [128, 4], F32)
        onr = Ring([sbt(st, "gon%d" % i, [128, 4, 64], BF16) for i in range(2)])
        outr = Ring([sbt(st, "gout%d" % i, [128, 2, 128], BF16) for i in range(2)])
        for t in range(L // 128):
            rows = slice(t * 128, (t + 1) * 128)
            o0, o1, on, ot = o0r.next(), o1r.next(), onr.next(), outr.next()
            p.dma('sp', o0[:], s["go"][0, rows, :].rearrange("t (h v) -> t h v", h=4), writes=[o0])
            p.dma('sp', o1[:], s["go"][1, rows, :].rearrange("t (h v) -> t h v", h=4), writes=[o1])
            p.do('pool', 'tensor_tensor', o0[:], o0[:], o1[:], ALU.add, reads=[o0, o1], writes=[o0])
            p.do('act', 'activation', sqo[:], o0[:], AF.Square, reads=[o0], writes=[sqo])
            p.do('dve', 'tensor_reduce', ss4[:], sqo[:], AX.X, ALU.add, reads=[sqo], writes=[ss4])
            p.do('act', 'activation', ss4[:], ss4[:], AF.Sqrt, scale=1.0 / 64, bias=EPS, reads=[ss4], writes=[ss4])
            p.do('dve', 'reciprocal', ss4[:], ss4[:], reads=[ss4], writes=[ss4])
            p.do('dve', 'tensor_tensor', on[:], o0[:], bcast(ss4[:].unsqueeze(2), [128, 4, 64]), ALU.mult, reads=[o0, ss4], writes=[on])
            pv = pbf[:, 0:256].rearrange("p (c t) -> p c t", c=2)
            for c in range(2):
                p.do('pe', 'transpose', pv[:, c, :], on[:, 2 * c:2 * c + 2, :].rearrange("p h v -> p (h v)"), k.identb[:], reads=[on, k.identb], writes=[pbf])
            for c in range(2):
                p.do('dve', 'scalar_tensor_tensor', ot[:, c, :], pv[:, c, :], ng[:, 0:1], gz[:, c, rows], ALU.mult, ALU.mult, reads=[pbf, ng, gz], writes=[ot])
            p.dma('sp', s["mixT"][6:8, :, rows].rearrange("c p t -> p c t"), ot[:], reads=[ot])
        end_phase(k)


FULL_CFG = dict(LS=4096, LP=256, NP=2, DEPTH=4, PAST=256, stages="MABC")


def kernel(**inputs):
    cfg = dict(FULL_CFG)
    nc = build(cfg)
    consts = const_tables(cfg)
    w = host_weights(inputs, cfg)
    in_maps = []
    for c in range(NCORES):
        m = {}
        m.update(consts)
        m.update(w)
        m.update(host_core_inputs(inputs, c, cfg))
        in_maps.append(m)
    res = run_bass_kernel_spmd(nc, in_maps, core_ids=list(range(NCORES)))
    R = res.results
    NP, LP, DEPTH = cfg["NP"], cfg["LP"], cfg["DEPTH"]
    y_sample = np.stack([np.asarray(R[c]["y_s"], dtype=np.float32) for c in range(NCORES)], axis=0)
    y_prompt = np.concatenate([np.asarray(R[c]["y_p"], dtype=np.float32).reshape(NP, LP, D) for c in range(NCORES)], axis=0)
    nk = np.concatenate([np.asarray(R[c]["nk"], dtype=np.float32).reshape(NP, DEPTH, LP, 2, 64) for c in range(NCORES)], axis=0)
    nv = np.concatenate([np.asarray(R[c]["nv"], dtype=np.float32).reshape(NP, DEPTH, LP, 2, 64) for c in range(NCORES)], axis=0)
    nssm = np.concatenate([np.asarray(R[c]["nssm"], dtype=np.float32) for c in range(NCORES)], axis=0)
    ngdn = np.concatenate([np.asarray(R[c]["ngdn"], dtype=np.float32) for c in range(NCORES)], axis=0)
    return (y_prompt, y_sample, nk, nv, nssm, ngdn)
```

```python
import contextlib
import math
import numpy as np
import concourse.bass as bass
import concourse.mybir as mybir
from concourse.bass_utils import run_bass_kernel_spmd

F32 = mybir.dt.float32
BF16 = mybir.dt.bfloat16
AF = mybir.ActivationFunctionType
ALU = mybir.AluOpType
AX = mybir.AxisListType

D = 1024
KC = 8
EPS = 1e-6
IN_W = 2064
NCORES = 8


class Buf:
    __slots__ = ("name", "ap", "last_w", "readers")

    def __init__(self, ap=None, name=""):
        self.ap = ap
        self.name = name
        self.last_w = None
        self.readers = []

    def __getitem__(self, k):
        return self.ap[k]


class Ring:
    def __init__(self, bufs):
        self.bufs = bufs
        self.i = 0

    def next(self):
        b = self.bufs[self.i]
        self.i = (self.i + 1) % len(self.bufs)
        return b


class Prog:
    ENG = ("pe", "act", "dve", "pool", "sp")

    def __init__(self, nc, stack, n_dma_sems=40):
        self.nc = nc
        self.lists = {e: [] for e in self.ENG}
        self.count = {e: 0 for e in self.ENG}
        self.known = {e: {} for e in self.ENG}
        self.n_dma_sems = n_dma_sems
        self.dma_val = [0] * n_dma_sems
        self.dma_rr = 0
        self.dma_rr_pool = 0
        self.esem = {e: stack.enter_context(nc.semaphore("sem_" + e)) for e in self.ENG}
        self.dsem = [stack.enter_context(nc.semaphore("dsem%d" % i)) for i in range(n_dma_sems)]
        self.ninst = 0

    def _need(self, eng, tok, waits):
        if tok is None:
            return
        key = (tok[0], tok[1])
        if self.known[eng].get(key, 0) >= tok[2]:
            return
        if tok[2] > waits.get(key, 0):
            waits[key] = tok[2]

    def _collect(self, eng, reads, writes, pe_ok=False):
        waits = {}
        for b in reads:
            self._need(eng, b.last_w, waits)
        for b in writes:
            self._need(eng, b.last_w, waits)
            for r in b.readers:
                self._need(eng, r, waits)
        out = []
        for key, val in waits.items():
            if key == ('e', 'pe') and eng == 'pe':
                continue
            out.append((key, val))
            self.known[eng][key] = val
        return out

    def _mark(self, tok, reads, writes):
        for b in reads:
            b.readers.append(tok)
            if len(b.readers) > 16:
                best = {}
                for t in b.readers:
                    k = (t[0], t[1])
                    if k not in best or best[k][2] < t[2]:
                        best[k] = t
                b.readers = list(best.values())
        for b in writes:
            b.last_w = tok
            b.readers = []

    def op(self, eng, fn, reads=(), writes=()):
        for key, val in self._collect(eng, reads, writes):
            self.lists[eng].append(('w', key, val))
        self.count[eng] += 1
        self.lists[eng].append(('i', fn))
        tok = ('e', eng, self.count[eng])
        self._mark(tok, reads, writes)
        self.ninst += 1
        return tok

    def do(self, eng, meth, *args, reads=(), writes=(), **kw):
        return self.op(eng, lambda h: getattr(h, meth)(*args, **kw), reads=reads, writes=writes)

    def dma(self, eng, out_ap, in_ap, reads=(), writes=(), **kw):
        half = self.n_dma_sems // 2
        if eng == 'pool':
            idx = half + self.dma_rr_pool
            self.dma_rr_pool = (self.dma_rr_pool + 1) % (self.n_dma_sems - half)
        else:
            idx = self.dma_rr
            self.dma_rr = (self.dma_rr + 1) % half
        wl = self._collect(eng, reads, writes)
        prev = self.dma_val[idx]
        if prev > 0 and self.known[eng].get(('d', idx), 0) < prev:
            wl.append((('d', idx), prev))
            self.known[eng][('d', idx)] = prev
        for key, val in wl:
            self.lists[eng].append(('w', key, val))
        self.dma_val[idx] += 16
        self.lists[eng].append(('dma', idx, out_ap, in_ap, kw))
        tok = ('d', idx, self.dma_val[idx])
        self._mark(tok, reads, writes)
        self.ninst += 1
        return tok

    def barrier(self):
        for eng in self.ENG:
            for other in self.ENG:
                if other == eng:
                    continue
                v = self.count[other]
                if v > 0 and self.known[eng].get(('e', other), 0) < v:
                    self.lists[eng].append(('w', ('e', other), v))
                    self.known[eng][('e', other)] = v
            for idx in range(self.n_dma_sems):
                v = self.dma_val[idx]
                if v > 0 and self.known[eng].get(('d', idx), 0) < v:
                    self.lists[eng].append(('w', ('d', idx), v))
                    self.known[eng][('d', idx)] = v

    def flush(self):
        nc = self.nc
        lists = self.lists
        self.lists = {e: [] for e in self.ENG}
        if not any(lists.values()):
            return
        esem, dsem = self.esem, self.dsem
        with nc.Block() as block:
            def replay(eng, h):
                my = esem[eng]
                for item in lists[eng]:
                    k = item[0]
                    if k == 'w':
                        key, val = item[1], item[2]
                        h.wait_ge(esem[key[1]] if key[0] == 'e' else dsem[key[1]], val)
                    elif k == 'i':
                        item[1](h).then_inc(my, 1)
                    else:
                        _, idx, o, i, kw = item
                        h.dma_start(out=o, in_=i, **kw).then_inc(dsem[idx], 16)

            @block.tensor
            def _(h):
                replay('pe', h)

            @block.scalar
            def _(h):
                replay('act', h)

            @block.vector
            def _(h):
                replay('dve', h)

            @block.gpsimd
            def _(h):
                replay('pool', h)

            @block.sync
            def _(h):
                replay('sp', h)


class K:
    pass


def bcast(ap, shape):
    return ap.to_broadcast(shape)


def build(cfg, dbg=False):
    LS, LP, NP, DEPTH, PAST = cfg["LS"], cfg["LP"], cfg["NP"], cfg["DEPTH"], cfg["PAST"]
    nc = bass.Bass("TRN2", target_bir_lowering=False)
    k = K()
    k.nc, k.cfg = nc, cfg

    def din(name, shape, dt=F32):
        return nc.dram_tensor(name, list(shape), dt, kind="ExternalInput").ap()

    def dout(name, shape, dt=F32):
        return nc.dram_tensor(name, list(shape), dt, kind="ExternalOutput").ap()

    def dscr(name, shape, dt=F32):
        return nc.dram_tensor(name, list(shape), dt, kind="ExternalOutput" if dbg else "Internal").ap()

    I = {}
    I["x_s"] = din("x_s", [LS, D])
    I["x_p"] = din("x_p", [NP * LP, D])
    I["cond2"] = din("cond2", [128, 2, KC])
    I["cache_k"] = din("cache_k", [DEPTH, PAST, 128])
    I["cache_v"] = din("cache_v", [DEPTH, PAST, 128])
    I["ssm_h0"] = din("ssm_h0", [DEPTH, 128, 16, 2])
    I["gdn_s0"] = din("gdn_s0", [DEPTH, 64, 8, 64])
    I["w_mod"] = din("w_mod", [DEPTH, D, 6 * D])
    I["b_modT"] = din("b_modT", [DEPTH, 128, 48])
    I["n1g"] = din("n1g", [DEPTH, 128, KC])
    I["n2g"] = din("n2g", [DEPTH, 128, KC])
    I["w_in"] = din("w_in", [DEPTH, D, IN_W])
    I["qk_g"] = din("qk_g", [DEPTH, 128, 2, 64])
    I["sink_b"] = din("sink_b", [DEPTH, 128, 8])
    I["lam"] = din("lam", [DEPTH, 128, 3, 16])
    I["ssm_bt"] = din("ssm_bt", [DEPTH, 128, 16, 2, 128])
    I["ssm_ct"] = din("ssm_ct", [DEPTH, 128, 16, 2, 128])
    I["ssm_dT"] = din("ssm_dT", [DEPTH, 128, 2])
    I["glu_bT"] = din("glu_bT", [DEPTH, 128, 2])
    I["w_glu"] = din("w_glu", [DEPTH, 256, 256])
    I["conv_wT"] = din("conv_wT", [DEPTH, 128, 6, 3])
    I["gdn_ab"] = din("gdn_ab", [DEPTH, 64, 2, 8])
    I["gdn_ng"] = din("gdn_ng", [DEPTH, 128, 1])
    I["w_out"] = din("w_out", [DEPTH, D, D])
    I["w_ff1"] = din("w_ff1", [DEPTH, D, 4 * D])
    I["w_ff2"] = din("w_ff2", [DEPTH, 4 * D, D])
    I["ident"] = din("ident", [128, 128])
    I["rope"] = din("rope", [LS, 2, 32])
    I["amask"] = din("amask", [128, 2, 128])
    I["gmask"] = din("gmask", [64, 6, 8, 64])
    I["gtri"] = din("gtri", [64, 4, 64])
    k.I = I
    O = {}
    O["y_s"] = dout("y_s", [LS, D])
    O["y_p"] = dout("y_p", [NP * LP, D])
    O["nk"] = dout("nk", [NP, DEPTH, LP, 128])
    O["nv"] = dout("nv", [NP, DEPTH, LP, 128])
    O["nssm"] = dout("nssm", [NP, DEPTH, 2, 2, 16, 64])
    O["ngdn"] = dout("ngdn", [NP, DEPTH, 2, 4, 64, 64])
    k.O = O
    seqs = [dict(name="s", L=LS, ctx=False, xin=I["x_s"], y=O["y_s"], pi=-1, cond=0)]
    for i in range(NP):
        seqs.append(dict(name="p%d" % i, L=LP, ctx=True, xin=I["x_p"][i * LP:(i + 1) * LP, :],
                         y=O["y_p"][i * LP:(i + 1) * LP, :], pi=i, cond=1))
    for s in seqs:
        L = s["L"]
        n = s["name"]
        s["qT"] = dscr("qT_" + n, [4, 128, L], BF16)
        s["kT"] = dscr("kT_" + n, [2, 128, L], BF16)
        s["v"] = dscr("v_" + n, [L, 128], BF16)
        s["fT"] = dscr("fT_" + n, [10, 128, L], BF16)
        s["gates"] = dscr("gates_" + n, [64, L // 64, 16], F32)
        s["mixT"] = dscr("mixT_" + n, [8, 128, L], BF16)
        s["go"] = dscr("go_" + n, [2, L, 256], F32)
    k.seqs = seqs

    with contextlib.ExitStack() as top:
        p = Prog(nc, top)
        k.p = p

        uid = [0]

        def sbt(st, name, shape, dt):
            uid[0] += 1
            t = st.enter_context(nc.sbuf_tensor("sb%d_%s" % (uid[0], name), list(shape), dt))
            return Buf(t, name)

        def pst(st, name, shape, dt):
            uid[0] += 1
            t = st.enter_context(nc.psum_tensor("ps%d_%s" % (uid[0], name), list(shape), dt))
            return Buf(t, name)

        k.sbt, k.pst = sbt, pst
        k.ident = sbt(top, "ident", [128, 128], F32)
        k.identb = sbt(top, "identb", [128, 128], BF16)
        k.modT = sbt(top, "modT", [128, DEPTH, 48, 2], F32)
        k.ones_f = sbt(top, "ones_f", [128, 128], F32)
        k.ones_b = sbt(top, "ones_b", [128, 128], BF16)
        p.dma('sp', k.ident[:], I["ident"][:, :], writes=[k.ident])
        p.do('dve', 'tensor_copy', k.identb[:], k.ident[:], reads=[k.ident], writes=[k.identb])
        p.do('dve', 'memset', k.ones_f[:], 1.0, writes=[k.ones_f])
        p.do('dve', 'memset', k.ones_b[:], 1.0, writes=[k.ones_b])

        stages = cfg.get("stages", "MABC")
        if "M" in stages:
            prologue_mod(k)
        if dbg:
            dm = dout("dbg_modT", [128, DEPTH * 96])
            p.dma('sp', dm[:, :], k.modT[:].rearrange("p l c t -> p (l c t)"), reads=[k.modT])
        for l in range(DEPTH):
            if "A" in stages:
                phase_a(k, l)
            if "B" in stages:
                for s in seqs:
                    if cfg.get("attn", True):
                        phase_attn(k, l, s)
                    if cfg.get("ssm", True):
                        phase_ssm(k, l, s)
                    if cfg.get("gdn", True):
                        phase_gdn(k, l, s)
            if "C" in stages:
                phase_c(k, l)
        p.barrier()
        p.flush()
    return nc


def end_phase(k):
    k.p.barrier()
    k.p.flush()


def prologue_mod(k):
    nc, p, I = k.nc, k.p, k.I
    DEPTH = k.cfg["DEPTH"]
    with contextlib.ExitStack() as st:
        cond = k.sbt(st, "cond", [128, 2, KC], F32)
        sc = k.sbt(st, "sc", [128, KC, 2], F32)
        bm = k.sbt(st, "bm", [128, DEPTH, 48], F32)
        wring = Ring([k.sbt(st, "wm%d" % i, [128, 6 * D], F32) for i in range(2)])
        pring = Ring([k.pst(st, "pm%d" % i, [128, 256, 2], F32) for i in range(2)])
        p.dma('sp', cond[:], I["cond2"][:, :, :], writes=[cond])
        p.dma('sp', bm[:], I["b_modT"].rearrange("l p c -> p l c"), writes=[bm])
        p.do('act', 'activation', sc[:].rearrange("p k c -> p c k"), cond[:], AF.Silu, reads=[cond], writes=[sc])
        for l in range(DEPTH):
            for kc in range(KC):
                w = wring.next()
                p.dma('sp' if kc % 2 == 0 else 'pool', w[:], I["w_mod"][l, kc * 128:(kc + 1) * 128, :], writes=[w])
                ps = pring.next()
                for fc in range(48):
                    p.do('pe', 'matmul', ps[:, fc, :], w[:, fc * 128:(fc + 1) * 128], sc[:, kc, :], start=True, stop=True,
                         reads=[w, sc], writes=[ps])
                if kc == 0:
                    p.do('dve', 'tensor_tensor', k.modT[:, l, :, :], ps[:, 0:48, :], bcast(bm[:, l, :].unsqueeze(2), [128, 48, 2]), ALU.add,
                         reads=[ps, bm], writes=[k.modT])
                else:
                    p.do('dve', 'tensor_tensor', k.modT[:, l, :, :], ps[:, 0:48, :], k.modT[:, l, :, :], ALU.add,
                         reads=[ps, k.modT], writes=[k.modT])
        end_phase(k)


def phase_a(k, l):
    nc, p, I, O = k.nc, k.p, k.I, k.O
    with contextlib.ExitStack() as st:
        sbt, pst = k.sbt, k.pst
        win = sbt(st, "win", [128, KC, IN_W], BF16)
        wsrc = I["w_in"][l].rearrange("(c p) n -> p c n", p=128)
        for c in range(KC):
            for hh in range(2):
                p.dma('pool', win[:, c, hh * 1032:(hh + 1) * 1032], wsrc[:, c, hh * 1032:(hh + 1) * 1032], writes=[win])
        n1g = sbt(st, "n1g", [128, KC], F32)
        sc1 = sbt(st, "sc1", [128, KC, 2], F32)
        qkg = sbt(st, "qkg", [128, 2, 64], F32)
        p.dma('sp', n1g[:], I["n1g"][l], writes=[n1g])
        p.dma('sp', qkg[:], I["qk_g"][l], writes=[qkg])
        p.do('dve', 'tensor_scalar', sc1[:], k.modT[:, l, 8:16, :], 1.0, None, ALU.add, reads=[k.modT], writes=[sc1])
        p.do('dve', 'tensor_tensor', sc1[:], sc1[:], bcast(n1g[:].unsqueeze(2), [128, KC, 2]), ALU.mult, reads=[sc1, n1g], writes=[sc1])
        xring = Ring([sbt(st, "xa%d" % i, [128, D], F32) for i in range(2)])
        junk = sbt(st, "junka", [128, D], F32)
        ssr = Ring([sbt(st, "ssa%d" % i, [128, 1], F32) for i in range(2)])
        xnr = Ring([sbt(st, "xna%d" % i, [128, D], BF16) for i in range(2)])
        hTr = Ring([sbt(st, "hTa%d" % i, [128, KC, 512], BF16) for i in range(2)])
        sq = sbt(st, "sqa", [128, 10, 64], F32)
        ss10 = sbt(st, "ss10", [128, 10], F32)
        qn = sbt(st, "qna", [128, 10, 64], F32)
        tmp = [sbt(st, "ropet%d" % i, [128, 10, 32], F32) for i in range(4)]
        qr = sbt(st, "qra", [128, 10, 64], BF16)
        kd = sbt(st, "kda", [128, 4, 64], BF16)
        vb = sbt(st, "vba", [128, 128], BF16)
        vf = sbt(st, "vfa", [128, 128], F32)
        rp = sbt(st, "rpa", [128, 2, 32], F32)
        stager = Ring([sbt(st, "stga%d" % i, [128, 6, 512], BF16) for i in range(2)])
        fstr = Ring([sbt(st, "fsta%d" % i, [128, 512], BF16) for i in range(3)])
        gsb = sbt(st, "gsba", [64, 8, 16], F32)
        pTr = Ring([pst(st, "pTa%d" % i, [128, KC, 128], BF16) for i in range(2)])
        pq = pst(st, "pqa", [128, 512], F32)
        pkv = pst(st, "pkva", [128, 512], F32)
        ptr = pst(st, "ptra", [128, 8, 128], BF16)
        pfr = Ring([pst(st, "pfa%d" % i, [128, 512], F32) for i in range(2)])
        pg = pst(st, "pga", [128, 32, 16], F32)
        ev = [0]
        p.barrier()

        for s in (k.seqs[::-1] if k.cfg.get('rev') else k.seqs):
            L, ctx, cond = s["L"], s["ctx"], s["cond"]
            xsrc = s["xin"] if l == 0 else s["y"]
            TB = min(512, L)
            ntb = TB // 128
            for b in range(L // TB):
                hT = hTr.next()
                for ti in range(ntb):
                    r0 = b * TB + ti * 128
                    xt = xring.next()
                    ss = ssr.next()
                    xn = xnr.next()
                    pT = pTr.next()
                    p.dma('sp', xt[:], xsrc[r0:r0 + 128, :], writes=[xt])
                    p.do('act', 'activation', junk[:], xt[:], AF.Square, accum_out=ss[:], reads=[xt], writes=[junk, ss])
                    p.do('act', 'activation', ss[:], ss[:], AF.Sqrt, scale=1.0 / D, bias=EPS, reads=[ss], writes=[ss])
                    p.do('dve', 'reciprocal', ss[:], ss[:], reads=[ss], writes=[ss])
                    p.do('dve', 'tensor_scalar', xn[:], xt[:], ss[:, 0:1], None, ALU.mult, reads=[xt, ss], writes=[xn])
                    for c in range(KC):
                        p.do('pe', 'transpose', pT[:, c, :], xn[:, c * 128:(c + 1) * 128], k.identb[:], reads=[xn, k.identb], writes=[pT])
                    for c in range(KC):
                        if c % 2 == 0:
                            p.do('act', 'activation', hT[:, c, ti * 128:(ti + 1) * 128], pT[:, c, :], AF.Identity, scale=sc1[:, c, cond:cond + 1], bias=k.modT[:, l, c, cond:cond + 1],
                                 reads=[pT, sc1, k.modT], writes=[hT])
                        else:
                            p.do('dve', 'tensor_scalar', hT[:, c, ti * 128:(ti + 1) * 128], pT[:, c, :], sc1[:, c, cond:cond + 1], k.modT[:, l, c, cond:cond + 1], ALU.mult, ALU.add,
                                 reads=[pT, sc1, k.modT], writes=[hT])
                stage = stager.next()
                if k.cfg.get("dbg_hT") and not hasattr(k, "_dh"):
                    k._dh = nc.dram_tensor("dbg_hT", [128, KC, 512], BF16, kind="ExternalOutput").ap()
                    p.dma('sp', k._dh[:, :, 0:TB], hT[:, :, 0:TB], reads=[hT])
                    k._dw = nc.dram_tensor("dbg_win", [128, KC, IN_W], BF16, kind="ExternalOutput").ap()
                    p.dma('sp', k._dw[:, :, :], win[:, :, :], reads=[win])
                for ti in range(ntb):
                    r0 = b * TB + ti * 128
                    tsl = slice(ti * 128, (ti + 1) * 128)
                    for kc in range(KC):
                        p.do('pe', 'matmul', pq[:], hT[:, kc, tsl], win[:, kc, 0:512], start=(kc == 0), stop=(kc == KC - 1), reads=[hT, win], writes=[pq])
                    for kc in range(KC):
                        p.do('pe', 'matmul', pkv[:, 0:256], hT[:, kc, tsl], win[:, kc, 512:768], start=(kc == 0), stop=(kc == KC - 1), reads=[hT, win], writes=[pkv])
                    p.do('act', 'activation', sq[:, 0:8, :], pq[:].rearrange("p (h d) -> p h d", d=64), AF.Square, reads=[pq], writes=[sq])
                    p.do('act', 'activation', sq[:, 8:10, :], pkv[:, 0:128].rearrange("p (h d) -> p h d", d=64), AF.Square, reads=[pkv], writes=[sq])
                    p.do('dve', 'tensor_reduce', ss10[:], sq[:], AX.X, ALU.add, reads=[sq], writes=[ss10])
                    p.do('act', 'activation', ss10[:], ss10[:], AF.Sqrt, scale=1.0 / 64, bias=EPS, reads=[ss10], writes=[ss10])
                    p.do('dve', 'reciprocal', ss10[:], ss10[:], reads=[ss10], writes=[ss10])
                    p.do('dve', 'tensor_tensor', qn[:, 0:8, :], pq[:].rearrange("p (h d) -> p h d", d=64), bcast(ss10[:, 0:8].unsqueeze(2), [128, 8, 64]), ALU.mult, reads=[pq, ss10], writes=[qn])
                    p.do('dve', 'tensor_tensor', qn[:, 8:10, :], pkv[:, 0:128].rearrange("p (h d) -> p h d", d=64), bcast(ss10[:, 8:10].unsqueeze(2), [128, 2, 64]), ALU.mult, reads=[pkv, ss10], writes=[qn])
                    p.do('pool', 'tensor_tensor', qn[:, 0:8, :], qn[:, 0:8, :], bcast(qkg[:, 0:1, :], [128, 8, 64]), ALU.mult, reads=[qn, qkg], writes=[qn])
                    p.do('pool', 'tensor_tensor', qn[:, 8:10, :], qn[:, 8:10, :], bcast(qkg[:, 1:2, :], [128, 2, 64]), ALU.mult, reads=[qn, qkg], writes=[qn])
                    p.do('act', 'copy', vb[:], pkv[:, 128:256], reads=[pkv], writes=[vb])
                    p.dma('sp', s["v"][r0:r0 + 128, :], vb[:], reads=[vb])
                    if ctx:
                        pi = s["pi"]
                        p.do('act', 'copy', vf[:], pkv[:, 128:256], reads=[pkv], writes=[vf])
                        p.dma('sp', O["nv"][pi, l, r0:r0 + 128, :], vf[:], reads=[vf])
                        p.dma('sp', O["nk"][pi, l, r0:r0 + 128, :], qn[:, 8:10, :].rearrange("p h d -> p (h d)"), reads=[qn])
                        p.do('dve', 'tensor_copy', qr[:], qn[:], reads=[qn], writes=[qr])
                    else:
                        p.dma('sp', rp[:], I["rope"][r0:r0 + 128, :, :], writes=[rp])
                        q4 = qn[:].rearrange("p h (i two) -> p h i two", two=2)
                        r4 = qr[:].rearrange("p h (i two) -> p h i two", two=2)
                        cosb = bcast(rp[:, 0:1, :], [128, 10, 32])
                        sinb = bcast(rp[:, 1:2, :], [128, 10, 32])
                        p.do('dve', 'tensor_tensor', tmp[0][:], q4[:, :, :, 0], cosb, ALU.mult, reads=[qn, rp], writes=[tmp[0]])
                        p.do('pool', 'tensor_tensor', tmp[1][:], q4[:, :, :, 1], sinb, ALU.mult, reads=[qn, rp], writes=[tmp[1]])
                        p.do('dve', 'tensor_tensor', tmp[2][:], q4[:, :, :, 0], sinb, ALU.mult, reads=[qn, rp], writes=[tmp[2]])
                        p.do('pool', 'tensor_tensor', tmp[3][:], q4[:, :, :, 1], cosb, ALU.mult, reads=[qn, rp], writes=[tmp[3]])
                        p.do('dve', 'tensor_tensor', r4[:, :, :, 0], tmp[0][:], tmp[1][:], ALU.subtract, reads=[tmp[0], tmp[1]], writes=[qr])
                        p.do('pool', 'tensor_tensor', r4[:, :, :, 1], tmp[2][:], tmp[3][:], ALU.add, reads=[tmp[2], tmp[3]], writes=[qr])
                    p.do('pool', 'tensor_copy', kd[:, 0:2, :], bcast(qr[:, 8:9, :], [128, 2, 64]), reads=[qr], writes=[kd])
                    p.do('pool', 'tensor_copy', kd[:, 2:4, :], bcast(qr[:, 9:10, :], [128, 2, 64]), reads=[qr], writes=[kd])
                    for c in range(4):
                        p.do('pe', 'transpose', ptr[:, c, :], qr[:, 2 * c:2 * c + 2, :].rearrange("p h d -> p (h d)"), k.identb[:], reads=[qr, k.identb], writes=[ptr])
                    for g in range(2):
                        p.do('pe', 'transpose', ptr[:, 4 + g, :], kd[:, 2 * g:2 * g + 2, :].rearrange("p h d -> p (h d)"), k.identb[:], reads=[kd, k.identb], writes=[ptr])
                    p.do('act', 'copy', stage[:, :, tsl], ptr[:, 0:6, :], reads=[ptr], writes=[stage])
                cols = slice(b * TB, (b + 1) * TB)
                p.dma('sp', s["qT"][:, :, cols].rearrange("c p t -> p c t"), stage[:, 0:4, 0:TB], reads=[stage])
                p.dma('sp', s["kT"][:, :, cols].rearrange("c p t -> p c t"), stage[:, 4:6, 0:TB], reads=[stage])
                for fc in range(10):
                    pf = pfr.next()
                    fst = fstr.next()
                    for kc in range(KC):
                        p.do('pe', 'matmul', pf[:, 0:TB], win[:, kc, 768 + fc * 128:768 + (fc + 1) * 128], hT[:, kc, 0:TB], start=(kc == 0), stop=(kc == KC - 1), reads=[hT, win], writes=[pf])
                    ev[0] += 1
                    if ev[0] % 2 == 0:
                        p.do('act', 'copy', fst[:, 0:TB], pf[:, 0:TB], reads=[pf], writes=[fst])
                    else:
                        p.do('dve', 'tensor_copy', fst[:, 0:TB], pf[:, 0:TB], reads=[pf], writes=[fst])
                    p.dma('sp', s["fT"][fc, :, cols], fst[:, 0:TB], reads=[fst])
                nch = TB // 64
                for j in range(nch):
                    for kc in range(KC):
                        p.do('pe', 'matmul', pg[0:64, j, :], hT[:, kc, j * 64:(j + 1) * 64], win[:, kc, 2048:2064], start=(kc == 0), stop=(kc == KC - 1), reads=[hT, win], writes=[pg])
                p.do('dve', 'tensor_copy', gsb[:, 0:nch, :], pg[0:64, 0:nch, :], reads=[pg], writes=[gsb])
                p.dma('sp', s["gates"][:, b * nch:(b + 1) * nch, :], gsb[:, 0:nch, :], reads=[gsb])
        end_phase(k)


def _chunkT(v, n=128):
    sh = v.shape[:-1]
    return np.ascontiguousarray(np.swapaxes(v.reshape(sh + (-1, n)), -1, -2))


def const_tables(cfg):
    LS = cfg["LS"]
    t = {}
    t["ident"] = np.eye(128, dtype=np.float32)
    n_rows = max(LS // 64, 1)
    rows = np.repeat(np.arange(n_rows, dtype=np.float32), 64)[:LS]
    cols = np.tile(np.arange(64, dtype=np.float32), n_rows)[:LS]
    inv_freq = np.power(np.float32(10000.0), -np.arange(16, dtype=np.float32) / np.float32(16)).astype(np.float32)
    ang = np.concatenate([rows[:, None] * inv_freq, cols[:, None] * inv_freq], axis=-1).astype(np.float32)
    t["rope"] = np.ascontiguousarray(np.stack([np.cos(ang), np.sin(ang)], axis=1).astype(np.float32))
    kk = np.arange(128)[:, None]
    qq = np.arange(128)[None, :]
    t["amask"] = np.ascontiguousarray(np.stack([(kk >= qq), (kk <= qq)], axis=1).astype(np.float32))
    i = np.arange(64)[:, None]
    j = np.arange(64)[None, :]
    low_incl = (i >= j).astype(np.float32)
    low_strict = (i > j).astype(np.float32)
    gm = np.zeros((64, 6, 8, 64), np.float32)
    for e in range(8):
        fwd = e < 4
        gm[:, 0, e, :] = low_strict if fwd else low_strict.T
        gm[:, 1, e, :] = low_incl if fwd else low_incl.T
        gm[:, 2, e, :] = low_strict.T if fwd else low_strict
        gm[:, 3, e, :] = low_incl.T if fwd else low_incl
        gm[:, 4, e, :] = np.eye(64, dtype=np.float32)
    t["gmask"] = gm
    gt = np.zeros((64, 4, 64), np.float32)
    gt[:, 0, :] = low_incl.T
    gt[:, 1, :] = low_incl
    gt[63, 2, :] = 1.0
    gt[0, 3, :] = 1.0
    t["gtri"] = gt
    return t


def host_weights(inp, cfg):
    DEPTH = cfg["DEPTH"]
    f = lambda a: np.ascontiguousarray(np.asarray(a, dtype=np.float32))
    w = {}
    w["w_mod"] = f(inp["w_mod"][:DEPTH])
    w["b_modT"] = _chunkT(f(inp["b_mod"][:DEPTH]))
    w["n1g"] = _chunkT(f(inp["norm1_g"][:DEPTH]))
    w["n2g"] = _chunkT(f(inp["norm2_g"][:DEPTH]))
    w["w_in"] = f(inp["w_in"][:DEPTH])
    qk = np.stack([f(inp["q_norm_g"][:DEPTH]), f(inp["k_norm_g"][:DEPTH])], axis=1)
    w["qk_g"] = np.ascontiguousarray(np.broadcast_to(qk[:, None], (DEPTH, 128, 2, 64)))
    w["sink_b"] = np.ascontiguousarray(np.broadcast_to(f(inp["attn_sink"][:DEPTH])[:, None], (DEPTH, 128, 8)))
    lam = np.zeros((DEPTH, 128, 3, 16), np.float32)
    bt = np.zeros((DEPTH, 128, 16, 2, 128), np.float32)
    ct = np.zeros((DEPTH, 128, 16, 2, 128), np.float32)
    lre, lim, lst = f(inp["ssm_lam_re"]), f(inp["ssm_lam_im"]), f(inp["ssm_log_step"])
    bre, bim, cre, cim = f(inp["ssm_b_re"]), f(inp["ssm_b_im"]), f(inp["ssm_c_re"]), f(inp["ssm_c_im"])
    for gp in range(8):
        for d in range(2):
            inst = gp * 2 + d
            for g2 in range(2):
                g = 2 * gp + g2
                gl = g % 8
                ps = slice(g2 * 64, (g2 + 1) * 64)
                lam[:, ps, 0, inst] = lre[:DEPTH, d, g, :]
                lam[:, ps, 1, inst] = lim[:DEPTH, d, g, :]
                lam[:, ps, 2, inst] = lst[:DEPTH, d, g][:, None]
                bt[:, ps, inst, 0, gl * 16:(gl + 1) * 16] = bre[:DEPTH, d, g]
                bt[:, ps, inst, 1, gl * 16:(gl + 1) * 16] = bim[:DEPTH, d, g]
                co0 = 32 * (gp % 4) + g2 * 16
                ct[:, ps, inst, 0, co0:co0 + 16] = np.swapaxes(cre[:DEPTH, d, g], -1, -2)
                ct[:, ps, inst, 1, co0:co0 + 16] = np.swapaxes(cim[:DEPTH, d, g], -1, -2)
    w["lam"], w["ssm_bt"], w["ssm_ct"] = lam, bt, ct
    w["ssm_dT"] = _chunkT(f(inp["ssm_d"][:DEPTH]))
    w["glu_bT"] = _chunkT(f(inp["ssm_b_glu"][:DEPTH]))
    w["w_glu"] = f(inp["ssm_w_glu"][:DEPTH])
    cw = f(inp["gdn_conv_w"][:DEPTH])
    w["conv_wT"] = np.ascontiguousarray(np.transpose(cw.reshape(DEPTH, 3, 6, 128), (0, 3, 2, 1)))
    ab = np.stack([f(inp["gdn_a_log"][:DEPTH]).reshape(DEPTH, 8), f(inp["gdn_dt_bias"][:DEPTH]).reshape(DEPTH, 8)], axis=1)
    w["gdn_ab"] = np.ascontiguousarray(np.broadcast_to(ab[:, None], (DEPTH, 64, 2, 8)))
    ng = f(inp["gdn_norm_g"][:DEPTH])
    w["gdn_ng"] = np.ascontiguousarray(np.concatenate([ng, ng], axis=1)[:, :, None])
    w["w_out"] = f(inp["w_out"][:DEPTH])
    w["w_ff1"] = f(inp["w_ff1"][:DEPTH])
    w["w_ff2"] = f(inp["w_ff2"][:DEPTH])
    return w


def host_core_inputs(inp, core, cfg):
    LS, LP, NP, DEPTH, PAST = cfg["LS"], cfg["LP"], cfg["NP"], cfg["DEPTH"], cfg["PAST"]
    f = lambda a: np.ascontiguousarray(np.asarray(a, dtype=np.float32))
    m = {}
    m["x_s"] = f(inp["x_sample"][core, :LS])
    m["x_p"] = f(inp["x_prompt"][core * NP:(core + 1) * NP, :LP]).reshape(NP * LP, D)
    c2 = np.stack([f(inp["c"][core]), f(inp["c_ctx"])], axis=0)
    m["cond2"] = np.ascontiguousarray(np.transpose(c2.reshape(2, KC, 128), (2, 0, 1)))
    m["cache_k"] = f(inp["cache_k"][core, :DEPTH, :PAST]).reshape(DEPTH, PAST, 128)
    m["cache_v"] = f(inp["cache_v"][core, :DEPTH, :PAST]).reshape(DEPTH, PAST, 128)
    ss = f(inp["state_ssm"][core, :DEPTH])
    h0 = np.zeros((DEPTH, 128, 16, 2), np.float32)
    for gp in range(8):
        for d in range(2):
            for g2 in range(2):
                h0[:, g2 * 64:(g2 + 1) * 64, gp * 2 + d, :] = np.transpose(ss[:, d, :, 2 * gp + g2, :], (0, 2, 1))
    m["ssm_h0"] = h0
    sg = f(inp["state_gdn"][core, :DEPTH])
    m["gdn_s0"] = np.ascontiguousarray(np.transpose(sg, (0, 3, 1, 2, 4)).reshape(DEPTH, 64, 8, 64))
    return m


def phase_c(k, l):
    nc, p, I, O = k.nc, k.p, k.I, k.O
    use_mix = k.cfg.get("use_mix", True)
    with contextlib.ExitStack() as st:
        sbt, pst = k.sbt, k.pst
        wout = sbt(st, "wout", [128, KC, D], BF16)
        wff1 = sbt(st, "wff1", [128, KC, 4 * D], BF16)
        wff2 = sbt(st, "wff2", [128, 32, D], BF16)
        s_out = I["w_out"][l].rearrange("(c p) n -> p c n", p=128)
        s_ff1 = I["w_ff1"][l].rearrange("(c p) n -> p c n", p=128)
        s_ff2 = I["w_ff2"][l].rearrange("(c p) n -> p c n", p=128)
        for c in range(KC):
            p.dma('pool', wout[:, c, :], s_out[:, c, :], writes=[wout])
        for c in range(KC):
            for q4 in range(4):
                p.dma('pool', wff1[:, c, q4 * D:(q4 + 1) * D], s_ff1[:, c, q4 * D:(q4 + 1) * D], writes=[wff1])
        for c in range(32):
            p.dma('pool', wff2[:, c, :], s_ff2[:, c, :], writes=[wff2])
        n2g = sbt(st, "n2g", [128, KC], F32)
        sc2 = sbt(st, "sc2", [128, KC, 2], F32)
        p.dma('sp', n2g[:], I["n2g"][l], writes=[n2g])
        p.do('dve', 'tensor_scalar', sc2[:], k.modT[:, l, 32:40, :], 1.0, None, ALU.add, reads=[k.modT], writes=[sc2])
        p.do('dve', 'tensor_tensor', sc2[:], sc2[:], bcast(n2g[:].unsqueeze(2), [128, KC, 2]), ALU.mult, reads=[sc2, n2g], writes=[sc2])
        gA1 = sbt(st, "gA", [128, D], F32)
        gM1 = sbt(st, "gM", [128, D], F32)
        gA, gM = [gA1, gA1], [gM1, gM1]
        gcol = Ring([sbt(st, "gcol%d" % i, [128, 128], F32) for i in range(1)])
        pgb = Ring([pst(st, "pgb%d" % i, [128, 512], F32) for i in range(2)])
        cur_cond = [None]

        def build_gates(cond):
            if cur_cond[0] == cond:
                return
            cur_cond[0] = cond
            for gi, (dst, base) in enumerate(((gA[cond], 16), (gM[cond], 40))):
                for c in range(KC):
                    gc_ = gcol.next()
                    pb = pgb.next()
                    p.do('dve', 'tensor_copy', gc_[:], bcast(k.modT[:, l, base + c, cond:cond + 1], [128, 128]), reads=[k.modT], writes=[gc_])
                    p.do('pe', 'matmul', pb[:, 0:128], gc_[:], k.ident[:], start=True, stop=True, reads=[gc_, k.ident], writes=[pb])
                    p.do('act', 'copy', dst[:, c * 128:(c + 1) * 128], pb[:, 0:128], reads=[pb], writes=[dst])
        xring = Ring([sbt(st, "xc%d" % i, [128, D], F32) for i in range(3)])
        x1r = Ring([sbt(st, "x1c%d" % i, [128, D], F32) for i in range(2)])
        tmpc = sbt(st, "tmpc", [128, D], F32)
        junk = tmpc
        ssr = Ring([sbt(st, "ssc%d" % i, [128, 1], F32) for i in range(2)])
        xnr = Ring([sbt(st, "xnc%d" % i, [128, D], BF16) for i in range(1)])
        mixr = Ring([sbt(st, "mixc%d" % i, [128, KC, 128], BF16) for i in range(2)])
        h2r = Ring([sbt(st, "h2c%d" % i, [128, KC, 128], BF16) for i in range(2)])
        aTr = Ring([sbt(st, "aTc%d" % i, [128, 32, 128], BF16) for i in range(2)])
        rr = Ring([sbt(st, "rc%d" % i, [128, 4, 128], BF16) for i in range(2)])
        pTr = Ring([pst(st, "pTc%d" % i, [128, KC, 128], BF16) for i in range(1)])
        por = Ring([pst(st, "poc%d" % i, [128, 512], F32) for i in range(3)])
        pfr = Ring([pst(st, "pfc%d" % i, [128, 4, 128], F32) for i in range(2)])
        p.barrier()
        items = [(s, t) for s in k.seqs for t in range(s["L"] // 128)]

        def prefetch(it):
            s_, t_ = it
            rows_ = slice(t_ * 128, (t_ + 1) * 128)
            xsrc_ = s_["xin"] if (l == 0) else s_["y"]
            xt_ = xring.next()
            p.dma('sp', xt_[:], xsrc_[rows_, :], writes=[xt_])
            mx_ = None
            if use_mix:
                mx_ = mixr.next()
                p.dma('sp', mx_[:], s_["mixT"][:, :, rows_].rearrange("c p t -> p c t"), writes=[mx_])
            return xt_, mx_

        def front(it, ld):
            s, t = it
            cond = s["cond"]
            xt, mx = ld
            build_gates(cond)
            x1 = x1r.next()
            if use_mix:
                for hh in range(2):
                    po = por.next()
                    for kc in range(KC):
                        p.do('pe', 'matmul', po[:], mx[:, kc, :], wout[:, kc, hh * 512:(hh + 1) * 512], start=(kc == 0), stop=(kc == KC - 1), reads=[mx, wout], writes=[po])
                    hs = slice(hh * 512, (hh + 1) * 512)
                    p.do('dve', 'tensor_tensor', x1[:, hs], po[:], gA[cond][:, hs], ALU.mult, reads=[po, gA[cond]], writes=[x1])
                    p.do('pool', 'tensor_tensor', x1[:, hs], x1[:, hs], xt[:, hs], ALU.add, reads=[x1, xt], writes=[x1])
            else:
                p.do('pool', 'tensor_copy', x1[:], xt[:], reads=[xt], writes=[x1])
            ss = ssr.next()
            xn = xnr.next()
            pT = pTr.next()
            h2 = h2r.next()
            p.do('act', 'activation', junk[:], x1[:], AF.Square, accum_out=ss[:], reads=[x1], writes=[junk, ss])
            p.do('act', 'activation', ss[:], ss[:], AF.Sqrt, scale=1.0 / D, bias=EPS, reads=[ss], writes=[ss])
            p.do('dve', 'reciprocal', ss[:], ss[:], reads=[ss], writes=[ss])
            p.do('dve', 'tensor_scalar', xn[:], x1[:], ss[:, 0:1], None, ALU.mult, reads=[x1, ss], writes=[xn])
            for c in range(KC):
                p.do('pe', 'transpose', pT[:, c, :], xn[:, c * 128:(c + 1) * 128], k.identb[:], reads=[xn, k.identb], writes=[pT])
            for c in range(KC):
                if c % 2 == 0:
                    p.do('act', 'activation', h2[:, c, :], pT[:, c, :], AF.Identity, scale=sc2[:, c, cond:cond + 1], bias=k.modT[:, l, 24 + c, cond:cond + 1], reads=[pT, sc2, k.modT], writes=[h2])
                else:
                    p.do('dve', 'tensor_scalar', h2[:, c, :], pT[:, c, :], sc2[:, c, cond:cond + 1], k.modT[:, l, 24 + c, cond:cond + 1], ALU.mult, ALU.add, reads=[pT, sc2, k.modT], writes=[h2])
            return dict(xt=xt, x1=x1, h2=h2)

        def back(it, C):
            s, t = it
            cond = s["cond"]
            rows = slice(t * 128, (t + 1) * 128)
            xt, x1, h2 = C["xt"], C["x1"], C["h2"]
            aT = aTr.next()
            for f4 in range(8):
                pf = pfr.next()
                r_ = rr.next()
                for j in range(4):
                    fc = f4 * 4 + j
                    for kc in range(KC):
                        p.do('pe', 'matmul', pf[:, j, :], wff1[:, kc, fc * 128:(fc + 1) * 128], h2[:, kc, :], start=(kc == 0), stop=(kc == KC - 1), reads=[wff1, h2], writes=[pf])
                p.do('act', 'activation', r_[:], pf[:], AF.Relu, reads=[pf], writes=[r_])
                p.do('pool', 'tensor_tensor', aT[:, f4 * 4:(f4 + 1) * 4, :], r_[:], r_[:], ALU.mult, reads=[r_], writes=[aT])
            for hh in range(2):
                po = por.next()
                for fc in range(32):
                    p.do('pe', 'matmul', po[:], aT[:, fc, :], wff2[:, fc, hh * 512:(hh + 1) * 512], start=(fc == 0), stop=(fc == 31), reads=[aT, wff2], writes=[po])
                hs = slice(hh * 512, (hh + 1) * 512)
                p.do('dve', 'tensor_tensor', xt[:, hs], po[:], gM[cond][:, hs], ALU.mult, reads=[po, gM[cond]], writes=[xt])
                p.do('dve', 'tensor_tensor', xt[:, hs], xt[:, hs], x1[:, hs], ALU.add, reads=[xt, x1], writes=[xt])
            p.dma('sp', s["y"][rows, :], xt[:], reads=[xt])

        n_it = len(items)
        loads = {0: prefetch(items[0])}
        if n_it > 1:
            loads[1] = prefetch(items[1])
        ctxs = {0: front(items[0], loads[0])}
        for ii in range(n_it):
            if ii + 2 < n_it:
                loads[ii + 2] = prefetch(items[ii + 2])
            same = ii + 1 < n_it and items[ii + 1][0]["cond"] == items[ii][0]["cond"]
            if ii + 1 < n_it and same:
                ctxs[ii + 1] = front(items[ii + 1], loads[ii + 1])
            back(items[ii], ctxs[ii])
            if ii + 1 < n_it and not same:
                ctxs[ii + 1] = front(items[ii + 1], loads[ii + 1])
        end_phase(k)


def phase_attn(k, l, s):
    nc, p, I, O = k.nc, k.p, k.I, k.O
    L, ctx = s["L"], s["ctx"]
    PAST = k.cfg["PAST"]
    nb = L // 128
    SC = 0.125
    with contextlib.ExitStack() as st:
        sbt, pst = k.sbt, k.pst
        kT = sbt(st, "akT", [128, 2, L], BF16)
        vsb = sbt(st, "avsb", [128, nb, 128], BF16)
        vdup = sbt(st, "avdup", [128, nb, 2, 128], BF16)
        esk = sbt(st, "aesk", [128, 8], F32)
        mkf = sbt(st, "amkf", [128, 2, 128], F32)
        mk = sbt(st, "amk", [128, 2, 128], BF16)
        p.dma('sp', kT[:], s["kT"].rearrange("g p t -> p g t"), writes=[kT])
        vsrc = s["v"].rearrange("(n p) f -> p n f", p=128)
        for n0 in range(0, nb, 4):
            n1 = min(nb, n0 + 4)
            p.dma('sp', vsb[:, n0:n1, :], vsrc[:, n0:n1, :], writes=[vsb])
        p.dma('sp', esk[:], I["sink_b"][l], writes=[esk])
        p.dma('sp', mkf[:], I["amask"][:, :, :], writes=[mkf])
        p.do('act', 'activation', esk[:], esk[:], AF.Exp, reads=[esk], writes=[esk])
        p.do('dve', 'tensor_copy', mk[:], mkf[:], reads=[mkf], writes=[mk])
        for g in range(2):
            for hf in range(2):
                p.do('dve' if hf == 0 else 'pool', 'tensor_copy', vdup[:, :, g, hf * 64:(hf + 1) * 64], vsb[:, :, g * 64:(g + 1) * 64], reads=[vsb], writes=[vdup])
        pst_ring = Ring([pst(st, "aST%d" % i, [128, 4, 128], F32) for i in range(2)])
        po_ring = Ring([pst(st, "aO%d" % i, [128, 4, 128], F32) for i in range(2)])
        ps_ring = Ring([pst(st, "aS%d" % i, [128, 4, 128], F32) for i in range(2)])
        nctx = 0
        if not ctx:
            nctx = PAST // 128
            ckf = sbt(st, "ackf", [128, nctx, 128], F32)
            cvf = sbt(st, "acvf", [128, nctx, 128], F32)
            ckd = sbt(st, "ackd", [128, nctx, 2, 128], BF16)
            cvd = sbt(st, "acvd", [128, nctx, 2, 128], BF16)
            ckT = sbt(st, "ackT", [128, 2, PAST], BF16)
            pck = pst(st, "apck", [128, 8, 128], BF16)
            p.dma('sp', ckf[:], I["cache_k"][l].rearrange("(n p) f -> p n f", p=128), writes=[ckf])
            p.dma('sp', cvf[:], I["cache_v"][l].rearrange("(n p) f -> p n f", p=128), writes=[cvf])
            for g in range(2):
                for hf in range(2):
                    p.do('dve', 'tensor_copy', ckd[:, :, g, hf * 64:(hf + 1) * 64], ckf[:, :, g * 64:(g + 1) * 64], reads=[ckf], writes=[ckd])
                    p.do('pool', 'tensor_copy', cvd[:, :, g, hf * 64:(hf + 1) * 64], cvf[:, :, g * 64:(g + 1) * 64], reads=[cvf], writes=[cvd])
            for j in range(nctx):
                for g in range(2):
                    p.do('pe', 'transpose', pck[:, j * 2 + g, :], ckd[:, j, g, :], k.identb[:], reads=[ckd, k.identb], writes=[pck])
            for j in range(nctx):
                for g in range(2):
                    p.do('act', 'copy', ckT[:, g, j * 128:(j + 1) * 128], pck[:, j * 2 + g, :], reads=[pck], writes=[ckT])
        qr = Ring([sbt(st, "aq%d" % i, [128, 4, 128], BF16) for i in range(2)])
        qzr = Ring([sbt(st, "aqz%d" % i, [128, 4, 2, 128], BF16) for i in range(2)])
        for qz_ in qzr.bufs:
            p.do('pool', 'memset', qz_[:], 0.0, writes=[qz_])
        er = Ring([sbt(st, "aE%d" % i, [128, 4, 128], BF16) for i in range(3)])
        den = sbt(st, "aden", [128, 4, 128], F32)
        outr = Ring([sbt(st, "aout%d" % i, [128, 4, 128], BF16) for i in range(2)])
        lvl = k.cfg.get('att_lvl', 9)
        for i in range(nb if lvl >= 2 else 0):
            cols = slice(i * 128, (i + 1) * 128)
            q = qr.next()
            ao = outr.next()
            p.dma('sp', q[:], s["qT"][:, :, cols].rearrange("c p t -> p c t"), writes=[q])
            qz = qzr.next()
            p.do('act', 'copy', qz[0:64, :, 0, :], q[0:64, :, :], reads=[q], writes=[qz])
            p.do('dve', 'tensor_copy', qz[64:128, :, 1, :], q[64:128, :, :], reads=[q], writes=[qz])
            for g in range(2):
                kbs = []
                if ctx:
                    for j in range(nb):
                        kbs.append((kT, j, vdup[:, j, g, :], None, vdup))
                else:
                    for j, mi in ((i - 1, 0), (i, None), (i + 1, 1)):
                        if 0 <= j < nb:
                            kbs.append((kT, j, vdup[:, j, g, :], mi, vdup))
                    for j in range(nctx):
                        kbs.append((ckT, j, cvd[:, j, g, :], None, cvd))
                po = po_ring.next()
                psm = ps_ring.next()
                pend = None
                for bi in range(len(kbs) + 1):
                    cur = None
                    if bi < len(kbs):
                        ksrc, j, vap, mi, vbuf = kbs[bi]
                        pS = pst_ring.next()
                        for hh in range(4):
                            h = 4 * g + hh
                            hf, c = h % 2, h // 2
                            p.do('pe', 'matmul', pS[:, hh, :], ksrc[:, g, j * 128:(j + 1) * 128], qz[:, c, hf, :], start=True, stop=True, reads=[ksrc, qz], writes=[pS])
                        E = er.next()
                        p.do('act', 'activation', E[:], pS[:], AF.Exp, scale=SC, reads=[pS], writes=[E])
                        if mi is not None:
                            p.do('dve', 'tensor_tensor', E[:], E[:], bcast(mk[:, mi:mi + 1, :], [128, 4, 128]), ALU.mult, reads=[E, mk], writes=[E])
                        cur = (E, vap, vbuf, bi)
                    if pend is not None:
                        E_, vap_, vbuf_, b_ = pend
                        first, last = b_ == 0, b_ == len(kbs) - 1
                        p.do('pe', 'matmul', po[:].rearrange("p a b -> p (a b)"), vap_, E_[:].rearrange("p a b -> p (a b)"), start=first, stop=last, reads=[vbuf_, E_], writes=[po])
                        p.do('pe', 'matmul', psm[:].rearrange("p a b -> p (a b)"), k.ones_b[:], E_[:].rearrange("p a b -> p (a b)"), start=first, stop=last, reads=[k.ones_b, E_], writes=[psm])
                    pend = cur
                if lvl < 5:
                    continue
                p.do('dve', 'tensor_tensor', den[:], psm[:], bcast(esk[:, 4 * g:4 * g + 4].unsqueeze(2), [128, 4, 128]), ALU.add, reads=[psm, esk], writes=[den])
                p.do('dve', 'reciprocal', den[:], den[:], reads=[den], writes=[den])
                for hf in range(2):
                    ps_ = slice(64 * hf, 64 * hf + 64)
                    p.do('dve', 'tensor_tensor', ao[ps_, 2 * g:2 * g + 2, :], po[ps_, hf::2, :], den[ps_, hf::2, :], ALU.mult, reads=[po, den], writes=[ao])
            if lvl >= 6:
                p.dma('sp', s["mixT"][0:4, :, cols].rearrange("c p t -> p c t"), ao[:], reads=[ao])
        end_phase(k)


def ssm_scan_levels(L):
    K_ = int(math.log2(L))
    ops = []
    for kk in range(K_):
        s_ = 2 ** (kk + 1)
        ops.append((kk, 2 ** kk - 1, s_ - 1, s_, L // s_))
    for kk in range(K_ - 2, -1, -1):
        s_ = 2 ** (kk + 1)
        n = L // s_ - 1
        if n > 0:
            ops.append((kk, s_ - 1, s_ + 2 ** kk - 1, s_, n))
    return ops


def phase_ssm(k, l, s):
    nc, p, I, O = k.nc, k.p, k.I, k.O
    L, ctx = s["L"], s["ctx"]
    TB = min(512, L)
    nblk = L // TB
    NLV = int(math.log2(L))
    with contextlib.ExitStack() as st:
        sbt, pst = k.sbt, k.pst
        lam = sbt(st, "slam", [128, 3, 16], F32)
        dtm = sbt(st, "sdt", [128, 16], F32)
        res_ = sbt(st, "sres", [128, 16], F32)
        ims = sbt(st, "sims", [128, 16], F32)
        t = [sbt(st, "st%d" % i, [128, 16], F32) for i in range(6)]
        A = sbt(st, "sA", [128, 16, 12, 2], F32)
        nAi = sbt(st, "snAi", [128, 16, 12], F32)
        fre = sbt(st, "sfre", [128, 16], F32)
        fim = sbt(st, "sfim", [128, 16], F32)
        nfim = sbt(st, "snfim", [128, 16], F32)
        p.dma('sp', lam[:], I["lam"][l], writes=[lam])
        p.do('act', 'activation', dtm[:], lam[:, 2, :], AF.Exp, reads=[lam], writes=[dtm])
        p.do('dve', 'tensor_tensor', res_[:], lam[:, 0, :], dtm[:], ALU.mult, reads=[lam, dtm], writes=[res_])
        p.do('dve', 'tensor_tensor', ims[:], lam[:, 1, :], dtm[:], ALU.mult, reads=[lam, dtm], writes=[ims])
        mag, s8, sh, c8, x_, y_ = t
        p.do('act', 'activation', mag[:], res_[:], AF.Exp, scale=0.125, reads=[res_], writes=[mag])
        p.do('act', 'activation', s8[:], ims[:], AF.Sin, scale=0.125, reads=[ims], writes=[s8])
        p.do('act', 'activation', sh[:], ims[:], AF.Sin, scale=0.0625, reads=[ims], writes=[sh])
        p.do('dve', 'tensor_tensor', c8[:], sh[:], sh[:], ALU.mult, reads=[sh], writes=[c8])
        p.do('dve', 'tensor_scalar', c8[:], c8[:], -2.0, 1.0, ALU.mult, ALU.add, reads=[c8], writes=[c8])
        p.do('dve', 'tensor_tensor', x_[:], mag[:], c8[:], ALU.mult, reads=[mag, c8], writes=[x_])
        p.do('dve', 'tensor_tensor', y_[:], mag[:], s8[:], ALU.mult, reads=[mag, s8], writes=[y_])
        xb, yb = Buf(x_.ap, "x"), Buf(y_.ap, "y")

        def csquare(dst_re, dst_im, src_re, src_im, bufs_r, bufs_w):
            p.do('dve', 'tensor_tensor', mag[:], src_re, src_re, ALU.mult, reads=bufs_r, writes=[mag])
            p.do('dve', 'tensor_tensor', s8[:], src_im, src_im, ALU.mult, reads=bufs_r, writes=[s8])
            p.do('dve', 'scalar_tensor_tensor', sh[:], src_re, 2.0, src_im, ALU.mult, ALU.mult, reads=bufs_r, writes=[sh])
            p.do('dve', 'tensor_tensor', dst_re, mag[:], s8[:], ALU.subtract, reads=[mag, s8], writes=bufs_w)
            p.do('dve', 'tensor_copy', dst_im, sh[:], reads=[sh], writes=bufs_w)

        for _ in range(2):
            csquare(x_[:], y_[:], x_[:], y_[:], [x_, y_], [x_, y_])
        csquare(A[:, :, 0, 0], A[:, :, 0, 1], x_[:], y_[:], [x_, y_], [A])
        for kk in range(1, 12):
            csquare(A[:, :, kk, 0], A[:, :, kk, 1], A[:, :, kk - 1, 0], A[:, :, kk - 1, 1], [A], [A])
        p.do('dve', 'tensor_scalar', nAi[:], A[:, :, :, 1], -1.0, None, ALU.mult, reads=[A], writes=[nAi])
        nr, den, rr_, q1 = t[0], t[1], t[2], t[3]
        p.do('dve', 'tensor_scalar', nr[:], A[:, :, 0, 0], -1.0, None, ALU.add, reads=[A], writes=[nr])
        p.do('dve', 'tensor_tensor', den[:], lam[:, 0, :], lam[:, 0, :], ALU.mult, reads=[lam], writes=[den])
        p.do('dve', 'tensor_tensor', q1[:], lam[:, 1, :], lam[:, 1, :], ALU.mult, reads=[lam], writes=[q1])
        p.do('dve', 'tensor_tensor', den[:], den[:], q1[:], ALU.add, reads=[den, q1], writes=[den])
        p.do('dve', 'reciprocal', den[:], den[:], reads=[den], writes=[den])
        p.do('dve', 'tensor_tensor', fre[:], nr[:], lam[:, 0, :], ALU.mult, reads=[nr, lam], writes=[fre])
        p.do('dve', 'tensor_tensor', q1[:], A[:, :, 0, 1], lam[:, 1, :], ALU.mult, reads=[A, lam], writes=[q1])
        p.do('dve', 'tensor_tensor', fre[:], fre[:], q1[:], ALU.add, reads=[fre, q1], writes=[fre])
        p.do('dve', 'tensor_tensor', fre[:], fre[:], den[:], ALU.mult, reads=[fre, den], writes=[fre])
        p.do('dve', 'tensor_tensor', fim[:], A[:, :, 0, 1], lam[:, 0, :], ALU.mult, reads=[A, lam], writes=[fim])
        p.do('dve', 'tensor_tensor', q1[:], nr[:], lam[:, 1, :], ALU.mult, reads=[nr, lam], writes=[q1])
        p.do('dve', 'tensor_tensor', fim[:], fim[:], q1[:], ALU.subtract, reads=[fim, q1], writes=[fim])
        p.do('dve', 'tensor_tensor', fim[:], fim[:], den[:], ALU.mult, reads=[fim, den], writes=[fim])
        p.do('dve', 'tensor_scalar', nfim[:], fim[:], -1.0, None, ALU.mult, reads=[fim], writes=[nfim])
        lvl = k.cfg.get('att_lvl', 9)
        if lvl < 2:
            end_phase(k)
            return
        BbT = sbt(st, "sBbT", [128, 16, 2, 128], BF16)
        Ct = sbt(st, "sCt", [128, 16, 2, 128], F32)
        p.dma('sp', Ct[:], I["ssm_ct"][l], writes=[Ct])
        p.do('dve', 'tensor_scalar', Ct[:, :, 1, :], Ct[:, :, 1, :], -1.0, None, ALU.mult, reads=[Ct], writes=[Ct])
        btr = Ring([sbt(st, "sbt%d" % i, [128, 2, 128], F32) for i in range(2)])
        bbr = Ring([sbt(st, "sbb%d" % i, [128, 2, 128], F32) for i in range(2)])
        ptr = Ring([pst(st, "sptr%d" % i, [128, 4, 128], F32) for i in range(2)])
        for inst in range(16):
            bt_, bb, pt = btr.next(), bbr.next(), ptr.next()
            p.dma('sp', bt_[:], I["ssm_bt"][l][:, inst, :, :], writes=[bt_])
            fr, fi, nfi = fre[:, inst:inst + 1], fim[:, inst:inst + 1], nfim[:, inst:inst + 1]
            p.do('dve', 'tensor_scalar', bb[:, 0, :], bt_[:, 0, :], fr, None, ALU.mult, reads=[bt_, fre], writes=[bb])
            p.do('dve', 'scalar_tensor_tensor', bb[:, 0, :], bt_[:, 1, :], nfi, bb[:, 0, :], ALU.mult, ALU.add, reads=[bt_, nfim, bb], writes=[bb])
            p.do('dve', 'tensor_scalar', bb[:, 1, :], bt_[:, 1, :], fr, None, ALU.mult, reads=[bt_, fre], writes=[bb])
            p.do('dve', 'scalar_tensor_tensor', bb[:, 1, :], bt_[:, 0, :], fi, bb[:, 1, :], ALU.mult, ALU.add, reads=[bt_, fim, bb], writes=[bb])
            for ri in range(2):
                p.do('pe', 'transpose', pt[:, ri, :], bb[:, ri, :], k.ident[:], reads=[bb, k.ident], writes=[pt])
            p.do('act', 'copy', BbT[:, inst, :, :], pt[:, 0:2, :], reads=[pt], writes=[BbT])
        if lvl < 3:
            end_phase(k)
            return
        uT = sbt(st, "suT", [128, 2, L], BF16)
        yT = sbt(st, "syT", [128, 2, L], F32)
        p.dma('sp', uT[:], s["fT"][0:2].rearrange("c p t -> p c t"), writes=[uT])
        Hr = Ring([sbt(st, "sH%d" % i, [128, 2, L], F32) for i in range(2)])
        pbr = Ring([pst(st, "spb%d" % i, [128, 512], F32) for i in range(3)])
        pyr = Ring([pst(st, "spy%d" % i, [128, 512], F32) for i in range(2)])
        if not ctx:
            h0 = sbt(st, "sh0", [128, 16, 2], F32)
            t1s = [sbt(st, "sht1%d" % i, [128, 2], F32) for i in range(2)]
            p.dma('sp', h0[:], I["ssm_h0"][l], writes=[h0])
        sched = ssm_scan_levels(L)
        Hparts = {id(H_): (Buf(None, 're'), Buf(None, 'im')) for H_ in Hr.bufs}
        for gp in range(8):
            chunk, q = gp // 4, gp % 4
            Hs = [Hr.next(), Hr.next()]
            for d in range(2):
                inst = gp * 2 + d
                H = Hs[d]
                for b in range(nblk):
                    blk = slice(b * TB, (b + 1) * TB)
                    for ri in range(2):
                        pb = pbr.next()
                        p.do('pe', 'matmul', pb[:, 0:TB], BbT[:, inst, ri, :], uT[:, chunk, blk], start=True, stop=True, reads=[BbT, uT], writes=[pb])
                        p.do('act', 'copy', H[:, ri, blk], pb[:, 0:TB], reads=[pb], writes=[H])
                if not ctx:
                    pos = 0 if d == 0 else L - 1
                    ar, ai, nai = A[:, inst, 0, 0:1], A[:, inst, 0, 1:2], nAi[:, inst, 0:1]
                    t1 = t1s[d]
                    p.do('dve', 'scalar_tensor_tensor', t1[:, 0:1], h0[:, inst, 0:1], ar, H[:, 0, pos:pos + 1], ALU.mult, ALU.add, reads=[h0, A, H], writes=[t1])
                    p.do('dve', 'scalar_tensor_tensor', t1[:, 1:2], h0[:, inst, 1:2], ar, H[:, 1, pos:pos + 1], ALU.mult, ALU.add, reads=[h0, A, H], writes=[t1])
                    p.do('dve', 'scalar_tensor_tensor', H[:, 0, pos:pos + 1], h0[:, inst, 1:2], nai, t1[:, 0:1], ALU.mult, ALU.add, reads=[h0, nAi, t1], writes=[H])
                    p.do('dve', 'scalar_tensor_tensor', H[:, 1, pos:pos + 1], h0[:, inst, 0:1], ai, t1[:, 1:2], ALU.mult, ALU.add, reads=[h0, A, t1], writes=[H])
            for (kk, rr0, rw0, sd, cnt) in (sched if lvl >= 4 else []):
                for d in range(2):
                    inst = gp * 2 + d
                    H = Hs[d]
                    if d == 0:
                        rs = slice(rr0, rr0 + (cnt - 1) * sd + 1, sd)
                        ws = slice(rw0, rw0 + (cnt - 1) * sd + 1, sd)
                    else:
                        a_r = L - 1 - (rr0 + (cnt - 1) * sd)
                        a_w = L - 1 - (rw0 + (cnt - 1) * sd)
                        rs = slice(a_r, a_r + (cnt - 1) * sd + 1, sd)
                        ws = slice(a_w, a_w + (cnt - 1) * sd + 1, sd)
                    ar, ai, nai = A[:, inst, kk, 0:1], A[:, inst, kk, 1:2], nAi[:, inst, kk:kk + 1]
                    Hre, Him = Hparts[id(H)]
                    p.do('dve', 'scalar_tensor_tensor', H[:, :, ws], H[:, :, rs], ar, H[:, :, ws], ALU.mult, ALU.add, reads=[H, Hre, Him, A], writes=[Hre, Him])
                    p.do('dve', 'scalar_tensor_tensor', H[:, 0, ws], H[:, 1, rs], nai, H[:, 0, ws], ALU.mult, ALU.add, reads=[H, Hre, Him, nAi], writes=[Hre])
                    p.do('dve', 'scalar_tensor_tensor', H[:, 1, ws], H[:, 0, rs], ai, H[:, 1, ws], ALU.mult, ALU.add, reads=[H, Him, Hre, A], writes=[Him])
            if lvl < 5:
                continue
            if ctx:
                for d in range(2):
                    pos = L - 1 if d == 0 else 0
                    for ri in range(2):
                        dst = O["nssm"][s["pi"], l, d, ri, 2 * gp:2 * gp + 2, :].rearrange("g (p o) -> (g p) o", o=1)
                        p.dma('sp', dst, Hs[d][:, ri, pos:pos + 1], reads=[Hs[d]] + list(Hparts[id(Hs[d])]))
            for b in range(nblk):
                blk = slice(b * TB, (b + 1) * TB)
                py = pyr.next()
                n_ = 0
                for d in range(2):
                    inst = gp * 2 + d
                    for ri in range(2):
                        p.do('pe', 'matmul', py[:, 0:TB], Ct[:, inst, ri, :], Hs[d][:, ri, blk], start=(n_ == 0), stop=(n_ == 3), reads=[Ct, Hs[d]] + list(Hparts[id(Hs[d])]), writes=[py])
                        n_ += 1
                if q == 0:
                    p.do('act', 'copy', yT[:, chunk, blk], py[:, 0:TB], reads=[py], writes=[yT])
                else:
                    p.do('dve', 'tensor_tensor', yT[:, chunk, blk], py[:, 0:TB], yT[:, chunk, blk], ALU.add, reads=[py, yT], writes=[yT])
        if lvl < 6:
            end_phase(k)
            return
        dT = sbt(st, "sdT", [128, 2], F32)
        gb = sbt(st, "sgb", [128, 2], F32)
        wg = sbt(st, "swg", [128, 2, 256], BF16)
        p.dma('sp', dT[:], I["ssm_dT"][l], writes=[dT])
        p.dma('sp', gb[:], I["glu_bT"][l], writes=[gb])
        p.dma('pool', wg[:], I["w_glu"][l].rearrange("(c p) n -> p c n", p=128), writes=[wg])
        zT = sbt(st, "szT", [128, 2, L], BF16)
        ytr = Ring([sbt(st, "syt%d" % i, [128, 512], F32) for i in range(2)])
        u1r = Ring([sbt(st, "su1%d" % i, [128, 512], F32) for i in range(2)])
        sgr = Ring([sbt(st, "ssg%d" % i, [128, 512], F32) for i in range(2)])
        outr = Ring([sbt(st, "sout%d" % i, [128, 512], BF16) for i in range(2)])
        for b in range(nblk):
            blk = slice(b * TB, (b + 1) * TB)
            for c in range(2):
                yt, u1, sg = ytr.next(), u1r.next(), sgr.next()
                p.do('dve', 'scalar_tensor_tensor', yt[:, 0:TB], uT[:, c, blk], dT[:, c:c + 1], yT[:, c, blk], ALU.mult, ALU.add, reads=[uT, dT, yT], writes=[yt])
                p.do('pool', 'tensor_tensor', u1[:, 0:TB], yt[:, 0:TB], yt[:, 0:TB], ALU.mult, reads=[yt], writes=[u1])
                p.do('pool', 'tensor_scalar', u1[:, 0:TB], u1[:, 0:TB], 0.044715, 1.0, ALU.mult, ALU.add, reads=[u1], writes=[u1])
                p.do('pool', 'tensor_tensor', u1[:, 0:TB], u1[:, 0:TB], yt[:, 0:TB], ALU.mult, reads=[u1, yt], writes=[u1])
                p.do('act', 'activation', sg[:, 0:TB], u1[:, 0:TB], AF.Sigmoid, scale=1.5957691216057308, reads=[u1], writes=[sg])
                p.do('dve', 'tensor_tensor', zT[:, c, blk], sg[:, 0:TB], yt[:, 0:TB], ALU.mult, reads=[sg, yt], writes=[zT])
            for mo in range(2):
                pg = pbr.next()
                sg = sgr.next()
                ot = outr.next()
                for kc in range(2):
                    p.do('pe', 'matmul', pg[:, 0:TB], wg[:, kc, mo * 128:(mo + 1) * 128], zT[:, kc, blk], start=(kc == 0), stop=(kc == 1), reads=[wg, zT], writes=[pg])
                p.do('act', 'activation', sg[:, 0:TB], pg[:, 0:TB], AF.Sigmoid, bias=gb[:, mo:mo + 1], reads=[pg, gb], writes=[sg])
                p.do('dve', 'tensor_tensor', ot[:, 0:TB], sg[:, 0:TB], zT[:, mo, blk], ALU.mult, reads=[sg, zT], writes=[ot])
                p.dma('sp', s["mixT"][4 + mo, :, blk], ot[:, 0:TB], reads=[ot])
        end_phase(k)


def phase_gdn(k, l, s):
    nc, p, I, O = k.nc, k.p, k.I, k.O
    L, ctx = s["L"], s["ctx"]
    nC = L // 64
    TB = min(512, L)
    with contextlib.ExitStack() as st:
        sbt, pst = k.sbt, k.pst
        qkv = sbt(st, "gqkv", [128, 6, L], BF16)
        kz = sbt(st, "gkz", [128, 4, L], BF16)
        bank = Ring([pst(st, "gbank%d" % i, [128, 512], F32) for i in range(7)])
        pbf = pst(st, "gpbf", [128, 1024], BF16)
        st1 = contextlib.ExitStack()
        xp = sbt(st1, "gxp", [128, 6, L + 2], BF16)
        cw = sbt(st1, "gcw", [128, 6, 3], F32)
        bo = sbt(st1, "gbo", [128, 128], BF16)
        p.dma('sp', cw[:], I["conv_wT"][l], writes=[cw])
        p.do('pool', 'memset', xp[:, :, 0:1], 0.0, writes=[xp])
        p.do('pool', 'memset', xp[:, :, L + 1:L + 2], 0.0, writes=[xp])
        for c in range(6):
            p.dma('sp', xp[:, c, 1:L + 1], s["fT"][2 + c, :, :], writes=[xp])
        p.do('pool', 'memset', bo[:], 0.0, writes=[bo])
        p.do('pool', 'memset', bo[0:64, 0:64], 1.0, writes=[bo])
        p.do('pool', 'memset', bo[64:128, 64:128], 1.0, writes=[bo])
        accr = Ring([sbt(st1, "gacc%d" % i, [128, 512], F32) for i in range(2)])
        silr = Ring([sbt(st1, "gsil%d" % i, [128, 512], F32) for i in range(2)])
        sqr = Ring([sbt(st1, "gsq%d" % i, [128, 512], BF16) for i in range(2)])
        rnr = Ring([sbt(st1, "grn%d" % i, [128, 512], F32) for i in range(2)])
        for c in range(6):
            for b in range(L // TB):
                cs = b * TB
                acc, sil = accr.next(), silr.next()
                p.do('dve', 'tensor_scalar', acc[:, 0:TB], xp[:, c, cs:cs + TB], cw[:, c, 0:1], None, ALU.mult, reads=[xp, cw], writes=[acc])
                p.do('dve', 'scalar_tensor_tensor', acc[:, 0:TB], xp[:, c, cs + 1:cs + 1 + TB], cw[:, c, 1:2], acc[:, 0:TB], ALU.mult, ALU.add, reads=[xp, cw, acc], writes=[acc])
                p.do('dve', 'scalar_tensor_tensor', acc[:, 0:TB], xp[:, c, cs + 2:cs + 2 + TB], cw[:, c, 2:3], acc[:, 0:TB], ALU.mult, ALU.add, reads=[xp, cw, acc], writes=[acc])
                if c >= 4:
                    p.do('act', 'activation', qkv[:, c, cs:cs + TB], acc[:, 0:TB], AF.Silu, reads=[acc], writes=[qkv])
                    continue
                sq, rn, pb = sqr.next(), rnr.next(), bank.next()
                p.do('act', 'activation', sil[:, 0:TB], acc[:, 0:TB], AF.Silu, reads=[acc], writes=[sil])
                p.do('pool', 'tensor_tensor', sq[:, 0:TB], sil[:, 0:TB], sil[:, 0:TB], ALU.mult, reads=[sil], writes=[sq])
                p.do('pe', 'matmul', pb[:, 0:TB], bo[:], sq[:, 0:TB], start=True, stop=True, reads=[bo, sq], writes=[pb])
                p.do('act', 'activation', rn[:, 0:TB], pb[:, 0:TB], AF.Sqrt, bias=EPS, reads=[pb], writes=[rn])
                p.do('dve', 'reciprocal', rn[:, 0:TB], rn[:, 0:TB], reads=[rn], writes=[rn])
                if c < 2:
                    p.do('dve', 'scalar_tensor_tensor', qkv[:, c, cs:cs + TB], sil[:, 0:TB], 0.125, rn[:, 0:TB], ALU.mult, ALU.mult, reads=[sil, rn], writes=[qkv])
                else:
                    p.do('dve', 'tensor_tensor', qkv[:, c, cs:cs + TB], sil[:, 0:TB], rn[:, 0:TB], ALU.mult, reads=[sil, rn], writes=[qkv])
        p.do('pool', 'memset', kz[:], 0.0, writes=[kz])
        for h_ in range(4):
            hs_ = slice(64 * (h_ % 2), 64 * (h_ % 2) + 64)
            p.do('dve' if h_ % 2 == 0 else 'pool', 'tensor_copy', kz[hs_, h_, :], qkv[hs_, 2 + h_ // 2, :], reads=[qkv, kz], writes=[kz])
        p.barrier()
        p.flush()
        st1.close()
        st2 = contextlib.ExitStack()
        gt = sbt(st2, "ggt", [64, nC, 16], F32)
        ab = sbt(st2, "gab", [64, 2, 8], F32)
        gm = sbt(st2, "ggm", [64, 6, 8, 64], F32)
        gtri = sbt(st2, "ggtri", [64, 4, 64], F32)
        p.dma('sp', gt[:], s["gates"][:, :, :], writes=[gt])
        p.dma('sp', ab[:], I["gdn_ab"][l], writes=[ab])
        p.dma('sp', gm[:], I["gmask"][:, :, :, :], writes=[gm])
        p.dma('sp', gtri[:], I["gtri"][:, :, :], writes=[gtri])
        names = ["gG", "gBeta", "gGs", "gBetas", "gGc", "gEgc", "gBg", "gGl", "gEgl", "gKd", "gT0"]
        T_ = {n: sbt(st2, n, [64, nC, 8], F32) for n in names}
        g_, be_, gS, beS, gcS, egcS, bgS, glS, eglS, kdS, t0 = [T_[n] for n in names]
        Aexp = sbt(st2, "gAexp", [64, 8], F32)
        p.do('act', 'activation', Aexp[:], ab[:, 0, :], AF.Exp, reads=[ab], writes=[Aexp])
        p.do('dve', 'tensor_tensor', t0[:], gt[:, :, 0:8], bcast(ab[:, 1:2, :], [64, nC, 8]), ALU.add, reads=[gt, ab], writes=[t0])
        p.do('act', 'activation', t0[:], t0[:], AF.Exp, reads=[t0], writes=[t0])
        p.do('act', 'activation', t0[:], t0[:], AF.Ln, bias=1.0, reads=[t0], writes=[t0])
        p.do('dve', 'tensor_tensor', g_[:], t0[:], bcast(Aexp[:].unsqueeze(1), [64, nC, 8]), ALU.mult, reads=[t0, Aexp], writes=[g_])
        p.do('dve', 'tensor_scalar', g_[:], g_[:], -1.0, None, ALU.mult, reads=[g_], writes=[g_])
        p.do('act', 'activation', be_[:], gt[:, :, 8:16], AF.Sigmoid, reads=[gt], writes=[be_])
        p.do('pool', 'tensor_copy', gS[:, :, 0:4], g_[:, :, 0:4], reads=[g_], writes=[gS])
        p.do('pool', 'tensor_copy', beS[:, :, 0:4], be_[:, :, 0:4], reads=[be_], writes=[beS])
        for sidx in range(nC):
            cb = nC - 1 - sidx
            p.do('pool', 'tensor_copy', gS[:, sidx, 4:8], g_[:, cb, 4:8], reads=[g_], writes=[gS])
            p.do('pool', 'tensor_copy', beS[:, sidx, 4:8], be_[:, cb, 4:8], reads=[be_], writes=[beS])
        NG = nC * 8
        def dir_matmul(dst, src, i_f, i_b):
            pa, pb_ = bank.next(), bank.next()
            flat = src[:].rearrange("p c e -> p (c e)")
            p.do('pe', 'matmul', pa[0:64, 0:NG], gtri[:, i_f, :], flat, start=True, stop=True, reads=[gtri, src], writes=[pa])
            p.do('pe', 'matmul', pb_[0:64, 0:NG], gtri[:, i_b, :], flat, start=True, stop=True, reads=[gtri, src], writes=[pb_])
            p.do('dve', 'tensor_copy', dst[:, :, 0:4], pa[0:64, 0:NG].rearrange("p (c e) -> p c e", e=8)[:, :, 0:4], reads=[pa], writes=[dst])
            p.do('dve', 'tensor_copy', dst[:, :, 4:8], pb_[0:64, 0:NG].rearrange("p (c e) -> p c e", e=8)[:, :, 4:8], reads=[pb_], writes=[dst])

        dir_matmul(gcS, gS, 0, 1)
        p.do('act', 'activation', egcS[:], gcS[:], AF.Exp, reads=[gcS], writes=[egcS])
        p.do('dve', 'tensor_tensor', bgS[:], beS[:], egcS[:], ALU.mult, reads=[beS, egcS], writes=[bgS])
        dir_matmul(glS, gcS, 2, 3)
        p.do('act', 'activation', eglS[:], glS[:], AF.Exp, reads=[glS], writes=[eglS])
        p.do('dve', 'tensor_tensor', kdS[:], glS[:], gcS[:], ALU.subtract, reads=[glS, gcS], writes=[kdS])
        p.do('act', 'activation', kdS[:], kdS[:], AF.Exp, reads=[kdS], writes=[kdS])
        use_r = k.cfg.get('fp32r', False)

        off = k.cfg.get('r32_off', '')

        def r32(ap):
            return ap.bitcast(mybir.dt.float32r) if (use_r and 'u' not in off) else ap

        def r32l(ap):
            return ap.bitcast(mybir.dt.float32r) if (use_r and 'l' not in off) else ap

        def r32s(ap):
            return ap.bitcast(mybir.dt.float32r) if (use_r and 's' not in off) else ap

        S = sbt(st2, "gS", [64, 8, 64], F32)
        Sb = sbt(st2, "gSb", [64, 8, 64], BF16)
        S0t = sbt(st2, "gS0t", [64, 8, 64], F32)
        if ctx:
            p.do('pool', 'memset', S0t[:], 0.0, writes=[S0t])
        else:
            p.dma('sp', S0t[:], I["gdn_s0"][l], writes=[S0t])
        p.do('dve', 'tensor_copy', r32s(S[:]), S0t[:], reads=[S0t], writes=[S])
        p.do('pool', 'tensor_copy', Sb[:], S[:], reads=[S], writes=[Sb])

        def T3(name, dt=F32, n=2):
            return Ring([sbt(st2, "%s%d" % (name, i), [64, 8, 64], dt) for i in range(n)])

        dgR, XR, E1R, E2R = T3("gdg", n=4), T3("gX"), T3("gE1"), T3("gE2")
        NR, NtR, AtR, AccR = T3("gN", n=4), T3("gNt", n=4), T3("gAt"), T3("gAcc")
        VbR, RR, KdR = T3("gVb"), T3("gR"), T3("gKd_")
        UR, WTR, VnR, OR, TmR = T3("gU"), T3("gWT"), T3("gVn"), T3("gO"), T3("gTm")
        I8 = gm[:, 4, :, :]
        qodR = Ring([sbt(st2, "gqod%d" % i, [64, 2, 2, 64], BF16) for i in range(2)])

        def bview(b):
            return b[0:64, :].rearrange("p (e j) -> p e j", e=8)

        def colb(tab, sidx):
            return bcast(tab[:, sidx, :].unsqueeze(2), [64, 8, 64])

        def stage1(sidx, C):
            ce = [sidx if e < 4 else nC - 1 - sidx for e in range(8)]
            tk = [slice(ce[e] * 64, ce[e] * 64 + 64) for e in range(8)]
            qT = [qkv[64 * (e % 4 % 2):64 * (e % 4 % 2) + 64, 0 + (e % 4) // 2, tk[e]] for e in range(8)]
            pt4 = pbf[0:64, :].rearrange("p (a j) -> p a j", a=8)
            ptv = pbf[0:64, :].rearrange("p (a j) -> p a j", a=16)
            for kind in range(2):
                for d_ in range(2):
                    for c_ in range(2):
                        e0 = d_ * 4 + c_ * 2
                        p.do('pe', 'transpose', pt4[:, kind * 4 + d_ * 2 + c_, :], qkv[:, 2 + 2 * kind + c_, tk[e0]], k.identb[:], reads=[qkv, k.identb], writes=[pbf])
            Vb, R_, Kd = VbR.next(), RR.next(), KdR.next()
            p.do('dve', 'tensor_tensor', r32(Vb[:]), ptv[:, 8:16, :], colb(beS, sidx), ALU.mult, reads=[pbf, beS], writes=[Vb])
            p.do('dve', 'tensor_tensor', r32(R_[:]), ptv[:, 0:8, :], colb(bgS, sidx), ALU.mult, reads=[pbf, bgS], writes=[R_])
            p.do('dve', 'tensor_tensor', r32s(Kd[:]), ptv[:, 0:8, :], colb(kdS, sidx), ALU.mult, reads=[pbf, kdS], writes=[Kd])
            yield
            dg, X, E1, E2 = dgR.next(), XR.next(), E1R.next(), E2R.next()
            PGb, PBb = bank.next(), bank.next()
            PG, PB = bview(PGb), bview(PBb)
            p.do('pool', 'tensor_tensor', dg[:], I8, colb(gcS, sidx), ALU.mult, reads=[gm, gcS], writes=[dg])
            for e in range(8):
                p.do('pe', 'matmul', PG[:, e, :], k.ones_f[0:64, 0:64], dg[:, e, :], start=True, stop=True, reads=[k.ones_f, dg], writes=[PGb])
            p.do('dve', 'tensor_tensor', X[:], PG, colb(gcS, sidx), ALU.subtract, reads=[PGb, gcS], writes=[X])
            yield
            p.do('act', 'activation', E1[:], X[:], AF.Relu, reads=[X], writes=[E1])
            p.do('act', 'activation', E2[:], X[:], AF.Relu, scale=-1.0, reads=[X], writes=[E2])
            p.do('act', 'activation', E1[:], E1[:], AF.Exp, scale=-1.0, reads=[E1], writes=[E1])
            p.do('act', 'activation', E2[:], E2[:], AF.Exp, scale=-1.0, reads=[E2], writes=[E2])
            yield
            A1b, A2b = bank.next(), bank.next()
            A1, A2 = bview(A1b), bview(A2b)
            for e in range(8):
                h_ = e % 4
                p.do('pe', 'matmul', A1[:, e, :], kz[:, h_, tk[e]], qkv[:, 2 + h_ // 2, tk[e]], start=True, stop=True, reads=[kz, qkv], writes=[A1b])
            for e in range(8):
                h_ = e % 4
                p.do('pe', 'matmul', A2[:, e, :], kz[:, h_, tk[e]], qkv[:, 0 + h_ // 2, tk[e]], start=True, stop=True, reads=[kz, qkv], writes=[A2b])
            Ds, Dts, Dti = E1, X, E2
            p.do('dve', 'tensor_tensor', Ds[:], E1[:], gm[:, 0, :, :], ALU.mult, reads=[E1, gm], writes=[Ds])
            p.do('dve', 'tensor_tensor', Dts[:], E2[:], gm[:, 2, :, :], ALU.mult, reads=[E2, gm], writes=[Dts])
            p.do('dve', 'tensor_tensor', Dti[:], E2[:], gm[:, 3, :, :], ALU.mult, reads=[E2, gm], writes=[Dti])
            dg2 = dgR.next()
            p.do('pool', 'tensor_tensor', dg2[:], I8, colb(beS, sidx), ALU.mult, reads=[gm, beS], writes=[dg2])
            for e in range(8):
                p.do('pe', 'matmul', PB[:, e, :], k.ones_f[0:64, 0:64], dg2[:, e, :], start=True, stop=True, reads=[k.ones_f, dg2], writes=[PBb])
            N_, Nt, At, Acc = NR.next(), NtR.next(), AtR.next(), AccR.next()
            p.do('dve', 'tensor_tensor', r32l(N_[:]), A1, Ds[:], ALU.mult, reads=[A1b, Ds], writes=[N_])
            p.do('dve', 'tensor_tensor', r32l(N_[:]), N_[:], colb(beS, sidx), ALU.mult, reads=[N_, beS], writes=[N_])
            p.do('dve', 'tensor_tensor', r32l(Nt[:]), A1, Dts[:], ALU.mult, reads=[A1b, Dts], writes=[Nt])
            p.do('dve', 'tensor_tensor', r32l(Nt[:]), PB, Nt[:], ALU.mult, reads=[PBb, Nt], writes=[Nt])
            p.do('dve', 'tensor_tensor', r32s(At[:]), A2, Dti[:], ALU.mult, reads=[A2b, Dti], writes=[At])
            p.do('dve', 'scalar_tensor_tensor', r32(Acc[:]), Nt[:], -1.0, I8, ALU.mult, ALU.add, reads=[Nt, gm], writes=[Acc])
            yield
            P_, Pt = N_, Nt
            for lv in range(5):
                PPa, PPb, PAb = bank.next(), bank.next(), bank.next()
                Pn, Ptn = NR.next(), NtR.next()
                for e in range(8):
                    p.do('pe', 'matmul', bview(PPa)[:, e, :], r32l(Pt[:, e, :]), r32l(P_[:, e, :]), start=True, stop=True, reads=[Pt, P_], writes=[PPa])
                for e in range(8):
                    p.do('pe', 'matmul', bview(PPb)[:, e, :], r32l(P_[:, e, :]), r32l(Pt[:, e, :]), start=True, stop=True, reads=[Pt, P_], writes=[PPb])
                p.do('act', 'copy', r32l(Pn[:]), bview(PPa), reads=[PPa], writes=[Pn])
                p.do('act', 'copy', r32l(Ptn[:]), bview(PPb), reads=[PPb], writes=[Ptn])
                yield
                for e in range(8):
                    p.do('pe', 'matmul', bview(PAb)[:, e, :], r32(Pn[:, e, :]), r32(Acc[:, e, :]), start=True, stop=True, reads=[Pn, Acc], writes=[PAb])
                p.do('dve', 'tensor_tensor', r32(Acc[:]), Acc[:], bview(PAb), ALU.add, reads=[Acc, PAb], writes=[Acc])
                P_, Pt = Pn, Ptn
                yield
            PUb, PWb = bank.next(), bank.next()
            U_, WT = UR.next(), WTR.next()
            for e in range(8):
                p.do('pe', 'matmul', bview(PUb)[:, e, :], r32(Acc[:, e, :]), r32(Vb[:, e, :]), start=True, stop=True, reads=[Acc, Vb], writes=[PUb])
            for e in range(8):
                p.do('pe', 'matmul', bview(PWb)[:, e, :], r32(R_[:, e, :]), r32(Acc[:, e, :]), start=True, stop=True, reads=[Acc, R_], writes=[PWb])
            p.do('act', 'copy', U_[:], bview(PUb), reads=[PUb], writes=[U_])
            p.do('act', 'copy', r32s(WT[:]), bview(PWb), reads=[PWb], writes=[WT])
            C.update(dict(U_=U_, WT=WT, At=At, Kd=Kd, qT=qT))
            return

        def stage2(sidx, C):
            U_, WT, At, Kd, qT = C['U_'], C['WT'], C['At'], C['Kd'], C['qT']
            PWSb, POb, PO2b, PKVb = bank.next(), bank.next(), bank.next(), bank.next()
            Vn, Oo, Tm = VnR.next(), OR.next(), TmR.next()
            for e in range(8):
                p.do('pe', 'matmul', bview(PWSb)[:, e, :], r32s(WT[:, e, :]), r32s(S[:, e, :]), start=True, stop=True, reads=[WT, S], writes=[PWSb])
            p.do('dve', 'tensor_tensor', r32s(Vn[:]), U_[:], bview(PWSb), ALU.subtract, reads=[U_, PWSb], writes=[Vn])
            qod = qodR.next()
            for d_ in range(2):
                cc = sidx if d_ == 0 else nC - 1 - sidx
                p.do('act', 'copy', qod[:, d_, :, :], qkv[64:128, 0:2, cc * 64:(cc + 1) * 64], reads=[qkv], writes=[qod])
            for e in range(8):
                h_ = e % 4
                lq = qT[e] if h_ % 2 == 0 else qod[:, e // 4, h_ // 2, :]
                p.do('pe', 'matmul', bview(POb)[:, e, :], lq, Sb[:, e, :], start=True, stop=True, reads=[qkv, qod, Sb], writes=[POb])
            for e in range(8):
                p.do('pe', 'matmul', bview(PO2b)[:, e, :], r32s(At[:, e, :]), r32s(Vn[:, e, :]), start=True, stop=True, reads=[At, Vn], writes=[PO2b])
            for e in range(8):
                p.do('pe', 'matmul', bview(PKVb)[:, e, :], r32s(Kd[:, e, :]), r32s(Vn[:, e, :]), start=True, stop=True, reads=[Kd, Vn], writes=[PKVb])
            p.do('dve', 'tensor_tensor', Tm[:], bview(POb), colb(egcS, sidx), ALU.mult, reads=[POb, egcS], writes=[Tm])
            p.do('dve', 'tensor_tensor', Oo[:], Tm[:], bview(PO2b), ALU.add, reads=[Tm, PO2b], writes=[Oo])
            cf, cb = sidx, nC - 1 - sidx
            p.dma('sp', s["go"][0, cf * 64:(cf + 1) * 64, :].rearrange("t (h v) -> t h v", h=4), Oo[:, 0:4, :], reads=[Oo])
            p.dma('sp', s["go"][1, cb * 64:(cb + 1) * 64, :].rearrange("t (h v) -> t h v", h=4), Oo[:, 4:8, :], reads=[Oo])
            Tm2 = TmR.next()
            p.do('pool', 'tensor_tensor', Tm2[:], S[:], colb(eglS, sidx), ALU.mult, reads=[S, eglS], writes=[Tm2])
            p.do('dve', 'tensor_tensor', r32s(S[:]), Tm2[:], bview(PKVb), ALU.add, reads=[Tm2, PKVb], writes=[S])
            p.do('act', 'copy', Sb[:], S[:], reads=[S], writes=[Sb])

        GG = k.cfg.get("gdn_group", 2)
        for s0_ in range(0, nC, GG):
            grp = list(range(s0_, min(nC, s0_ + GG)))
            ctxs = {si: {} for si in grp}
            gens = [stage1(si, ctxs[si]) for si in grp]
            alive = list(gens)
            while alive:
                nxt = []
                for g_it in alive:
                    try:
                        next(g_it)
                        nxt.append(g_it)
                    except StopIteration:
                        pass
                alive = nxt
            for si in grp:
                stage2(si, ctxs[si])
        if ctx:
            p.dma('sp', O["ngdn"][s["pi"], l].rearrange("d h k v -> k (d h) v"), S[:], reads=[S])
        p.barrier()
        p.flush()
        st2.close()
        gz = sbt(st, "ggz", [128, 2, L], BF16)
        ng = sbt(st, "gng", [128, 1], F32)
        p.dma('sp', gz[:], s["fT"][8:10].rearrange("c p t -> p c t"), writes=[gz])
        p.dma('sp', ng[:], I["gdn_ng"][l], writes=[ng])
        p.do('act', 'activation', gz[:], gz[:], AF.Silu, reads=[gz], writes=[gz])
        o0r = Ring([sbt(st, "go0%d" % i, [128, 4, 64], F32) for i in range(2)])
        o1r = Ring([sbt(st, "go1%d" % i, [128, 4, 64], F32) for i in range(2)])
        sqo = sbt(st, "gsqo", [128, 4, 64], F32)
        ss4 = sbt(st, "gss4", [128, 4], F32)
        onr = Ring([sbt(st, "gon%d" % i, [128, 4, 64], BF16) for i in range(2)])
        outr = Ring([sbt(st, "gout%d" % i, [128, 2, 128], BF16) for i in range(2)])
        for t in range(L // 128):
            rows = slice(t * 128, (t + 1) * 128)
            o0, o1, on, ot = o0r.next(), o1r.next(), onr.next(), outr.next()
            p.dma('sp', o0[:], s["go"][0, rows, :].rearrange("t (h v) -> t h v", h=4), writes=[o0])
            p.dma('sp', o1[:], s["go"][1, rows, :].rearrange("t (h v) -> t h v", h=4), writes=[o1])
            p.do('pool', 'tensor_tensor', o0[:], o0[:], o1[:], ALU.add, reads=[o0, o1], writes=[o0])
            p.do('act', 'activation', sqo[:], o0[:], AF.Square, reads=[o0], writes=[sqo])
            p.do('dve', 'tensor_reduce', ss4[:], sqo[:], AX.X, ALU.add, reads=[sqo], writes=[ss4])
            p.do('act', 'activation', ss4[:], ss4[:], AF.Sqrt, scale=1.0 / 64, bias=EPS, reads=[ss4], writes=[ss4])
            p.do('dve', 'reciprocal', ss4[:], ss4[:], reads=[ss4], writes=[ss4])
            p.do('dve', 'tensor_tensor', on[:], o0[:], bcast(ss4[:].unsqueeze(2), [128, 4, 64]), ALU.mult, reads=[o0, ss4], writes=[on])
            pv = pbf[:, 0:256].rearrange("p (c t) -> p c t", c=2)
            for c in range(2):
                p.do('pe', 'transpose', pv[:, c, :], on[:, 2 * c:2 * c + 2, :].rearrange("p h v -> p (h v)"), k.identb[:], reads=[on, k.identb], writes=[pbf])
            for c in range(2):
                p.do('dve', 'scalar_tensor_tensor', ot[:, c, :], pv[:, c, :], ng[:, 0:1], gz[:, c, rows], ALU.mult, ALU.mult, reads=[pbf, ng, gz], writes=[ot])
            p.dma('sp', s["mixT"][6:8, :, rows].rearrange("c p t -> p c t"), ot[:], reads=[ot])
        end_phase(k)


FULL_CFG = dict(LS=4096, LP=256, NP=2, DEPTH=4, PAST=256, stages="MABC")


def kernel(**inputs):
    cfg = dict(FULL_CFG)
    nc = build(cfg)
    consts = const_tables(cfg)
    w = host_weights(inputs, cfg)
    in_maps = []
    for c in range(NCORES):
        m = {}
        m.update(consts)
        m.update(w)
        m.update(host_core_inputs(inputs, c, cfg))
        in_maps.append(m)
    res = run_bass_kernel_spmd(nc, in_maps, core_ids=list(range(NCORES)))
    R = res.results
    NP, LP, DEPTH = cfg["NP"], cfg["LP"], cfg["DEPTH"]
    y_sample = np.stack([np.asarray(R[c]["y_s"], dtype=np.float32) for c in range(NCORES)], axis=0)
    y_prompt = np.concatenate([np.asarray(R[c]["y_p"], dtype=np.float32).reshape(NP, LP, D) for c in range(NCORES)], axis=0)
    nk = np.concatenate([np.asarray(R[c]["nk"], dtype=np.float32).reshape(NP, DEPTH, LP, 2, 64) for c in range(NCORES)], axis=0)
    nv = np.concatenate([np.asarray(R[c]["nv"], dtype=np.float32).reshape(NP, DEPTH, LP, 2, 64) for c in range(NCORES)], axis=0)
    nssm = np.concatenate([np.asarray(R[c]["nssm"], dtype=np.float32) for c in range(NCORES)], axis=0)
    ngdn = np.concatenate([np.asarray(R[c]["ngdn"], dtype=np.float32) for c in range(NCORES)], axis=0)
    return (y_prompt, y_sample, nk, nv, nssm, ngdn)
```
